# Optimizing a Trainium2 kernel written in Bass

```python
import jax, jax.numpy as jnp
from jax import lax
import numpy as np

D_MODEL = 1024
BATCH = 8
SEQ = 2048
DEPTH = 2

ROPE_THETA = 500000.0
Q_BLOCK = 128
NEG_INF = -1e30
NORM_EPS = 1e-6
POS_OFFSET_MAX = 512
HEAD_DIM = 64
PARTIAL_ROT = HEAD_DIM // 4
N_BRANCH = 4
BRANCH_W = 4 * HEAD_DIM
D_FF = 4 * D_MODEL

MLA_HEADS = 4
MLA_NOPE = 64
MLA_ROPE = 32
MLA_V = 64
MLA_Q_LORA = 256
MLA_KV_LORA = 128

NSA_HEADS = 4
NSA_CMP_LEN = 32
NSA_CMP_STRIDE = 16
NSA_SEL_LEN = 64
NSA_TOP_N = 8
NSA_WINDOW = 512
NSA_N_PATHS = 3
NSA_FORCED_SCORE = 1e4

FOX_HEADS = 4

DSA_HEADS = 4
IDX_HEADS = 8
IDX_DIM = 32
IDX_ROT = IDX_DIM // 4
DSA_TOP_K = 256
DSA_KEEP_DIV = 4

MLA_COLS = MLA_Q_LORA + MLA_KV_LORA + MLA_ROPE
NSA_COLS = NSA_HEADS * HEAD_DIM + 6 * HEAD_DIM + NSA_N_PATHS * NSA_HEADS
FOX_COLS = 3 * FOX_HEADS * HEAD_DIM + FOX_HEADS
DSA_COLS = DSA_HEADS * HEAD_DIM + 2 * HEAD_DIM + IDX_HEADS * IDX_DIM + IDX_DIM + IDX_HEADS
GATE_COLS = N_BRANCH * D_MODEL
IN_COLS = MLA_COLS + NSA_COLS + FOX_COLS + DSA_COLS + GATE_COLS

kernel_name = "hybrid_mla_nsa_fox_dsa_gated_block"


def split_cols(z, sizes):
    cuts = [int(c) for c in np.cumsum(sizes)[:-1]]
    return jnp.split(z, cuts, axis=-1)


def rmsnorm(x, g):
    xf = x.astype(jnp.float32)
    y = xf * lax.rsqrt(jnp.mean(xf * xf, axis=-1, keepdims=True) + NORM_EPS)
    return (y * g.astype(jnp.float32)).astype(x.dtype)


def rope_tables(positions, rot_dim):
    inv = ROPE_THETA ** (-jnp.arange(0, rot_dim, 2, dtype=jnp.float32) / rot_dim)
    ang = positions.astype(jnp.float32)[..., None] * inv
    return (jnp.cos(ang), jnp.sin(ang))


def apply_rope(x, rope):
    cos, sin = rope
    half = cos.shape[-1]
    if x.ndim == 4:
        cos, sin = cos[:, :, None, :], sin[:, :, None, :]
    cos, sin = cos.astype(x.dtype), sin.astype(x.dtype)
    x1, x2 = x[..., :half], x[..., half:2 * half]
    return jnp.concatenate([x1 * cos - x2 * sin, x2 * cos + x1 * sin, x[..., 2 * half:]], axis=-1)


def masked_softmax(s, mask):
    s = jnp.where(mask, s, NEG_INF)
    m = jnp.max(s, axis=-1, keepdims=True)
    p = jnp.exp(s - m) * mask
    return p / jnp.maximum(jnp.sum(p, axis=-1, keepdims=True), 1e-30)


def sweep_query_blocks(fn, seq):
    out = lax.map(fn, jnp.arange(seq // Q_BLOCK))
    out = jnp.moveaxis(out, 0, 1)
    return out.reshape((out.shape[0], seq) + out.shape[3:])


def blocked_causal_attention(q, k, v, scale, decay=None):
    S = q.shape[1]
    key_pos = jnp.arange(S)
    decay_t = None if decay is None else jnp.swapaxes(decay, 1, 2)

    def one_block(i):
        start = i * Q_BLOCK
        qb = lax.dynamic_slice_in_dim(q, start, Q_BLOCK, axis=1)
        s = jnp.einsum('bqhd,bkhd->bhqk', qb, k).astype(jnp.float32) * scale
        if decay_t is not None:
            cq = lax.dynamic_slice_in_dim(decay_t, start, Q_BLOCK, axis=2)
            s = s + (cq[..., :, None] - decay_t[..., None, :])
        q_pos = start + jnp.arange(Q_BLOCK)
        mask = key_pos[None, :] <= q_pos[:, None]
        p = jax.nn.softmax(jnp.where(mask, s, NEG_INF), axis=-1)
        return jnp.einsum('bhqk,bkhd->bqhd', p.astype(v.dtype), v)

    return sweep_query_blocks(one_block, S)


def mla_mixer(z, q_norm_g, w_uq, kv_norm_g, w_ukv, rope):
    B, S, _ = z.shape
    H = MLA_HEADS
    c_q, c_kv, k_rope = split_cols(z, [MLA_Q_LORA, MLA_KV_LORA, MLA_ROPE])
    q = (rmsnorm(c_q, q_norm_g) @ w_uq).reshape(B, S, H, MLA_NOPE + MLA_ROPE)
    q = jnp.concatenate([q[..., :MLA_NOPE], apply_rope(q[..., MLA_NOPE:], rope)], axis=-1)
    kv = (rmsnorm(c_kv, kv_norm_g) @ w_ukv).reshape(B, S, H, MLA_NOPE + MLA_V)
    k_rope = apply_rope(k_rope, rope)
    k = jnp.concatenate([kv[..., :MLA_NOPE],
                         jnp.broadcast_to(k_rope[:, :, None, :], (B, S, H, MLA_ROPE))], axis=-1)
    v = kv[..., MLA_NOPE:]
    o = blocked_causal_attention(q, k, v, (MLA_NOPE + MLA_ROPE) ** -0.5)
    return o.reshape(B, S, H * MLA_V)


def nsa_mixer(z, cmp_w, cmp_pe, rope):
    B, S, _ = z.shape
    H, D = NSA_HEADS, HEAD_DIM
    q, kc, vc, ks, vs, kw, vw, g = split_cols(z, [H * D] + [D] * 6 + [NSA_N_PATHS * H])
    q = apply_rope(q.reshape(B, S, H, D), rope)
    kc, ks, kw = apply_rope(kc, rope), apply_rope(ks, rope), apply_rope(kw, rope)
    g = jax.nn.sigmoid(g.reshape(B, S, NSA_N_PATHS, H))
    scale = D ** -0.5
    t_pos = jnp.arange(S)
    bidx = jnp.arange(B)[:, None, None]

    n_cmp = (S - NSA_CMP_LEN) // NSA_CMP_STRIDE + 1
    cmp_start = jnp.arange(n_cmp) * NSA_CMP_STRIDE
    win_idx = cmp_start[:, None] + jnp.arange(NSA_CMP_LEN)[None, :]

    def compress(t, w, pe):
        blocks = t[:, win_idx] + pe
        return blocks.reshape(B, n_cmp, NSA_CMP_LEN * D) @ w

    k_cmp = compress(kc, cmp_w[0], cmp_pe[0])
    v_cmp = compress(vc, cmp_w[1], cmp_pe[1])
    cmp_mask = (cmp_start + NSA_CMP_LEN - 1)[None, :] <= t_pos[:, None]
    s_cmp = jnp.einsum('bthd,bcd->bhtc', q, k_cmp).astype(jnp.float32) * scale
    p_cmp = masked_softmax(s_cmp, cmp_mask)
    o_cmp = jnp.einsum('bhtc,bcd->bthd', p_cmp.astype(v_cmp.dtype), v_cmp)

    n_sb = S // NSA_SEL_LEN
    sb = jnp.arange(n_sb)
    sb_start = sb * NSA_SEL_LEN
    overlap = jnp.maximum(
        jnp.minimum(cmp_start[:, None] + NSA_CMP_LEN, sb_start[None, :] + NSA_SEL_LEN)
        - jnp.maximum(cmp_start[:, None], sb_start[None, :]), 0).astype(jnp.float32) / NSA_CMP_LEN
    imp = jnp.einsum('bhtc,cj->btj', p_cmp, overlap)
    t_blk = t_pos[:, None] // NSA_SEL_LEN
    forced = (sb[None, :] == 0) | (sb[None, :] == t_blk) | (sb[None, :] == t_blk - 1)
    imp = jnp.where(forced, NSA_FORCED_SCORE, imp)
    imp = jnp.where(sb_start[None, :] <= t_pos[:, None], imp, NEG_INF)
    n_top = min(NSA_TOP_N, n_sb)
    _, sel_idx = lax.top_k(imp, n_top)
    k_blk = ks.reshape(B, n_sb, NSA_SEL_LEN, D)
    v_blk = vs.reshape(B, n_sb, NSA_SEL_LEN, D)

    def sel_block(i):
        start = i * Q_BLOCK
        qb = lax.dynamic_slice_in_dim(q, start, Q_BLOCK, axis=1)
        ib = lax.dynamic_slice_in_dim(sel_idx, start, Q_BLOCK, axis=1)
        kg = k_blk[bidx, ib].reshape(B, Q_BLOCK, n_top * NSA_SEL_LEN, D)
        vg = v_blk[bidx, ib].reshape(B, Q_BLOCK, n_top * NSA_SEL_LEN, D)
        qp = start + jnp.arange(Q_BLOCK)
        kpos = (ib[..., None] * NSA_SEL_LEN + jnp.arange(NSA_SEL_LEN)).reshape(B, Q_BLOCK, -1)
        mask = kpos <= qp[None, :, None]
        s = jnp.einsum('bqhd,bqmd->bhqm', qb, kg).astype(jnp.float32) * scale
        p = masked_softmax(s, mask[:, None])
        return jnp.einsum('bhqm,bqmd->bqhd', p.astype(vg.dtype), vg)

    o_sel = sweep_query_blocks(sel_block, S)

    kpad = jnp.pad(kw, ((0, 0), (NSA_WINDOW, 0), (0, 0)))
    vpad = jnp.pad(vw, ((0, 0), (NSA_WINDOW, 0), (0, 0)))

    def win_block(i):
        start = i * Q_BLOCK
        qb = lax.dynamic_slice_in_dim(q, start, Q_BLOCK, axis=1)
        kb = lax.dynamic_slice_in_dim(kpad, start, NSA_WINDOW + Q_BLOCK, axis=1)
        vb = lax.dynamic_slice_in_dim(vpad, start, NSA_WINDOW + Q_BLOCK, axis=1)
        kpos = start - NSA_WINDOW + jnp.arange(NSA_WINDOW + Q_BLOCK)
        qp = start + jnp.arange(Q_BLOCK)
        mask = ((kpos[None, :] <= qp[:, None]) & (kpos[None, :] > qp[:, None] - NSA_WINDOW)
                & (kpos[None, :] >= 0))
        s = jnp.einsum('bqhd,bkd->bhqk', qb, kb).astype(jnp.float32) * scale
        p = masked_softmax(s, mask)
        return jnp.einsum('bhqk,bkd->bqhd', p.astype(vb.dtype), vb)

    o_win = sweep_query_blocks(win_block, S)

    o = (g[:, :, 0, :, None] * o_cmp + g[:, :, 1, :, None] * o_sel
         + g[:, :, 2, :, None] * o_win)
    return o.reshape(B, S, H * D)


def fox_mixer(z, f_bias):
    B, S, _ = z.shape
    H, D = FOX_HEADS, HEAD_DIM
    q, k, v, f = split_cols(z, [H * D, H * D, H * D, H])
    log_f = jax.nn.log_sigmoid((f + f_bias).astype(jnp.float32))
    c = jnp.cumsum(log_f, axis=1)
    o = blocked_causal_attention(q.reshape(B, S, H, D), k.reshape(B, S, H, D),
                                 v.reshape(B, S, H, D), D ** -0.5, decay=c)
    return o.reshape(B, S, H * D)


def dsa_mixer(z, rope_head, rope_idx):
    B, S, _ = z.shape
    H, D = DSA_HEADS, HEAD_DIM
    q, k, v, qi, ki, w = split_cols(z, [H * D, D, D, IDX_HEADS * IDX_DIM, IDX_DIM, IDX_HEADS])
    q = apply_rope(q.reshape(B, S, H, D), rope_head)
    k = apply_rope(k, rope_head)
    qi = apply_rope(qi.reshape(B, S, IDX_HEADS, IDX_DIM), rope_idx)
    ki = apply_rope(ki, rope_idx)
    n_keep = min(DSA_TOP_K, S // DSA_KEEP_DIV)
    scale = D ** -0.5
    key_pos = jnp.arange(S)
    bidx = jnp.arange(B)[:, None, None]

    def one_block(i):
        start = i * Q_BLOCK
        qb = lax.dynamic_slice_in_dim(q, start, Q_BLOCK, axis=1)
        qib = lax.dynamic_slice_in_dim(qi, start, Q_BLOCK, axis=1)
        wb = lax.dynamic_slice_in_dim(w, start, Q_BLOCK, axis=1).astype(jnp.float32)
        logits = jnp.einsum('bqhd,bkd->bqhk', qib, ki).astype(jnp.float32) * IDX_DIM ** -0.5
        score = jnp.einsum('bqhk,bqh->bqk', jax.nn.relu(logits), wb) * IDX_HEADS ** -0.5
        qp = start + jnp.arange(Q_BLOCK)
        score = jnp.where(key_pos[None, None, :] <= qp[None, :, None], score, NEG_INF)
        _, idx = lax.top_k(score, n_keep)
        kg = k[bidx, idx]
        vg = v[bidx, idx]
        s = jnp.einsum('bqhd,bqkd->bhqk', qb, kg).astype(jnp.float32) * scale
        valid = idx <= qp[None, :, None]
        p = masked_softmax(s, valid[:, None])
        return jnp.einsum('bhqk,bqkd->bqhd', p.astype(vg.dtype), vg)

    o = sweep_query_blocks(one_block, S)
    return o.reshape(B, S, H * D)


def setup_inputs(seed: int = 0) -> dict:
    key = jax.random.key(seed)
    ks = jax.random.split(key, 18)

    def nrm(k, shape, fan_in):
        return jax.random.normal(k, shape, jnp.float32) * fan_in ** -0.5

    def gain(k, shape):
        return 1.0 + 0.02 * jax.random.normal(k, shape, jnp.float32)

    x = jax.random.normal(ks[0], (BATCH, SEQ, D_MODEL), jnp.float32)
    offset = jax.random.randint(ks[1], (BATCH, 1), 0, POS_OFFSET_MAX, dtype=jnp.int32)
    positions = offset + jnp.arange(SEQ, dtype=jnp.int32)[None, :]
    return {
        "x": x,
        "positions": positions,
        "norm1_g": gain(ks[2], (DEPTH, D_MODEL)),
        "w_in": nrm(ks[3], (DEPTH, D_MODEL, IN_COLS), D_MODEL),
        "mla_q_norm_g": gain(ks[4], (DEPTH, MLA_Q_LORA)),
        "mla_w_uq": nrm(ks[5], (DEPTH, MLA_Q_LORA, MLA_HEADS * (MLA_NOPE + MLA_ROPE)), MLA_Q_LORA),
        "mla_kv_norm_g": gain(ks[6], (DEPTH, MLA_KV_LORA)),
        "mla_w_ukv": nrm(ks[7], (DEPTH, MLA_KV_LORA, MLA_HEADS * (MLA_NOPE + MLA_V)), MLA_KV_LORA),
        "nsa_cmp_pe": 0.1 * jax.random.normal(ks[8], (DEPTH, 2, NSA_CMP_LEN, HEAD_DIM), jnp.float32),
        "nsa_cmp_w": nrm(ks[9], (DEPTH, 2, NSA_CMP_LEN * HEAD_DIM, HEAD_DIM), NSA_CMP_LEN * HEAD_DIM),
        "fox_f_bias": 1.0 + 0.1 * jax.random.normal(ks[10], (DEPTH, FOX_HEADS), jnp.float32),
        "w_branch": nrm(ks[11], (DEPTH, N_BRANCH, BRANCH_W, D_MODEL), BRANCH_W),
        "w_out": nrm(ks[12], (DEPTH, D_MODEL, D_MODEL), D_MODEL),
        "norm2_g": gain(ks[13], (DEPTH, D_MODEL)),
        "w_up": nrm(ks[14], (DEPTH, D_MODEL, D_FF), D_MODEL),
        "w_down": nrm(ks[15], (DEPTH, D_FF, D_MODEL), D_FF),
        "final_g": gain(ks[16], (D_MODEL,)),
    }


def reference(x, positions, norm1_g, w_in, mla_q_norm_g, mla_w_uq, mla_kv_norm_g, mla_w_ukv,
              nsa_cmp_pe, nsa_cmp_w, fox_f_bias, w_branch, w_out, norm2_g, w_up, w_down, final_g):
    B, S, _ = x.shape
    rope_mla = rope_tables(positions, MLA_ROPE)
    rope_head = rope_tables(positions, PARTIAL_ROT)
    rope_idx = rope_tables(positions, IDX_ROT)
    for l in range(DEPTH):
        h = rmsnorm(x, norm1_g[l])
        z = h @ w_in[l]
        z_mla, z_nsa, z_fox, z_dsa, z_gate = split_cols(
            z, [MLA_COLS, NSA_COLS, FOX_COLS, DSA_COLS, GATE_COLS])
        o_mla = mla_mixer(z_mla, mla_q_norm_g[l], mla_w_uq[l], mla_kv_norm_g[l], mla_w_ukv[l], rope_mla)
        o_nsa = nsa_mixer(z_nsa, nsa_cmp_w[l], nsa_cmp_pe[l], rope_head)
        o_fox = fox_mixer(z_fox, fox_f_bias[l])
        o_dsa = dsa_mixer(z_dsa, rope_head, rope_idx)
        branches = jnp.stack([o_mla, o_nsa, o_fox, o_dsa], axis=2)
        lifted = jnp.einsum('bsnc,ncd->bsnd', branches, w_branch[l])
        gates = jax.nn.sigmoid(z_gate.reshape(B, S, N_BRANCH, D_MODEL))
        mixed = jnp.sum(gates * lifted, axis=2)
        x = x + mixed @ w_out[l]
        h2 = rmsnorm(x, norm2_g[l])
        x = x + jnp.square(jax.nn.relu(h2 @ w_up[l])) @ w_down[l]
    return rmsnorm(x, final_g)
```

```python
import numpy as np
import concourse.bass as bass
import concourse.mybir as mybir
from concourse.bass_utils import run_bass_kernel_spmd

F32 = mybir.dt.float32
BF16 = mybir.dt.bfloat16
I32 = mybir.dt.int32
ALU = mybir.AluOpType
AF = mybir.ActivationFunctionType
AX = mybir.AxisListType

S_LEN = 2048
NT = 16
D = 1024
KD = 8
DEPTH = 2
NEGB = -30000.0
NDUMMY = 1
NWARM = 0
PI = float(np.pi)


class StopBuild(Exception):
    pass


class Sched:
    limit = None

    def __init__(self, nc, n_dma_sems=24):
        self.nc = nc
        self.eng = {"pe": nc.tensor, "dve": nc.vector, "act": nc.scalar, "pool": nc.gpsimd, "sp": nc.sync}
        self.sem = {k: nc.alloc_semaphore("s_" + k) for k in self.eng}
        self.cnt = {k: 0 for k in self.eng}
        self.dsem = [nc.alloc_semaphore("d%d" % i) for i in range(n_dma_sems)]
        self.dcnt = [0] * n_dma_sems
        half = n_dma_sems // 2
        self.dpool = {"sp": list(range(0, half)), "pool": list(range(half, n_dma_sems))}
        self.dnext = {"sp": 0, "pool": 0}
        self.seen = {k: {} for k in self.eng}
        self.lastw = {}
        self.readers = {}
        self.pending = {k: ([], []) for k in self.eng}
        self.semobj = {}
        self.all_tokens = {}
        self.nwaits = 0
        self.nops = 0

    def _tok_wait(self, e, tok):
        sid, val = tok
        if self.seen[e].get(sid, 0) >= val:
            return
        self.seen[e][sid] = val
        self.eng[e].wait_ge(self.semobj[sid], val)
        self.nwaits += 1

    def _deps(self, e, reads, writes):
        writes = list(writes) + [k for k in reads if k.startswith("ps_")]
        toks = []
        for k in reads:
            t = self.lastw.get(k)
            if t is not None:
                toks.append(t)
        for k in writes:
            t = self.lastw.get(k)
            if t is not None:
                toks.append(t)
            toks.extend(self.readers.get(k, ()))
        own = id(self.sem[e])
        for t in toks:
            if e == "pe" and t[0] == own:
                continue
            self._tok_wait(e, t)

    def _commit(self, tok, reads, writes):
        writes = list(writes) + [k for k in reads if k.startswith("ps_")]
        for k in writes:
            self.lastw[k] = tok
            self.readers[k] = []
        for k in reads:
            lst = self.readers.setdefault(k, [])
            lst.append(tok)
            if len(lst) > 16:
                best = {}
                for s, v in lst:
                    best[s] = max(best.get(s, 0), v)
                self.readers[k] = list(best.items())
        self.all_tokens[tok[0]] = max(self.all_tokens.get(tok[0], 0), tok[1])

    def op(self, e, fn, reads=(), writes=(), sig=True):
        self.nops += 1
        if self.limit is not None and self.nops > self.limit:
            raise StopBuild()
        self._deps(e, reads, writes)
        ins = fn(self.eng[e])
        if not sig:
            pr, pw = self.pending[e]
            pr.extend(reads)
            pw.extend(writes)
            return
        self.cnt[e] += 1
        s = self.sem[e]
        self.semobj[id(s)] = s
        ins.then_inc(s, 1)
        tok = (id(s), self.cnt[e])
        pr, pw = self.pending[e]
        self._commit(tok, list(reads) + pr, list(writes) + pw)
        self.pending[e] = ([], [])

    def dma(self, out, in_, reads=(), writes=(), q="sp", **kw):
        self.nops += 1
        if self.limit is not None and self.nops > self.limit:
            raise StopBuild()
        self._deps(q, reads, writes)
        lst = self.dpool[q]
        i = lst[self.dnext[q] % len(lst)]
        self.dnext[q] += 1
        s = self.dsem[i]
        self.semobj[id(s)] = s
        self.dcnt[i] += 16
        self.eng[q].dma_start(out=out, in_=in_, **kw).then_inc(s, 16)
        tok = (id(s), self.dcnt[i])
        self._commit(tok, reads, writes)
        return tok

    def barrier(self):
        for e in self.eng:
            for sid, val in list(self.all_tokens.items()):
                self._tok_wait(e, (sid, val))
        self.lastw = {}
        self.readers = {}

    def wait_all(self, e="sp"):
        for sid, val in list(self.all_tokens.items()):
            self._tok_wait(e, (sid, val))


def make_consts():
    c = {}
    k = np.arange(128)[:, None]
    q = np.arange(128)[None, :]
    c["c_ident"] = np.eye(128, dtype=np.float32)
    c["c_caust"] = np.where(q >= k, 0.0, NEGB).astype(np.float32)
    c["c_wint"] = np.where(k > q, 0.0, NEGB).astype(np.float32)
    c["c_causqk"] = np.where(q <= k, 0.0, -1e30).astype(np.float32)
    cc = np.arange(128)[:, None]
    t = np.arange(S_LEN)[None, :]
    cmpb = np.where((16 * cc + 31 <= t) & (cc < 127), 0.0, NEGB).astype(np.float32)
    c["c_cmpbias"] = cmpb
    j = np.arange(32)[:, None]
    kk = np.arange(S_LEN)[None, :]
    c["c_E"] = (kk // 64 == j).astype(np.float32)
    cs = np.arange(128) * 16
    sb = np.arange(32) * 64
    ov = np.maximum(np.minimum(cs[:, None] + 32, sb[None, :] + 64) - np.maximum(cs[:, None], sb[None, :]), 0) / 32.0
    ov[127, :] = 0.0
    c["c_ovl"] = ov.astype(np.float32)
    tt = np.arange(S_LEN)
    tb = tt[:, None] // 64
    sbi = np.arange(32)[None, :]
    forced = (sbi == 0) | (sbi == tb) | (sbi == tb - 1)
    causal = (sbi * 64) <= tt[:, None]
    base = np.where(causal, np.where(forced, 1e4, 0.0), -1e30).astype(np.float32)
    keep = np.where(causal & ~forced, 1.0, 0.0).astype(np.float32)
    c["c_fbase"] = base.reshape(NT, 128, 32).transpose(1, 0, 2).copy()
    c["c_fkeep"] = keep.reshape(NT, 128, 32).transpose(1, 0, 2).copy()
    theta = np.float32(500000.0)
    def inv(rot):
        return (theta ** (-np.arange(0, rot, 2, dtype=np.float32) / np.float32(rot))).astype(np.float32)
    invs = np.concatenate([inv(32), inv(16), inv(8)]).astype(np.float32)
    c["c_inv"] = np.tile(invs[None, :], (128, 1)).astype(np.float32)
    c["c_pow2"] = np.tile((2.0 ** -np.arange(32, dtype=np.float32))[None, :], (128, 1)).astype(np.float32)
    c["c_U"] = (k <= q).astype(np.float32)
    c["c_ones"] = np.ones((128, 128), np.float32)
    return c


CONST_SHAPES = {k: v.shape for k, v in make_consts().items()}

IN_SHAPES = {
    "x": ([S_LEN, D], F32), "pos": ([128, NT], I32),
    "norm1_g": ([DEPTH, D], F32), "w_in": ([DEPTH, D, 6616], F32),
    "mla_q_norm_g": ([DEPTH, 256], F32), "mla_w_uq": ([DEPTH, 256, 384], F32),
    "mla_kv_norm_g": ([DEPTH, 128], F32), "mla_w_ukv": ([DEPTH, 128, 512], F32),
    "nsa_cmp_pe": ([DEPTH, 2, 2048], F32), "nsa_cmp_w": ([DEPTH, 2, 2048, 64], F32),
    "fox_f_bias": ([DEPTH, 4], F32), "w_branch": ([DEPTH, 4, 256, D], F32),
    "w_out": ([DEPTH, D, D], F32), "norm2_g": ([DEPTH, D], F32),
    "w_up": ([DEPTH, D, 4096], F32), "w_down": ([DEPTH, 4096, D], F32), "final_g": ([1, D], F32),
}

MLA0 = 0
NSA0 = 416
FOX0 = NSA0 + 652
DSA0 = FOX0 + 772
GATE0 = DSA0 + 680


def build_program(depth=DEPTH, debug=None, mixers=("mla", "nsa", "fox", "dsa"), stop=None, limit=None):
    nc = bass.Bass("TRN2", target_bir_lowering=False)
    S = Sched(nc)
    S.limit = limit
    dbg = {}
    try:
        _build_body(nc, S, dbg, depth, debug, mixers, stop)
    except StopBuild:
        S.limit = None
        print("stopped at limit", limit, flush=True)
    S.wait_all("sp")
    print("program built: ops", S.nops, "waits", S.nwaits, flush=True)
    return nc, dbg


def _build_body(nc, S, dbg, depth, debug, mixers, stop):
    dr = {}
    for name, (shape, dt) in IN_SHAPES.items():
        dr[name] = nc.dram_tensor(name, shape, dt, kind="ExternalInput").ap()
    for name, shape in CONST_SHAPES.items():
        dr[name] = nc.dram_tensor(name, list(shape), F32, kind="ExternalInput").ap()
    y = nc.dram_tensor("y", [S_LEN, D], F32, kind="ExternalOutput").ap()
    xs = nc.dram_tensor("xs", [S_LEN, D], F32, kind="Internal").ap()

    BASE = 16512
    KB = 1024

    def A(name, shape, dt, off):
        assert off % 32 == 0, (name, off)
        nbytes = int(np.prod(shape[1:])) * (2 if dt == BF16 else 4)
        assert off + nbytes <= 207 * KB + 512, (name, off, nbytes)
        return nc.alloc_sbuf_tensor_at(name, list(shape), dt, offset=BASE + off)

    def dump(name, ap, shape, reads):
        if debug is None or name not in debug:
            return
        t = nc.dram_tensor("dbg_" + name, list(shape), ap.dtype if hasattr(ap, "dtype") else F32, kind="ExternalOutput").ap()
        S.dma(t, ap, reads=reads, writes=["dbg_" + name])
        dbg[name] = t

    o = 0
    def CA(name, shape, dt):
        nonlocal o
        t = A(name, shape, dt, o)
        o += ((int(np.prod(shape[1:])) * (2 if dt == BF16 else 4) + 31) // 32) * 32
        return t
    identb = CA("identb", [128, 128], BF16)
    identf = CA("identf", [128, 128], F32)
    caust = CA("caust", [128, 128], BF16)
    wint = CA("wint", [128, 128], BF16)
    causqk = CA("causqk", [128, 128], F32)
    cmpbias = CA("cmpbias", [128, S_LEN], BF16)
    Emat = CA("Emat", [32, S_LEN], BF16)
    ovl = CA("ovl", [128, 32], BF16)
    fbase = CA("fbase", [128, NT, 32], F32)
    fkeep = CA("fkeep", [128, NT, 32], F32)
    invt = CA("invt", [128, 28], F32)
    pow2 = CA("pow2", [128, 32], F32)
    Umat = CA("Umat", [128, 128], F32)
    onesf = CA("onesf", [128, 128], F32)
    trig = CA("trig", [128, NT, 56], F32)
    posi = CA("posi", [128, NT], I32)
    posf = CA("posf", [128, NT], F32)
    cst = CA("cst", [128, 8], F32)
    g2 = CA("g2", [128, D], F32)
    assert o <= 24 * KB, o
    for dst, src, q in [(identb, "c_ident", "pool"), (identf, "c_ident", "sp"), (caust, "c_caust", "pool"), (wint, "c_wint", "pool"),
                        (causqk, "c_causqk", "sp"), (cmpbias, "c_cmpbias", "pool"), (Emat, "c_E", "pool"), (ovl, "c_ovl", "pool"),
                        (fbase, "c_fbase", "sp"), (fkeep, "c_fkeep", "sp"), (invt, "c_inv", "sp"), (pow2, "c_pow2", "sp"),
                        (Umat, "c_U", "sp"), (onesf, "c_ones", "sp")]:
        S.dma(dst[:], dr[src], writes=["const"], q=q)
    S.dma(posi[:], dr["pos"], writes=["const"])
    S.op("dve", lambda e: e.memset(cst[:, 0:1], 1e-6), writes=["const"])
    S.op("dve", lambda e: e.memset(cst[:, 1:2], 1.0), writes=["const"])
    S.op("dve", lambda e: e.memset(cst[:, 2:3], 0.0), writes=["const"])
    S.op("dve", lambda e: e.memset(cst[:, 3:4], 1e-30), writes=["const"])

    ACT0 = 24 * KB
    actT = A("actT", [128, KD, S_LEN], BF16, ACT0)
    OFF0 = 56 * KB

    ps_st = [nc.alloc_psum_tensor("ps_st%d" % i, [128, 512], F32) for i in range(2)]
    ps_acc = [nc.alloc_psum_tensor("ps_acc%d" % i, [128, 4, 128], F32) for i in range(2)]
    ps_mm = [nc.alloc_psum_tensor("ps_mm%d" % i, [128, 512], F32) for i in range(2)]
    ps_trs = [nc.alloc_psum_tensor("ps_tr%d" % i, [128, 8, 128], BF16) for i in range(2)]
    ps_x = ps_trs[1][:].rearrange("p a b -> p (a b)").bitcast(F32)
    PX = "ps_tr1"

    with_scr = A("ropescr", [128, NT, 56], F32, OFF0)
    kfi = A("ropekfi", [128, NT, 56], I32, OFF0 + 4 * KB)
    kff = A("ropekff", [128, NT, 56], F32, OFF0 + 8 * KB)
    ang = A("ropeang", [128, NT, 56], F32, OFF0 + 12 * KB)
    S.op("dve", lambda e: e.tensor_copy(out=posf[:], in_=posi[:]), reads=["const"], writes=["posf"])
    S.op("dve", lambda e: e.tensor_tensor(out=ang[:, :, 0:28], in0=posf[:].unsqueeze(2).to_broadcast([128, NT, 28]),
                                          in1=invt[:].unsqueeze(1).to_broadcast([128, NT, 28]), op=ALU.mult), reads=["posf", "const"], writes=["ang"])
    S.op("dve", lambda e: e.tensor_scalar(out=ang[:, :, 28:56], in0=ang[:, :, 0:28], scalar1=PI / 2, scalar2=None, op0=ALU.add), reads=["ang"], writes=["ang"])
    S.op("dve", lambda e: e.tensor_scalar(out=with_scr[:], in0=ang[:], scalar1=float(1.0 / (2 * np.pi)), scalar2=None, op0=ALU.mult), reads=["ang"], writes=["rscr"])
    S.op("dve", lambda e: e.tensor_copy(out=kfi[:], in_=with_scr[:]), reads=["rscr"], writes=["kfi"])
    S.op("dve", lambda e: e.tensor_copy(out=kff[:], in_=kfi[:]), reads=["kfi"], writes=["kff"])
    C1 = 6.28125
    C2 = float(2 * np.pi - 6.28125)
    S.op("dve", lambda e: e.scalar_tensor_tensor(out=with_scr[:], in0=kff[:], scalar=-C1, in1=ang[:], op0=ALU.mult, op1=ALU.add), reads=["kff", "ang"], writes=["rscr"])
    S.op("dve", lambda e: e.scalar_tensor_tensor(out=ang[:], in0=kff[:], scalar=-C2, in1=with_scr[:], op0=ALU.mult, op1=ALU.add), reads=["kff", "rscr"], writes=["ang"])
    S.op("dve", lambda e: e.tensor_scalar(out=ang[:], in0=ang[:], scalar1=-3.1415925, scalar2=3.1415925, op0=ALU.max, op1=ALU.min), reads=["ang"], writes=["ang"])
    S.op("act", lambda e: e.activation(out=trig[:], in_=ang[:], func=AF.Sin, bias=cst[:, 2:3], scale=1.0), reads=["ang", "const"], writes=["trig"])
    dump("trig", trig[:], [128, NT, 56], ["trig"])
    SIN = lambda t, a, b: trig[:, t, a:b]
    COS = lambda t, a, b: trig[:, t, 28 + a:28 + b]
    S.barrier()
    if stop == "p0":
        print("stop", stop, S.nops)
        return

    def rmsnorm_tile(src_ap, n, g_ap, dst_ap, scr, tag, rkeys, wkeys, src_keys):
        ss, sq = scr
        S.op("act", lambda e: e.activation(out=sq, in_=src_ap, func=AF.Square, accum_out=ss[:, 0:1]), reads=src_keys, writes=[tag + "sq", tag + "ss"])
        S.op("act", lambda e: e.activation(out=ss[:, 1:2], in_=ss[:, 0:1], func=AF.Sqrt, bias=cst[:, 0:1], scale=1.0 / n), reads=[tag + "ss"], writes=[tag + "ss1"])
        S.op("dve", lambda e: e.reciprocal(out=ss[:, 2:3], in_=ss[:, 1:2]), reads=[tag + "ss1"], writes=[tag + "ss2"])
        S.op("dve", lambda e: e.scalar_tensor_tensor(out=dst_ap, in0=src_ap, scalar=ss[:, 2:3], in1=g_ap, op0=ALU.mult, op1=ALU.mult),
             reads=src_keys + [tag + "ss2"] + rkeys, writes=wkeys)

    tr_i = [0]

    def transposes(dst_ap, srcs, np_out, rkeys, wkeys, eng="act", bank=None):
        n = len(srcs)
        if bank is None:
            bank = tr_i[0] % 2
            tr_i[0] += 1
        ps_tr = ps_trs[bank]
        tk = "ps_tr%d" % bank
        for j, s_ap in enumerate(srcs):
            w = s_ap.shape[-1]
            S.op("pe", lambda e, j=j, s_ap=s_ap, w=w: e.transpose(out=ps_tr[0:w, j, :], in_=s_ap, identity=identb[:]),
                 reads=rkeys + ["const"], writes=[tk], sig=(j == n - 1))
        if eng == "act":
            S.op("act", lambda e: e.copy(out=dst_ap, in_=ps_tr[0:np_out, 0:n, :]), reads=[tk], writes=wkeys)
        else:
            S.op("dve", lambda e: e.tensor_copy(out=dst_ap, in_=ps_tr[0:np_out, 0:n, :]), reads=[tk], writes=wkeys)

    def rope(dst1, dst2, x1, x2, cos, sin, shape, tA, tB, rkeys, wkeys, tag):
        cb = cos.unsqueeze(1).to_broadcast(shape) if len(shape) == 3 else cos
        sb = sin.unsqueeze(1).to_broadcast(shape) if len(shape) == 3 else sin
        S.op("dve", lambda e: e.tensor_tensor(out=tA, in0=x1, in1=cb, op=ALU.mult), reads=rkeys + ["trig"], writes=[tag + "A"])
        S.op("dve", lambda e: e.tensor_tensor(out=tB, in0=x2, in1=sb, op=ALU.mult), reads=rkeys + ["trig"], writes=[tag + "B"])
        S.op("dve", lambda e: e.tensor_tensor(out=dst1, in0=tA, in1=tB, op=ALU.subtract), reads=[tag + "A", tag + "B"], writes=wkeys)
        S.op("dve", lambda e: e.tensor_tensor(out=tA, in0=x2, in1=cb, op=ALU.mult), reads=rkeys + ["trig"], writes=[tag + "A"])
        S.op("dve", lambda e: e.tensor_tensor(out=tB, in0=x1, in1=sb, op=ALU.mult), reads=rkeys + ["trig"], writes=[tag + "B"])
        S.op("dve", lambda e: e.tensor_tensor(out=dst2, in0=tA, in1=tB, op=ALU.add), reads=[tag + "A", tag + "B"], writes=wkeys)

    def load_w_cols(dst, l, segs, wkey):
        for (dc, sc, n) in segs:
            src = dr["w_in"][l, :, sc:sc + n].rearrange("(k p) c -> p k c", p=128)
            for kh in range(2):
                S.dma(dst[:, kh * 4:(kh + 1) * 4, dc:dc + n], src[:, kh * 4:(kh + 1) * 4, :], writes=[wkey], q="pool")

    def project_tile(t, Wz, col_groups, wkey):
        outs = []
        for gi, (c0, n) in enumerate(col_groups):
            if t % 2 == 0:
                pt, pk = ps_mm[gi], "ps_mm%d" % gi
            else:
                pt, pk = ps_st[gi], "ps_st%d" % gi
            for k in range(KD):
                S.op("pe", lambda e, k=k, pt=pt, c0=c0, n=n: e.matmul(pt[:, 0:n], lhsT=actT[:, k, t * 128:(t + 1) * 128], rhs=Wz[:, k, c0:c0 + n],
                                                                     start=(k == 0), stop=(k == KD - 1)),
                     reads=["actT", wkey], writes=[pk], sig=(k == KD - 1))
            outs.append((pt, pk))
        return outs

    def warm(n, bank=0):
        if not NDUMMY:
            return
        dout = ps_acc[bank][:].rearrange("p a b -> p (a b)")
        for _ in range(n):
            S.op("pe", lambda e: e.matmul(dout, lhsT=identb[:], rhs=cmpbias[:, 0:512], start=True, stop=True, skip_group_check=True), sig=False)

    PT = [A("PT%d" % i, [128, 512], BF16, OFF0 + 32 * KB + 13 * KB + i * KB) for i in range(4)]
    misc = A("misc", [128, 512], F32, OFF0 + 32 * KB + 17 * KB)
    WS0 = OFF0 + 32 * KB + 19 * KB
    st_ctr = [0]

    def attn_core(tag, qT, kT, vaug, scale, steps_fn, bias_fn, fin_fn, rkeys, nk=128, vw=65, range_bias_fn=None, ndummy=0, dummy_out=None):
        steps = []
        for c in range(4):
            ss = steps_fn(c)
            contrib = {}
            for (kt, qlo, qhi) in ss:
                for qt in range(qlo, qhi):
                    contrib.setdefault(qt, []).append(kt)
            for si, (kt, qlo, qhi) in enumerate(ss):
                steps.append((c, kt, qlo, qhi, contrib, si == len(ss) - 1, si == 0))

        def issue_st(step, slot):
            c, kt, qlo, qhi, contrib, last, _f = step
            st = ps_st[slot % 2]
            skey = "ps_st%d" % (slot % 2)
            n = (qhi - qlo) * 128
            bl = []
            for qt in range(qlo, qhi):
                for (bl_l, bl_r, bkeys) in bias_fn(kt, qt):
                    bl.append(((qt - qlo) * 128, (qt - qlo + 1) * 128, bl_l, bl_r, bkeys))
            if range_bias_fn is not None:
                for (bl_l, bl_r, bkeys) in range_bias_fn(kt, qlo, qhi):
                    bl.append((0, n, bl_l, bl_r, bkeys))
            S.op("pe", lambda e: e.matmul(st[0:nk, 0:n], lhsT=kT(kt), rhs=qT(qlo * 128, qhi * 128), start=True, stop=(len(bl) == 0), skip_group_check=True),
                 reads=rkeys, writes=[skey], sig=(not bl))
            for bi, (c0, c1, bl_l, bl_r, bkeys) in enumerate(bl):
                S.op("pe", lambda e, c0=c0, c1=c1, bl_l=bl_l, bl_r=bl_r, bi=bi: e.matmul(st[0:nk, c0:c1], lhsT=bl_l, rhs=bl_r, start=False, stop=(bi == len(bl) - 1), skip_group_check=True),
                     reads=bkeys, writes=[skey], sig=(bi == len(bl) - 1))
            for _ in range(ndummy):
                dout = ps_mm[0][:, 0:512] if dummy_out is None else dummy_out
                S.op("pe", lambda e: e.matmul(dout, lhsT=identb[:], rhs=cmpbias[:, 0:512], start=True, stop=True, skip_group_check=True), sig=False)

        def issue_rest(step, slot):
            c, kt, qlo, qhi, contrib, last, _f = step
            st = ps_st[slot % 2]
            skey = "ps_st%d" % (slot % 2)
            n = (qhi - qlo) * 128
            pi = slot % 4
            pt = PT[pi]
            S.op("act", lambda e: e.activation(out=pt[0:nk, 0:n], in_=st[0:nk, 0:n], func=AF.Exp, scale=scale), reads=[skey], writes=["PT%d" % pi])
            acc = ps_acc[c % 2]
            for qt in range(qlo, qhi):
                first = (step[6] and qt == qlo)
                S.op("pe", lambda e, qt=qt, first=first: e.matmul(acc[:, qt - 4 * c, 0:vw], lhsT=pt[0:nk, (qt - qlo) * 128:(qt - qlo + 1) * 128], rhs=vaug(kt),
                                                       start=first, stop=(kt == contrib[qt][-1]), skip_group_check=True),
                     reads=["PT%d" % pi] + rkeys, writes=["ps_acc%d" % (c % 2)], sig=(qt == qhi - 1))
            if last:
                fin_fn(c, acc, "ps_acc%d" % (c % 2))

        base = st_ctr[0]
        for i, step in enumerate(steps):
            if i == 0:
                issue_st(step, base)
            if i + 1 < len(steps):
                issue_st(steps[i + 1], base + i + 1)
            issue_rest(step, base + i)
        st_ctr[0] = base + len(steps)

    def causal_steps(c):
        return [(kt, max(4 * c, kt), 4 * c + 4) for kt in range(4 * c + 4)]

    def causal_bias(kt, qt):
        if kt == qt:
            return [(identb[:], caust[:], ["const"])]
        return []

    def std_fin(o_tm, h, okey, gate=None, accumulate=False, vdim=64):
        def fin(c, acc, akey):
            rec = misc[:, 64:68]
            S.op("dve", lambda e: e.tensor_scalar(out=rec, in0=acc[:, :, vdim], scalar1=1e-30, scalar2=None, op0=ALU.max), reads=[akey], writes=["rec"])
            S.op("dve", lambda e: e.reciprocal(out=misc[:, 68:72], in_=rec), reads=["rec"], writes=["rec2"])
            r2 = misc[:, 68:72]
            rk = ["rec2"]
            if gate is not None:
                gap, gkey = gate(c)
                S.op("dve", lambda e: e.tensor_tensor(out=misc[:, 72:76], in0=r2, in1=gap, op=ALU.mult), reads=["rec2", gkey], writes=["rec3"])
                r2 = misc[:, 72:76]
                rk = ["rec3"]
            dst = o_tm[:, 4 * c:4 * c + 4, h * 64:(h + 1) * 64]
            if not accumulate:
                S.op("dve", lambda e: e.tensor_tensor(out=dst, in0=acc[:, :, 0:64], in1=r2.unsqueeze(2).to_broadcast([128, 4, 64]), op=ALU.mult),
                     reads=[akey] + rk, writes=[okey])
            else:
                tmp = misc[:, 128:384].rearrange("p (a b) -> p a b", a=4)
                S.op("dve", lambda e: e.tensor_tensor(out=tmp, in0=acc[:, :, 0:64], in1=r2.unsqueeze(2).to_broadcast([128, 4, 64]), op=ALU.mult),
                     reads=[akey] + rk, writes=["fintmp"])
                S.op("pool", lambda e: e.tensor_tensor(out=dst, in0=dst, in1=tmp, op=ALU.add), reads=["fintmp", okey], writes=[okey])
        return fin

    oT = A("oT", [128, 4, 2, S_LEN], BF16, OFF0)

    def finish_mixer(mi, o_tm, okey):
        for t in range(NT):
            transposes(oT[:, mi, :, t * 128:(t + 1) * 128], [o_tm[:, t, 0:128], o_tm[:, t, 128:256]], 128, [okey], ["oT"], eng=("act" if t % 2 else "dve"))

    for l in range(depth):
        x_src = dr["x"] if l == 0 else xs
        gbc = A("gbc", [128, D], F32, OFF0)
        xt = [A("xt%d" % i, [128, D], F32, OFF0 + 4 * KB + i * 4 * KB) for i in range(4)]
        hb = [A("hb%d" % i, [128, D], BF16, OFF0 + 20 * KB + i * 2 * KB) for i in range(3)]
        sqs_ = [A("sqs%d" % i, [128, D], BF16, OFF0 + 26 * KB + i * 2 * KB) for i in range(2)]
        nst_ = [A("nst%d" % i, [128, 8], F32, OFF0 + 30 * KB + i * 32) for i in range(2)]
        S.dma(gbc[:], dr["norm1_g"][l:l + 1, :].to_broadcast([128, D]), writes=["gbc"])

        def p1_load(t):
            S.dma(xt[t % 4][:], x_src[t * 128:(t + 1) * 128, :], writes=["xt%d" % (t % 4)])

        def p1_norm(t):
            rmsnorm_tile(xt[t % 4][:], D, gbc[:], hb[t % 3][:], (nst_[t % 2], sqs_[t % 2][:]), "n1_%d" % (t % 2), ["gbc"], ["hb%d" % (t % 3)], ["xt%d" % (t % 4)])

        def p1_tr(t):
            transposes(actT[:, :, t * 128:(t + 1) * 128], [hb[t % 3][:, k * 128:(k + 1) * 128] for k in range(KD)], 128, ["hb%d" % (t % 3)], ["actT"], eng=("act" if t % 2 else "dve"))
        p1_load(0)
        p1_load(1)
        for step in range(NT + 1):
            if step + 2 < NT:
                p1_load(step + 2)
            if step < NT:
                p1_norm(step)
            if step >= 1:
                p1_tr(step - 1)
        if l == 0:
            dump("hT", actT[:], [128, KD, S_LEN], ["actT"])
        S.barrier()

        if stop == "p1":
            print("stop", stop, S.nops)
            return
        Wz = A("Wz", [128, KD, 776], BF16, OFF0 + 32 * KB)

        if "mla" in mixers:
            o = WS0
            cqnT = A("cqnT", [128, 2, S_LEN], BF16, o); o += 8 * KB
            ckvnT = A("ckvnT", [128, S_LEN], BF16, o); o += 4 * KB
            wuq = A("wuq", [128, 2, 384], BF16, o); o += 1536
            wukv = A("wukv", [128, 512], BF16, o); o += 1024
            QT = A("mQT", [96, 4, S_LEN], BF16, o); o += 16 * KB
            KTt = A("mKT", [96, 4, S_LEN], BF16, o); o += 16 * KB
            Vg = A("mV", [128, NT, 4, 66], BF16, o); o += NT * 4 * 66 * 2
            o = (o + 31) // 32 * 32
            qtm_ = [A("mqtm%d" % i, [128, 4, 96], BF16, o + i * 768) for i in range(2)]; o += 1536
            ktm_ = [A("mktm%d" % i, [128, 4, 96], BF16, o + i * 768) for i in range(3)]; o += 2304
            cqn_ = [A("mcqn%d" % i, [128, 256], BF16, o + i * 512) for i in range(2)]; o += 1024
            ckvn_ = [A("mckvn%d" % i, [128, 128], BF16, o + i * 256) for i in range(2)]; o += 512
            gq = A("mgq", [128, 256], F32, o); o += 1024
            gkv = A("mgkv", [128, 128], F32, o); o += 512
            rA_ = [A("mrA%d" % i, [128, 64], F32, o + i * 256) for i in range(2)]; o += 512
            rB_ = [A("mrB%d" % i, [128, 64], F32, o + i * 256) for i in range(2)]; o += 512
            sq2_ = [A("msq%d" % i, [128, 256], F32, o + i * 1024) for i in range(2)]; o += 2048
            nst2_ = [A("mnst%d" % i, [128, 8], F32, o + i * 32) for i in range(2)]; o += 64
            o_tm = A("mo_tm", [128, NT, 256], BF16, o); o += 8 * KB
            load_w_cols(Wz, l, [(0, MLA0, 416)], "Wz")
            S.dma(wuq[:], dr["mla_w_uq"][l].rearrange("(k p) c -> p k c", p=128), writes=["wuq"], q="pool")
            S.dma(wukv[:], dr["mla_w_ukv"][l], writes=["wukv"], q="pool")
            S.dma(gq[:], dr["mla_q_norm_g"][l:l + 1, :].to_broadcast([128, 256]), writes=["gq"])
            S.dma(gkv[:], dr["mla_kv_norm_g"][l:l + 1, :].to_broadcast([128, 128]), writes=["gkv"])
            S.op("pool", lambda e: e.memset(Vg[:, :, :, 64:65], 1.0), writes=["mV"])
            zs_ = [A("mzs%d" % i, [128, 416], F32, o + i * 1664) for i in range(2)]; o += 3328
            qs_ = [A("mqs%d" % i, [128, 384], F32, o + i * 1536) for i in range(2)]; o += 3072
            kvs_ = [A("mkvs%d" % i, [128, 512], F32, o + i * 2048) for i in range(2)]; o += 4096

            def mla_vars(t):
                p = t % 2
                return p, str(p), qtm_[p], ktm_[t % 3], cqn_[p], ckvn_[p], rA_[p], rB_[p], sq2_[p], nst2_[p]

            def mla_A1(t):
                p, sp, qtm, ktm, cqn, ckvn, rA, rB, sq2, nst2 = mla_vars(t)
                zs = zs_[p]
                zk = "mzs" + sp
                ((pz, pzk),) = project_tile(t, Wz, [(0, 416)], "Wz")
                S.op("act", lambda e: e.copy(out=zs[:], in_=pz[:, 0:416]), reads=[pzk], writes=[zk])

            def mla_A1b(t):
                p, sp, qtm, ktm, cqn, ckvn, rA, rB, sq2, nst2 = mla_vars(t)
                zs = zs_[p]
                zk = "mzs" + sp
                rmsnorm_tile(zs[:, 0:256], 256, gq[:], cqn[:], (nst2, sq2[:]), "mq" + sp, ["gq"], ["cqn" + sp], [zk])
                rmsnorm_tile(zs[:, 256:384], 128, gkv[:], ckvn[:], (nst2, sq2[:, 0:128]), "mq" + sp, ["gkv"], ["ckvn" + sp], [zk])
                rope(ktm[:, 0, 64:80], ktm[:, 0, 80:96], zs[:, 384:400], zs[:, 400:416], COS(t, 0, 16), SIN(t, 0, 16), [128, 16],
                     rA[:, 0:16], rB[:, 0:16], [zk], ["ktm%d" % (t % 3)], "mr" + sp)
                S.op("pool", lambda e: e.tensor_copy(out=ktm[:, 1:4, 64:96], in_=ktm[:, 0:1, 64:96].to_broadcast([128, 3, 32])), reads=["ktm%d" % (t % 3)], writes=["ktm%d" % (t % 3)])

            def mla_A2(t):
                p, sp, qtm, ktm, cqn, ckvn, rA, rB, sq2, nst2 = mla_vars(t)
                transposes(cqnT[:, :, t * 128:(t + 1) * 128], [cqn[:, 0:128], cqn[:, 128:256]], 128, ["cqn" + sp], ["cqnT%d" % p])
                transposes(ckvnT[:, t * 128:(t + 1) * 128].unsqueeze(1), [ckvn[:]], 128, ["ckvn" + sp], ["ckvnT%d" % p])
                pq, pqk = (ps_mm[1], "ps_mm1") if p == 0 else (ps_st[1], "ps_st1")
                for kk in range(2):
                    S.op("pe", lambda e, kk=kk: e.matmul(pq[:, 0:384], lhsT=cqnT[:, kk, t * 128:(t + 1) * 128], rhs=wuq[:, kk, :], start=(kk == 0), stop=(kk == 1)),
                         reads=["cqnT%d" % p, "wuq"], writes=[pqk], sig=(kk == 1))
                S.op("act", lambda e: e.copy(out=qs_[p][:], in_=pq[:, 0:384]), reads=[pqk], writes=["mqs" + sp])
                pkv = ps_acc[p][:].rearrange("p a b -> p (a b)")
                pkk = "ps_acc%d" % p
                S.op("pe", lambda e: e.matmul(pkv[:, 0:512], lhsT=ckvnT[:, t * 128:(t + 1) * 128], rhs=wukv[:], start=True, stop=True),
                     reads=["ckvnT%d" % p, "wukv"], writes=[pkk])
                S.op("act", lambda e: e.copy(out=kvs_[p][:], in_=pkv[:, 0:512]), reads=[pkk], writes=["mkvs" + sp])

            def mla_B(t):
                p, sp, qtm, ktm, cqn, ckvn, rA, rB, sq2, nst2 = mla_vars(t)
                pq3 = qs_[p][:].rearrange("p (h d) -> p h d", h=4)
                pqk = "mqs" + sp
                S.op("act", lambda e: e.copy(out=qtm[:, :, 0:64], in_=pq3[:, :, 0:64]), reads=[pqk], writes=["qtmN" + sp])
                rope(qtm[:, :, 64:80], qtm[:, :, 80:96], pq3[:, :, 64:80], pq3[:, :, 80:96], COS(t, 0, 16), SIN(t, 0, 16), [128, 4, 16],
                     rA[:].rearrange("p (h d) -> p h d", h=4), rB[:].rearrange("p (h d) -> p h d", h=4), [pqk], ["qtm" + sp], "mr" + sp)
                transposes(QT[:, :, t * 128:(t + 1) * 128], [qtm[:, h, :] for h in range(4)], 96, ["qtm" + sp, "qtmN" + sp], ["mQT"], eng="act")
                pkv3 = kvs_[p][:].rearrange("p (h d) -> p h d", h=4)
                pkk = "mkvs" + sp
                S.op("dve", lambda e: e.tensor_copy(out=ktm[:, :, 0:64], in_=pkv3[:, :, 0:64]), reads=[pkk], writes=["ktmN%d" % (t % 3)])
                S.op("dve", lambda e: e.tensor_copy(out=Vg[:, t, :, 0:64], in_=pkv3[:, :, 64:128]), reads=[pkk], writes=["mV"])
                transposes(KTt[:, :, t * 128:(t + 1) * 128], [ktm[:, h, :] for h in range(4)], 96, ["ktm%d" % (t % 3), "ktmN%d" % (t % 3)], ["mKT"])
            for step in range(NT + 2):
                if step < NT:
                    mla_A1(step)
                if 1 <= step <= NT:
                    mla_A2(step - 1)
                if step >= 2:
                    mla_B(step - 2)
                if step < NT:
                    mla_A1b(step)
            if stop == "mla_prep":
                dump("trig", QT[:], [96, 4, S_LEN], ["mQT"]) if False else None
                print("stop", stop, S.nops)
                return
            for h in range(4):
                attn_core("mla", lambda q0, q1, h=h: QT[:, h, q0:q1], lambda kt, h=h: KTt[:, h, kt * 128:(kt + 1) * 128],
                          lambda kt, h=h: Vg[:, kt, h, 0:65], float(96 ** -0.5), causal_steps, causal_bias,
                          std_fin(o_tm, h, "mo_tm"), ["mQT", "mKT", "mV"], ndummy=NDUMMY)
            if l == 0:
                dump("o_mla", o_tm[:], [128, NT, 256], ["mo_tm"])
            finish_mixer(0, o_tm, "mo_tm")
            S.barrier()

        if "fox" in mixers:
            o = WS0
            QT = A("fQT", [70, 4, S_LEN], BF16, o); o += 16 * KB
            KTt = A("fKT", [70, 4, S_LEN], BF16, o); o += 16 * KB
            Vg = A("fV", [128, NT, 4, 66], BF16, o); o += NT * 4 * 66 * 2
            o = (o + 31) // 32 * 32
            qk_tm = A("fqk_tm", [128, NT, 2, 4, 70], BF16, o); o += NT * 2 * 4 * 70 * 2
            o = (o + 31) // 32 * 32
            logf = A("flogf", [128, NT, 4], F32, o); o += 256
            cum = A("fcum", [128, NT, 4], F32, o); o += 256
            tot = A("ftot", [128, NT, 4], F32, o); o += 256
            car = A("fcar", [128, NT, 4], F32, o); o += 256
            fb = A("ffb", [128, 4], F32, o); o += 32
            ftmp = A("fftmp", [128, NT, 4], F32, o); o += 256
            chi = A("fchi", [128, NT, 4], BF16, o); o += 128
            cmid = A("fcmid", [128, NT, 4], BF16, o); o += 128
            clo = A("fclo", [128, NT, 4], BF16, o); o += 128
            r1 = A("fr1", [128, NT, 4], F32, o); o += 256
            r2_ = A("fr2", [128, NT, 4], F32, o); o += 256
            o_tm = A("fo_tm", [128, NT, 256], BF16, o); o += 8 * KB
            load_w_cols(Wz, l, [(0, FOX0, 772)], "Wz")
            S.dma(fb[:], dr["fox_f_bias"][l:l + 1, :].to_broadcast([128, 4]), writes=["ffb"])
            S.op("pool", lambda e: e.memset(Vg[:, :, :, 64:65], 1.0), writes=["fV"])
            S.op("pool", lambda e: e.memset(qk_tm[:, :, 0, :, 67:70], 1.0), writes=["fqk_tm"])
            S.op("pool", lambda e: e.memset(qk_tm[:, :, 1, :, 64:67], 1.0), writes=["fqk_tm"])
            for t in range(NT):
                (pa, pak), (pb, pbk) = project_tile(t, Wz, [(0, 512), (512, 260)], "Wz")
                warm(NWARM)
                pa4 = pa[:, 0:512].rearrange("p (a h d) -> p a h d", a=2, h=4)
                S.op("act", lambda e: e.copy(out=qk_tm[:, t, :, :, 0:64], in_=pa4), reads=[pak], writes=["fqk_tm"])
                S.op("dve", lambda e: e.tensor_copy(out=Vg[:, t, :, 0:64], in_=pb[:, 0:256].rearrange("p (h d) -> p h d", h=4)), reads=[pbk], writes=["fV"])
                S.op("dve", lambda e: e.tensor_tensor(out=ftmp[:, t, :], in0=pb[:, 256:260], in1=fb[:], op=ALU.add), reads=[pbk, "ffb"], writes=["fftmp"])
            S.op("act", lambda e: e.activation(out=logf[:], in_=ftmp[:], func=AF.Exp, scale=-1.0), reads=["fftmp"], writes=["flogf"])
            S.op("act", lambda e: e.activation(out=logf[:], in_=logf[:], func=AF.Ln, bias=cst[:, 1:2], scale=1.0), reads=["flogf", "const"], writes=["flogf"])
            S.op("dve", lambda e: e.tensor_scalar(out=logf[:], in0=logf[:], scalar1=-1.0, scalar2=None, op0=ALU.mult), reads=["flogf"], writes=["flogf"])
            pc = ps_x
            S.op("pe", lambda e: e.matmul(pc[:, 0:64], lhsT=Umat[:], rhs=logf[:].rearrange("p a b -> p (a b)"), start=True, stop=True), reads=["flogf", "const"], writes=[PX])
            S.op("dve", lambda e: e.tensor_copy(out=cum[:].rearrange("p a b -> p (a b)"), in_=pc[:, 0:64]), reads=[PX], writes=["fcum"])
            S.op("pe", lambda e: e.matmul(pc[:, 0:64], lhsT=onesf[:], rhs=logf[:].rearrange("p a b -> p (a b)"), start=True, stop=True), reads=["flogf", "const", "fcum"], writes=[PX])
            S.op("dve", lambda e: e.tensor_copy(out=tot[:].rearrange("p a b -> p (a b)"), in_=pc[:, 0:64]), reads=[PX], writes=["ftot"])
            S.op("dve", lambda e: e.memset(car[:, 0, :], 0.0), writes=["fcar"])
            for t in range(1, NT):
                S.op("dve", lambda e, t=t: e.tensor_tensor(out=car[:, t, :], in0=car[:, t - 1, :], in1=tot[:, t - 1, :], op=ALU.add), reads=["fcar", "ftot"], writes=["fcar"])
            S.op("dve", lambda e: e.tensor_tensor(out=cum[:], in0=cum[:], in1=car[:], op=ALU.add), reads=["fcum", "fcar"], writes=["fcum"])
            if l == 0:
                dump("fox_c", cum[:], [128, NT, 4], ["fcum"])
            S.op("dve", lambda e: e.tensor_scalar(out=r1[:], in0=cum[:], scalar1=8.0, scalar2=None, op0=ALU.mult), reads=["fcum"], writes=["fr1"])
            S.op("dve", lambda e: e.tensor_copy(out=chi[:], in_=r1[:]), reads=["fr1"], writes=["fchi"])
            S.op("dve", lambda e: e.tensor_tensor(out=r2_[:], in0=r1[:], in1=chi[:], op=ALU.subtract), reads=["fr1", "fchi"], writes=["fr2"])
            S.op("dve", lambda e: e.tensor_copy(out=cmid[:], in_=r2_[:]), reads=["fr2"], writes=["fcmid"])
            S.op("dve", lambda e: e.tensor_tensor(out=r1[:], in0=r2_[:], in1=cmid[:], op=ALU.subtract), reads=["fr2", "fcmid"], writes=["fr1"])
            S.op("dve", lambda e: e.tensor_copy(out=clo[:], in_=r1[:]), reads=["fr1"], writes=["fclo"])
            for j, part in enumerate([chi, cmid, clo]):
                S.op("dve", lambda e, j=j, part=part: e.tensor_copy(out=qk_tm[:, :, 0, :, 64 + j], in_=part[:]), reads=["fchi", "fcmid", "fclo"], writes=["fqk_tm"])
                S.op("dve", lambda e, j=j, part=part: e.tensor_scalar(out=qk_tm[:, :, 1, :, 67 + j], in0=part[:], scalar1=-1.0, scalar2=None, op0=ALU.mult),
                     reads=["fchi", "fcmid", "fclo"], writes=["fqk_tm"])
            for t in range(NT):
                transposes(QT[:, :, t * 128:(t + 1) * 128], [qk_tm[:, t, 0, h, :] for h in range(4)], 70, ["fqk_tm"], ["fQT"], eng="act")
                transposes(KTt[:, :, t * 128:(t + 1) * 128], [qk_tm[:, t, 1, h, :] for h in range(4)], 70, ["fqk_tm"], ["fKT"])
            for h in range(4):
                attn_core("fox", lambda q0, q1, h=h: QT[:, h, q0:q1], lambda kt, h=h: KTt[:, h, kt * 128:(kt + 1) * 128],
                          lambda kt, h=h: Vg[:, kt, h, 0:65], 0.125, causal_steps, causal_bias,
                          std_fin(o_tm, h, "fo_tm"), ["fQT", "fKT", "fV"], ndummy=NDUMMY)
            if l == 0:
                dump("o_fox", o_tm[:], [128, NT, 256], ["fo_tm"])
            finish_mixer(2, o_tm, "fo_tm")
            S.barrier()

        if "dsa" in mixers:
            o = WS0
            QK3 = A("dQK3", [128, 3, S_LEN], BF16, o); o += 12 * KB
            QT2 = QK3[:, 0:2, :]
            KT2 = QK3[:, 2, :]
            Vg = A("dV", [128, NT, 66], BF16, o); o += NT * 66 * 2
            o = (o + 31) // 32 * 32
            qki = A("dqki", [96, 4, S_LEN], BF16, o); o += 16 * KB
            qiT = qki[:, 0:3, :]
            kiT = qki[:, 3, :]
            score = [A("dscore%d" % i, [128, S_LEN], F32, o + i * 8 * KB) for i in range(4)]; o += 32 * KB
            Mb = [A("dMb0", [128, 4, 1536], BF16, o), A("dMb1", [128, 4, S_LEN], BF16, o + 12 * KB)]; o += 28 * KB
            o_c = [A("do_c%d" % i, [128, 4, 256], BF16, o + i * 2 * KB) for i in range(2)]; o += 4 * KB
            qk6 = A("dqk6", [128, 6, 64], BF16, o); o += 768
            qi9 = A("dqi9", [128, 12, 32], BF16, o); o += 768
            wq = A("dwq", [128, NT, 8], F32, o); o += 512
            rA = A("drA", [128, 9, 8], F32, o); o += 288
            rB = A("drB", [128, 9, 8], F32, o); o += 288
            bs = A("dbs", [128, 2, 64], F32, o); o += 512
            ki3 = A("dki3", [128, 96], BF16, o); o += 192
            junk1 = A("djunk1", [128, 16], BF16, o); o += 32
            wzo = OFF0 + 32 * KB
            Rb = [A("dR%d" % i, [128, 512], BF16, wzo + i * KB) for i in range(4)]
            diag = [A("ddiag%d" % i, [128, 8, 128], BF16, wzo + 4 * KB + i * 2 * KB) for i in range(2)]
            load_w_cols(Wz, l, [(0, DSA0, 680)], "Wz")
            S.op("pool", lambda e: e.memset(Vg[:, :, 64:65], 1.0), writes=["dV"])
            S.op("pool", lambda e: e.memset(qi9[:], 0.0), writes=["dqi90", "dqi9N0"])
            qk6_ = [qk6, A("dqk6b", [128, 6, 64], BF16, o)]; o += 768
            qi9_ = [qi9, A("dqi9b", [128, 12, 32], BF16, o)]; o += 768
            ki3_ = [ki3, A("dki3b", [128, 96], BF16, o)]; o += 192
            rA_ = [rA, A("drAb", [128, 9, 8], F32, o)]; o += 288
            rB_ = [rB, A("drBb", [128, 9, 8], F32, o)]; o += 288
            S.op("pool", lambda e: e.memset(qi9_[1][:], 0.0), writes=["dqi91", "dqi9N1"])
            ptr0 = OFF0 + 32 * KB + 13 * KB
            zsA_ = [A("dzsA%d" % i, [128, 384], F32, ptr0 + i * 1536) for i in range(2)]
            zsB_ = [A("dzsB%d" % i, [128, 296], F32, ptr0 + 3072 + i * 1184) for i in range(2)]

            def dsa_A(t):
                p = t % 2
                sp = str(p)
                qk6, qi9, ki3, rA, rB = qk6_[p], qi9_[p], ki3_[p], rA_[p], rB_[p]
                (pa, pak0), (pb, pbk0) = project_tile(t, Wz, [(0, 384), (384, 296)], "Wz")
                warm(NWARM)
                za, zb = zsA_[p], zsB_[p]
                pak, pbk = "dzsA" + sp, "dzsB" + sp
                S.op("act", lambda e: e.copy(out=za[:], in_=pa[:, 0:384]), reads=[pak0], writes=[pak])
                S.op("act", lambda e: e.copy(out=zb[:], in_=pb[:, 0:296]), reads=[pbk0], writes=[pbk])
                pa3 = za[:, 0:320].rearrange("p (h d) -> p h d", h=5)
                rope(qk6[:, 0:5, 0:8], qk6[:, 0:5, 8:16], pa3[:, :, 0:8], pa3[:, :, 8:16], COS(t, 16, 24), SIN(t, 16, 24), [128, 5, 8],
                     rA[:, 0:5, :], rB[:, 0:5, :], [pak], ["dqk6" + sp], "dr" + sp)
                S.op("act", lambda e: e.copy(out=qk6[:, 0:5, 16:64], in_=pa3[:, :, 16:64]), reads=[pak], writes=["dqk6N" + sp])
                S.op("dve", lambda e: e.tensor_copy(out=Vg[:, t, 0:64], in_=za[:, 320:384]), reads=[pak], writes=["dV"])
                S.op("pool", lambda e: e.tensor_copy(out=qk6[:, 5, :], in_=qk6[:, 4, :]), reads=["dqk6" + sp, "dqk6N" + sp], writes=["dqk6D" + sp])
                pb3 = zb[:, 0:288].rearrange("p (h d) -> p h d", h=9)
                rope(qi9[:, 0:9, 0:4], qi9[:, 0:9, 4:8], pb3[:, :, 0:4], pb3[:, :, 4:8], COS(t, 24, 28), SIN(t, 24, 28), [128, 9, 4],
                     rA[:, :, 0:4], rB[:, :, 0:4], [pbk], ["dqi9" + sp], "dr" + sp)
                S.op("act", lambda e: e.copy(out=qi9[:, 0:9, 8:32], in_=pb3[:, :, 8:32]), reads=[pbk], writes=["dqi9N" + sp])
                S.op("dve", lambda e: e.tensor_copy(out=wq[:, t, :], in_=zb[:, 288:296]), reads=[pbk], writes=["dwq"])
                S.op("pool", lambda e: e.tensor_copy(out=ki3[:].rearrange("p (a b) -> p a b", a=3), in_=qi9[:, 8:9, :].to_broadcast([128, 3, 32])), reads=["dqi9" + sp, "dqi9N" + sp], writes=["dki3" + sp])

            def dsa_B(t):
                p = t % 2
                sp = str(p)
                qk6, qi9, ki3, rA, rB = qk6_[p], qi9_[p], ki3_[p], rA_[p], rB_[p]
                qf = qk6[:].rearrange("p a b -> p (a b)")
                transposes(QK3[:, :, t * 128:(t + 1) * 128], [qf[:, 0:128], qf[:, 128:256], qf[:, 256:384]], 128, ["dqk6" + sp, "dqk6N" + sp, "dqk6D" + sp], ["dQT", "dKT"], eng="act")
                qflat = qi9[:].rearrange("p a b -> p (a b)")
                transposes(qki[:, :, t * 128:(t + 1) * 128], [qflat[:, 0:96], qflat[:, 96:192], qflat[:, 192:288], ki3[:]], 96,
                           ["dqi9" + sp, "dqi9N" + sp, "dki3" + sp], ["dqiT", "dkiT"], eng="act")
                warm(NWARM)
            for step in range(NT + 1):
                if step < NT:
                    dsa_A(step)
                if step >= 1:
                    dsa_B(step - 1)
            S.barrier()
            NIT = 21
            ddum = ps_trs[0][:].rearrange("p a b -> p (a b)").bitcast(F32)
            pairs = [(8, 9), (10, 11), (12, 13), (14, 15), (4, 5), (6, 7), (2, 3)]
            pair_pos = {qt: i for i, pr_ in enumerate(pairs) for qt in pr_}
            CB = {2: 0, 3: 1, 1: 0, 0: 1}

            def sbuf(qt):
                i = 2 * (pair_pos[qt] % 2) + (qt % 2)
                return score[i], "dscore%d" % i

            def dsa_scores(pair):
                for qt in pair:
                    L = (qt + 1) * 128
                    sc, sk = sbuf(qt)
                    dg = diag[qt % 2]
                    dk = "ddiag%d" % (qt % 2)
                    S.op("dve", lambda e: e.tensor_tensor(out=dg[:], in0=identb[:].unsqueeze(1).to_broadcast([128, 8, 128]),
                                                          in1=wq[:, qt, :].unsqueeze(2).to_broadcast([128, 8, 128]), op=ALU.mult), reads=["const", "dwq"], writes=[dk])
                    nkc = (L + 511) // 512
                    for kc in range(nkc):
                        k0 = kc * 512
                        n = min(512, L - k0)

                        def logit(h):
                            g, jj = divmod(h, 3)
                            pl = ps_mm[h % 2]
                            S.op("pe", lambda e: e.matmul(pl[:, 0:n], lhsT=qiT[32 * jj:32 * jj + 32, g, qt * 128:(qt + 1) * 128],
                                                          rhs=kiT[32 * jj:32 * jj + 32, k0:k0 + n], start=True, stop=True),
                                 reads=["dqiT", "dkiT"], writes=["ps_mm%d" % (h % 2)])
                            r = Rb[h % 4]
                            S.op("act", lambda e: e.activation(out=r[:, 0:n], in_=pl[:, 0:n], func=AF.Relu), reads=["ps_mm%d" % (h % 2)], writes=["dR%d" % (h % 4)])

                        def hsum(h):
                            r = Rb[h % 4]
                            S.op("pe", lambda e: e.matmul(ps_x[:, 0:n], lhsT=dg[:, h, :], rhs=r[:, 0:n], start=(h == 0), stop=(h == 7)),
                                 reads=["dR%d" % (h % 4), dk], writes=[PX])
                        logit(0)
                        for h in range(8):
                            if h + 1 < 8:
                                logit(h + 1)
                            hsum(h)
                            if h % 2 == 1 and NDUMMY:
                                S.op("pe", lambda e: e.matmul(ddum, lhsT=identb[:], rhs=cmpbias[:, 0:512], start=True, stop=True, skip_group_check=True), sig=False)
                        S.op("act", lambda e: e.copy(out=sc[:, k0:k0 + n], in_=ps_x[:, 0:n]), reads=[PX], writes=[sk])

            def dsa_bisect(pair, pi):
                st = {}
                for j, qt in enumerate(pair):
                    L = (qt + 1) * 128
                    sc, sk = sbuf(qt)
                    b = bs[:, j, :]
                    kx = "b%d_" % j
                    S.op("dve", lambda e, b=b, sc=sc, L=L: e.tensor_reduce(out=b[:, 0:1], in_=sc[:, 0:L], axis=AX.X, op=ALU.max, apply_absolute_value=True), reads=[sk], writes=[kx + "M"])
                    S.op("pool", lambda e, sc=sc, L=L: e.tensor_tensor(out=sc[:, L - 128:L], in0=sc[:, L - 128:L], in1=causqk[:], op=ALU.add), reads=[sk, "const", kx + "M"], writes=[sk])
                    S.op("pool", lambda e, b=b: e.tensor_scalar(out=b[:, 8:8 + NIT + 1], in0=pow2[:, 0:NIT + 1], scalar1=b[:, 0:1], scalar2=None, op0=ALU.mult), reads=[kx + "M", "const"], writes=[kx + "d"])
                    S.op("pool", lambda e, b=b: e.memset(b[:, 4:5], 0.0), writes=[kx + "mid0"])
                    st[qt] = (b, kx, sc, sk, L)
                for it in range(NIT):
                    for j, qt in enumerate(pair):
                        b, kx, sc, sk, L = st[qt]
                        m = b[:, 4 + (it % 2):5 + (it % 2)]
                        nm = b[:, 4 + ((it + 1) % 2):5 + ((it + 1) % 2)]
                        mk, nmk = kx + "mid%d" % (it % 2), kx + "mid%d" % ((it + 1) % 2)
                        S.op("dve", lambda e, m=m, sc=sc, L=L, b=b, j=j: e.tensor_scalar(out=junk1[:, j:j + 1].to_broadcast([128, L]), in0=sc[:, 0:L], scalar1=m, scalar2=0.0, op0=ALU.is_ge, op1=ALU.add,
                                                                                accum_out=b[:, 6:7]), reads=[sk, mk], writes=[kx + "cnt", kx + "junk"])
                        S.op("pool", lambda e, b=b, it=it: e.tensor_scalar(out=b[:, 7:8], in0=b[:, 6:7], scalar1=255.5, scalar2=b[:, 8 + it:9 + it], op0=ALU.is_ge, op1=ALU.mult),
                             reads=[kx + "cnt", kx + "d"], writes=[kx + "sel"])
                        S.op("pool", lambda e, b=b, it=it, m=m, nm=nm: e.tensor_scalar(out=nm, in0=b[:, 7:8], scalar1=m, scalar2=b[:, 9 + it:10 + it], op0=ALU.add, op1=ALU.subtract),
                             reads=[kx + "sel", mk, kx + "d"], writes=[nmk])
                for j, qt in enumerate(pair):
                    b, kx, sc, sk, L = st[qt]
                    c = qt // 4
                    mb = Mb[CB[c]]
                    fm = b[:, 4 + (NIT % 2):5 + (NIT % 2)]
                    S.op("pool", lambda e, b=b, fm=fm: e.tensor_tensor(out=b[:, 3:4], in0=fm, in1=b[:, 8 + NIT:9 + NIT], op=ALU.subtract), reads=[kx + "mid%d" % (NIT % 2), kx + "d"], writes=[kx + "thr"])
                    S.op("dve", lambda e, b=b, sc=sc, L=L, mb=mb, qt=qt, c=c: e.tensor_scalar(out=mb[:, qt - 4 * c, 0:L], in0=sc[:, 0:L], scalar1=b[:, 3:4], scalar2=NEGB, op0=ALU.is_lt, op1=ALU.mult),
                         reads=[sk, kx + "thr"], writes=["dMb%d" % CB[c]])

            def dsa_attn(c):
                mb = Mb[CB[c]]
                mbk = "dMb%d" % CB[c]
                oc = o_c[CB[c]]
                ock = "do_c%d" % CB[c]

                def dsa_bias(kt, qt):
                    if qt < 2:
                        return causal_bias(kt, qt)
                    return [(mb[:, qt - 4 * c, kt * 128:(kt + 1) * 128], identb[:], [mbk, "const"])]

                def fin_for(h):
                    def fin(cc, acc, akey):
                        rl_ = misc[:, 80:84]
                        rc_ = misc[:, 84:88]
                        S.op("act", lambda e: e.activation(out=rl_, in_=acc[:, :, 64], func=AF.Ln, bias=cst[:, 3:4], scale=1.0), reads=[akey, "const"], writes=["arec"])
                        S.op("act", lambda e: e.activation(out=rc_, in_=rl_, func=AF.Exp, scale=-1.0), reads=["arec"], writes=["arec2"])
                        for j in range(4):
                            S.op("act", lambda e, j=j: e.activation(out=oc[:, j, h * 64:(h + 1) * 64], in_=acc[:, j, 0:64], func=AF.Copy, scale=rc_[:, j:j + 1]),
                                 reads=[akey, "arec2"], writes=[ock])
                    return fin
                for h in range(4):
                    p0 = (h % 2) * 64
                    attn_core("dsa", lambda q0, q1, h=h, p0=p0: QT2[p0:p0 + 64, h // 2, q0:q1], lambda kt, p0=p0: KT2[p0:p0 + 64, kt * 128:(kt + 1) * 128],
                              lambda kt: Vg[:, kt, 0:65], 0.125, lambda cc: causal_steps(c) if cc == c else [], dsa_bias,
                              fin_for(h), ["dQT", "dKT", "dV"], ndummy=NDUMMY, dummy_out=ddum)
                for tt in range(4):
                    t = 4 * c + tt
                    transposes(oT[:, 3, :, t * 128:(t + 1) * 128], [oc[:, tt, 0:128], oc[:, tt, 128:256]], 128, [ock], ["oT"], eng="act", bank=1)
                if l == 0:
                    dump("o_dsa%d" % c, oc[:], [128, 4, 256], [ock])

            dsa_scores(pairs[0])
            for i, pr_ in enumerate(pairs):
                if i + 1 < len(pairs):
                    dsa_scores(pairs[i + 1])
                dsa_bisect(pr_, i)
                if pr_[1] % 4 == 3:
                    dsa_attn(pr_[1] // 4)
            S.barrier()

        if "nsa" in mixers:
            o = WS0
            QT = A("nQT", [96, 4, S_LEN], BF16, o); o += 16 * KB
            k4T = A("nk4T", [96, 4, S_LEN], BF16, o); o += 16 * KB
            kcT = k4T[:, 0, :]
            ksT = k4T[:, 1, :]
            kwT = k4T[:, 2, :]
            vcT = k4T[:, 3, :]
            Vs = A("nVs", [128, NT, 66], BF16, o); o += NT * 66 * 2
            Vw = A("nVw", [128, NT, 66], BF16, o); o += NT * 66 * 2
            o = (o + 31) // 32 * 32
            Wk = A("nWk", [64, 32, 64], BF16, o); o += 4 * KB
            Wv = A("nWv", [64, 32, 64], BF16, o); o += 4 * KB
            Wkf = A("nWkf", [128, 16, 64], BF16, o); o += 2 * KB
            Wvf = A("nWvf", [128, 16, 64], BF16, o); o += 2 * KB
            pek = A("npek", [128, 16], BF16, o); o += 32
            pev = A("npev", [128, 16], BF16, o); o += 32
            kcmpT = A("nkcmpT", [64, 128], BF16, o); o += 256
            vcx = A("nvcx", [128, 97], BF16, o); o += 224
            imp = A("nimp", [128, NT, 32], F32, o); o += 2 * KB
            blkb = A("nblkb", [128, 96], BF16, o); o += 192
            gt = A("ngt", [128, NT, 12], F32, o); o += 768
            oacc = A("noacc", [128, NT, 256], F32, o); o += 16 * KB
            o_tm = A("no_tm", [128, NT, 256], BF16, o); o += 8 * KB
            q7 = A("nq7", [128, 7, 64], BF16, o); o += 896
            vc_tm = A("nvc_tm", [128, 64], BF16, o); o += 128
            rA = A("nrA", [128, 7, 8], F32, o); o += 224
            rB = A("nrB", [128, 7, 8], F32, o); o += 224
            m8 = A("nm8", [128, 8], F32, o); o += 32
            itmp = A("nitmp", [128, 4, 32], F32, o); o += 512
            n0 = NSA0
            load_w_cols(Wz, l, [(0, n0, 320), (320, n0 + 384, 64), (384, n0 + 512, 64),
                                (448, n0 + 320, 64), (512, n0 + 448, 64), (576, n0 + 576, 64), (640, n0 + 640, 12)], "Wz")
            S.dma(Wk[:], dr["nsa_cmp_w"][l, 0].rearrange("(l d) o -> d l o", d=64), writes=["nWk"], q="pool")
            S.dma(Wv[:], dr["nsa_cmp_w"][l, 1].rearrange("(l d) o -> d l o", d=64), writes=["nWv"], q="pool")
            S.dma(Wkf[:], dr["nsa_cmp_w"][l, 0].rearrange("(j p) o -> p j o", p=128), writes=["nWkf"], q="pool")
            S.dma(Wvf[:], dr["nsa_cmp_w"][l, 1].rearrange("(j p) o -> p j o", p=128), writes=["nWvf"], q="pool")
            S.dma(pek[:], dr["nsa_cmp_pe"][l, 0].rearrange("(j p) -> p j", p=128), writes=["npek"], q="pool", allow_slow_non_contiguous=True)
            S.dma(pev[:], dr["nsa_cmp_pe"][l, 1].rearrange("(j p) -> p j", p=128), writes=["npev"], q="pool", allow_slow_non_contiguous=True)
            S.dma(ksT[64:96, :], dr["c_E"], writes=["nksT"], q="pool")
            S.op("pool", lambda e: e.memset(blkb[:], 0.0), writes=["nblkb"])
            S.op("pool", lambda e: e.memset(Vs[:, :, 64:65], 1.0), writes=["nVs"])
            S.op("pool", lambda e: e.memset(Vw[:, :, 64:65], 1.0), writes=["nVw"])
            q7_ = [q7, A("nq7b", [128, 7, 64], BF16, o)]; o += 896
            vc_tm_ = [vc_tm, A("nvc_tmb", [128, 64], BF16, o)]; o += 128
            rA_ = [rA, A("nrAb", [128, 7, 8], F32, o)]; o += 224
            rB_ = [rB, A("nrBb", [128, 7, 8], F32, o)]; o += 224
            zsA_ = [A("nzsA%d" % i, [128, 448], F32, o + i * 1792) for i in range(2)]; o += 3584
            zsB_ = [A("nzsB%d" % i, [128, 204], F32, o + i * 832) for i in range(2)]; o += 1664

            def nsa_A(t):
                p = t % 2
                sp = str(p)
                q7, vc_tm, rA, rB = q7_[p], vc_tm_[p], rA_[p], rB_[p]
                (pa, pak0), (pb, pbk0) = project_tile(t, Wz, [(0, 448), (448, 204)], "Wz")
                warm(NWARM)
                za, zb = zsA_[p], zsB_[p]
                pak, pbk = "nzsA" + sp, "nzsB" + sp
                S.op("act", lambda e: e.copy(out=za[:], in_=pa[:, 0:448]), reads=[pak0], writes=[pak])
                S.op("act", lambda e: e.copy(out=zb[:], in_=pb[:, 0:204]), reads=[pbk0], writes=[pbk])
                pa3 = za[:].rearrange("p (h d) -> p h d", h=7)
                rope(q7[:, :, 0:8], q7[:, :, 8:16], pa3[:, :, 0:8], pa3[:, :, 8:16], COS(t, 16, 24), SIN(t, 16, 24), [128, 7, 8],
                     rA[:], rB[:], [pak], ["nq7" + sp], "nr" + sp)
                S.op("act", lambda e: e.copy(out=q7[:, :, 16:64], in_=pa3[:, :, 16:64]), reads=[pak], writes=["nq7N" + sp])
                S.op("dve", lambda e: e.tensor_copy(out=vc_tm[:], in_=zb[:, 0:64]), reads=[pbk], writes=["nvc_tm" + sp])
                S.op("dve", lambda e: e.tensor_copy(out=Vs[:, t, 0:64], in_=zb[:, 64:128]), reads=[pbk], writes=["nVs"])
                S.op("dve", lambda e: e.tensor_copy(out=Vw[:, t, 0:64], in_=zb[:, 128:192]), reads=[pbk], writes=["nVw"])
                S.op("act", lambda e: e.activation(out=gt[:, t, :], in_=zb[:, 192:204], func=AF.Sigmoid), reads=[pbk], writes=["ngt"])

            def nsa_B(t):
                p = t % 2
                sp = str(p)
                q7, vc_tm, rA, rB = q7_[p], vc_tm_[p], rA_[p], rB_[p]
                transposes(QT[0:64, :, t * 128:(t + 1) * 128], [q7[:, h, :] for h in range(4)], 64, ["nq7" + sp, "nq7N" + sp], ["nQT"], eng="act")
                ts_ = slice(t * 128, (t + 1) * 128)
                transposes(k4T[0:64, :, ts_], [q7[:, 4, :], q7[:, 5, :], q7[:, 6, :], vc_tm[:]], 64, ["nq7" + sp, "nq7N" + sp, "nvc_tm" + sp],
                           ["nkcT", "nksT", "nkwT", "nvcT"], eng="act")
                warm(NWARM)
            for step in range(NT + 1):
                if step < NT:
                    nsa_A(step)
                if step >= 1:
                    nsa_B(step - 1)
            pk = ps_mm[0]
            for li in range(32):
                S.op("pe", lambda e, li=li: e.matmul(pk[0:64, 0:127], lhsT=Wk[:, li, :], rhs=kcT[0:64, li:li + 16 * 126 + 1:16], start=(li == 0), stop=False),
                     reads=["nWk", "nkcT"], writes=["ps_mm0"], sig=False)
            for j in range(16):
                S.op("pe", lambda e, j=j: e.matmul(pk[0:64, 0:127], lhsT=Wkf[:, j, :], rhs=pek[:, j:j + 1].to_broadcast([128, 127]), start=False, stop=(j == 15)),
                     reads=["nWkf", "npek"], writes=["ps_mm0"], sig=(j == 15))
            S.op("dve", lambda e: e.tensor_copy(out=kcmpT[:, 0:127], in_=pk[0:64, 0:127]), reads=["ps_mm0"], writes=["nkcmpT"])
            pv = ps_mm[1]
            for li in range(32):
                S.op("pe", lambda e, li=li: e.matmul(pv[0:127, 0:64], lhsT=vcT[0:64, li:li + 16 * 126 + 1:16], rhs=Wv[:, li, :], start=(li == 0), stop=False),
                     reads=["nWv", "nvcT"], writes=["ps_mm1"], sig=False)
            for j in range(16):
                S.op("pe", lambda e, j=j: e.matmul(pv[0:127, 0:64], lhsT=pev[:, j:j + 1].to_broadcast([128, 127]), rhs=Wvf[:, j, :], start=False, stop=(j == 15)),
                     reads=["nWvf", "npev"], writes=["ps_mm1"], sig=(j == 15))
            S.op("pool", lambda e: e.memset(vcx[:, 64:65], 1.0), writes=["nvcx"])
            S.op("dve", lambda e: e.tensor_copy(out=vcx[0:127, 0:64], in_=pv[0:127, 0:64]), reads=["ps_mm1"], writes=["nvcx"])
            S.op("pool", lambda e: e.tensor_copy(out=vcx[:, 65:97], in_=ovl[:]), reads=["const"], writes=["nvcx"])
            if l == 0:
                dump("nsa_kcmpT", kcmpT[:], [64, 128], ["nkcmpT"])
                dump("nsa_vcx", vcx[:], [128, 97], ["nvcx"])

            def gate_fn(path, h):
                return lambda c: (gt[:, 4 * c:4 * c + 4, path * 4 + h], "ngt")
            for h in range(4):
                def cmp_fin(c, acc, akey, h=h):
                    rec = misc[:, 64:68]
                    S.op("dve", lambda e: e.tensor_scalar(out=rec, in0=acc[:, :, 64], scalar1=1e-30, scalar2=None, op0=ALU.max), reads=[akey], writes=["rec"])
                    S.op("dve", lambda e: e.reciprocal(out=misc[:, 68:72], in_=rec), reads=["rec"], writes=["rec2"])
                    S.op("dve", lambda e: e.tensor_tensor(out=misc[:, 72:76], in0=misc[:, 68:72], in1=gt[:, 4 * c:4 * c + 4, h], op=ALU.mult), reads=["rec2", "ngt"], writes=["rec3"])
                    dst = oacc[:, 4 * c:4 * c + 4, h * 64:(h + 1) * 64]
                    S.op("dve", lambda e: e.tensor_tensor(out=dst, in0=acc[:, :, 0:64], in1=misc[:, 72:76].unsqueeze(2).to_broadcast([128, 4, 64]), op=ALU.mult),
                         reads=[akey, "rec3"], writes=["noacc"])
                    idst = imp[:, 4 * c:4 * c + 4, :]
                    if h == 0:
                        S.op("dve", lambda e: e.tensor_tensor(out=idst, in0=acc[:, :, 65:97], in1=misc[:, 68:72].unsqueeze(2).to_broadcast([128, 4, 32]), op=ALU.mult),
                             reads=[akey, "rec2"], writes=["nimp"])
                    else:
                        S.op("dve", lambda e: e.tensor_tensor(out=itmp[:], in0=acc[:, :, 65:97], in1=misc[:, 68:72].unsqueeze(2).to_broadcast([128, 4, 32]), op=ALU.mult),
                             reads=[akey, "rec2"], writes=["nitmp"])
                        S.op("pool", lambda e: e.tensor_tensor(out=idst, in0=idst, in1=itmp[:], op=ALU.add), reads=["nitmp", "nimp"], writes=["nimp"])
                attn_core("ncmp", lambda q0, q1, h=h: QT[0:64, h, q0:q1], lambda kt: kcmpT[:, 0:127], lambda kt: vcx[0:127, 0:97], 0.125,
                          lambda c: [(0, 4 * c, 4 * c + 4)],
                          lambda kt, qt: [],
                          cmp_fin, ["nQT", "nkcmpT", "nvcx"], nk=127, vw=97,
                          range_bias_fn=lambda kt, qlo, qhi: [(identb[0:127, 0:127], cmpbias[0:127, qlo * 128:qhi * 128], ["const"])])
            S.op("dve", lambda e: e.tensor_tensor(out=imp[:], in0=imp[:], in1=fkeep[:], op=ALU.mult), reads=["nimp", "const"], writes=["nimp"])
            S.op("dve", lambda e: e.tensor_tensor(out=imp[:], in0=imp[:], in1=fbase[:], op=ALU.add), reads=["nimp", "const"], writes=["nimp"])
            for t in range(NT):
                S.op("dve", lambda e, t=t: e.max(out=m8[:], in_=imp[:, t, :]), reads=["nimp"], writes=["nm8"])
                S.op("dve", lambda e, t=t: e.tensor_scalar(out=blkb[:, 64:96], in0=imp[:, t, :], scalar1=m8[:, 7:8], scalar2=NEGB, op0=ALU.is_lt, op1=ALU.mult), reads=["nimp", "nm8"], writes=["nblkb"])
                bank = t % 2
                ptr = ps_trs[bank]
                tk = "ps_tr%d" % bank
                S.op("pe", lambda e, ptr=ptr: e.transpose(out=ptr[0:96, 0, :], in_=blkb[:], identity=identb[:]), reads=["nblkb", "const"], writes=[tk])
                S.op("act", lambda e, ptr=ptr, t=t: e.copy(out=QT[64:96, :, t * 128:(t + 1) * 128], in_=ptr[64:96, 0:1, :].to_broadcast([32, 4, 128])), reads=[tk], writes=["nQT"])
            if l == 0:
                dump("nsa_imp", imp[:], [128, NT, 32], ["nimp"])

            def sel_bias(kt, qt):
                if kt == qt:
                    return [(identb[:], caust[:], ["const"])]
                return []

            def win_steps(c):
                out = []
                for kt in range(max(0, 4 * c - 4), 4 * c + 4):
                    qlo = max(4 * c, kt)
                    qhi = min(4 * c + 4, kt + 5)
                    if qhi > qlo:
                        out.append((kt, qlo, qhi))
                return out

            def win_bias(kt, qt):
                if kt == qt:
                    return [(identb[:], caust[:], ["const"])]
                if kt == qt - 4:
                    return [(identb[:], wint[:], ["const"])]
                return []
            for h in range(4):
                attn_core("nsel", lambda q0, q1, h=h: QT[0:96, h, q0:q1], lambda kt: ksT[0:96, kt * 128:(kt + 1) * 128], lambda kt: Vs[:, kt, 0:65], 0.125,
                          causal_steps, sel_bias, std_fin(oacc, h, "noacc", gate=gate_fn(1, h), accumulate=True), ["nQT", "nksT", "nVs"], ndummy=NDUMMY)
                attn_core("nwin", lambda q0, q1, h=h: QT[0:64, h, q0:q1], lambda kt: kwT[0:64, kt * 128:(kt + 1) * 128], lambda kt: Vw[:, kt, 0:65], 0.125,
                          win_steps, win_bias, std_fin(oacc, h, "noacc", gate=gate_fn(2, h), accumulate=True), ["nQT", "nkwT", "nVw"], ndummy=NDUMMY)
            for t in range(NT):
                S.op("pool", lambda e, t=t: e.tensor_copy(out=o_tm[:, t, :], in_=oacc[:, t, :]), reads=["noacc"], writes=["no_tm"])
            if l == 0:
                dump("o_nsa", o_tm[:], [128, NT, 256], ["no_tm"])
            finish_mixer(1, o_tm, "no_tm")
            S.barrier()

        mixed = A("mixed", [128, NT, D], BF16, OFF0 + 32 * KB)
        x_sb = A("x_sb", [128, NT, D], F32, OFF0 + 64 * KB)
        wo = A("wo", [128, KD, D], BF16, OFF0 + 128 * KB)
        for kh in range(4):
            S.dma(wo[:, kh * 2:(kh + 1) * 2, :], dr["w_out"][l].rearrange("(k p) c -> p k c", p=128)[:, kh * 2:(kh + 1) * 2, :], writes=["wo"], q="pool")
        for t in range(7, NT):
            S.dma(x_sb[:, t, :], x_src[t * 128:(t + 1) * 128, :], writes=["x_sb%d" % t])
        S.dma(g2[:], dr["norm2_g"][l:l + 1, :].to_broadcast([128, D]), writes=["g2"])
        o = OFF0 + 64 * KB
        Wg = [A("Wg%d" % i, [128, KD, 512], BF16, o + i * 8 * KB) for i in range(2)]; o += 16 * KB
        Wb = [A("Wb%d" % i, [128, 2, 512], BF16, o + i * 2 * KB) for i in range(2)]; o += 4 * KB
        sg = [A("sg%d" % i, [128, 512], F32, o + i * 2 * KB) for i in range(2)]; o += 4 * KB
        pr = [A("pr%d" % i, [128, 512], BF16, o + i * KB) for i in range(2)]; o += 2 * KB
        it = 0

        def load_gate_w(i):
            n_, cc_ = divmod(i, 2)
            b_ = i % 2
            gsrc = dr["w_in"][l, :, GATE0 + n_ * D + cc_ * 512:GATE0 + n_ * D + (cc_ + 1) * 512].rearrange("(k p) c -> p k c", p=128)
            for kh in range(2):
                S.dma(Wg[b_][:, kh * 4:(kh + 1) * 4, :], gsrc[:, kh * 4:(kh + 1) * 4, :], writes=["Wg%d" % b_], q="pool")
            S.dma(Wb[b_][:], dr["w_branch"][l, n_, :, cc_ * 512:(cc_ + 1) * 512].rearrange("(k p) c -> p k c", p=128), writes=["Wb%d" % b_], q="pool")
        load_gate_w(0)
        for n in range(4):
            for cc in range(2):
                b = it % 2
                it += 1
                if it < 8:
                    load_gate_w(it)
                for t in range(NT):
                    pg = ps_mm[t % 2]
                    pgk = "ps_mm%d" % (t % 2)
                    for k in range(KD):
                        S.op("pe", lambda e, k=k, pg=pg: e.matmul(pg[:, 0:512], lhsT=actT[:, k, t * 128:(t + 1) * 128], rhs=Wg[b][:, k, :], start=(k == 0), stop=(k == KD - 1)),
                             reads=["actT", "Wg%d" % b], writes=[pgk], sig=(k == KD - 1))
                    plf = ps_st[t % 2]
                    plk = "ps_st%d" % (t % 2)
                    for k in range(2):
                        S.op("pe", lambda e, k=k, plf=plf: e.matmul(plf[:, 0:512], lhsT=oT[:, n, k, t * 128:(t + 1) * 128], rhs=Wb[b][:, k, :], start=(k == 0), stop=(k == 1)),
                             reads=["oT", "Wb%d" % b], writes=[plk], sig=(k == 1))
                    s_ = sg[t % 2]
                    sk = "sg%d" % (t % 2)
                    S.op("act", lambda e, s_=s_, pg=pg: e.activation(out=s_[:], in_=pg[:, 0:512], func=AF.Sigmoid), reads=[pgk], writes=[sk])
                    dst = mixed[:, t, cc * 512:(cc + 1) * 512]
                    if n == 0:
                        S.op("dve", lambda e, s_=s_, plf=plf, dst=dst: e.tensor_tensor(out=dst, in0=s_[:], in1=plf[:, 0:512], op=ALU.mult), reads=[sk, plk], writes=["mixed"])
                    else:
                        p_ = pr[t % 2]
                        pk_ = "pr%d" % (t % 2)
                        S.op("dve", lambda e, s_=s_, plf=plf, p_=p_: e.tensor_tensor(out=p_[:], in0=s_[:], in1=plf[:, 0:512], op=ALU.mult), reads=[sk, plk], writes=[pk_])
                        S.op("pool", lambda e, p_=p_, dst=dst: e.tensor_tensor(out=dst, in0=dst, in1=p_[:], op=ALU.add), reads=[pk_, "mixed"], writes=["mixed"])
        if l == 0:
            dump("mixed", mixed[:], [128, NT, D], ["mixed"])
        S.barrier()
        for t in range(7):
            S.dma(x_sb[:, t, :], x_src[t * 128:(t + 1) * 128, :], writes=["x_sb%d" % t])
        for t in range(NT):
            transposes(actT[:, :, t * 128:(t + 1) * 128], [mixed[:, t, k * 128:(k + 1) * 128] for k in range(KD)], 128, ["mixed"], ["actT"], eng=("act" if t % 2 else "dve"))
        h2T = A("h2T", [128, KD, S_LEN], BF16, OFF0)
        aT = A("aT", [128, 8, S_LEN], BF16, ACT0)
        wup = A("wup", [128, KD, 1024], BF16, OFF0 + 32 * KB)
        wdn = A("wdn", [128, 8, D], BF16, OFF0 + 48 * KB)
        o = OFF0 + 144 * KB
        hb2 = [A("hb2_%d" % i, [128, D], BF16, o + i * 2 * KB) for i in range(2)]; o += 4 * KB
        sq4 = A("sq4", [128, D], BF16, o); o += 2 * KB
        nst4 = A("nst4", [128, 8], F32, o); o += 32
        usq = [A("usq%d" % i, [128, 512], F32, OFF0 + 128 * KB + i * 2 * KB) for i in range(2)]
        xkeys = ["x_sb%d" % t for t in range(NT)]
        def p3_mm(t):
            xk = "x_sb%d" % t
            for cc in range(2):
                pg = ps_mm[cc]
                for k in range(KD):
                    S.op("pe", lambda e, k=k, pg=pg, cc=cc: e.matmul(pg[:, 0:512], lhsT=actT[:, k, t * 128:(t + 1) * 128], rhs=wo[:, k, cc * 512:(cc + 1) * 512], start=(k == 0), stop=(k == KD - 1)),
                         reads=["actT", "wo"], writes=["ps_mm%d" % cc], sig=(k == KD - 1))
                dst = x_sb[:, t, cc * 512:(cc + 1) * 512]
                S.op("dve", lambda e, pg=pg, dst=dst: e.tensor_tensor(out=dst, in0=dst, in1=pg[:, 0:512], op=ALU.add), reads=["ps_mm%d" % cc, xk], writes=[xk])

        def p3_norm(t):
            xk = "x_sb%d" % t
            b = t % 2
            rmsnorm_tile(x_sb[:, t, :], D, g2[:], hb2[b][:], (nst4, sq4[:]), "n2", ["g2"], ["hb2_%d" % b], [xk])
            transposes(h2T[:, :, t * 128:(t + 1) * 128], [hb2[b][:, k * 128:(k + 1) * 128] for k in range(KD)], 128, ["hb2_%d" % b], ["h2T"], eng=("act" if t % 2 else "dve"))
        for step in range(NT + 2):
            if step < NT:
                p3_mm(step)
            if step >= 2:
                p3_norm(step - 2)
        if l == 0:
            dump("x_attn", x_sb[:], [128, NT, D], xkeys)
        ui = 0
        for g in range(4):
            usrc = dr["w_up"][l, :, g * 1024:(g + 1) * 1024].rearrange("(k p) c -> p k c", p=128)
            for kh in range(4):
                S.dma(wup[:, kh * 2:(kh + 1) * 2, :], usrc[:, kh * 2:(kh + 1) * 2, :], writes=["wup"] + (["mixed"] if g == 0 else []), q="pool")
            dsrc = dr["w_down"][l, g * 1024:(g + 1) * 1024, :].rearrange("(f p) c -> p f c", p=128)
            for kh in range(4):
                S.dma(wdn[:, kh * 2:(kh + 1) * 2, :], dsrc[:, kh * 2:(kh + 1) * 2, :], writes=["wdn"] + (["mixed"] if g == 0 else []), q="pool")
            for fc in range(8):
                for tc4 in range(4):
                    pu = ps_mm[ui % 2]
                    puk = "ps_mm%d" % (ui % 2)
                    for k in range(KD):
                        S.op("pe", lambda e, k=k, pu=pu, fc=fc, tc4=tc4: e.matmul(pu[:, 0:512], lhsT=wup[:, k, fc * 128:(fc + 1) * 128], rhs=h2T[:, k, tc4 * 512:(tc4 + 1) * 512],
                                                                              start=(k == 0), stop=(k == KD - 1)),
                             reads=["wup", "h2T"], writes=[puk], sig=(k == KD - 1))
                    u2 = usq[ui % 2]
                    uk = "usq%d" % (ui % 2)
                    S.op("act", lambda e, pu=pu, u2=u2: e.activation(out=u2[:], in_=pu[:, 0:512], func=AF.Square), reads=[puk], writes=[uk])
                    S.op("dve", lambda e, pu=pu, u2=u2, fc=fc, tc4=tc4: e.scalar_tensor_tensor(out=aT[:, fc, tc4 * 512:(tc4 + 1) * 512], in0=pu[:, 0:512], scalar=0.0, in1=u2[:],
                                                                                         op0=ALU.is_gt, op1=ALU.mult), reads=[puk, uk], writes=["aT", "actT"])
                    ui += 1
            for t in range(NT):
                for cc in range(2):
                    pd = ps_st[cc]
                    pdk = "ps_st%d" % cc
                    for fc in range(8):
                        S.op("pe", lambda e, fc=fc, pd=pd, cc=cc: e.matmul(pd[:, 0:512], lhsT=aT[:, fc, t * 128:(t + 1) * 128], rhs=wdn[:, fc, cc * 512:(cc + 1) * 512], start=(fc == 0), stop=(fc == 7)),
                             reads=["aT", "wdn"], writes=[pdk], sig=(fc == 7))
                    dst = x_sb[:, t, cc * 512:(cc + 1) * 512]
                    S.op("dve", lambda e, pd=pd, dst=dst: e.tensor_tensor(out=dst, in0=dst, in1=pd[:, 0:512], op=ALU.add), reads=[pdk, "x_sb"], writes=["x_sb"])
        if l == 0:
            dump("x_l0", x_sb[:], [128, NT, D], ["x_sb"])
        if l < depth - 1:
            for t in range(NT):
                S.dma(xs[t * 128:(t + 1) * 128, :], x_sb[:, t, :], reads=["x_sb"], writes=["xs"])
        else:
            S.dma(g2[:], dr["final_g"][0:1, :].to_broadcast([128, D]), reads=["g2"], writes=["g2"])
            fo = [A("fo%d" % i, [128, D], F32, OFF0 + 32 * KB + i * 4 * KB) for i in range(2)]
            for t in range(NT):
                b = t % 2
                rmsnorm_tile(x_sb[:, t, :], D, g2[:], fo[b][:], (nst4, sq4[:]), "n3", ["g2"], ["fo%d" % b], ["x_sb"])
                S.dma(y[t * 128:(t + 1) * 128, :], fo[b][:], reads=["fo%d" % b], writes=["y"])
        S.barrier()


_CACHE = {}


def prepare_inputs(inputs, b):
    m = {}
    m["x"] = np.ascontiguousarray(inputs["x"][b]).astype(np.float32, copy=False)
    m["pos"] = np.ascontiguousarray(np.asarray(inputs["positions"][b]).reshape(NT, 128).T).astype(np.int32)
    for k in ["norm1_g", "w_in", "mla_q_norm_g", "mla_w_uq", "mla_kv_norm_g", "mla_w_ukv", "nsa_cmp_w", "fox_f_bias",
              "w_branch", "w_out", "norm2_g", "w_up", "w_down"]:
        m[k] = np.ascontiguousarray(inputs[k], dtype=np.float32)
    m["nsa_cmp_pe"] = np.ascontiguousarray(np.asarray(inputs["nsa_cmp_pe"], dtype=np.float32).reshape(DEPTH, 2, 2048))
    m["final_g"] = np.ascontiguousarray(np.asarray(inputs["final_g"], dtype=np.float32).reshape(1, D))
    m.update(make_consts())
    return m


def kernel(**inputs):
    inputs = {k: np.asarray(v) for k, v in inputs.items()}
    if "nc" not in _CACHE:
        _CACHE["nc"] = build_program()[0]
    nc = _CACHE["nc"]
    B = inputs["x"].shape[0]
    in_maps = [prepare_inputs(inputs, b) for b in range(B)]
    res = run_bass_kernel_spmd(nc, in_maps, core_ids=list(range(B)))
    out = np.stack([np.asarray(r["y"]) for r in res.results], axis=0).astype(np.float32)
    return out
```

```python
import numpy as np
import concourse.bass as bass
import concourse.mybir as mybir
from concourse.bass_utils import run_bass_kernel_spmd

F32 = mybir.dt.float32
BF16 = mybir.dt.bfloat16
I32 = mybir.dt.int32
ALU = mybir.AluOpType
AF = mybir.ActivationFunctionType
AX = mybir.AxisListType

S_LEN = 2048
NT = 16
D = 1024
KD = 8
DEPTH = 2
NEGB = -30000.0
NDUMMY = 1
NWARM = 0
PI = float(np.pi)


class StopBuild(Exception):
    pass


class Sched:
    limit = None

    def __init__(self, nc, n_dma_sems=24):
        self.nc = nc
        self.eng = {"pe": nc.tensor, "dve": nc.vector, "act": nc.scalar, "pool": nc.gpsimd, "sp": nc.sync}
        self.sem = {k: nc.alloc_semaphore("s_" + k) for k in self.eng}
        self.cnt = {k: 0 for k in self.eng}
        self.dsem = [nc.alloc_semaphore("d%d" % i) for i in range(n_dma_sems)]
        self.dcnt = [0] * n_dma_sems
        half = n_dma_sems // 2
        self.dpool = {"sp": list(range(0, half)), "pool": list(range(half, n_dma_sems))}
        self.dnext = {"sp": 0, "pool": 0}
        self.seen = {k: {} for k in self.eng}
        self.lastw = {}
        self.readers = {}
        self.pending = {k: ([], []) for k in self.eng}
        self.semobj = {}
        self.all_tokens = {}
        self.nwaits = 0
        self.nops = 0

    def _tok_wait(self, e, tok):
        sid, val = tok
        if self.seen[e].get(sid, 0) >= val:
            return
        self.seen[e][sid] = val
        self.eng[e].wait_ge(self.semobj[sid], val)
        self.nwaits += 1

    def _deps(self, e, reads, writes):
        writes = list(writes) + [k for k in reads if k.startswith("ps_")]
        toks = []
        for k in reads:
            t = self.lastw.get(k)
            if t is not None:
                toks.append(t)
        for k in writes:
            t = self.lastw.get(k)
            if t is not None:
                toks.append(t)
            toks.extend(self.readers.get(k, ()))
        own = id(self.sem[e])
        for t in toks:
            if e == "pe" and t[0] == own:
                continue
            self._tok_wait(e, t)

    def _commit(self, tok, reads, writes):
        writes = list(writes) + [k for k in reads if k.startswith("ps_")]
        for k in writes:
            self.lastw[k] = tok
            self.readers[k] = []
        for k in reads:
            lst = self.readers.setdefault(k, [])
            lst.append(tok)
            if len(lst) > 16:
                best = {}
                for s, v in lst:
                    best[s] = max(best.get(s, 0), v)
                self.readers[k] = list(best.items())
        self.all_tokens[tok[0]] = max(self.all_tokens.get(tok[0], 0), tok[1])

    def op(self, e, fn, reads=(), writes=(), sig=True):
        self.nops += 1
        if self.limit is not None and self.nops > self.limit:
            raise StopBuild()
        self._deps(e, reads, writes)
        ins = fn(self.eng[e])
        if not sig:
            pr, pw = self.pending[e]
            pr.extend(reads)
            pw.extend(writes)
            return
        self.cnt[e] += 1
        s = self.sem[e]
        self.semobj[id(s)] = s
        ins.then_inc(s, 1)
        tok = (id(s), self.cnt[e])
        pr, pw = self.pending[e]
        self._commit(tok, list(reads) + pr, list(writes) + pw)
        self.pending[e] = ([], [])

    def dma(self, out, in_, reads=(), writes=(), q="sp", **kw):
        self.nops += 1
        if self.limit is not None and self.nops > self.limit:
            raise StopBuild()
        self._deps(q, reads, writes)
        lst = self.dpool[q]
        i = lst[self.dnext[q] % len(lst)]
        self.dnext[q] += 1
        s = self.dsem[i]
        self.semobj[id(s)] = s
        self.dcnt[i] += 16
        self.eng[q].dma_start(out=out, in_=in_, **kw).then_inc(s, 16)
        tok = (id(s), self.dcnt[i])
        self._commit(tok, reads, writes)
        return tok

    def barrier(self):
        for e in self.eng:
            for sid, val in list(self.all_tokens.items()):
                self._tok_wait(e, (sid, val))
        self.lastw = {}
        self.readers = {}

    def wait_all(self, e="sp"):
        for sid, val in list(self.all_tokens.items()):
            self._tok_wait(e, (sid, val))


def make_consts():
    c = {}
    k = np.arange(128)[:, None]
    q = np.arange(128)[None, :]
    c["c_ident"] = np.eye(128, dtype=np.float32)
    c["c_caust"] = np.where(q >= k, 0.0, NEGB).astype(np.float32)
    c["c_wint"] = np.where(k > q, 0.0, NEGB).astype(np.float32)
    c["c_causqk"] = np.where(q <= k, 0.0, -1e30).astype(np.float32)
    cc = np.arange(128)[:, None]
    t = np.arange(S_LEN)[None, :]
    cmpb = np.where((16 * cc + 31 <= t) & (cc < 127), 0.0, NEGB).astype(np.float32)
    c["c_cmpbias"] = cmpb
    j = np.arange(32)[:, None]
    kk = np.arange(S_LEN)[None, :]
    c["c_E"] = (kk // 64 == j).astype(np.float32)
    cs = np.arange(128) * 16
    sb = np.arange(32) * 64
    ov = np.maximum(np.minimum(cs[:, None] + 32, sb[None, :] + 64) - np.maximum(cs[:, None], sb[None, :]), 0) / 32.0
    ov[127, :] = 0.0
    c["c_ovl"] = ov.astype(np.float32)
    tt = np.arange(S_LEN)
    tb = tt[:, None] // 64
    sbi = np.arange(32)[None, :]
    forced = (sbi == 0) | (sbi == tb) | (sbi == tb - 1)
    causal = (sbi * 64) <= tt[:, None]
    base = np.where(causal, np.where(forced, 1e4, 0.0), -1e30).astype(np.float32)
    keep = np.where(causal & ~forced, 1.0, 0.0).astype(np.float32)
    c["c_fbase"] = base.reshape(NT, 128, 32).transpose(1, 0, 2).copy()
    c["c_fkeep"] = keep.reshape(NT, 128, 32).transpose(1, 0, 2).copy()
    theta = np.float32(500000.0)
    def inv(rot):
        return (theta ** (-np.arange(0, rot, 2, dtype=np.float32) / np.float32(rot))).astype(np.float32)
    invs = np.concatenate([inv(32), inv(16), inv(8)]).astype(np.float32)
    c["c_inv"] = np.tile(invs[None, :], (128, 1)).astype(np.float32)
    c["c_pow2"] = np.tile((2.0 ** -np.arange(32, dtype=np.float32))[None, :], (128, 1)).astype(np.float32)
    c["c_U"] = (k <= q).astype(np.float32)
    c["c_ones"] = np.ones((128, 128), np.float32)
    return c


CONST_SHAPES = {k: v.shape for k, v in make_consts().items()}

IN_SHAPES = {
    "x": ([S_LEN, D], F32), "pos": ([128, NT], I32),
    "norm1_g": ([DEPTH, D], F32), "w_in": ([DEPTH, D, 6616], F32),
    "mla_q_norm_g": ([DEPTH, 256], F32), "mla_w_uq": ([DEPTH, 256, 384], F32),
    "mla_kv_norm_g": ([DEPTH, 128], F32), "mla_w_ukv": ([DEPTH, 128, 512], F32),
    "nsa_cmp_pe": ([DEPTH, 2, 2048], F32), "nsa_cmp_w": ([DEPTH, 2, 2048, 64], F32),
    "fox_f_bias": ([DEPTH, 4], F32), "w_branch": ([DEPTH, 4, 256, D], F32),
    "w_out": ([DEPTH, D, D], F32), "norm2_g": ([DEPTH, D], F32),
    "w_up": ([DEPTH, D, 4096], F32), "w_down": ([DEPTH, 4096, D], F32), "final_g": ([1, D], F32),
}

MLA0 = 0
NSA0 = 416
FOX0 = NSA0 + 652
DSA0 = FOX0 + 772
GATE0 = DSA0 + 680


def build_program(depth=DEPTH, debug=None, mixers=("mla", "nsa", "fox", "dsa"), stop=None, limit=None):
    nc = bass.Bass("TRN2", target_bir_lowering=False)
    S = Sched(nc)
    S.limit = limit
    dbg = {}
    try:
        _build_body(nc, S, dbg, depth, debug, mixers, stop)
    except StopBuild:
        S.limit = None
        print("stopped at limit", limit, flush=True)
    S.wait_all("sp")
    print("program built: ops", S.nops, "waits", S.nwaits, flush=True)
    return nc, dbg


def _build_body(nc, S, dbg, depth, debug, mixers, stop):
    dr = {}
    for name, (shape, dt) in IN_SHAPES.items():
        dr[name] = nc.dram_tensor(name, shape, dt, kind="ExternalInput").ap()
    for name, shape in CONST_SHAPES.items():
        dr[name] = nc.dram_tensor(name, list(shape), F32, kind="ExternalInput").ap()
    y = nc.dram_tensor("y", [S_LEN, D], F32, kind="ExternalOutput").ap()
    xs = nc.dram_tensor("xs", [S_LEN, D], F32, kind="Internal").ap()

    BASE = 16512
    KB = 1024

    def A(name, shape, dt, off):
        assert off % 32 == 0, (name, off)
        nbytes = int(np.prod(shape[1:])) * (2 if dt == BF16 else 4)
        assert off + nbytes <= 207 * KB + 512, (name, off, nbytes)
        return nc.alloc_sbuf_tensor_at(name, list(shape), dt, offset=BASE + off)

    def dump(name, ap, shape, reads):
        if debug is None or name not in debug:
            return
        t = nc.dram_tensor("dbg_" + name, list(shape), ap.dtype if hasattr(ap, "dtype") else F32, kind="ExternalOutput").ap()
        S.dma(t, ap, reads=reads, writes=["dbg_" + name])
        dbg[name] = t

    o = 0
    def CA(name, shape, dt):
        nonlocal o
        t = A(name, shape, dt, o)
        o += ((int(np.prod(shape[1:])) * (2 if dt == BF16 else 4) + 31) // 32) * 32
        return t
    identb = CA("identb", [128, 128], BF16)
    identf = CA("identf", [128, 128], F32)
    caust = CA("caust", [128, 128], BF16)
    wint = CA("wint", [128, 128], BF16)
    causqk = CA("causqk", [128, 128], F32)
    cmpbias = CA("cmpbias", [128, S_LEN], BF16)
    Emat = CA("Emat", [32, S_LEN], BF16)
    ovl = CA("ovl", [128, 32], BF16)
    fbase = CA("fbase", [128, NT, 32], F32)
    fkeep = CA("fkeep", [128, NT, 32], F32)
    invt = CA("invt", [128, 28], F32)
    pow2 = CA("pow2", [128, 32], F32)
    Umat = CA("Umat", [128, 128], F32)
    onesf = CA("onesf", [128, 128], F32)
    trig = CA("trig", [128, NT, 56], F32)
    posi = CA("posi", [128, NT], I32)
    posf = CA("posf", [128, NT], F32)
    cst = CA("cst", [128, 8], F32)
    g2 = CA("g2", [128, D], F32)
    assert o <= 24 * KB, o
    for dst, src, q in [(identb, "c_ident", "pool"), (identf, "c_ident", "sp"), (caust, "c_caust", "pool"), (wint, "c_wint", "pool"),
                        (causqk, "c_causqk", "sp"), (cmpbias, "c_cmpbias", "pool"), (Emat, "c_E", "pool"), (ovl, "c_ovl", "pool"),
                        (fbase, "c_fbase", "sp"), (fkeep, "c_fkeep", "sp"), (invt, "c_inv", "sp"), (pow2, "c_pow2", "sp"),
                        (Umat, "c_U", "sp"), (onesf, "c_ones", "sp")]:
        S.dma(dst[:], dr[src], writes=["const"], q=q)
    S.dma(posi[:], dr["pos"], writes=["const"])
    S.op("dve", lambda e: e.memset(cst[:, 0:1], 1e-6), writes=["const"])
    S.op("dve", lambda e: e.memset(cst[:, 1:2], 1.0), writes=["const"])
    S.op("dve", lambda e: e.memset(cst[:, 2:3], 0.0), writes=["const"])

    ACT0 = 24 * KB
    actT = A("actT", [128, KD, S_LEN], BF16, ACT0)
    OFF0 = 56 * KB

    ps_st = [nc.alloc_psum_tensor("ps_st%d" % i, [128, 512], F32) for i in range(2)]
    ps_acc = [nc.alloc_psum_tensor("ps_acc%d" % i, [128, 4, 128], F32) for i in range(2)]
    ps_mm = [nc.alloc_psum_tensor("ps_mm%d" % i, [128, 512], F32) for i in range(2)]
    ps_trs = [nc.alloc_psum_tensor("ps_tr%d" % i, [128, 8, 128], BF16) for i in range(2)]
    ps_x = ps_trs[1][:].rearrange("p a b -> p (a b)").bitcast(F32)
    PX = "ps_tr1"

    with_scr = A("ropescr", [128, NT, 56], F32, OFF0)
    kfi = A("ropekfi", [128, NT, 56], I32, OFF0 + 4 * KB)
    kff = A("ropekff", [128, NT, 56], F32, OFF0 + 8 * KB)
    ang = A("ropeang", [128, NT, 56], F32, OFF0 + 12 * KB)
    S.op("dve", lambda e: e.tensor_copy(out=posf[:], in_=posi[:]), reads=["const"], writes=["posf"])
    S.op("dve", lambda e: e.tensor_tensor(out=ang[:, :, 0:28], in0=posf[:].unsqueeze(2).to_broadcast([128, NT, 28]),
                                          in1=invt[:].unsqueeze(1).to_broadcast([128, NT, 28]), op=ALU.mult), reads=["posf", "const"], writes=["ang"])
    S.op("dve", lambda e: e.tensor_scalar(out=ang[:, :, 28:56], in0=ang[:, :, 0:28], scalar1=PI / 2, scalar2=None, op0=ALU.add), reads=["ang"], writes=["ang"])
    S.op("dve", lambda e: e.tensor_scalar(out=with_scr[:], in0=ang[:], scalar1=float(1.0 / (2 * np.pi)), scalar2=None, op0=ALU.mult), reads=["ang"], writes=["rscr"])
    S.op("dve", lambda e: e.tensor_copy(out=kfi[:], in_=with_scr[:]), reads=["rscr"], writes=["kfi"])
    S.op("dve", lambda e: e.tensor_copy(out=kff[:], in_=kfi[:]), reads=["kfi"], writes=["kff"])
    C1 = 6.28125
    C2 = float(2 * np.pi - 6.28125)
    S.op("dve", lambda e: e.scalar_tensor_tensor(out=with_scr[:], in0=kff[:], scalar=-C1, in1=ang[:], op0=ALU.mult, op1=ALU.add), reads=["kff", "ang"], writes=["rscr"])
    S.op("dve", lambda e: e.scalar_tensor_tensor(out=ang[:], in0=kff[:], scalar=-C2, in1=with_scr[:], op0=ALU.mult, op1=ALU.add), reads=["kff", "rscr"], writes=["ang"])
    S.op("dve", lambda e: e.tensor_scalar(out=ang[:], in0=ang[:], scalar1=-3.1415925, scalar2=3.1415925, op0=ALU.max, op1=ALU.min), reads=["ang"], writes=["ang"])
    S.op("act", lambda e: e.activation(out=trig[:], in_=ang[:], func=AF.Sin, bias=cst[:, 2:3], scale=1.0), reads=["ang", "const"], writes=["trig"])
    dump("trig", trig[:], [128, NT, 56], ["trig"])
    SIN = lambda t, a, b: trig[:, t, a:b]
    COS = lambda t, a, b: trig[:, t, 28 + a:28 + b]
    S.barrier()
    if stop == "p0":
        print("stop", stop, S.nops)
        return

    def rmsnorm_tile(src_ap, n, g_ap, dst_ap, scr, tag, rkeys, wkeys, src_keys):
        ss, sq = scr
        S.op("act", lambda e: e.activation(out=sq, in_=src_ap, func=AF.Square, accum_out=ss[:, 0:1]), reads=src_keys, writes=[tag + "sq", tag + "ss"])
        S.op("act", lambda e: e.activation(out=ss[:, 1:2], in_=ss[:, 0:1], func=AF.Sqrt, bias=cst[:, 0:1], scale=1.0 / n), reads=[tag + "ss"], writes=[tag + "ss1"])
        S.op("dve", lambda e: e.reciprocal(out=ss[:, 2:3], in_=ss[:, 1:2]), reads=[tag + "ss1"], writes=[tag + "ss2"])
        S.op("dve", lambda e: e.scalar_tensor_tensor(out=dst_ap, in0=src_ap, scalar=ss[:, 2:3], in1=g_ap, op0=ALU.mult, op1=ALU.mult),
             reads=src_keys + [tag + "ss2"] + rkeys, writes=wkeys)

    tr_i = [0]

    def transposes(dst_ap, srcs, np_out, rkeys, wkeys, eng="act", bank=None):
        n = len(srcs)
        if bank is None:
            bank = tr_i[0] % 2
            tr_i[0] += 1
        ps_tr = ps_trs[bank]
        tk = "ps_tr%d" % bank
        for j, s_ap in enumerate(srcs):
            w = s_ap.shape[-1]
            S.op("pe", lambda e, j=j, s_ap=s_ap, w=w: e.transpose(out=ps_tr[0:w, j, :], in_=s_ap, identity=identb[:]),
                 reads=rkeys + ["const"], writes=[tk], sig=(j == n - 1))
        if eng == "act":
            S.op("act", lambda e: e.copy(out=dst_ap, in_=ps_tr[0:np_out, 0:n, :]), reads=[tk], writes=wkeys)
        else:
            S.op("dve", lambda e: e.tensor_copy(out=dst_ap, in_=ps_tr[0:np_out, 0:n, :]), reads=[tk], writes=wkeys)

    def rope(dst1, dst2, x1, x2, cos, sin, shape, tA, tB, rkeys, wkeys, tag):
        cb = cos.unsqueeze(1).to_broadcast(shape) if len(shape) == 3 else cos
        sb = sin.unsqueeze(1).to_broadcast(shape) if len(shape) == 3 else sin
        S.op("dve", lambda e: e.tensor_tensor(out=tA, in0=x1, in1=cb, op=ALU.mult), reads=rkeys + ["trig"], writes=[tag + "A"])
        S.op("dve", lambda e: e.tensor_tensor(out=tB, in0=x2, in1=sb, op=ALU.mult), reads=rkeys + ["trig"], writes=[tag + "B"])
        S.op("dve", lambda e: e.tensor_tensor(out=dst1, in0=tA, in1=tB, op=ALU.subtract), reads=[tag + "A", tag + "B"], writes=wkeys)
        S.op("dve", lambda e: e.tensor_tensor(out=tA, in0=x2, in1=cb, op=ALU.mult), reads=rkeys + ["trig"], writes=[tag + "A"])
        S.op("dve", lambda e: e.tensor_tensor(out=tB, in0=x1, in1=sb, op=ALU.mult), reads=rkeys + ["trig"], writes=[tag + "B"])
        S.op("dve", lambda e: e.tensor_tensor(out=dst2, in0=tA, in1=tB, op=ALU.add), reads=[tag + "A", tag + "B"], writes=wkeys)

    def load_w_cols(dst, l, segs, wkey):
        for (dc, sc, n) in segs:
            src = dr["w_in"][l, :, sc:sc + n].rearrange("(k p) c -> p k c", p=128)
            for kh in range(2):
                S.dma(dst[:, kh * 4:(kh + 1) * 4, dc:dc + n], src[:, kh * 4:(kh + 1) * 4, :], writes=[wkey], q="pool")

    def project_tile(t, Wz, col_groups, wkey):
        outs = []
        for gi, (c0, n) in enumerate(col_groups):
            if t % 2 == 0:
                pt, pk = ps_mm[gi], "ps_mm%d" % gi
            else:
                pt, pk = ps_st[gi], "ps_st%d" % gi
            for k in range(KD):
                S.op("pe", lambda e, k=k, pt=pt, c0=c0, n=n: e.matmul(pt[:, 0:n], lhsT=actT[:, k, t * 128:(t + 1) * 128], rhs=Wz[:, k, c0:c0 + n],
                                                                     start=(k == 0), stop=(k == KD - 1)),
                     reads=["actT", wkey], writes=[pk], sig=(k == KD - 1))
            outs.append((pt, pk))
        return outs

    def warm(n, bank=0):
        if not NDUMMY:
            return
        dout = ps_acc[bank][:].rearrange("p a b -> p (a b)")
        for _ in range(n):
            S.op("pe", lambda e: e.matmul(dout, lhsT=identb[:], rhs=cmpbias[:, 0:512], start=True, stop=True, skip_group_check=True), sig=False)

    PT = [A("PT%d" % i, [128, 512], BF16, OFF0 + 32 * KB + 13 * KB + i * KB) for i in range(4)]
    misc = A("misc", [128, 512], F32, OFF0 + 32 * KB + 17 * KB)
    WS0 = OFF0 + 32 * KB + 19 * KB
    st_ctr = [0]

    def attn_core(tag, qT, kT, vaug, scale, steps_fn, bias_fn, fin_fn, rkeys, nk=128, vw=65, range_bias_fn=None, ndummy=0, dummy_out=None):
        steps = []
        for c in range(4):
            ss = steps_fn(c)
            contrib = {}
            for (kt, qlo, qhi) in ss:
                for qt in range(qlo, qhi):
                    contrib.setdefault(qt, []).append(kt)
            for si, (kt, qlo, qhi) in enumerate(ss):
                steps.append((c, kt, qlo, qhi, contrib, si == len(ss) - 1, si == 0))

        def issue_st(step, slot):
            c, kt, qlo, qhi, contrib, last, _f = step
            st = ps_st[slot % 2]
            skey = "ps_st%d" % (slot % 2)
            n = (qhi - qlo) * 128
            bl = []
            for qt in range(qlo, qhi):
                for (bl_l, bl_r, bkeys) in bias_fn(kt, qt):
                    bl.append(((qt - qlo) * 128, (qt - qlo + 1) * 128, bl_l, bl_r, bkeys))
            if range_bias_fn is not None:
                for (bl_l, bl_r, bkeys) in range_bias_fn(kt, qlo, qhi):
                    bl.append((0, n, bl_l, bl_r, bkeys))
            S.op("pe", lambda e: e.matmul(st[0:nk, 0:n], lhsT=kT(kt), rhs=qT(qlo * 128, qhi * 128), start=True, stop=(len(bl) == 0), skip_group_check=True),
                 reads=rkeys, writes=[skey], sig=(not bl))
            for bi, (c0, c1, bl_l, bl_r, bkeys) in enumerate(bl):
                S.op("pe", lambda e, c0=c0, c1=c1, bl_l=bl_l, bl_r=bl_r, bi=bi: e.matmul(st[0:nk, c0:c1], lhsT=bl_l, rhs=bl_r, start=False, stop=(bi == len(bl) - 1), skip_group_check=True),
                     reads=bkeys, writes=[skey], sig=(bi == len(bl) - 1))
            for _ in range(ndummy):
                dout = ps_mm[0][:, 0:512] if dummy_out is None else dummy_out
                S.op("pe", lambda e: e.matmul(dout, lhsT=identb[:], rhs=cmpbias[:, 0:512], start=True, stop=True, skip_group_check=True), sig=False)

        def issue_rest(step, slot):
            c, kt, qlo, qhi, contrib, last, _f = step
            st = ps_st[slot % 2]
            skey = "ps_st%d" % (slot % 2)
            n = (qhi - qlo) * 128
            pi = slot % 4
            pt = PT[pi]
            S.op("act", lambda e: e.activation(out=pt[0:nk, 0:n], in_=st[0:nk, 0:n], func=AF.Exp, scale=scale), reads=[skey], writes=["PT%d" % pi])
            acc = ps_acc[c % 2]
            for qt in range(qlo, qhi):
                first = (step[6] and qt == qlo)
                S.op("pe", lambda e, qt=qt, first=first: e.matmul(acc[:, qt - 4 * c, 0:vw], lhsT=pt[0:nk, (qt - qlo) * 128:(qt - qlo + 1) * 128], rhs=vaug(kt),
                                                       start=first, stop=(kt == contrib[qt][-1]), skip_group_check=True),
                     reads=["PT%d" % pi] + rkeys, writes=["ps_acc%d" % (c % 2)], sig=(qt == qhi - 1))
            if last:
                fin_fn(c, acc, "ps_acc%d" % (c % 2))

        base = st_ctr[0]
        for i, step in enumerate(steps):
            if i == 0:
                issue_st(step, base)
            if i + 1 < len(steps):
                issue_st(steps[i + 1], base + i + 1)
            issue_rest(step, base + i)
        st_ctr[0] = base + len(steps)

    def causal_steps(c):
        return [(kt, max(4 * c, kt), 4 * c + 4) for kt in range(4 * c + 4)]

    def causal_bias(kt, qt):
        if kt == qt:
            return [(identb[:], caust[:], ["const"])]
        return []

    def std_fin(o_tm, h, okey, gate=None, accumulate=False, vdim=64):
        def fin(c, acc, akey):
            rec = misc[:, 64:68]
            S.op("dve", lambda e: e.tensor_scalar(out=rec, in0=acc[:, :, vdim], scalar1=1e-30, scalar2=None, op0=ALU.max), reads=[akey], writes=["rec"])
            S.op("dve", lambda e: e.reciprocal(out=misc[:, 68:72], in_=rec), reads=["rec"], writes=["rec2"])
            r2 = misc[:, 68:72]
            rk = ["rec2"]
            if gate is not None:
                gap, gkey = gate(c)
                S.op("dve", lambda e: e.tensor_tensor(out=misc[:, 72:76], in0=r2, in1=gap, op=ALU.mult), reads=["rec2", gkey], writes=["rec3"])
                r2 = misc[:, 72:76]
                rk = ["rec3"]
            dst = o_tm[:, 4 * c:4 * c + 4, h * 64:(h + 1) * 64]
            if not accumulate:
                S.op("dve", lambda e: e.tensor_tensor(out=dst, in0=acc[:, :, 0:64], in1=r2.unsqueeze(2).to_broadcast([128, 4, 64]), op=ALU.mult),
                     reads=[akey] + rk, writes=[okey])
            else:
                tmp = misc[:, 128:384].rearrange("p (a b) -> p a b", a=4)
                S.op("dve", lambda e: e.tensor_tensor(out=tmp, in0=acc[:, :, 0:64], in1=r2.unsqueeze(2).to_broadcast([128, 4, 64]), op=ALU.mult),
                     reads=[akey] + rk, writes=["fintmp"])
                S.op("pool", lambda e: e.tensor_tensor(out=dst, in0=dst, in1=tmp, op=ALU.add), reads=["fintmp", okey], writes=[okey])
        return fin

    oT = A("oT", [128, 4, 2, S_LEN], BF16, OFF0)

    def finish_mixer(mi, o_tm, okey):
        for t in range(NT):
            transposes(oT[:, mi, :, t * 128:(t + 1) * 128], [o_tm[:, t, 0:128], o_tm[:, t, 128:256]], 128, [okey], ["oT"], eng=("act" if t % 2 else "dve"))

    for l in range(depth):
        x_src = dr["x"] if l == 0 else xs
        gbc = A("gbc", [128, D], F32, OFF0)
        xt = [A("xt%d" % i, [128, D], F32, OFF0 + 4 * KB + i * 4 * KB) for i in range(4)]
        hb = [A("hb%d" % i, [128, D], BF16, OFF0 + 20 * KB + i * 2 * KB) for i in range(3)]
        sqs_ = [A("sqs%d" % i, [128, D], BF16, OFF0 + 26 * KB + i * 2 * KB) for i in range(2)]
        nst_ = [A("nst%d" % i, [128, 8], F32, OFF0 + 30 * KB + i * 32) for i in range(2)]
        S.dma(gbc[:], dr["norm1_g"][l:l + 1, :].to_broadcast([128, D]), writes=["gbc"])

        def p1_load(t):
            S.dma(xt[t % 4][:], x_src[t * 128:(t + 1) * 128, :], writes=["xt%d" % (t % 4)])

        def p1_norm(t):
            rmsnorm_tile(xt[t % 4][:], D, gbc[:], hb[t % 3][:], (nst_[t % 2], sqs_[t % 2][:]), "n1_%d" % (t % 2), ["gbc"], ["hb%d" % (t % 3)], ["xt%d" % (t % 4)])

        def p1_tr(t):
            transposes(actT[:, :, t * 128:(t + 1) * 128], [hb[t % 3][:, k * 128:(k + 1) * 128] for k in range(KD)], 128, ["hb%d" % (t % 3)], ["actT"], eng=("act" if t % 2 else "dve"))
        p1_load(0)
        p1_load(1)
        for step in range(NT + 1):
            if step + 2 < NT:
                p1_load(step + 2)
            if step < NT:
                p1_norm(step)
            if step >= 1:
                p1_tr(step - 1)
        if l == 0:
            dump("hT", actT[:], [128, KD, S_LEN], ["actT"])
        S.barrier()

        if stop == "p1":
            print("stop", stop, S.nops)
            return
        Wz = A("Wz", [128, KD, 776], BF16, OFF0 + 32 * KB)

        if "mla" in mixers:
            o = WS0
            cqnT = A("cqnT", [128, 2, S_LEN], BF16, o); o += 8 * KB
            ckvnT = A("ckvnT", [128, S_LEN], BF16, o); o += 4 * KB
            wuq = A("wuq", [128, 2, 384], BF16, o); o += 1536
            wukv = A("wukv", [128, 512], BF16, o); o += 1024
            QT = A("mQT", [96, 4, S_LEN], BF16, o); o += 16 * KB
            KTt = A("mKT", [96, 4, S_LEN], BF16, o); o += 16 * KB
            Vg = A("mV", [128, NT, 4, 66], BF16, o); o += NT * 4 * 66 * 2
            o = (o + 31) // 32 * 32
            qtm_ = [A("mqtm%d" % i, [128, 4, 96], BF16, o + i * 768) for i in range(2)]; o += 1536
            ktm_ = [A("mktm%d" % i, [128, 4, 96], BF16, o + i * 768) for i in range(3)]; o += 2304
            cqn_ = [A("mcqn%d" % i, [128, 256], BF16, o + i * 512) for i in range(2)]; o += 1024
            ckvn_ = [A("mckvn%d" % i, [128, 128], BF16, o + i * 256) for i in range(2)]; o += 512
            gq = A("mgq", [128, 256], F32, o); o += 1024
            gkv = A("mgkv", [128, 128], F32, o); o += 512
            rA_ = [A("mrA%d" % i, [128, 64], F32, o + i * 256) for i in range(2)]; o += 512
            rB_ = [A("mrB%d" % i, [128, 64], F32, o + i * 256) for i in range(2)]; o += 512
            sq2_ = [A("msq%d" % i, [128, 256], F32, o + i * 1024) for i in range(2)]; o += 2048
            nst2_ = [A("mnst%d" % i, [128, 8], F32, o + i * 32) for i in range(2)]; o += 64
            o_tm = A("mo_tm", [128, NT, 256], BF16, o); o += 8 * KB
            load_w_cols(Wz, l, [(0, MLA0, 416)], "Wz")
            S.dma(wuq[:], dr["mla_w_uq"][l].rearrange("(k p) c -> p k c", p=128), writes=["wuq"], q="pool")
            S.dma(wukv[:], dr["mla_w_ukv"][l], writes=["wukv"], q="pool")
            S.dma(gq[:], dr["mla_q_norm_g"][l:l + 1, :].to_broadcast([128, 256]), writes=["gq"])
            S.dma(gkv[:], dr["mla_kv_norm_g"][l:l + 1, :].to_broadcast([128, 128]), writes=["gkv"])
            S.op("pool", lambda e: e.memset(Vg[:, :, :, 64:65], 1.0), writes=["mV"])
            zs_ = [A("mzs%d" % i, [128, 416], F32, o + i * 1664) for i in range(2)]; o += 3328
            qs_ = [A("mqs%d" % i, [128, 384], F32, o + i * 1536) for i in range(2)]; o += 3072
            kvs_ = [A("mkvs%d" % i, [128, 512], F32, o + i * 2048) for i in range(2)]; o += 4096

            def mla_vars(t):
                p = t % 2
                return p, str(p), qtm_[p], ktm_[t % 3], cqn_[p], ckvn_[p], rA_[p], rB_[p], sq2_[p], nst2_[p]

            def mla_A1(t):
                p, sp, qtm, ktm, cqn, ckvn, rA, rB, sq2, nst2 = mla_vars(t)
                zs = zs_[p]
                zk = "mzs" + sp
                ((pz, pzk),) = project_tile(t, Wz, [(0, 416)], "Wz")
                S.op("act", lambda e: e.copy(out=zs[:], in_=pz[:, 0:416]), reads=[pzk], writes=[zk])

            def mla_A1b(t):
                p, sp, qtm, ktm, cqn, ckvn, rA, rB, sq2, nst2 = mla_vars(t)
                zs = zs_[p]
                zk = "mzs" + sp
                rmsnorm_tile(zs[:, 0:256], 256, gq[:], cqn[:], (nst2, sq2[:]), "mq" + sp, ["gq"], ["cqn" + sp], [zk])
                rmsnorm_tile(zs[:, 256:384], 128, gkv[:], ckvn[:], (nst2, sq2[:, 0:128]), "mq" + sp, ["gkv"], ["ckvn" + sp], [zk])
                rope(ktm[:, 0, 64:80], ktm[:, 0, 80:96], zs[:, 384:400], zs[:, 400:416], COS(t, 0, 16), SIN(t, 0, 16), [128, 16],
                     rA[:, 0:16], rB[:, 0:16], [zk], ["ktm%d" % (t % 3)], "mr" + sp)
                S.op("pool", lambda e: e.tensor_copy(out=ktm[:, 1:4, 64:96], in_=ktm[:, 0:1, 64:96].to_broadcast([128, 3, 32])), reads=["ktm%d" % (t % 3)], writes=["ktm%d" % (t % 3)])

            def mla_A2(t):
                p, sp, qtm, ktm, cqn, ckvn, rA, rB, sq2, nst2 = mla_vars(t)
                transposes(cqnT[:, :, t * 128:(t + 1) * 128], [cqn[:, 0:128], cqn[:, 128:256]], 128, ["cqn" + sp], ["cqnT%d" % p])
                transposes(ckvnT[:, t * 128:(t + 1) * 128].unsqueeze(1), [ckvn[:]], 128, ["ckvn" + sp], ["ckvnT%d" % p])
                pq, pqk = (ps_mm[1], "ps_mm1") if p == 0 else (ps_st[1], "ps_st1")
                for kk in range(2):
                    S.op("pe", lambda e, kk=kk: e.matmul(pq[:, 0:384], lhsT=cqnT[:, kk, t * 128:(t + 1) * 128], rhs=wuq[:, kk, :], start=(kk == 0), stop=(kk == 1)),
                         reads=["cqnT%d" % p, "wuq"], writes=[pqk], sig=(kk == 1))
                S.op("act", lambda e: e.copy(out=qs_[p][:], in_=pq[:, 0:384]), reads=[pqk], writes=["mqs" + sp])
                pkv = ps_acc[p][:].rearrange("p a b -> p (a b)")
                pkk = "ps_acc%d" % p
                S.op("pe", lambda e: e.matmul(pkv[:, 0:512], lhsT=ckvnT[:, t * 128:(t + 1) * 128], rhs=wukv[:], start=True, stop=True),
                     reads=["ckvnT%d" % p, "wukv"], writes=[pkk])
                S.op("act", lambda e: e.copy(out=kvs_[p][:], in_=pkv[:, 0:512]), reads=[pkk], writes=["mkvs" + sp])

            def mla_B(t):
                p, sp, qtm, ktm, cqn, ckvn, rA, rB, sq2, nst2 = mla_vars(t)
                pq3 = qs_[p][:].rearrange("p (h d) -> p h d", h=4)
                pqk = "mqs" + sp
                S.op("act", lambda e: e.copy(out=qtm[:, :, 0:64], in_=pq3[:, :, 0:64]), reads=[pqk], writes=["qtmN" + sp])
                rope(qtm[:, :, 64:80], qtm[:, :, 80:96], pq3[:, :, 64:80], pq3[:, :, 80:96], COS(t, 0, 16), SIN(t, 0, 16), [128, 4, 16],
                     rA[:].rearrange("p (h d) -> p h d", h=4), rB[:].rearrange("p (h d) -> p h d", h=4), [pqk], ["qtm" + sp], "mr" + sp)
                transposes(QT[:, :, t * 128:(t + 1) * 128], [qtm[:, h, :] for h in range(4)], 96, ["qtm" + sp, "qtmN" + sp], ["mQT"], eng="act")
                pkv3 = kvs_[p][:].rearrange("p (h d) -> p h d", h=4)
                pkk = "mkvs" + sp
                S.op("dve", lambda e: e.tensor_copy(out=ktm[:, :, 0:64], in_=pkv3[:, :, 0:64]), reads=[pkk], writes=["ktmN%d" % (t % 3)])
                S.op("dve", lambda e: e.tensor_copy(out=Vg[:, t, :, 0:64], in_=pkv3[:, :, 64:128]), reads=[pkk], writes=["mV"])
                transposes(KTt[:, :, t * 128:(t + 1) * 128], [ktm[:, h, :] for h in range(4)], 96, ["ktm%d" % (t % 3), "ktmN%d" % (t % 3)], ["mKT"])
            for step in range(NT + 2):
                if step < NT:
                    mla_A1(step)
                if 1 <= step <= NT:
                    mla_A2(step - 1)
                if step >= 2:
                    mla_B(step - 2)
                if step < NT:
                    mla_A1b(step)
            if stop == "mla_prep":
                dump("trig", QT[:], [96, 4, S_LEN], ["mQT"]) if False else None
                print("stop", stop, S.nops)
                return
            for h in range(4):
                attn_core("mla", lambda q0, q1, h=h: QT[:, h, q0:q1], lambda kt, h=h: KTt[:, h, kt * 128:(kt + 1) * 128],
                          lambda kt, h=h: Vg[:, kt, h, 0:65], float(96 ** -0.5), causal_steps, causal_bias,
                          std_fin(o_tm, h, "mo_tm"), ["mQT", "mKT", "mV"], ndummy=NDUMMY)
            if l == 0:
                dump("o_mla", o_tm[:], [128, NT, 256], ["mo_tm"])
            finish_mixer(0, o_tm, "mo_tm")
            S.barrier()

        if "fox" in mixers:
            o = WS0
            QT = A("fQT", [70, 4, S_LEN], BF16, o); o += 16 * KB
            KTt = A("fKT", [70, 4, S_LEN], BF16, o); o += 16 * KB
            Vg = A("fV", [128, NT, 4, 66], BF16, o); o += NT * 4 * 66 * 2
            o = (o + 31) // 32 * 32
            qk_tm = A("fqk_tm", [128, NT, 2, 4, 70], BF16, o); o += NT * 2 * 4 * 70 * 2
            o = (o + 31) // 32 * 32
            logf = A("flogf", [128, NT, 4], F32, o); o += 256
            cum = A("fcum", [128, NT, 4], F32, o); o += 256
            tot = A("ftot", [128, NT, 4], F32, o); o += 256
            car = A("fcar", [128, NT, 4], F32, o); o += 256
            fb = A("ffb", [128, 4], F32, o); o += 32
            ftmp = A("fftmp", [128, NT, 4], F32, o); o += 256
            chi = A("fchi", [128, NT, 4], BF16, o); o += 128
            cmid = A("fcmid", [128, NT, 4], BF16, o); o += 128
            clo = A("fclo", [128, NT, 4], BF16, o); o += 128
            r1 = A("fr1", [128, NT, 4], F32, o); o += 256
            r2_ = A("fr2", [128, NT, 4], F32, o); o += 256
            o_tm = A("fo_tm", [128, NT, 256], BF16, o); o += 8 * KB
            load_w_cols(Wz, l, [(0, FOX0, 772)], "Wz")
            S.dma(fb[:], dr["fox_f_bias"][l:l + 1, :].to_broadcast([128, 4]), writes=["ffb"])
            S.op("pool", lambda e: e.memset(Vg[:, :, :, 64:65], 1.0), writes=["fV"])
            S.op("pool", lambda e: e.memset(qk_tm[:, :, 0, :, 67:70], 1.0), writes=["fqk_tm"])
            S.op("pool", lambda e: e.memset(qk_tm[:, :, 1, :, 64:67], 1.0), writes=["fqk_tm"])
            for t in range(NT):
                (pa, pak), (pb, pbk) = project_tile(t, Wz, [(0, 512), (512, 260)], "Wz")
                warm(NWARM)
                pa4 = pa[:, 0:512].rearrange("p (a h d) -> p a h d", a=2, h=4)
                S.op("act", lambda e: e.copy(out=qk_tm[:, t, :, :, 0:64], in_=pa4), reads=[pak], writes=["fqk_tm"])
                S.op("dve", lambda e: e.tensor_copy(out=Vg[:, t, :, 0:64], in_=pb[:, 0:256].rearrange("p (h d) -> p h d", h=4)), reads=[pbk], writes=["fV"])
                S.op("dve", lambda e: e.tensor_tensor(out=ftmp[:, t, :], in0=pb[:, 256:260], in1=fb[:], op=ALU.add), reads=[pbk, "ffb"], writes=["fftmp"])
            S.op("act", lambda e: e.activation(out=logf[:], in_=ftmp[:], func=AF.Exp, scale=-1.0), reads=["fftmp"], writes=["flogf"])
            S.op("act", lambda e: e.activation(out=logf[:], in_=logf[:], func=AF.Ln, bias=cst[:, 1:2], scale=1.0), reads=["flogf", "const"], writes=["flogf"])
            S.op("dve", lambda e: e.tensor_scalar(out=logf[:], in0=logf[:], scalar1=-1.0, scalar2=None, op0=ALU.mult), reads=["flogf"], writes=["flogf"])
            pc = ps_x
            S.op("pe", lambda e: e.matmul(pc[:, 0:64], lhsT=Umat[:], rhs=logf[:].rearrange("p a b -> p (a b)"), start=True, stop=True), reads=["flogf", "const"], writes=[PX])
            S.op("dve", lambda e: e.tensor_copy(out=cum[:].rearrange("p a b -> p (a b)"), in_=pc[:, 0:64]), reads=[PX], writes=["fcum"])
            S.op("pe", lambda e: e.matmul(pc[:, 0:64], lhsT=onesf[:], rhs=logf[:].rearrange("p a b -> p (a b)"), start=True, stop=True), reads=["flogf", "const", "fcum"], writes=[PX])
            S.op("dve", lambda e: e.tensor_copy(out=tot[:].rearrange("p a b -> p (a b)"), in_=pc[:, 0:64]), reads=[PX], writes=["ftot"])
            S.op("dve", lambda e: e.memset(car[:, 0, :], 0.0), writes=["fcar"])
            for t in range(1, NT):
                S.op("dve", lambda e, t=t: e.tensor_tensor(out=car[:, t, :], in0=car[:, t - 1, :], in1=tot[:, t - 1, :], op=ALU.add), reads=["fcar", "ftot"], writes=["fcar"])
            S.op("dve", lambda e: e.tensor_tensor(out=cum[:], in0=cum[:], in1=car[:], op=ALU.add), reads=["fcum", "fcar"], writes=["fcum"])
            if l == 0:
                dump("fox_c", cum[:], [128, NT, 4], ["fcum"])
            S.op("dve", lambda e: e.tensor_scalar(out=r1[:], in0=cum[:], scalar1=8.0, scalar2=None, op0=ALU.mult), reads=["fcum"], writes=["fr1"])
            S.op("dve", lambda e: e.tensor_copy(out=chi[:], in_=r1[:]), reads=["fr1"], writes=["fchi"])
            S.op("dve", lambda e: e.tensor_tensor(out=r2_[:], in0=r1[:], in1=chi[:], op=ALU.subtract), reads=["fr1", "fchi"], writes=["fr2"])
            S.op("dve", lambda e: e.tensor_copy(out=cmid[:], in_=r2_[:]), reads=["fr2"], writes=["fcmid"])
            S.op("dve", lambda e: e.tensor_tensor(out=r1[:], in0=r2_[:], in1=cmid[:], op=ALU.subtract), reads=["fr2", "fcmid"], writes=["fr1"])
            S.op("dve", lambda e: e.tensor_copy(out=clo[:], in_=r1[:]), reads=["fr1"], writes=["fclo"])
            for j, part in enumerate([chi, cmid, clo]):
                S.op("dve", lambda e, j=j, part=part: e.tensor_copy(out=qk_tm[:, :, 0, :, 64 + j], in_=part[:]), reads=["fchi", "fcmid", "fclo"], writes=["fqk_tm"])
                S.op("dve", lambda e, j=j, part=part: e.tensor_scalar(out=qk_tm[:, :, 1, :, 67 + j], in0=part[:], scalar1=-1.0, scalar2=None, op0=ALU.mult),
                     reads=["fchi", "fcmid", "fclo"], writes=["fqk_tm"])
            for t in range(NT):
                transposes(QT[:, :, t * 128:(t + 1) * 128], [qk_tm[:, t, 0, h, :] for h in range(4)], 70, ["fqk_tm"], ["fQT"], eng="act")
                transposes(KTt[:, :, t * 128:(t + 1) * 128], [qk_tm[:, t, 1, h, :] for h in range(4)], 70, ["fqk_tm"], ["fKT"])
            for h in range(4):
                attn_core("fox", lambda q0, q1, h=h: QT[:, h, q0:q1], lambda kt, h=h: KTt[:, h, kt * 128:(kt + 1) * 128],
                          lambda kt, h=h: Vg[:, kt, h, 0:65], 0.125, causal_steps, causal_bias,
                          std_fin(o_tm, h, "fo_tm"), ["fQT", "fKT", "fV"], ndummy=NDUMMY)
            if l == 0:
                dump("o_fox", o_tm[:], [128, NT, 256], ["fo_tm"])
            finish_mixer(2, o_tm, "fo_tm")
            S.barrier()

        if "dsa" in mixers:
            o = WS0
            QK3 = A("dQK3", [128, 3, S_LEN], BF16, o); o += 12 * KB
            QT2 = QK3[:, 0:2, :]
            KT2 = QK3[:, 2, :]
            Vg = A("dV", [128, NT, 66], BF16, o); o += NT * 66 * 2
            o = (o + 31) // 32 * 32
            qki = A("dqki", [96, 4, S_LEN], BF16, o); o += 16 * KB
            qiT = qki[:, 0:3, :]
            kiT = qki[:, 3, :]
            score = [A("dscore%d" % i, [128, S_LEN], F32, o + i * 8 * KB) for i in range(4)]; o += 32 * KB
            Mb = [A("dMb0", [128, 4, 1536], BF16, o), A("dMb1", [128, 4, S_LEN], BF16, o + 12 * KB)]; o += 28 * KB
            o_c = [A("do_c%d" % i, [128, 4, 256], BF16, o + i * 2 * KB) for i in range(2)]; o += 4 * KB
            qk6 = A("dqk6", [128, 6, 64], BF16, o); o += 768
            qi9 = A("dqi9", [128, 12, 32], BF16, o); o += 768
            wq = A("dwq", [128, NT, 8], F32, o); o += 512
            rA = A("drA", [128, 9, 8], F32, o); o += 288
            rB = A("drB", [128, 9, 8], F32, o); o += 288
            bs = A("dbs", [128, 2, 64], F32, o); o += 512
            ki3 = A("dki3", [128, 96], BF16, o); o += 192
            junk1 = A("djunk1", [128, 16], BF16, o); o += 32
            wzo = OFF0 + 32 * KB
            Rb = [A("dR%d" % i, [128, 512], BF16, wzo + i * KB) for i in range(4)]
            diag = [A("ddiag%d" % i, [128, 8, 128], BF16, wzo + 4 * KB + i * 2 * KB) for i in range(2)]
            Rall = A("dRall", [128, S_LEN], BF16, wzo)
            load_w_cols(Wz, l, [(0, DSA0, 680)], "Wz")
            S.op("pool", lambda e: e.memset(Vg[:, :, 64:65], 1.0), writes=["dV"])
            S.op("pool", lambda e: e.memset(qi9[:], 0.0), writes=["dqi90", "dqi9N0"])
            qk6_ = [qk6, A("dqk6b", [128, 6, 64], BF16, o)]; o += 768
            qi9_ = [qi9, A("dqi9b", [128, 12, 32], BF16, o)]; o += 768
            ki3_ = [ki3, A("dki3b", [128, 96], BF16, o)]; o += 192
            rA_ = [rA, A("drAb", [128, 9, 8], F32, o)]; o += 288
            rB_ = [rB, A("drBb", [128, 9, 8], F32, o)]; o += 288
            S.op("pool", lambda e: e.memset(qi9_[1][:], 0.0), writes=["dqi91", "dqi9N1"])
            ptr0 = OFF0 + 32 * KB + 13 * KB
            zsA_ = [A("dzsA%d" % i, [128, 384], F32, ptr0 + i * 1536) for i in range(2)]
            zsB_ = [A("dzsB%d" % i, [128, 296], F32, ptr0 + 3072 + i * 1184) for i in range(2)]

            def dsa_A(t):
                p = t % 2
                sp = str(p)
                qk6, qi9, ki3, rA, rB = qk6_[p], qi9_[p], ki3_[p], rA_[p], rB_[p]
                (pa, pak0), (pb, pbk0) = project_tile(t, Wz, [(0, 384), (384, 296)], "Wz")
                warm(NWARM)
                za, zb = zsA_[p], zsB_[p]
                pak, pbk = "dzsA" + sp, "dzsB" + sp
                S.op("act", lambda e: e.copy(out=za[:], in_=pa[:, 0:384]), reads=[pak0], writes=[pak])
                S.op("act", lambda e: e.copy(out=zb[:], in_=pb[:, 0:296]), reads=[pbk0], writes=[pbk])
                pa3 = za[:, 0:320].rearrange("p (h d) -> p h d", h=5)
                rope(qk6[:, 0:5, 0:8], qk6[:, 0:5, 8:16], pa3[:, :, 0:8], pa3[:, :, 8:16], COS(t, 16, 24), SIN(t, 16, 24), [128, 5, 8],
                     rA[:, 0:5, :], rB[:, 0:5, :], [pak], ["dqk6" + sp], "dr" + sp)
                S.op("act", lambda e: e.copy(out=qk6[:, 0:5, 16:64], in_=pa3[:, :, 16:64]), reads=[pak], writes=["dqk6N" + sp])
                S.op("dve", lambda e: e.tensor_copy(out=Vg[:, t, 0:64], in_=za[:, 320:384]), reads=[pak], writes=["dV"])
                S.op("pool", lambda e: e.tensor_copy(out=qk6[:, 5, :], in_=qk6[:, 4, :]), reads=["dqk6" + sp, "dqk6N" + sp], writes=["dqk6D" + sp])
                pb3 = zb[:, 0:288].rearrange("p (h d) -> p h d", h=9)
                rope(qi9[:, 0:9, 0:4], qi9[:, 0:9, 4:8], pb3[:, :, 0:4], pb3[:, :, 4:8], COS(t, 24, 28), SIN(t, 24, 28), [128, 9, 4],
                     rA[:, :, 0:4], rB[:, :, 0:4], [pbk], ["dqi9" + sp], "dr" + sp)
                S.op("act", lambda e: e.copy(out=qi9[:, 0:9, 8:32], in_=pb3[:, :, 8:32]), reads=[pbk], writes=["dqi9N" + sp])
                S.op("dve", lambda e: e.tensor_copy(out=wq[:, t, :], in_=zb[:, 288:296]), reads=[pbk], writes=["dwq"])
                S.op("pool", lambda e: e.tensor_copy(out=ki3[:].rearrange("p (a b) -> p a b", a=3), in_=qi9[:, 8:9, :].to_broadcast([128, 3, 32])), reads=["dqi9" + sp, "dqi9N" + sp], writes=["dki3" + sp])

            def dsa_B(t):
                p = t % 2
                sp = str(p)
                qk6, qi9, ki3, rA, rB = qk6_[p], qi9_[p], ki3_[p], rA_[p], rB_[p]
                qf = qk6[:].rearrange("p a b -> p (a b)")
                transposes(QK3[:, :, t * 128:(t + 1) * 128], [qf[:, 0:128], qf[:, 128:256], qf[:, 256:384]], 128, ["dqk6" + sp, "dqk6N" + sp, "dqk6D" + sp], ["dQT", "dKT"], eng="act")
                qflat = qi9[:].rearrange("p a b -> p (a b)")
                transposes(qki[:, :, t * 128:(t + 1) * 128], [qflat[:, 0:96], qflat[:, 96:192], qflat[:, 192:288], ki3[:]], 96,
                           ["dqi9" + sp, "dqi9N" + sp, "dki3" + sp], ["dqiT", "dkiT"], eng="act")
                warm(NWARM)
            for step in range(NT + 1):
                if step < NT:
                    dsa_A(step)
                if step >= 1:
                    dsa_B(step - 1)
            S.barrier()
            NIT = 21
            ddum = ps_trs[0][:].rearrange("p a b -> p (a b)").bitcast(F32)
            pairs = [(2 * i, 2 * i + 1) for i in range(1, 8)]

            def sbuf(qt):
                i = qt % 4
                return score[i], "dscore%d" % i

            def dsa_scores(pair):
                for qt in pair:
                    L = (qt + 1) * 128
                    sc, sk = sbuf(qt)
                    dg = diag[qt % 2]
                    dk = "ddiag%d" % (qt % 2)
                    S.op("dve", lambda e: e.tensor_tensor(out=dg[:], in0=identb[:].unsqueeze(1).to_broadcast([128, 8, 128]),
                                                          in1=wq[:, qt, :].unsqueeze(2).to_broadcast([128, 8, 128]), op=ALU.mult), reads=["const", "dwq"], writes=[dk])
                    nkc = (L + 511) // 512
                    for kc in range(nkc):
                        k0 = kc * 512
                        n = min(512, L - k0)

                        def logit(h):
                            g, jj = divmod(h, 3)
                            pl = ps_mm[h % 2]
                            S.op("pe", lambda e: e.matmul(pl[:, 0:n], lhsT=qiT[32 * jj:32 * jj + 32, g, qt * 128:(qt + 1) * 128],
                                                          rhs=kiT[32 * jj:32 * jj + 32, k0:k0 + n], start=True, stop=True),
                                 reads=["dqiT", "dkiT"], writes=["ps_mm%d" % (h % 2)])
                            r = Rb[h % 4]
                            S.op("act", lambda e: e.activation(out=r[:, 0:n], in_=pl[:, 0:n], func=AF.Relu), reads=["ps_mm%d" % (h % 2)], writes=["dR%d" % (h % 4)])

                        def hsum(h):
                            r = Rb[h % 4]
                            S.op("pe", lambda e: e.matmul(ps_x[:, 0:n], lhsT=dg[:, h, :], rhs=r[:, 0:n], start=(h == 0), stop=(h == 7)),
                                 reads=["dR%d" % (h % 4), dk], writes=[PX])
                        logit(0)
                        for h in range(8):
                            if h + 1 < 8:
                                logit(h + 1)
                            hsum(h)
                            if h % 2 == 1 and NDUMMY:
                                S.op("pe", lambda e: e.matmul(ddum, lhsT=identb[:], rhs=cmpbias[:, 0:512], start=True, stop=True, skip_group_check=True), sig=False)
                        S.op("act", lambda e: e.copy(out=sc[:, k0:k0 + n], in_=ps_x[:, 0:n]), reads=[PX], writes=[sk])

            def dsa_bisect(pair, pi):
                st = {}
                for j, qt in enumerate(pair):
                    L = (qt + 1) * 128
                    sc, sk = sbuf(qt)
                    b = bs[:, j, :]
                    kx = "b%d_" % j
                    S.op("dve", lambda e, b=b, sc=sc, L=L: e.tensor_reduce(out=b[:, 0:1], in_=sc[:, 0:L], axis=AX.X, op=ALU.max, apply_absolute_value=True), reads=[sk], writes=[kx + "M"])
                    S.op("pool", lambda e, sc=sc, L=L: e.tensor_tensor(out=sc[:, L - 128:L], in0=sc[:, L - 128:L], in1=causqk[:], op=ALU.add), reads=[sk, "const", kx + "M"], writes=[sk])
                    S.op("pool", lambda e, b=b: e.tensor_scalar(out=b[:, 8:8 + NIT + 1], in0=pow2[:, 0:NIT + 1], scalar1=b[:, 0:1], scalar2=None, op0=ALU.mult), reads=[kx + "M", "const"], writes=[kx + "d"])
                    S.op("pool", lambda e, b=b: e.memset(b[:, 4:5], 0.0), writes=[kx + "mid0"])
                    st[qt] = (b, kx, sc, sk, L)
                for it in range(NIT):
                    for j, qt in enumerate(pair):
                        b, kx, sc, sk, L = st[qt]
                        m = b[:, 4 + (it % 2):5 + (it % 2)]
                        nm = b[:, 4 + ((it + 1) % 2):5 + ((it + 1) % 2)]
                        mk, nmk = kx + "mid%d" % (it % 2), kx + "mid%d" % ((it + 1) % 2)
                        if pi == len(pairs) - 1 and j == 1:
                            S.op("act", lambda e, m=m, sc=sc, L=L, b=b: e.activation(out=Rall[:, 0:L], in_=sc[:, 0:L], func=AF.Sign, bias=m, scale=-1.0, accum_out=b[:, 6:7]),
                                 reads=[sk, mk], writes=[kx + "cnt", "dR0", "dR1", "dR2", "dR3"])
                            S.op("pool", lambda e, b=b, it=it, L=L: e.tensor_scalar(out=b[:, 7:8], in0=b[:, 6:7], scalar1=float(L) - 510.5, scalar2=b[:, 8 + it:9 + it], op0=ALU.is_lt, op1=ALU.mult),
                                 reads=[kx + "cnt", kx + "d"], writes=[kx + "sel"])
                        else:
                            S.op("dve", lambda e, m=m, sc=sc, L=L, b=b, j=j: e.tensor_scalar(out=junk1[:, j:j + 1].to_broadcast([128, L]), in0=sc[:, 0:L], scalar1=m, scalar2=0.0, op0=ALU.is_ge, op1=ALU.add,
                                                                                    accum_out=b[:, 6:7]), reads=[sk, mk], writes=[kx + "cnt", kx + "junk"])
                            S.op("pool", lambda e, b=b, it=it: e.tensor_scalar(out=b[:, 7:8], in0=b[:, 6:7], scalar1=255.5, scalar2=b[:, 8 + it:9 + it], op0=ALU.is_ge, op1=ALU.mult),
                                 reads=[kx + "cnt", kx + "d"], writes=[kx + "sel"])
                        S.op("pool", lambda e, b=b, it=it, m=m, nm=nm: e.tensor_scalar(out=nm, in0=b[:, 7:8], scalar1=m, scalar2=b[:, 9 + it:10 + it], op0=ALU.add, op1=ALU.subtract),
                             reads=[kx + "sel", mk, kx + "d"], writes=[nmk])
                for j, qt in enumerate(pair):
                    b, kx, sc, sk, L = st[qt]
                    c = qt // 4
                    mb = Mb[c % 2]
                    fm = b[:, 4 + (NIT % 2):5 + (NIT % 2)]
                    S.op("pool", lambda e, b=b, fm=fm: e.tensor_tensor(out=b[:, 3:4], in0=fm, in1=b[:, 8 + NIT:9 + NIT], op=ALU.subtract), reads=[kx + "mid%d" % (NIT % 2), kx + "d"], writes=[kx + "thr"])
                    S.op("dve", lambda e, b=b, sc=sc, L=L, mb=mb, qt=qt, c=c: e.tensor_scalar(out=mb[:, qt - 4 * c, 0:L], in0=sc[:, 0:L], scalar1=b[:, 3:4], scalar2=NEGB, op0=ALU.is_lt, op1=ALU.mult),
                         reads=[sk, kx + "thr"], writes=["dMb%d" % (c % 2)])

            def dsa_attn(c):
                mb = Mb[c % 2]
                mbk = "dMb%d" % (c % 2)
                oc = o_c[c % 2]
                ock = "do_c%d" % (c % 2)

                def dsa_bias(kt, qt):
                    if qt < 2:
                        return causal_bias(kt, qt)
                    return [(mb[:, qt - 4 * c, kt * 128:(kt + 1) * 128], identb[:], [mbk, "const"])]

                def fin_for(h):
                    inner = std_fin(oc, h, ock)
                    return lambda cc, acc, akey: inner(0, acc, akey)
                for h in range(4):
                    p0 = (h % 2) * 64
                    attn_core("dsa", lambda q0, q1, h=h, p0=p0: QT2[p0:p0 + 64, h // 2, q0:q1], lambda kt, p0=p0: KT2[p0:p0 + 64, kt * 128:(kt + 1) * 128],
                              lambda kt: Vg[:, kt, 0:65], 0.125, lambda cc: causal_steps(c) if cc == c else [], dsa_bias,
                              fin_for(h), ["dQT", "dKT", "dV"], ndummy=NDUMMY, dummy_out=ddum)
                for tt in range(4):
                    t = 4 * c + tt
                    transposes(oT[:, 3, :, t * 128:(t + 1) * 128], [oc[:, tt, 0:128], oc[:, tt, 128:256]], 128, [ock], ["oT"], eng=("act" if tt % 2 else "dve"), bank=1)
                if l == 0:
                    dump("o_dsa%d" % c, oc[:], [128, 4, 256], [ock])

            dsa_scores(pairs[0])
            for i, pr_ in enumerate(pairs):
                if i + 1 < len(pairs):
                    dsa_scores(pairs[i + 1])
                dsa_bisect(pr_, i)
                if pr_[1] % 4 == 3:
                    dsa_attn(pr_[1] // 4)
            S.barrier()

        if "nsa" in mixers:
            o = WS0
            QT = A("nQT", [96, 4, S_LEN], BF16, o); o += 16 * KB
            k4T = A("nk4T", [96, 4, S_LEN], BF16, o); o += 16 * KB
            kcT = k4T[:, 0, :]
            ksT = k4T[:, 1, :]
            kwT = k4T[:, 2, :]
            vcT = k4T[:, 3, :]
            Vs = A("nVs", [128, NT, 66], BF16, o); o += NT * 66 * 2
            Vw = A("nVw", [128, NT, 66], BF16, o); o += NT * 66 * 2
            o = (o + 31) // 32 * 32
            Wk = A("nWk", [64, 32, 64], BF16, o); o += 4 * KB
            Wv = A("nWv", [64, 32, 64], BF16, o); o += 4 * KB
            Wkf = A("nWkf", [128, 16, 64], BF16, o); o += 2 * KB
            Wvf = A("nWvf", [128, 16, 64], BF16, o); o += 2 * KB
            pek = A("npek", [128, 16], BF16, o); o += 32
            pev = A("npev", [128, 16], BF16, o); o += 32
            kcmpT = A("nkcmpT", [64, 128], BF16, o); o += 256
            vcx = A("nvcx", [128, 97], BF16, o); o += 224
            imp = A("nimp", [128, NT, 32], F32, o); o += 2 * KB
            blkb = A("nblkb", [128, 96], BF16, o); o += 192
            gt = A("ngt", [128, NT, 12], F32, o); o += 768
            oacc = A("noacc", [128, NT, 256], F32, o); o += 16 * KB
            o_tm = A("no_tm", [128, NT, 256], BF16, o); o += 8 * KB
            q7 = A("nq7", [128, 7, 64], BF16, o); o += 896
            vc_tm = A("nvc_tm", [128, 64], BF16, o); o += 128
            rA = A("nrA", [128, 7, 8], F32, o); o += 224
            rB = A("nrB", [128, 7, 8], F32, o); o += 224
            m8 = A("nm8", [128, 8], F32, o); o += 32
            itmp = A("nitmp", [128, 4, 32], F32, o); o += 512
            n0 = NSA0
            load_w_cols(Wz, l, [(0, n0, 320), (320, n0 + 384, 64), (384, n0 + 512, 64),
                                (448, n0 + 320, 64), (512, n0 + 448, 64), (576, n0 + 576, 64), (640, n0 + 640, 12)], "Wz")
            S.dma(Wk[:], dr["nsa_cmp_w"][l, 0].rearrange("(l d) o -> d l o", d=64), writes=["nWk"], q="pool")
            S.dma(Wv[:], dr["nsa_cmp_w"][l, 1].rearrange("(l d) o -> d l o", d=64), writes=["nWv"], q="pool")
            S.dma(Wkf[:], dr["nsa_cmp_w"][l, 0].rearrange("(j p) o -> p j o", p=128), writes=["nWkf"], q="pool")
            S.dma(Wvf[:], dr["nsa_cmp_w"][l, 1].rearrange("(j p) o -> p j o", p=128), writes=["nWvf"], q="pool")
            S.dma(pek[:], dr["nsa_cmp_pe"][l, 0].rearrange("(j p) -> p j", p=128), writes=["npek"], q="pool", allow_slow_non_contiguous=True)
            S.dma(pev[:], dr["nsa_cmp_pe"][l, 1].rearrange("(j p) -> p j", p=128), writes=["npev"], q="pool", allow_slow_non_contiguous=True)
            S.dma(ksT[64:96, :], dr["c_E"], writes=["nksT"], q="pool")
            S.op("pool", lambda e: e.memset(blkb[:], 0.0), writes=["nblkb"])
            S.op("pool", lambda e: e.memset(Vs[:, :, 64:65], 1.0), writes=["nVs"])
            S.op("pool", lambda e: e.memset(Vw[:, :, 64:65], 1.0), writes=["nVw"])
            q7_ = [q7, A("nq7b", [128, 7, 64], BF16, o)]; o += 896
            vc_tm_ = [vc_tm, A("nvc_tmb", [128, 64], BF16, o)]; o += 128
            rA_ = [rA, A("nrAb", [128, 7, 8], F32, o)]; o += 224
            rB_ = [rB, A("nrBb", [128, 7, 8], F32, o)]; o += 224
            zsA_ = [A("nzsA%d" % i, [128, 448], F32, o + i * 1792) for i in range(2)]; o += 3584
            zsB_ = [A("nzsB%d" % i, [128, 204], F32, o + i * 832) for i in range(2)]; o += 1664

            def nsa_A(t):
                p = t % 2
                sp = str(p)
                q7, vc_tm, rA, rB = q7_[p], vc_tm_[p], rA_[p], rB_[p]
                (pa, pak0), (pb, pbk0) = project_tile(t, Wz, [(0, 448), (448, 204)], "Wz")
                warm(NWARM)
                za, zb = zsA_[p], zsB_[p]
                pak, pbk = "nzsA" + sp, "nzsB" + sp
                S.op("act", lambda e: e.copy(out=za[:], in_=pa[:, 0:448]), reads=[pak0], writes=[pak])
                S.op("act", lambda e: e.copy(out=zb[:], in_=pb[:, 0:204]), reads=[pbk0], writes=[pbk])
                pa3 = za[:].rearrange("p (h d) -> p h d", h=7)
                rope(q7[:, :, 0:8], q7[:, :, 8:16], pa3[:, :, 0:8], pa3[:, :, 8:16], COS(t, 16, 24), SIN(t, 16, 24), [128, 7, 8],
                     rA[:], rB[:], [pak], ["nq7" + sp], "nr" + sp)
                S.op("act", lambda e: e.copy(out=q7[:, :, 16:64], in_=pa3[:, :, 16:64]), reads=[pak], writes=["nq7N" + sp])
                S.op("dve", lambda e: e.tensor_copy(out=vc_tm[:], in_=zb[:, 0:64]), reads=[pbk], writes=["nvc_tm" + sp])
                S.op("dve", lambda e: e.tensor_copy(out=Vs[:, t, 0:64], in_=zb[:, 64:128]), reads=[pbk], writes=["nVs"])
                S.op("dve", lambda e: e.tensor_copy(out=Vw[:, t, 0:64], in_=zb[:, 128:192]), reads=[pbk], writes=["nVw"])
                S.op("act", lambda e: e.activation(out=gt[:, t, :], in_=zb[:, 192:204], func=AF.Sigmoid), reads=[pbk], writes=["ngt"])

            def nsa_B(t):
                p = t % 2
                sp = str(p)
                q7, vc_tm, rA, rB = q7_[p], vc_tm_[p], rA_[p], rB_[p]
                transposes(QT[0:64, :, t * 128:(t + 1) * 128], [q7[:, h, :] for h in range(4)], 64, ["nq7" + sp, "nq7N" + sp], ["nQT"], eng="act")
                ts_ = slice(t * 128, (t + 1) * 128)
                transposes(k4T[0:64, :, ts_], [q7[:, 4, :], q7[:, 5, :], q7[:, 6, :], vc_tm[:]], 64, ["nq7" + sp, "nq7N" + sp, "nvc_tm" + sp],
                           ["nkcT", "nksT", "nkwT", "nvcT"], eng="act")
                warm(NWARM)
            for step in range(NT + 1):
                if step < NT:
                    nsa_A(step)
                if step >= 1:
                    nsa_B(step - 1)
            pk = ps_mm[0]
            for li in range(32):
                S.op("pe", lambda e, li=li: e.matmul(pk[0:64, 0:127], lhsT=Wk[:, li, :], rhs=kcT[0:64, li:li + 16 * 126 + 1:16], start=(li == 0), stop=False),
                     reads=["nWk", "nkcT"], writes=["ps_mm0"], sig=False)
            for j in range(16):
                S.op("pe", lambda e, j=j: e.matmul(pk[0:64, 0:127], lhsT=Wkf[:, j, :], rhs=pek[:, j:j + 1].to_broadcast([128, 127]), start=False, stop=(j == 15)),
                     reads=["nWkf", "npek"], writes=["ps_mm0"], sig=(j == 15))
            S.op("dve", lambda e: e.tensor_copy(out=kcmpT[:, 0:127], in_=pk[0:64, 0:127]), reads=["ps_mm0"], writes=["nkcmpT"])
            pv = ps_mm[1]
            for li in range(32):
                S.op("pe", lambda e, li=li: e.matmul(pv[0:127, 0:64], lhsT=vcT[0:64, li:li + 16 * 126 + 1:16], rhs=Wv[:, li, :], start=(li == 0), stop=False),
                     reads=["nWv", "nvcT"], writes=["ps_mm1"], sig=False)
            for j in range(16):
                S.op("pe", lambda e, j=j: e.matmul(pv[0:127, 0:64], lhsT=pev[:, j:j + 1].to_broadcast([128, 127]), rhs=Wvf[:, j, :], start=False, stop=(j == 15)),
                     reads=["nWvf", "npev"], writes=["ps_mm1"], sig=(j == 15))
            S.op("pool", lambda e: e.memset(vcx[:, 64:65], 1.0), writes=["nvcx"])
            S.op("dve", lambda e: e.tensor_copy(out=vcx[0:127, 0:64], in_=pv[0:127, 0:64]), reads=["ps_mm1"], writes=["nvcx"])
            S.op("pool", lambda e: e.tensor_copy(out=vcx[:, 65:97], in_=ovl[:]), reads=["const"], writes=["nvcx"])
            if l == 0:
                dump("nsa_kcmpT", kcmpT[:], [64, 128], ["nkcmpT"])
                dump("nsa_vcx", vcx[:], [128, 97], ["nvcx"])

            def gate_fn(path, h):
                return lambda c: (gt[:, 4 * c:4 * c + 4, path * 4 + h], "ngt")
            for h in range(4):
                def cmp_fin(c, acc, akey, h=h):
                    rec = misc[:, 64:68]
                    S.op("dve", lambda e: e.tensor_scalar(out=rec, in0=acc[:, :, 64], scalar1=1e-30, scalar2=None, op0=ALU.max), reads=[akey], writes=["rec"])
                    S.op("dve", lambda e: e.reciprocal(out=misc[:, 68:72], in_=rec), reads=["rec"], writes=["rec2"])
                    S.op("dve", lambda e: e.tensor_tensor(out=misc[:, 72:76], in0=misc[:, 68:72], in1=gt[:, 4 * c:4 * c + 4, h], op=ALU.mult), reads=["rec2", "ngt"], writes=["rec3"])
                    dst = oacc[:, 4 * c:4 * c + 4, h * 64:(h + 1) * 64]
                    S.op("dve", lambda e: e.tensor_tensor(out=dst, in0=acc[:, :, 0:64], in1=misc[:, 72:76].unsqueeze(2).to_broadcast([128, 4, 64]), op=ALU.mult),
                         reads=[akey, "rec3"], writes=["noacc"])
                    idst = imp[:, 4 * c:4 * c + 4, :]
                    if h == 0:
                        S.op("dve", lambda e: e.tensor_tensor(out=idst, in0=acc[:, :, 65:97], in1=misc[:, 68:72].unsqueeze(2).to_broadcast([128, 4, 32]), op=ALU.mult),
                             reads=[akey, "rec2"], writes=["nimp"])
                    else:
                        S.op("dve", lambda e: e.tensor_tensor(out=itmp[:], in0=acc[:, :, 65:97], in1=misc[:, 68:72].unsqueeze(2).to_broadcast([128, 4, 32]), op=ALU.mult),
                             reads=[akey, "rec2"], writes=["nitmp"])
                        S.op("pool", lambda e: e.tensor_tensor(out=idst, in0=idst, in1=itmp[:], op=ALU.add), reads=["nitmp", "nimp"], writes=["nimp"])
                attn_core("ncmp", lambda q0, q1, h=h: QT[0:64, h, q0:q1], lambda kt: kcmpT[:, 0:127], lambda kt: vcx[0:127, 0:97], 0.125,
                          lambda c: [(0, 4 * c, 4 * c + 4)],
                          lambda kt, qt: [],
                          cmp_fin, ["nQT", "nkcmpT", "nvcx"], nk=127, vw=97,
                          range_bias_fn=lambda kt, qlo, qhi: [(identb[0:127, 0:127], cmpbias[0:127, qlo * 128:qhi * 128], ["const"])])
            S.op("dve", lambda e: e.tensor_tensor(out=imp[:], in0=imp[:], in1=fkeep[:], op=ALU.mult), reads=["nimp", "const"], writes=["nimp"])
            S.op("dve", lambda e: e.tensor_tensor(out=imp[:], in0=imp[:], in1=fbase[:], op=ALU.add), reads=["nimp", "const"], writes=["nimp"])
            for t in range(NT):
                S.op("dve", lambda e, t=t: e.max(out=m8[:], in_=imp[:, t, :]), reads=["nimp"], writes=["nm8"])
                S.op("dve", lambda e, t=t: e.tensor_scalar(out=blkb[:, 64:96], in0=imp[:, t, :], scalar1=m8[:, 7:8], scalar2=NEGB, op0=ALU.is_lt, op1=ALU.mult), reads=["nimp", "nm8"], writes=["nblkb"])
                bank = t % 2
                ptr = ps_trs[bank]
                tk = "ps_tr%d" % bank
                S.op("pe", lambda e, ptr=ptr: e.transpose(out=ptr[0:96, 0, :], in_=blkb[:], identity=identb[:]), reads=["nblkb", "const"], writes=[tk])
                S.op("act", lambda e, ptr=ptr, t=t: e.copy(out=QT[64:96, :, t * 128:(t + 1) * 128], in_=ptr[64:96, 0:1, :].to_broadcast([32, 4, 128])), reads=[tk], writes=["nQT"])
            if l == 0:
                dump("nsa_imp", imp[:], [128, NT, 32], ["nimp"])

            def sel_bias(kt, qt):
                if kt == qt:
                    return [(identb[:], caust[:], ["const"])]
                return []

            def win_steps(c):
                out = []
                for kt in range(max(0, 4 * c - 4), 4 * c + 4):
                    qlo = max(4 * c, kt)
                    qhi = min(4 * c + 4, kt + 5)
                    if qhi > qlo:
                        out.append((kt, qlo, qhi))
                return out

            def win_bias(kt, qt):
                if kt == qt:
                    return [(identb[:], caust[:], ["const"])]
                if kt == qt - 4:
                    return [(identb[:], wint[:], ["const"])]
                return []
            for h in range(4):
                attn_core("nsel", lambda q0, q1, h=h: QT[0:96, h, q0:q1], lambda kt: ksT[0:96, kt * 128:(kt + 1) * 128], lambda kt: Vs[:, kt, 0:65], 0.125,
                          causal_steps, sel_bias, std_fin(oacc, h, "noacc", gate=gate_fn(1, h), accumulate=True), ["nQT", "nksT", "nVs"], ndummy=NDUMMY)
                attn_core("nwin", lambda q0, q1, h=h: QT[0:64, h, q0:q1], lambda kt: kwT[0:64, kt * 128:(kt + 1) * 128], lambda kt: Vw[:, kt, 0:65], 0.125,
                          win_steps, win_bias, std_fin(oacc, h, "noacc", gate=gate_fn(2, h), accumulate=True), ["nQT", "nkwT", "nVw"], ndummy=NDUMMY)
            for t in range(NT):
                S.op("pool", lambda e, t=t: e.tensor_copy(out=o_tm[:, t, :], in_=oacc[:, t, :]), reads=["noacc"], writes=["no_tm"])
            if l == 0:
                dump("o_nsa", o_tm[:], [128, NT, 256], ["no_tm"])
            finish_mixer(1, o_tm, "no_tm")
            S.barrier()

        mixed = A("mixed", [128, NT, D], BF16, OFF0 + 32 * KB)
        x_sb = A("x_sb", [128, NT, D], F32, OFF0 + 64 * KB)
        wo = A("wo", [128, KD, D], BF16, OFF0 + 128 * KB)
        for kh in range(4):
            S.dma(wo[:, kh * 2:(kh + 1) * 2, :], dr["w_out"][l].rearrange("(k p) c -> p k c", p=128)[:, kh * 2:(kh + 1) * 2, :], writes=["wo"], q="pool")
        for t in range(7, NT):
            S.dma(x_sb[:, t, :], x_src[t * 128:(t + 1) * 128, :], writes=["x_sb%d" % t])
        S.dma(g2[:], dr["norm2_g"][l:l + 1, :].to_broadcast([128, D]), writes=["g2"])
        o = OFF0 + 64 * KB
        Wg = [A("Wg%d" % i, [128, KD, 512], BF16, o + i * 8 * KB) for i in range(2)]; o += 16 * KB
        Wb = [A("Wb%d" % i, [128, 2, 512], BF16, o + i * 2 * KB) for i in range(2)]; o += 4 * KB
        sg = [A("sg%d" % i, [128, 512], F32, o + i * 2 * KB) for i in range(2)]; o += 4 * KB
        pr = [A("pr%d" % i, [128, 512], BF16, o + i * KB) for i in range(2)]; o += 2 * KB
        it = 0

        def load_gate_w(i):
            n_, cc_ = divmod(i, 2)
            b_ = i % 2
            gsrc = dr["w_in"][l, :, GATE0 + n_ * D + cc_ * 512:GATE0 + n_ * D + (cc_ + 1) * 512].rearrange("(k p) c -> p k c", p=128)
            for kh in range(2):
                S.dma(Wg[b_][:, kh * 4:(kh + 1) * 4, :], gsrc[:, kh * 4:(kh + 1) * 4, :], writes=["Wg%d" % b_], q="pool")
            S.dma(Wb[b_][:], dr["w_branch"][l, n_, :, cc_ * 512:(cc_ + 1) * 512].rearrange("(k p) c -> p k c", p=128), writes=["Wb%d" % b_], q="pool")
        load_gate_w(0)
        for n in range(4):
            for cc in range(2):
                b = it % 2
                it += 1
                if it < 8:
                    load_gate_w(it)
                for t in range(NT):
                    pg = ps_mm[t % 2]
                    pgk = "ps_mm%d" % (t % 2)
                    for k in range(KD):
                        S.op("pe", lambda e, k=k, pg=pg: e.matmul(pg[:, 0:512], lhsT=actT[:, k, t * 128:(t + 1) * 128], rhs=Wg[b][:, k, :], start=(k == 0), stop=(k == KD - 1)),
                             reads=["actT", "Wg%d" % b], writes=[pgk], sig=(k == KD - 1))
                    plf = ps_st[t % 2]
                    plk = "ps_st%d" % (t % 2)
                    for k in range(2):
                        S.op("pe", lambda e, k=k, plf=plf: e.matmul(plf[:, 0:512], lhsT=oT[:, n, k, t * 128:(t + 1) * 128], rhs=Wb[b][:, k, :], start=(k == 0), stop=(k == 1)),
                             reads=["oT", "Wb%d" % b], writes=[plk], sig=(k == 1))
                    s_ = sg[t % 2]
                    sk = "sg%d" % (t % 2)
                    S.op("act", lambda e, s_=s_, pg=pg: e.activation(out=s_[:], in_=pg[:, 0:512], func=AF.Sigmoid), reads=[pgk], writes=[sk])
                    dst = mixed[:, t, cc * 512:(cc + 1) * 512]
                    if n == 0:
                        S.op("dve", lambda e, s_=s_, plf=plf, dst=dst: e.tensor_tensor(out=dst, in0=s_[:], in1=plf[:, 0:512], op=ALU.mult), reads=[sk, plk], writes=["mixed"])
                    else:
                        p_ = pr[t % 2]
                        pk_ = "pr%d" % (t % 2)
                        S.op("dve", lambda e, s_=s_, plf=plf, p_=p_: e.tensor_tensor(out=p_[:], in0=s_[:], in1=plf[:, 0:512], op=ALU.mult), reads=[sk, plk], writes=[pk_])
                        S.op("pool", lambda e, p_=p_, dst=dst: e.tensor_tensor(out=dst, in0=dst, in1=p_[:], op=ALU.add), reads=[pk_, "mixed"], writes=["mixed"])
        if l == 0:
            dump("mixed", mixed[:], [128, NT, D], ["mixed"])
        S.barrier()
        for t in range(7):
            S.dma(x_sb[:, t, :], x_src[t * 128:(t + 1) * 128, :], writes=["x_sb%d" % t])
        for t in range(NT):
            transposes(actT[:, :, t * 128:(t + 1) * 128], [mixed[:, t, k * 128:(k + 1) * 128] for k in range(KD)], 128, ["mixed"], ["actT"], eng=("act" if t % 2 else "dve"))
        h2T = A("h2T", [128, KD, S_LEN], BF16, OFF0)
        aT = A("aT", [128, 8, S_LEN], BF16, ACT0)
        wup = A("wup", [128, KD, 1024], BF16, OFF0 + 32 * KB)
        wdn = A("wdn", [128, 8, D], BF16, OFF0 + 48 * KB)
        o = OFF0 + 144 * KB
        hb2 = [A("hb2_%d" % i, [128, D], BF16, o + i * 2 * KB) for i in range(2)]; o += 4 * KB
        sq4 = A("sq4", [128, D], BF16, o); o += 2 * KB
        nst4 = A("nst4", [128, 8], F32, o); o += 32
        usq = [A("usq%d" % i, [128, 512], F32, OFF0 + 128 * KB + i * 2 * KB) for i in range(2)]
        xkeys = ["x_sb%d" % t for t in range(NT)]
        def p3_mm(t):
            xk = "x_sb%d" % t
            for cc in range(2):
                pg = ps_mm[cc]
                for k in range(KD):
                    S.op("pe", lambda e, k=k, pg=pg, cc=cc: e.matmul(pg[:, 0:512], lhsT=actT[:, k, t * 128:(t + 1) * 128], rhs=wo[:, k, cc * 512:(cc + 1) * 512], start=(k == 0), stop=(k == KD - 1)),
                         reads=["actT", "wo"], writes=["ps_mm%d" % cc], sig=(k == KD - 1))
                dst = x_sb[:, t, cc * 512:(cc + 1) * 512]
                S.op("dve", lambda e, pg=pg, dst=dst: e.tensor_tensor(out=dst, in0=dst, in1=pg[:, 0:512], op=ALU.add), reads=["ps_mm%d" % cc, xk], writes=[xk])

        def p3_norm(t):
            xk = "x_sb%d" % t
            b = t % 2
            rmsnorm_tile(x_sb[:, t, :], D, g2[:], hb2[b][:], (nst4, sq4[:]), "n2", ["g2"], ["hb2_%d" % b], [xk])
            transposes(h2T[:, :, t * 128:(t + 1) * 128], [hb2[b][:, k * 128:(k + 1) * 128] for k in range(KD)], 128, ["hb2_%d" % b], ["h2T"], eng=("act" if t % 2 else "dve"))
        for step in range(NT + 2):
            if step < NT:
                p3_mm(step)
            if step >= 2:
                p3_norm(step - 2)
        if l == 0:
            dump("x_attn", x_sb[:], [128, NT, D], xkeys)
        ui = 0
        for g in range(4):
            usrc = dr["w_up"][l, :, g * 1024:(g + 1) * 1024].rearrange("(k p) c -> p k c", p=128)
            for kh in range(4):
                S.dma(wup[:, kh * 2:(kh + 1) * 2, :], usrc[:, kh * 2:(kh + 1) * 2, :], writes=["wup"] + (["mixed"] if g == 0 else []), q="pool")
            dsrc = dr["w_down"][l, g * 1024:(g + 1) * 1024, :].rearrange("(f p) c -> p f c", p=128)
            for kh in range(4):
                S.dma(wdn[:, kh * 2:(kh + 1) * 2, :], dsrc[:, kh * 2:(kh + 1) * 2, :], writes=["wdn"] + (["mixed"] if g == 0 else []), q="pool")
            for fc in range(8):
                for tc4 in range(4):
                    pu = ps_mm[ui % 2]
                    puk = "ps_mm%d" % (ui % 2)
                    for k in range(KD):
                        S.op("pe", lambda e, k=k, pu=pu, fc=fc, tc4=tc4: e.matmul(pu[:, 0:512], lhsT=wup[:, k, fc * 128:(fc + 1) * 128], rhs=h2T[:, k, tc4 * 512:(tc4 + 1) * 512],
                                                                              start=(k == 0), stop=(k == KD - 1)),
                             reads=["wup", "h2T"], writes=[puk], sig=(k == KD - 1))
                    u2 = usq[ui % 2]
                    uk = "usq%d" % (ui % 2)
                    S.op("act", lambda e, pu=pu, u2=u2: e.activation(out=u2[:], in_=pu[:, 0:512], func=AF.Square), reads=[puk], writes=[uk])
                    S.op("dve", lambda e, pu=pu, u2=u2, fc=fc, tc4=tc4: e.scalar_tensor_tensor(out=aT[:, fc, tc4 * 512:(tc4 + 1) * 512], in0=pu[:, 0:512], scalar=0.0, in1=u2[:],
                                                                                         op0=ALU.is_gt, op1=ALU.mult), reads=[puk, uk], writes=["aT", "actT"])
                    ui += 1
            for t in range(NT):
                for cc in range(2):
                    pd = ps_st[cc]
                    pdk = "ps_st%d" % cc
                    for fc in range(8):
                        S.op("pe", lambda e, fc=fc, pd=pd, cc=cc: e.matmul(pd[:, 0:512], lhsT=aT[:, fc, t * 128:(t + 1) * 128], rhs=wdn[:, fc, cc * 512:(cc + 1) * 512], start=(fc == 0), stop=(fc == 7)),
                             reads=["aT", "wdn"], writes=[pdk], sig=(fc == 7))
                    dst = x_sb[:, t, cc * 512:(cc + 1) * 512]
                    S.op("dve", lambda e, pd=pd, dst=dst: e.tensor_tensor(out=dst, in0=dst, in1=pd[:, 0:512], op=ALU.add), reads=[pdk, "x_sb"], writes=["x_sb"])
        if l == 0:
            dump("x_l0", x_sb[:], [128, NT, D], ["x_sb"])
        if l < depth - 1:
            for t in range(NT):
                S.dma(xs[t * 128:(t + 1) * 128, :], x_sb[:, t, :], reads=["x_sb"], writes=["xs"])
        else:
            S.dma(g2[:], dr["final_g"][0:1, :].to_broadcast([128, D]), reads=["g2"], writes=["g2"])
            fo = [A("fo%d" % i, [128, D], F32, OFF0 + 32 * KB + i * 4 * KB) for i in range(2)]
            for t in range(NT):
                b = t % 2
                rmsnorm_tile(x_sb[:, t, :], D, g2[:], fo[b][:], (nst4, sq4[:]), "n3", ["g2"], ["fo%d" % b], ["x_sb"])
                S.dma(y[t * 128:(t + 1) * 128, :], fo[b][:], reads=["fo%d" % b], writes=["y"])
        S.barrier()


_CACHE = {}


def prepare_inputs(inputs, b):
    m = {}
    m["x"] = np.ascontiguousarray(inputs["x"][b]).astype(np.float32, copy=False)
    m["pos"] = np.ascontiguousarray(np.asarray(inputs["positions"][b]).reshape(NT, 128).T).astype(np.int32)
    for k in ["norm1_g", "w_in", "mla_q_norm_g", "mla_w_uq", "mla_kv_norm_g", "mla_w_ukv", "nsa_cmp_w", "fox_f_bias",
              "w_branch", "w_out", "norm2_g", "w_up", "w_down"]:
        m[k] = np.ascontiguousarray(inputs[k], dtype=np.float32)
    m["nsa_cmp_pe"] = np.ascontiguousarray(np.asarray(inputs["nsa_cmp_pe"], dtype=np.float32).reshape(DEPTH, 2, 2048))
    m["final_g"] = np.ascontiguousarray(np.asarray(inputs["final_g"], dtype=np.float32).reshape(1, D))
    m.update(make_consts())
    return m


def kernel(**inputs):
    inputs = {k: np.asarray(v) for k, v in inputs.items()}
    if "nc" not in _CACHE:
        _CACHE["nc"] = build_program()[0]
    nc = _CACHE["nc"]
    B = inputs["x"].shape[0]
    in_maps = [prepare_inputs(inputs, b) for b in range(B)]
    res = run_bass_kernel_spmd(nc, in_maps, core_ids=list(range(B)))
    out = np.stack([np.asarray(r["y"]) for r in res.results], axis=0).astype(np.float32)
    return out
```

```python
import numpy as np
import concourse.bass as bass
import concourse.mybir as mybir
from concourse.bass_utils import run_bass_kernel_spmd

F32 = mybir.dt.float32
BF16 = mybir.dt.bfloat16
I32 = mybir.dt.int32
ALU = mybir.AluOpType
AF = mybir.ActivationFunctionType
AX = mybir.AxisListType

S_LEN = 2048
NT = 16
D = 1024
KD = 8
DEPTH = 2
NEGB = -30000.0
NDUMMY = 1
NWARM = 0
PI = float(np.pi)


class StopBuild(Exception):
    pass


class Sched:
    limit = None

    def __init__(self, nc, n_dma_sems=24):
        self.nc = nc
        self.eng = {"pe": nc.tensor, "dve": nc.vector, "act": nc.scalar, "pool": nc.gpsimd, "sp": nc.sync}
        self.sem = {k: nc.alloc_semaphore("s_" + k) for k in self.eng}
        self.cnt = {k: 0 for k in self.eng}
        self.dsem = [nc.alloc_semaphore("d%d" % i) for i in range(n_dma_sems)]
        self.dcnt = [0] * n_dma_sems
        half = n_dma_sems // 2
        self.dpool = {"sp": list(range(0, half)), "pool": list(range(half, n_dma_sems))}
        self.dnext = {"sp": 0, "pool": 0}
        self.seen = {k: {} for k in self.eng}
        self.lastw = {}
        self.readers = {}
        self.pending = {k: ([], []) for k in self.eng}
        self.semobj = {}
        self.all_tokens = {}
        self.nwaits = 0
        self.nops = 0

    def _tok_wait(self, e, tok):
        sid, val = tok
        if self.seen[e].get(sid, 0) >= val:
            return
        self.seen[e][sid] = val
        self.eng[e].wait_ge(self.semobj[sid], val)
        self.nwaits += 1

    def _deps(self, e, reads, writes):
        writes = list(writes) + [k for k in reads if k.startswith("ps_")]
        toks = []
        for k in reads:
            t = self.lastw.get(k)
            if t is not None:
                toks.append(t)
        for k in writes:
            t = self.lastw.get(k)
            if t is not None:
                toks.append(t)
            toks.extend(self.readers.get(k, ()))
        own = id(self.sem[e])
        for t in toks:
            if e == "pe" and t[0] == own:
                continue
            self._tok_wait(e, t)

    def _commit(self, tok, reads, writes):
        writes = list(writes) + [k for k in reads if k.startswith("ps_")]
        for k in writes:
            self.lastw[k] = tok
            self.readers[k] = []
        for k in reads:
            lst = self.readers.setdefault(k, [])
            lst.append(tok)
            if len(lst) > 16:
                best = {}
                for s, v in lst:
                    best[s] = max(best.get(s, 0), v)
                self.readers[k] = list(best.items())
        self.all_tokens[tok[0]] = max(self.all_tokens.get(tok[0], 0), tok[1])

    def op(self, e, fn, reads=(), writes=(), sig=True):
        self.nops += 1
        if self.limit is not None and self.nops > self.limit:
            raise StopBuild()
        self._deps(e, reads, writes)
        ins = fn(self.eng[e])
        if not sig:
            pr, pw = self.pending[e]
            pr.extend(reads)
            pw.extend(writes)
            return
        self.cnt[e] += 1
        s = self.sem[e]
        self.semobj[id(s)] = s
        ins.then_inc(s, 1)
        tok = (id(s), self.cnt[e])
        pr, pw = self.pending[e]
        self._commit(tok, list(reads) + pr, list(writes) + pw)
        self.pending[e] = ([], [])

    def dma(self, out, in_, reads=(), writes=(), q="sp", **kw):
        self.nops += 1
        if self.limit is not None and self.nops > self.limit:
            raise StopBuild()
        self._deps(q, reads, writes)
        lst = self.dpool[q]
        i = lst[self.dnext[q] % len(lst)]
        self.dnext[q] += 1
        s = self.dsem[i]
        self.semobj[id(s)] = s
        self.dcnt[i] += 16
        self.eng[q].dma_start(out=out, in_=in_, **kw).then_inc(s, 16)
        tok = (id(s), self.dcnt[i])
        self._commit(tok, reads, writes)
        return tok

    def barrier(self):
        for e in self.eng:
            for sid, val in list(self.all_tokens.items()):
                self._tok_wait(e, (sid, val))
        self.lastw = {}
        self.readers = {}

    def wait_all(self, e="sp"):
        for sid, val in list(self.all_tokens.items()):
            self._tok_wait(e, (sid, val))


def make_consts():
    c = {}
    k = np.arange(128)[:, None]
    q = np.arange(128)[None, :]
    c["c_ident"] = np.eye(128, dtype=np.float32)
    c["c_caust"] = np.where(q >= k, 0.0, NEGB).astype(np.float32)
    c["c_wint"] = np.where(k > q, 0.0, NEGB).astype(np.float32)
    c["c_causqk"] = np.where(q <= k, 0.0, -1e30).astype(np.float32)
    cc = np.arange(128)[:, None]
    t = np.arange(S_LEN)[None, :]
    cmpb = np.where((16 * cc + 31 <= t) & (cc < 127), 0.0, NEGB).astype(np.float32)
    c["c_cmpbias"] = cmpb
    j = np.arange(32)[:, None]
    kk = np.arange(S_LEN)[None, :]
    c["c_E"] = (kk // 64 == j).astype(np.float32)
    cs = np.arange(128) * 16
    sb = np.arange(32) * 64
    ov = np.maximum(np.minimum(cs[:, None] + 32, sb[None, :] + 64) - np.maximum(cs[:, None], sb[None, :]), 0) / 32.0
    ov[127, :] = 0.0
    c["c_ovl"] = ov.astype(np.float32)
    tt = np.arange(S_LEN)
    tb = tt[:, None] // 64
    sbi = np.arange(32)[None, :]
    forced = (sbi == 0) | (sbi == tb) | (sbi == tb - 1)
    causal = (sbi * 64) <= tt[:, None]
    base = np.where(causal, np.where(forced, 1e4, 0.0), -1e30).astype(np.float32)
    keep = np.where(causal & ~forced, 1.0, 0.0).astype(np.float32)
    c["c_fbase"] = base.reshape(NT, 128, 32).transpose(1, 0, 2).copy()
    c["c_fkeep"] = keep.reshape(NT, 128, 32).transpose(1, 0, 2).copy()
    theta = np.float32(500000.0)
    def inv(rot):
        return (theta ** (-np.arange(0, rot, 2, dtype=np.float32) / np.float32(rot))).astype(np.float32)
    invs = np.concatenate([inv(32), inv(16), inv(8)]).astype(np.float32)
    c["c_inv"] = np.tile(invs[None, :], (128, 1)).astype(np.float32)
    c["c_pow2"] = np.tile((2.0 ** -np.arange(32, dtype=np.float32))[None, :], (128, 1)).astype(np.float32)
    c["c_U"] = (k <= q).astype(np.float32)
    c["c_ones"] = np.ones((128, 128), np.float32)
    return c


CONST_SHAPES = {k: v.shape for k, v in make_consts().items()}

IN_SHAPES = {
    "x": ([S_LEN, D], F32), "pos": ([128, NT], I32),
    "norm1_g": ([DEPTH, D], F32), "w_in": ([DEPTH, D, 6616], F32),
    "mla_q_norm_g": ([DEPTH, 256], F32), "mla_w_uq": ([DEPTH, 256, 384], F32),
    "mla_kv_norm_g": ([DEPTH, 128], F32), "mla_w_ukv": ([DEPTH, 128, 512], F32),
    "nsa_cmp_pe": ([DEPTH, 2, 2048], F32), "nsa_cmp_w": ([DEPTH, 2, 2048, 64], F32),
    "fox_f_bias": ([DEPTH, 4], F32), "w_branch": ([DEPTH, 4, 256, D], F32),
    "w_out": ([DEPTH, D, D], F32), "norm2_g": ([DEPTH, D], F32),
    "w_up": ([DEPTH, D, 4096], F32), "w_down": ([DEPTH, 4096, D], F32), "final_g": ([1, D], F32),
}

MLA0 = 0
NSA0 = 416
FOX0 = NSA0 + 652
DSA0 = FOX0 + 772
GATE0 = DSA0 + 680


def build_program(depth=DEPTH, debug=None, mixers=("mla", "nsa", "fox", "dsa"), stop=None, limit=None):
    nc = bass.Bass("TRN2", target_bir_lowering=False)
    S = Sched(nc)
    S.limit = limit
    dbg = {}
    try:
        _build_body(nc, S, dbg, depth, debug, mixers, stop)
    except StopBuild:
        S.limit = None
        print("stopped at limit", limit, flush=True)
    S.wait_all("sp")
    print("program built: ops", S.nops, "waits", S.nwaits, flush=True)
    return nc, dbg


def _build_body(nc, S, dbg, depth, debug, mixers, stop):
    dr = {}
    for name, (shape, dt) in IN_SHAPES.items():
        dr[name] = nc.dram_tensor(name, shape, dt, kind="ExternalInput").ap()
    for name, shape in CONST_SHAPES.items():
        dr[name] = nc.dram_tensor(name, list(shape), F32, kind="ExternalInput").ap()
    y = nc.dram_tensor("y", [S_LEN, D], F32, kind="ExternalOutput").ap()
    xs = nc.dram_tensor("xs", [S_LEN, D], F32, kind="Internal").ap()

    BASE = 16512
    KB = 1024

    def A(name, shape, dt, off):
        assert off % 32 == 0, (name, off)
        nbytes = int(np.prod(shape[1:])) * (2 if dt == BF16 else 4)
        assert off + nbytes <= 207 * KB + 512, (name, off, nbytes)
        return nc.alloc_sbuf_tensor_at(name, list(shape), dt, offset=BASE + off)

    def dump(name, ap, shape, reads):
        if debug is None or name not in debug:
            return
        t = nc.dram_tensor("dbg_" + name, list(shape), ap.dtype if hasattr(ap, "dtype") else F32, kind="ExternalOutput").ap()
        S.dma(t, ap, reads=reads, writes=["dbg_" + name])
        dbg[name] = t

    o = 0
    def CA(name, shape, dt):
        nonlocal o
        t = A(name, shape, dt, o)
        o += ((int(np.prod(shape[1:])) * (2 if dt == BF16 else 4) + 31) // 32) * 32
        return t
    identb = CA("identb", [128, 128], BF16)
    identf = CA("identf", [128, 128], F32)
    caust = CA("caust", [128, 128], BF16)
    wint = CA("wint", [128, 128], BF16)
    causqk = CA("causqk", [128, 128], F32)
    cmpbias = CA("cmpbias", [128, S_LEN], BF16)
    Emat = CA("Emat", [32, S_LEN], BF16)
    ovl = CA("ovl", [128, 32], BF16)
    fbase = CA("fbase", [128, NT, 32], F32)
    fkeep = CA("fkeep", [128, NT, 32], F32)
    invt = CA("invt", [128, 28], F32)
    pow2 = CA("pow2", [128, 32], F32)
    Umat = CA("Umat", [128, 128], F32)
    onesf = CA("onesf", [128, 128], F32)
    trig = CA("trig", [128, NT, 56], F32)
    posi = CA("posi", [128, NT], I32)
    posf = CA("posf", [128, NT], F32)
    cst = CA("cst", [128, 8], F32)
    g2 = CA("g2", [128, D], F32)
    assert o <= 24 * KB, o
    for dst, src, q in [(identb, "c_ident", "pool"), (identf, "c_ident", "sp"), (caust, "c_caust", "pool"), (wint, "c_wint", "pool"),
                        (causqk, "c_causqk", "sp"), (cmpbias, "c_cmpbias", "pool"), (Emat, "c_E", "pool"), (ovl, "c_ovl", "pool"),
                        (fbase, "c_fbase", "sp"), (fkeep, "c_fkeep", "sp"), (invt, "c_inv", "sp"), (pow2, "c_pow2", "sp"),
                        (Umat, "c_U", "sp"), (onesf, "c_ones", "sp")]:
        S.dma(dst[:], dr[src], writes=["const"], q=q)
    S.dma(posi[:], dr["pos"], writes=["const"])
    S.op("dve", lambda e: e.memset(cst[:, 0:1], 1e-6), writes=["const"])
    S.op("dve", lambda e: e.memset(cst[:, 1:2], 1.0), writes=["const"])
    S.op("dve", lambda e: e.memset(cst[:, 2:3], 0.0), writes=["const"])

    ACT0 = 24 * KB
    actT = A("actT", [128, KD, S_LEN], BF16, ACT0)
    OFF0 = 56 * KB

    ps_st = [nc.alloc_psum_tensor("ps_st%d" % i, [128, 512], F32) for i in range(2)]
    ps_acc = [nc.alloc_psum_tensor("ps_acc%d" % i, [128, 4, 128], F32) for i in range(2)]
    ps_mm = [nc.alloc_psum_tensor("ps_mm%d" % i, [128, 512], F32) for i in range(2)]
    ps_trs = [nc.alloc_psum_tensor("ps_tr%d" % i, [128, 8, 128], BF16) for i in range(2)]
    ps_x = ps_trs[1][:].rearrange("p a b -> p (a b)").bitcast(F32)
    PX = "ps_tr1"

    with_scr = A("ropescr", [128, NT, 56], F32, OFF0)
    kfi = A("ropekfi", [128, NT, 56], I32, OFF0 + 4 * KB)
    kff = A("ropekff", [128, NT, 56], F32, OFF0 + 8 * KB)
    ang = A("ropeang", [128, NT, 56], F32, OFF0 + 12 * KB)
    S.op("dve", lambda e: e.tensor_copy(out=posf[:], in_=posi[:]), reads=["const"], writes=["posf"])
    S.op("dve", lambda e: e.tensor_tensor(out=ang[:, :, 0:28], in0=posf[:].unsqueeze(2).to_broadcast([128, NT, 28]),
                                          in1=invt[:].unsqueeze(1).to_broadcast([128, NT, 28]), op=ALU.mult), reads=["posf", "const"], writes=["ang"])
    S.op("dve", lambda e: e.tensor_scalar(out=ang[:, :, 28:56], in0=ang[:, :, 0:28], scalar1=PI / 2, scalar2=None, op0=ALU.add), reads=["ang"], writes=["ang"])
    S.op("dve", lambda e: e.tensor_scalar(out=with_scr[:], in0=ang[:], scalar1=float(1.0 / (2 * np.pi)), scalar2=None, op0=ALU.mult), reads=["ang"], writes=["rscr"])
    S.op("dve", lambda e: e.tensor_copy(out=kfi[:], in_=with_scr[:]), reads=["rscr"], writes=["kfi"])
    S.op("dve", lambda e: e.tensor_copy(out=kff[:], in_=kfi[:]), reads=["kfi"], writes=["kff"])
    C1 = 6.28125
    C2 = float(2 * np.pi - 6.28125)
    S.op("dve", lambda e: e.scalar_tensor_tensor(out=with_scr[:], in0=kff[:], scalar=-C1, in1=ang[:], op0=ALU.mult, op1=ALU.add), reads=["kff", "ang"], writes=["rscr"])
    S.op("dve", lambda e: e.scalar_tensor_tensor(out=ang[:], in0=kff[:], scalar=-C2, in1=with_scr[:], op0=ALU.mult, op1=ALU.add), reads=["kff", "rscr"], writes=["ang"])
    S.op("dve", lambda e: e.tensor_scalar(out=ang[:], in0=ang[:], scalar1=-3.1415925, scalar2=3.1415925, op0=ALU.max, op1=ALU.min), reads=["ang"], writes=["ang"])
    S.op("act", lambda e: e.activation(out=trig[:], in_=ang[:], func=AF.Sin, bias=cst[:, 2:3], scale=1.0), reads=["ang", "const"], writes=["trig"])
    dump("trig", trig[:], [128, NT, 56], ["trig"])
    SIN = lambda t, a, b: trig[:, t, a:b]
    COS = lambda t, a, b: trig[:, t, 28 + a:28 + b]
    S.barrier()
    if stop == "p0":
        print("stop", stop, S.nops)
        return

    def rmsnorm_tile(src_ap, n, g_ap, dst_ap, scr, tag, rkeys, wkeys, src_keys):
        ss, sq = scr
        S.op("act", lambda e: e.activation(out=sq, in_=src_ap, func=AF.Square, accum_out=ss[:, 0:1]), reads=src_keys, writes=[tag + "sq", tag + "ss"])
        S.op("act", lambda e: e.activation(out=ss[:, 1:2], in_=ss[:, 0:1], func=AF.Sqrt, bias=cst[:, 0:1], scale=1.0 / n), reads=[tag + "ss"], writes=[tag + "ss1"])
        S.op("dve", lambda e: e.reciprocal(out=ss[:, 2:3], in_=ss[:, 1:2]), reads=[tag + "ss1"], writes=[tag + "ss2"])
        S.op("dve", lambda e: e.scalar_tensor_tensor(out=dst_ap, in0=src_ap, scalar=ss[:, 2:3], in1=g_ap, op0=ALU.mult, op1=ALU.mult),
             reads=src_keys + [tag + "ss2"] + rkeys, writes=wkeys)

    tr_i = [0]

    def transposes(dst_ap, srcs, np_out, rkeys, wkeys, eng="act", bank=None):
        n = len(srcs)
        if bank is None:
            bank = tr_i[0] % 2
            tr_i[0] += 1
        ps_tr = ps_trs[bank]
        tk = "ps_tr%d" % bank
        for j, s_ap in enumerate(srcs):
            w = s_ap.shape[-1]
            S.op("pe", lambda e, j=j, s_ap=s_ap, w=w: e.transpose(out=ps_tr[0:w, j, :], in_=s_ap, identity=identb[:]),
                 reads=rkeys + ["const"], writes=[tk], sig=(j == n - 1))
        if eng == "act":
            S.op("act", lambda e: e.copy(out=dst_ap, in_=ps_tr[0:np_out, 0:n, :]), reads=[tk], writes=wkeys)
        else:
            S.op("dve", lambda e: e.tensor_copy(out=dst_ap, in_=ps_tr[0:np_out, 0:n, :]), reads=[tk], writes=wkeys)

    def rope(dst1, dst2, x1, x2, cos, sin, shape, tA, tB, rkeys, wkeys, tag):
        cb = cos.unsqueeze(1).to_broadcast(shape) if len(shape) == 3 else cos
        sb = sin.unsqueeze(1).to_broadcast(shape) if len(shape) == 3 else sin
        S.op("dve", lambda e: e.tensor_tensor(out=tA, in0=x1, in1=cb, op=ALU.mult), reads=rkeys + ["trig"], writes=[tag + "A"])
        S.op("dve", lambda e: e.tensor_tensor(out=tB, in0=x2, in1=sb, op=ALU.mult), reads=rkeys + ["trig"], writes=[tag + "B"])
        S.op("dve", lambda e: e.tensor_tensor(out=dst1, in0=tA, in1=tB, op=ALU.subtract), reads=[tag + "A", tag + "B"], writes=wkeys)
        S.op("dve", lambda e: e.tensor_tensor(out=tA, in0=x2, in1=cb, op=ALU.mult), reads=rkeys + ["trig"], writes=[tag + "A"])
        S.op("dve", lambda e: e.tensor_tensor(out=tB, in0=x1, in1=sb, op=ALU.mult), reads=rkeys + ["trig"], writes=[tag + "B"])
        S.op("dve", lambda e: e.tensor_tensor(out=dst2, in0=tA, in1=tB, op=ALU.add), reads=[tag + "A", tag + "B"], writes=wkeys)

    def load_w_cols(dst, l, segs, wkey):
        for (dc, sc, n) in segs:
            src = dr["w_in"][l, :, sc:sc + n].rearrange("(k p) c -> p k c", p=128)
            for kh in range(2):
                S.dma(dst[:, kh * 4:(kh + 1) * 4, dc:dc + n], src[:, kh * 4:(kh + 1) * 4, :], writes=[wkey], q="pool")

    def project_tile(t, Wz, col_groups, wkey):
        outs = []
        for gi, (c0, n) in enumerate(col_groups):
            if t % 2 == 0:
                pt, pk = ps_mm[gi], "ps_mm%d" % gi
            else:
                pt, pk = ps_st[gi], "ps_st%d" % gi
            for k in range(KD):
                S.op("pe", lambda e, k=k, pt=pt, c0=c0, n=n: e.matmul(pt[:, 0:n], lhsT=actT[:, k, t * 128:(t + 1) * 128], rhs=Wz[:, k, c0:c0 + n],
                                                                     start=(k == 0), stop=(k == KD - 1)),
                     reads=["actT", wkey], writes=[pk], sig=(k == KD - 1))
            outs.append((pt, pk))
        return outs

    def warm(n, bank=0):
        if not NDUMMY:
            return
        dout = ps_acc[bank][:].rearrange("p a b -> p (a b)")
        for _ in range(n):
            S.op("pe", lambda e: e.matmul(dout, lhsT=identb[:], rhs=cmpbias[:, 0:512], start=True, stop=True, skip_group_check=True), sig=False)

    PT = [A("PT%d" % i, [128, 512], BF16, OFF0 + 32 * KB + 13 * KB + i * KB) for i in range(4)]
    misc = A("misc", [128, 512], F32, OFF0 + 32 * KB + 17 * KB)
    WS0 = OFF0 + 32 * KB + 19 * KB
    st_ctr = [0]

    def attn_core(tag, qT, kT, vaug, scale, steps_fn, bias_fn, fin_fn, rkeys, nk=128, vw=65, range_bias_fn=None, ndummy=0, dummy_out=None):
        steps = []
        for c in range(4):
            ss = steps_fn(c)
            contrib = {}
            for (kt, qlo, qhi) in ss:
                for qt in range(qlo, qhi):
                    contrib.setdefault(qt, []).append(kt)
            for si, (kt, qlo, qhi) in enumerate(ss):
                steps.append((c, kt, qlo, qhi, contrib, si == len(ss) - 1, si == 0))

        def issue_st(step, slot):
            c, kt, qlo, qhi, contrib, last, _f = step
            st = ps_st[slot % 2]
            skey = "ps_st%d" % (slot % 2)
            n = (qhi - qlo) * 128
            bl = []
            for qt in range(qlo, qhi):
                for (bl_l, bl_r, bkeys) in bias_fn(kt, qt):
                    bl.append(((qt - qlo) * 128, (qt - qlo + 1) * 128, bl_l, bl_r, bkeys))
            if range_bias_fn is not None:
                for (bl_l, bl_r, bkeys) in range_bias_fn(kt, qlo, qhi):
                    bl.append((0, n, bl_l, bl_r, bkeys))
            S.op("pe", lambda e: e.matmul(st[0:nk, 0:n], lhsT=kT(kt), rhs=qT(qlo * 128, qhi * 128), start=True, stop=(len(bl) == 0), skip_group_check=True),
                 reads=rkeys, writes=[skey], sig=(not bl))
            for bi, (c0, c1, bl_l, bl_r, bkeys) in enumerate(bl):
                S.op("pe", lambda e, c0=c0, c1=c1, bl_l=bl_l, bl_r=bl_r, bi=bi: e.matmul(st[0:nk, c0:c1], lhsT=bl_l, rhs=bl_r, start=False, stop=(bi == len(bl) - 1), skip_group_check=True),
                     reads=bkeys, writes=[skey], sig=(bi == len(bl) - 1))
            for _ in range(ndummy):
                dout = ps_mm[0][:, 0:512] if dummy_out is None else dummy_out
                S.op("pe", lambda e: e.matmul(dout, lhsT=identb[:], rhs=cmpbias[:, 0:512], start=True, stop=True, skip_group_check=True), sig=False)

        def issue_rest(step, slot):
            c, kt, qlo, qhi, contrib, last, _f = step
            st = ps_st[slot % 2]
            skey = "ps_st%d" % (slot % 2)
            n = (qhi - qlo) * 128
            pi = slot % 4
            pt = PT[pi]
            S.op("act", lambda e: e.activation(out=pt[0:nk, 0:n], in_=st[0:nk, 0:n], func=AF.Exp, scale=scale), reads=[skey], writes=["PT%d" % pi])
            acc = ps_acc[c % 2]
            for qt in range(qlo, qhi):
                first = (step[6] and qt == qlo)
                S.op("pe", lambda e, qt=qt, first=first: e.matmul(acc[:, qt - 4 * c, 0:vw], lhsT=pt[0:nk, (qt - qlo) * 128:(qt - qlo + 1) * 128], rhs=vaug(kt),
                                                       start=first, stop=(kt == contrib[qt][-1]), skip_group_check=True),
                     reads=["PT%d" % pi] + rkeys, writes=["ps_acc%d" % (c % 2)], sig=(qt == qhi - 1))
            if last:
                fin_fn(c, acc, "ps_acc%d" % (c % 2))

        base = st_ctr[0]
        for i, step in enumerate(steps):
            if i == 0:
                issue_st(step, base)
            if i + 1 < len(steps):
                issue_st(steps[i + 1], base + i + 1)
            issue_rest(step, base + i)
        st_ctr[0] = base + len(steps)

    def causal_steps(c):
        return [(kt, max(4 * c, kt), 4 * c + 4) for kt in range(4 * c + 4)]

    def causal_bias(kt, qt):
        if kt == qt:
            return [(identb[:], caust[:], ["const"])]
        return []

    def std_fin(o_tm, h, okey, gate=None, accumulate=False, vdim=64):
        def fin(c, acc, akey):
            rec = misc[:, 64:68]
            S.op("dve", lambda e: e.tensor_scalar(out=rec, in0=acc[:, :, vdim], scalar1=1e-30, scalar2=None, op0=ALU.max), reads=[akey], writes=["rec"])
            S.op("dve", lambda e: e.reciprocal(out=misc[:, 68:72], in_=rec), reads=["rec"], writes=["rec2"])
            r2 = misc[:, 68:72]
            rk = ["rec2"]
            if gate is not None:
                gap, gkey = gate(c)
                S.op("dve", lambda e: e.tensor_tensor(out=misc[:, 72:76], in0=r2, in1=gap, op=ALU.mult), reads=["rec2", gkey], writes=["rec3"])
                r2 = misc[:, 72:76]
                rk = ["rec3"]
            dst = o_tm[:, 4 * c:4 * c + 4, h * 64:(h + 1) * 64]
            if not accumulate:
                S.op("dve", lambda e: e.tensor_tensor(out=dst, in0=acc[:, :, 0:64], in1=r2.unsqueeze(2).to_broadcast([128, 4, 64]), op=ALU.mult),
                     reads=[akey] + rk, writes=[okey])
            else:
                tmp = misc[:, 128:384].rearrange("p (a b) -> p a b", a=4)
                S.op("dve", lambda e: e.tensor_tensor(out=tmp, in0=acc[:, :, 0:64], in1=r2.unsqueeze(2).to_broadcast([128, 4, 64]), op=ALU.mult),
                     reads=[akey] + rk, writes=["fintmp"])
                S.op("pool", lambda e: e.tensor_tensor(out=dst, in0=dst, in1=tmp, op=ALU.add), reads=["fintmp", okey], writes=[okey])
        return fin

    oT = A("oT", [128, 4, 2, S_LEN], BF16, OFF0)

    def finish_mixer(mi, o_tm, okey):
        for t in range(NT):
            transposes(oT[:, mi, :, t * 128:(t + 1) * 128], [o_tm[:, t, 0:128], o_tm[:, t, 128:256]], 128, [okey], ["oT"], eng=("act" if t % 2 else "dve"))

    for l in range(depth):
        x_src = dr["x"] if l == 0 else xs
        gbc = A("gbc", [128, D], F32, OFF0)
        xt = [A("xt%d" % i, [128, D], F32, OFF0 + 4 * KB + i * 4 * KB) for i in range(4)]
        hb = [A("hb%d" % i, [128, D], BF16, OFF0 + 20 * KB + i * 2 * KB) for i in range(3)]
        sqs_ = [A("sqs%d" % i, [128, D], BF16, OFF0 + 26 * KB + i * 2 * KB) for i in range(2)]
        nst_ = [A("nst%d" % i, [128, 8], F32, OFF0 + 30 * KB + i * 32) for i in range(2)]
        S.dma(gbc[:], dr["norm1_g"][l:l + 1, :].to_broadcast([128, D]), writes=["gbc"])

        def p1_load(t):
            S.dma(xt[t % 4][:], x_src[t * 128:(t + 1) * 128, :], writes=["xt%d" % (t % 4)])

        def p1_norm(t):
            rmsnorm_tile(xt[t % 4][:], D, gbc[:], hb[t % 3][:], (nst_[t % 2], sqs_[t % 2][:]), "n1_%d" % (t % 2), ["gbc"], ["hb%d" % (t % 3)], ["xt%d" % (t % 4)])

        def p1_tr(t):
            transposes(actT[:, :, t * 128:(t + 1) * 128], [hb[t % 3][:, k * 128:(k + 1) * 128] for k in range(KD)], 128, ["hb%d" % (t % 3)], ["actT"], eng=("act" if t % 2 else "dve"))
        p1_load(0)
        p1_load(1)
        for step in range(NT + 1):
            if step + 2 < NT:
                p1_load(step + 2)
            if step < NT:
                p1_norm(step)
            if step >= 1:
                p1_tr(step - 1)
        if l == 0:
            dump("hT", actT[:], [128, KD, S_LEN], ["actT"])

        if stop == "p1":
            print("stop", stop, S.nops)
            return
        Wz = A("Wz", [128, KD, 776], BF16, OFF0 + 32 * KB)

        if "mla" in mixers:
            o = WS0
            cqnT = A("cqnT", [128, 2, S_LEN], BF16, o); o += 8 * KB
            ckvnT = A("ckvnT", [128, S_LEN], BF16, o); o += 4 * KB
            wuq = A("wuq", [128, 2, 384], BF16, o); o += 1536
            wukv = A("wukv", [128, 512], BF16, o); o += 1024
            QT = A("mQT", [96, 4, S_LEN], BF16, o); o += 16 * KB
            KTt = A("mKT", [96, 4, S_LEN], BF16, o); o += 16 * KB
            Vg = A("mV", [128, NT, 4, 66], BF16, o); o += NT * 4 * 66 * 2
            o = (o + 31) // 32 * 32
            qtm_ = [A("mqtm%d" % i, [128, 4, 96], BF16, o + i * 768) for i in range(2)]; o += 1536
            ktm_ = [A("mktm%d" % i, [128, 4, 96], BF16, o + i * 768) for i in range(3)]; o += 2304
            cqn_ = [A("mcqn%d" % i, [128, 256], BF16, o + i * 512) for i in range(2)]; o += 1024
            ckvn_ = [A("mckvn%d" % i, [128, 128], BF16, o + i * 256) for i in range(2)]; o += 512
            gq = A("mgq", [128, 256], F32, o); o += 1024
            gkv = A("mgkv", [128, 128], F32, o); o += 512
            rA_ = [A("mrA%d" % i, [128, 64], F32, o + i * 256) for i in range(2)]; o += 512
            rB_ = [A("mrB%d" % i, [128, 64], F32, o + i * 256) for i in range(2)]; o += 512
            sq2_ = [A("msq%d" % i, [128, 256], F32, o + i * 1024) for i in range(2)]; o += 2048
            nst2_ = [A("mnst%d" % i, [128, 8], F32, o + i * 32) for i in range(2)]; o += 64
            o_tm = A("mo_tm", [128, NT, 256], BF16, o); o += 8 * KB
            load_w_cols(Wz, l, [(0, MLA0, 416)], "Wz")
            S.dma(wuq[:], dr["mla_w_uq"][l].rearrange("(k p) c -> p k c", p=128), writes=["wuq"], q="pool")
            S.dma(wukv[:], dr["mla_w_ukv"][l], writes=["wukv"], q="pool")
            S.dma(gq[:], dr["mla_q_norm_g"][l:l + 1, :].to_broadcast([128, 256]), writes=["gq"])
            S.dma(gkv[:], dr["mla_kv_norm_g"][l:l + 1, :].to_broadcast([128, 128]), writes=["gkv"])
            S.op("pool", lambda e: e.memset(Vg[:, :, :, 64:65], 1.0), writes=["mV"])
            zs_ = [A("mzs%d" % i, [128, 416], F32, o + i * 1664) for i in range(2)]; o += 3328
            qs_ = [A("mqs%d" % i, [128, 384], F32, o + i * 1536) for i in range(2)]; o += 3072
            kvs_ = [A("mkvs%d" % i, [128, 512], F32, o + i * 2048) for i in range(2)]; o += 4096

            def mla_vars(t):
                p = t % 2
                return p, str(p), qtm_[p], ktm_[t % 3], cqn_[p], ckvn_[p], rA_[p], rB_[p], sq2_[p], nst2_[p]

            def mla_A1(t):
                p, sp, qtm, ktm, cqn, ckvn, rA, rB, sq2, nst2 = mla_vars(t)
                zs = zs_[p]
                zk = "mzs" + sp
                ((pz, pzk),) = project_tile(t, Wz, [(0, 416)], "Wz")
                S.op("act", lambda e: e.copy(out=zs[:], in_=pz[:, 0:416]), reads=[pzk], writes=[zk])

            def mla_A1b(t):
                p, sp, qtm, ktm, cqn, ckvn, rA, rB, sq2, nst2 = mla_vars(t)
                zs = zs_[p]
                zk = "mzs" + sp
                rmsnorm_tile(zs[:, 0:256], 256, gq[:], cqn[:], (nst2, sq2[:]), "mq" + sp, ["gq"], ["cqn" + sp], [zk])
                rmsnorm_tile(zs[:, 256:384], 128, gkv[:], ckvn[:], (nst2, sq2[:, 0:128]), "mq" + sp, ["gkv"], ["ckvn" + sp], [zk])
                rope(ktm[:, 0, 64:80], ktm[:, 0, 80:96], zs[:, 384:400], zs[:, 400:416], COS(t, 0, 16), SIN(t, 0, 16), [128, 16],
                     rA[:, 0:16], rB[:, 0:16], [zk], ["ktm%d" % (t % 3)], "mr" + sp)
                S.op("pool", lambda e: e.tensor_copy(out=ktm[:, 1:4, 64:96], in_=ktm[:, 0:1, 64:96].to_broadcast([128, 3, 32])), reads=["ktm%d" % (t % 3)], writes=["ktm%d" % (t % 3)])

            def mla_A2(t):
                p, sp, qtm, ktm, cqn, ckvn, rA, rB, sq2, nst2 = mla_vars(t)
                transposes(cqnT[:, :, t * 128:(t + 1) * 128], [cqn[:, 0:128], cqn[:, 128:256]], 128, ["cqn" + sp], ["cqnT%d" % p])
                transposes(ckvnT[:, t * 128:(t + 1) * 128].unsqueeze(1), [ckvn[:]], 128, ["ckvn" + sp], ["ckvnT%d" % p])
                pq, pqk = (ps_mm[1], "ps_mm1") if p == 0 else (ps_st[1], "ps_st1")
                for kk in range(2):
                    S.op("pe", lambda e, kk=kk: e.matmul(pq[:, 0:384], lhsT=cqnT[:, kk, t * 128:(t + 1) * 128], rhs=wuq[:, kk, :], start=(kk == 0), stop=(kk == 1)),
                         reads=["cqnT%d" % p, "wuq"], writes=[pqk], sig=(kk == 1))
                S.op("act", lambda e: e.copy(out=qs_[p][:], in_=pq[:, 0:384]), reads=[pqk], writes=["mqs" + sp])
                pkv = ps_acc[p][:].rearrange("p a b -> p (a b)")
                pkk = "ps_acc%d" % p
                S.op("pe", lambda e: e.matmul(pkv[:, 0:512], lhsT=ckvnT[:, t * 128:(t + 1) * 128], rhs=wukv[:], start=True, stop=True),
                     reads=["ckvnT%d" % p, "wukv"], writes=[pkk])
                S.op("act", lambda e: e.copy(out=kvs_[p][:], in_=pkv[:, 0:512]), reads=[pkk], writes=["mkvs" + sp])

            def mla_B(t):
                p, sp, qtm, ktm, cqn, ckvn, rA, rB, sq2, nst2 = mla_vars(t)
                pq3 = qs_[p][:].rearrange("p (h d) -> p h d", h=4)
                pqk = "mqs" + sp
                S.op("act", lambda e: e.copy(out=qtm[:, :, 0:64], in_=pq3[:, :, 0:64]), reads=[pqk], writes=["qtmN" + sp])
                rope(qtm[:, :, 64:80], qtm[:, :, 80:96], pq3[:, :, 64:80], pq3[:, :, 80:96], COS(t, 0, 16), SIN(t, 0, 16), [128, 4, 16],
                     rA[:].rearrange("p (h d) -> p h d", h=4), rB[:].rearrange("p (h d) -> p h d", h=4), [pqk], ["qtm" + sp], "mr" + sp)
                transposes(QT[:, :, t * 128:(t + 1) * 128], [qtm[:, h, :] for h in range(4)], 96, ["qtm" + sp, "qtmN" + sp], ["mQT"], eng="act")
                pkv3 = kvs_[p][:].rearrange("p (h d) -> p h d", h=4)
                pkk = "mkvs" + sp
                S.op("dve", lambda e: e.tensor_copy(out=ktm[:, :, 0:64], in_=pkv3[:, :, 0:64]), reads=[pkk], writes=["ktmN%d" % (t % 3)])
                S.op("dve", lambda e: e.tensor_copy(out=Vg[:, t, :, 0:64], in_=pkv3[:, :, 64:128]), reads=[pkk], writes=["mV"])
                transposes(KTt[:, :, t * 128:(t + 1) * 128], [ktm[:, h, :] for h in range(4)], 96, ["ktm%d" % (t % 3), "ktmN%d" % (t % 3)], ["mKT"])
            for step in range(NT + 2):
                if step < NT:
                    mla_A1(step)
                if 1 <= step <= NT:
                    mla_A2(step - 1)
                if step >= 2:
                    mla_B(step - 2)
                if step < NT:
                    mla_A1b(step)
            if stop == "mla_prep":
                dump("trig", QT[:], [96, 4, S_LEN], ["mQT"]) if False else None
                print("stop", stop, S.nops)
                return
            for h in range(4):
                attn_core("mla", lambda q0, q1, h=h: QT[:, h, q0:q1], lambda kt, h=h: KTt[:, h, kt * 128:(kt + 1) * 128],
                          lambda kt, h=h: Vg[:, kt, h, 0:65], float(96 ** -0.5), causal_steps, causal_bias,
                          std_fin(o_tm, h, "mo_tm"), ["mQT", "mKT", "mV"], ndummy=NDUMMY)
            if l == 0:
                dump("o_mla", o_tm[:], [128, NT, 256], ["mo_tm"])
            finish_mixer(0, o_tm, "mo_tm")
            S.barrier()

        if "fox" in mixers:
            o = WS0
            QT = A("fQT", [70, 4, S_LEN], BF16, o); o += 16 * KB
            KTt = A("fKT", [70, 4, S_LEN], BF16, o); o += 16 * KB
            Vg = A("fV", [128, NT, 4, 66], BF16, o); o += NT * 4 * 66 * 2
            o = (o + 31) // 32 * 32
            qk_tm = A("fqk_tm", [128, NT, 2, 4, 70], BF16, o); o += NT * 2 * 4 * 70 * 2
            o = (o + 31) // 32 * 32
            logf = A("flogf", [128, NT, 4], F32, o); o += 256
            cum = A("fcum", [128, NT, 4], F32, o); o += 256
            tot = A("ftot", [128, NT, 4], F32, o); o += 256
            car = A("fcar", [128, NT, 4], F32, o); o += 256
            fb = A("ffb", [128, 4], F32, o); o += 32
            ftmp = A("fftmp", [128, NT, 4], F32, o); o += 256
            chi = A("fchi", [128, NT, 4], BF16, o); o += 128
            cmid = A("fcmid", [128, NT, 4], BF16, o); o += 128
            clo = A("fclo", [128, NT, 4], BF16, o); o += 128
            r1 = A("fr1", [128, NT, 4], F32, o); o += 256
            r2_ = A("fr2", [128, NT, 4], F32, o); o += 256
            o_tm = A("fo_tm", [128, NT, 256], BF16, o); o += 8 * KB
            load_w_cols(Wz, l, [(0, FOX0, 772)], "Wz")
            S.dma(fb[:], dr["fox_f_bias"][l:l + 1, :].to_broadcast([128, 4]), writes=["ffb"])
            S.op("pool", lambda e: e.memset(Vg[:, :, :, 64:65], 1.0), writes=["fV"])
            S.op("pool", lambda e: e.memset(qk_tm[:, :, 0, :, 67:70], 1.0), writes=["fqk_tm"])
            S.op("pool", lambda e: e.memset(qk_tm[:, :, 1, :, 64:67], 1.0), writes=["fqk_tm"])
            for t in range(NT):
                (pa, pak), (pb, pbk) = project_tile(t, Wz, [(0, 512), (512, 260)], "Wz")
                warm(NWARM)
                pa4 = pa[:, 0:512].rearrange("p (a h d) -> p a h d", a=2, h=4)
                S.op("act", lambda e: e.copy(out=qk_tm[:, t, :, :, 0:64], in_=pa4), reads=[pak], writes=["fqk_tm"])
                S.op("dve", lambda e: e.tensor_copy(out=Vg[:, t, :, 0:64], in_=pb[:, 0:256].rearrange("p (h d) -> p h d", h=4)), reads=[pbk], writes=["fV"])
                S.op("dve", lambda e: e.tensor_tensor(out=ftmp[:, t, :], in0=pb[:, 256:260], in1=fb[:], op=ALU.add), reads=[pbk, "ffb"], writes=["fftmp"])
            S.op("act", lambda e: e.activation(out=logf[:], in_=ftmp[:], func=AF.Exp, scale=-1.0), reads=["fftmp"], writes=["flogf"])
            S.op("act", lambda e: e.activation(out=logf[:], in_=logf[:], func=AF.Ln, bias=cst[:, 1:2], scale=1.0), reads=["flogf", "const"], writes=["flogf"])
            S.op("dve", lambda e: e.tensor_scalar(out=logf[:], in0=logf[:], scalar1=-1.0, scalar2=None, op0=ALU.mult), reads=["flogf"], writes=["flogf"])
            pc = ps_x
            S.op("pe", lambda e: e.matmul(pc[:, 0:64], lhsT=Umat[:], rhs=logf[:].rearrange("p a b -> p (a b)"), start=True, stop=True), reads=["flogf", "const"], writes=[PX])
            S.op("dve", lambda e: e.tensor_copy(out=cum[:].rearrange("p a b -> p (a b)"), in_=pc[:, 0:64]), reads=[PX], writes=["fcum"])
            S.op("pe", lambda e: e.matmul(pc[:, 0:64], lhsT=onesf[:], rhs=logf[:].rearrange("p a b -> p (a b)"), start=True, stop=True), reads=["flogf", "const", "fcum"], writes=[PX])
            S.op("dve", lambda e: e.tensor_copy(out=tot[:].rearrange("p a b -> p (a b)"), in_=pc[:, 0:64]), reads=[PX], writes=["ftot"])
            S.op("dve", lambda e: e.memset(car[:, 0, :], 0.0), writes=["fcar"])
            for t in range(1, NT):
                S.op("dve", lambda e, t=t: e.tensor_tensor(out=car[:, t, :], in0=car[:, t - 1, :], in1=tot[:, t - 1, :], op=ALU.add), reads=["fcar", "ftot"], writes=["fcar"])
            S.op("dve", lambda e: e.tensor_tensor(out=cum[:], in0=cum[:], in1=car[:], op=ALU.add), reads=["fcum", "fcar"], writes=["fcum"])
            if l == 0:
                dump("fox_c", cum[:], [128, NT, 4], ["fcum"])
            S.op("dve", lambda e: e.tensor_scalar(out=r1[:], in0=cum[:], scalar1=8.0, scalar2=None, op0=ALU.mult), reads=["fcum"], writes=["fr1"])
            S.op("dve", lambda e: e.tensor_copy(out=chi[:], in_=r1[:]), reads=["fr1"], writes=["fchi"])
            S.op("dve", lambda e: e.tensor_tensor(out=r2_[:], in0=r1[:], in1=chi[:], op=ALU.subtract), reads=["fr1", "fchi"], writes=["fr2"])
            S.op("dve", lambda e: e.tensor_copy(out=cmid[:], in_=r2_[:]), reads=["fr2"], writes=["fcmid"])
            S.op("dve", lambda e: e.tensor_tensor(out=r1[:], in0=r2_[:], in1=cmid[:], op=ALU.subtract), reads=["fr2", "fcmid"], writes=["fr1"])
            S.op("dve", lambda e: e.tensor_copy(out=clo[:], in_=r1[:]), reads=["fr1"], writes=["fclo"])
            for j, part in enumerate([chi, cmid, clo]):
                S.op("dve", lambda e, j=j, part=part: e.tensor_copy(out=qk_tm[:, :, 0, :, 64 + j], in_=part[:]), reads=["fchi", "fcmid", "fclo"], writes=["fqk_tm"])
                S.op("dve", lambda e, j=j, part=part: e.tensor_scalar(out=qk_tm[:, :, 1, :, 67 + j], in0=part[:], scalar1=-1.0, scalar2=None, op0=ALU.mult),
                     reads=["fchi", "fcmid", "fclo"], writes=["fqk_tm"])
            for t in range(NT):
                transposes(QT[:, :, t * 128:(t + 1) * 128], [qk_tm[:, t, 0, h, :] for h in range(4)], 70, ["fqk_tm"], ["fQT"], eng="act")
                transposes(KTt[:, :, t * 128:(t + 1) * 128], [qk_tm[:, t, 1, h, :] for h in range(4)], 70, ["fqk_tm"], ["fKT"])
            for h in range(4):
                attn_core("fox", lambda q0, q1, h=h: QT[:, h, q0:q1], lambda kt, h=h: KTt[:, h, kt * 128:(kt + 1) * 128],
                          lambda kt, h=h: Vg[:, kt, h, 0:65], 0.125, causal_steps, causal_bias,
                          std_fin(o_tm, h, "fo_tm"), ["fQT", "fKT", "fV"], ndummy=NDUMMY)
            if l == 0:
                dump("o_fox", o_tm[:], [128, NT, 256], ["fo_tm"])
            finish_mixer(2, o_tm, "fo_tm")
            S.barrier()

        if "dsa" in mixers:
            o = WS0
            QK3 = A("dQK3", [128, 3, S_LEN], BF16, o); o += 12 * KB
            QT2 = QK3[:, 0:2, :]
            KT2 = QK3[:, 2, :]
            Vg = A("dV", [128, NT, 66], BF16, o); o += NT * 66 * 2
            o = (o + 31) // 32 * 32
            qki = A("dqki", [96, 4, S_LEN], BF16, o); o += 16 * KB
            qiT = qki[:, 0:3, :]
            kiT = qki[:, 3, :]
            score = [A("dscore%d" % i, [128, S_LEN], F32, o + i * 8 * KB) for i in range(4)]; o += 32 * KB
            Mb = [A("dMb0", [128, 4, 1536], BF16, o), A("dMb1", [128, 4, S_LEN], BF16, o + 12 * KB)]; o += 28 * KB
            o_c = [A("do_c%d" % i, [128, 4, 256], BF16, o + i * 2 * KB) for i in range(2)]; o += 4 * KB
            qk6 = A("dqk6", [128, 6, 64], BF16, o); o += 768
            qi9 = A("dqi9", [128, 12, 32], BF16, o); o += 768
            wq = A("dwq", [128, NT, 8], F32, o); o += 512
            rA = A("drA", [128, 9, 8], F32, o); o += 288
            rB = A("drB", [128, 9, 8], F32, o); o += 288
            bs = A("dbs", [128, 2, 64], F32, o); o += 512
            ki3 = A("dki3", [128, 96], BF16, o); o += 192
            junk1 = A("djunk1", [128, 16], BF16, o); o += 32
            wzo = OFF0 + 32 * KB
            Rb = [A("dR%d" % i, [128, 512], BF16, wzo + i * KB) for i in range(4)]
            diag = [A("ddiag%d" % i, [128, 8, 128], BF16, wzo + 4 * KB + i * 2 * KB) for i in range(2)]
            Rall = A("dRall", [128, S_LEN], BF16, wzo)
            load_w_cols(Wz, l, [(0, DSA0, 680)], "Wz")
            S.op("pool", lambda e: e.memset(Vg[:, :, 64:65], 1.0), writes=["dV"])
            S.op("pool", lambda e: e.memset(qi9[:], 0.0), writes=["dqi90", "dqi9N0"])
            qk6_ = [qk6, A("dqk6b", [128, 6, 64], BF16, o)]; o += 768
            qi9_ = [qi9, A("dqi9b", [128, 12, 32], BF16, o)]; o += 768
            ki3_ = [ki3, A("dki3b", [128, 96], BF16, o)]; o += 192
            rA_ = [rA, A("drAb", [128, 9, 8], F32, o)]; o += 288
            rB_ = [rB, A("drBb", [128, 9, 8], F32, o)]; o += 288
            S.op("pool", lambda e: e.memset(qi9_[1][:], 0.0), writes=["dqi91", "dqi9N1"])
            ptr0 = OFF0 + 32 * KB + 13 * KB
            zsA_ = [A("dzsA%d" % i, [128, 384], F32, ptr0 + i * 1536) for i in range(2)]
            zsB_ = [A("dzsB%d" % i, [128, 296], F32, ptr0 + 3072 + i * 1184) for i in range(2)]

            def dsa_A(t):
                p = t % 2
                sp = str(p)
                qk6, qi9, ki3, rA, rB = qk6_[p], qi9_[p], ki3_[p], rA_[p], rB_[p]
                (pa, pak0), (pb, pbk0) = project_tile(t, Wz, [(0, 384), (384, 296)], "Wz")
                warm(NWARM)
                za, zb = zsA_[p], zsB_[p]
                pak, pbk = "dzsA" + sp, "dzsB" + sp
                S.op("act", lambda e: e.copy(out=za[:], in_=pa[:, 0:384]), reads=[pak0], writes=[pak])
                S.op("act", lambda e: e.copy(out=zb[:], in_=pb[:, 0:296]), reads=[pbk0], writes=[pbk])
                pa3 = za[:, 0:320].rearrange("p (h d) -> p h d", h=5)
                rope(qk6[:, 0:5, 0:8], qk6[:, 0:5, 8:16], pa3[:, :, 0:8], pa3[:, :, 8:16], COS(t, 16, 24), SIN(t, 16, 24), [128, 5, 8],
                     rA[:, 0:5, :], rB[:, 0:5, :], [pak], ["dqk6" + sp], "dr" + sp)
                S.op("act", lambda e: e.copy(out=qk6[:, 0:5, 16:64], in_=pa3[:, :, 16:64]), reads=[pak], writes=["dqk6N" + sp])
                S.op("dve", lambda e: e.tensor_copy(out=Vg[:, t, 0:64], in_=za[:, 320:384]), reads=[pak], writes=["dV"])
                S.op("pool", lambda e: e.tensor_copy(out=qk6[:, 5, :], in_=qk6[:, 4, :]), reads=["dqk6" + sp, "dqk6N" + sp], writes=["dqk6D" + sp])
                pb3 = zb[:, 0:288].rearrange("p (h d) -> p h d", h=9)
                rope(qi9[:, 0:9, 0:4], qi9[:, 0:9, 4:8], pb3[:, :, 0:4], pb3[:, :, 4:8], COS(t, 24, 28), SIN(t, 24, 28), [128, 9, 4],
                     rA[:, :, 0:4], rB[:, :, 0:4], [pbk], ["dqi9" + sp], "dr" + sp)
                S.op("act", lambda e: e.copy(out=qi9[:, 0:9, 8:32], in_=pb3[:, :, 8:32]), reads=[pbk], writes=["dqi9N" + sp])
                S.op("dve", lambda e: e.tensor_copy(out=wq[:, t, :], in_=zb[:, 288:296]), reads=[pbk], writes=["dwq"])
                S.op("pool", lambda e: e.tensor_copy(out=ki3[:].rearrange("p (a b) -> p a b", a=3), in_=qi9[:, 8:9, :].to_broadcast([128, 3, 32])), reads=["dqi9" + sp, "dqi9N" + sp], writes=["dki3" + sp])

            def dsa_B(t):
                p = t % 2
                sp = str(p)
                qk6, qi9, ki3, rA, rB = qk6_[p], qi9_[p], ki3_[p], rA_[p], rB_[p]
                qf = qk6[:].rearrange("p a b -> p (a b)")
                transposes(QK3[:, :, t * 128:(t + 1) * 128], [qf[:, 0:128], qf[:, 128:256], qf[:, 256:384]], 128, ["dqk6" + sp, "dqk6N" + sp, "dqk6D" + sp], ["dQT", "dKT"], eng="act")
                qflat = qi9[:].rearrange("p a b -> p (a b)")
                transposes(qki[:, :, t * 128:(t + 1) * 128], [qflat[:, 0:96], qflat[:, 96:192], qflat[:, 192:288], ki3[:]], 96,
                           ["dqi9" + sp, "dqi9N" + sp, "dki3" + sp], ["dqiT", "dkiT"], eng="act")
                warm(NWARM)
            for step in range(NT + 1):
                if step < NT:
                    dsa_A(step)
                if step >= 1:
                    dsa_B(step - 1)
            S.barrier()
            NIT = 21
            ddum = ps_trs[0][:].rearrange("p a b -> p (a b)").bitcast(F32)
            pairs = [(2 * i, 2 * i + 1) for i in range(1, 8)]

            def sbuf(qt):
                i = qt % 4
                return score[i], "dscore%d" % i

            def dsa_scores(pair):
                for qt in pair:
                    L = (qt + 1) * 128
                    sc, sk = sbuf(qt)
                    dg = diag[qt % 2]
                    dk = "ddiag%d" % (qt % 2)
                    S.op("dve", lambda e: e.tensor_tensor(out=dg[:], in0=identb[:].unsqueeze(1).to_broadcast([128, 8, 128]),
                                                          in1=wq[:, qt, :].unsqueeze(2).to_broadcast([128, 8, 128]), op=ALU.mult), reads=["const", "dwq"], writes=[dk])
                    nkc = (L + 511) // 512
                    for kc in range(nkc):
                        k0 = kc * 512
                        n = min(512, L - k0)

                        def logit(h):
                            g, jj = divmod(h, 3)
                            pl = ps_mm[h % 2]
                            S.op("pe", lambda e: e.matmul(pl[:, 0:n], lhsT=qiT[32 * jj:32 * jj + 32, g, qt * 128:(qt + 1) * 128],
                                                          rhs=kiT[32 * jj:32 * jj + 32, k0:k0 + n], start=True, stop=True),
                                 reads=["dqiT", "dkiT"], writes=["ps_mm%d" % (h % 2)])
                            r = Rb[h % 4]
                            S.op("act", lambda e: e.activation(out=r[:, 0:n], in_=pl[:, 0:n], func=AF.Relu), reads=["ps_mm%d" % (h % 2)], writes=["dR%d" % (h % 4)])

                        def hsum(h):
                            r = Rb[h % 4]
                            S.op("pe", lambda e: e.matmul(ps_x[:, 0:n], lhsT=dg[:, h, :], rhs=r[:, 0:n], start=(h == 0), stop=(h == 7)),
                                 reads=["dR%d" % (h % 4), dk], writes=[PX])
                        logit(0)
                        for h in range(8):
                            if h + 1 < 8:
                                logit(h + 1)
                            hsum(h)
                            if h % 2 == 1 and NDUMMY:
                                S.op("pe", lambda e: e.matmul(ddum, lhsT=identb[:], rhs=cmpbias[:, 0:512], start=True, stop=True, skip_group_check=True), sig=False)
                        S.op("act", lambda e: e.copy(out=sc[:, k0:k0 + n], in_=ps_x[:, 0:n]), reads=[PX], writes=[sk])

            def dsa_bisect(pair, pi):
                st = {}
                for j, qt in enumerate(pair):
                    L = (qt + 1) * 128
                    sc, sk = sbuf(qt)
                    b = bs[:, j, :]
                    kx = "b%d_" % j
                    S.op("dve", lambda e, b=b, sc=sc, L=L: e.tensor_reduce(out=b[:, 0:1], in_=sc[:, 0:L], axis=AX.X, op=ALU.max, apply_absolute_value=True), reads=[sk], writes=[kx + "M"])
                    S.op("pool", lambda e, sc=sc, L=L: e.tensor_tensor(out=sc[:, L - 128:L], in0=sc[:, L - 128:L], in1=causqk[:], op=ALU.add), reads=[sk, "const", kx + "M"], writes=[sk])
                    S.op("pool", lambda e, b=b: e.tensor_scalar(out=b[:, 8:8 + NIT + 1], in0=pow2[:, 0:NIT + 1], scalar1=b[:, 0:1], scalar2=None, op0=ALU.mult), reads=[kx + "M", "const"], writes=[kx + "d"])
                    S.op("pool", lambda e, b=b: e.memset(b[:, 4:5], 0.0), writes=[kx + "mid0"])
                    st[qt] = (b, kx, sc, sk, L)
                for it in range(NIT):
                    for j, qt in enumerate(pair):
                        b, kx, sc, sk, L = st[qt]
                        m = b[:, 4 + (it % 2):5 + (it % 2)]
                        nm = b[:, 4 + ((it + 1) % 2):5 + ((it + 1) % 2)]
                        mk, nmk = kx + "mid%d" % (it % 2), kx + "mid%d" % ((it + 1) % 2)
                        if pi == len(pairs) - 1 and j == 1:
                            S.op("act", lambda e, m=m, sc=sc, L=L, b=b: e.activation(out=Rall[:, 0:L], in_=sc[:, 0:L], func=AF.Sign, bias=m, scale=-1.0, accum_out=b[:, 6:7]),
                                 reads=[sk, mk], writes=[kx + "cnt", "dR0", "dR1", "dR2", "dR3"])
                            S.op("pool", lambda e, b=b, it=it, L=L: e.tensor_scalar(out=b[:, 7:8], in0=b[:, 6:7], scalar1=float(L) - 510.5, scalar2=b[:, 8 + it:9 + it], op0=ALU.is_lt, op1=ALU.mult),
                                 reads=[kx + "cnt", kx + "d"], writes=[kx + "sel"])
                        else:
                            S.op("dve", lambda e, m=m, sc=sc, L=L, b=b, j=j: e.tensor_scalar(out=junk1[:, j:j + 1].to_broadcast([128, L]), in0=sc[:, 0:L], scalar1=m, scalar2=0.0, op0=ALU.is_ge, op1=ALU.add,
                                                                                    accum_out=b[:, 6:7]), reads=[sk, mk], writes=[kx + "cnt", kx + "junk"])
                            S.op("pool", lambda e, b=b, it=it: e.tensor_scalar(out=b[:, 7:8], in0=b[:, 6:7], scalar1=255.5, scalar2=b[:, 8 + it:9 + it], op0=ALU.is_ge, op1=ALU.mult),
                                 reads=[kx + "cnt", kx + "d"], writes=[kx + "sel"])
                        S.op("pool", lambda e, b=b, it=it, m=m, nm=nm: e.tensor_scalar(out=nm, in0=b[:, 7:8], scalar1=m, scalar2=b[:, 9 + it:10 + it], op0=ALU.add, op1=ALU.subtract),
                             reads=[kx + "sel", mk, kx + "d"], writes=[nmk])
                for j, qt in enumerate(pair):
                    b, kx, sc, sk, L = st[qt]
                    c = qt // 4
                    mb = Mb[c % 2]
                    fm = b[:, 4 + (NIT % 2):5 + (NIT % 2)]
                    S.op("pool", lambda e, b=b, fm=fm: e.tensor_tensor(out=b[:, 3:4], in0=fm, in1=b[:, 8 + NIT:9 + NIT], op=ALU.subtract), reads=[kx + "mid%d" % (NIT % 2), kx + "d"], writes=[kx + "thr"])
                    S.op("dve", lambda e, b=b, sc=sc, L=L, mb=mb, qt=qt, c=c: e.tensor_scalar(out=mb[:, qt - 4 * c, 0:L], in0=sc[:, 0:L], scalar1=b[:, 3:4], scalar2=NEGB, op0=ALU.is_lt, op1=ALU.mult),
                         reads=[sk, kx + "thr"], writes=["dMb%d" % (c % 2)])

            def dsa_attn(c):
                mb = Mb[c % 2]
                mbk = "dMb%d" % (c % 2)
                oc = o_c[c % 2]
                ock = "do_c%d" % (c % 2)

                def dsa_bias(kt, qt):
                    if qt < 2:
                        return causal_bias(kt, qt)
                    return [(mb[:, qt - 4 * c, kt * 128:(kt + 1) * 128], identb[:], [mbk, "const"])]

                def fin_for(h):
                    inner = std_fin(oc, h, ock)
                    return lambda cc, acc, akey: inner(0, acc, akey)
                for h in range(4):
                    p0 = (h % 2) * 64
                    attn_core("dsa", lambda q0, q1, h=h, p0=p0: QT2[p0:p0 + 64, h // 2, q0:q1], lambda kt, p0=p0: KT2[p0:p0 + 64, kt * 128:(kt + 1) * 128],
                              lambda kt: Vg[:, kt, 0:65], 0.125, lambda cc: causal_steps(c) if cc == c else [], dsa_bias,
                              fin_for(h), ["dQT", "dKT", "dV"], ndummy=NDUMMY, dummy_out=ddum)
                for tt in range(4):
                    t = 4 * c + tt
                    transposes(oT[:, 3, :, t * 128:(t + 1) * 128], [oc[:, tt, 0:128], oc[:, tt, 128:256]], 128, [ock], ["oT"], eng=("act" if tt % 2 else "dve"), bank=1)
                if l == 0:
                    dump("o_dsa%d" % c, oc[:], [128, 4, 256], [ock])

            dsa_scores(pairs[0])
            for i, pr_ in enumerate(pairs):
                if i + 1 < len(pairs):
                    dsa_scores(pairs[i + 1])
                dsa_bisect(pr_, i)
                if pr_[1] % 4 == 3:
                    dsa_attn(pr_[1] // 4)
            S.barrier()

        if "nsa" in mixers:
            o = WS0
            QT = A("nQT", [96, 4, S_LEN], BF16, o); o += 16 * KB
            k4T = A("nk4T", [96, 4, S_LEN], BF16, o); o += 16 * KB
            kcT = k4T[:, 0, :]
            ksT = k4T[:, 1, :]
            kwT = k4T[:, 2, :]
            vcT = k4T[:, 3, :]
            Vs = A("nVs", [128, NT, 66], BF16, o); o += NT * 66 * 2
            Vw = A("nVw", [128, NT, 66], BF16, o); o += NT * 66 * 2
            o = (o + 31) // 32 * 32
            Wk = A("nWk", [64, 32, 64], BF16, o); o += 4 * KB
            Wv = A("nWv", [64, 32, 64], BF16, o); o += 4 * KB
            Wkf = A("nWkf", [128, 16, 64], BF16, o); o += 2 * KB
            Wvf = A("nWvf", [128, 16, 64], BF16, o); o += 2 * KB
            pek = A("npek", [128, 16], BF16, o); o += 32
            pev = A("npev", [128, 16], BF16, o); o += 32
            kcmpT = A("nkcmpT", [64, 128], BF16, o); o += 256
            vcx = A("nvcx", [128, 97], BF16, o); o += 224
            imp = A("nimp", [128, NT, 32], F32, o); o += 2 * KB
            blkb = A("nblkb", [128, 96], BF16, o); o += 192
            gt = A("ngt", [128, NT, 12], F32, o); o += 768
            oacc = A("noacc", [128, NT, 256], F32, o); o += 16 * KB
            o_tm = A("no_tm", [128, NT, 256], BF16, o); o += 8 * KB
            q7 = A("nq7", [128, 7, 64], BF16, o); o += 896
            vc_tm = A("nvc_tm", [128, 64], BF16, o); o += 128
            rA = A("nrA", [128, 7, 8], F32, o); o += 224
            rB = A("nrB", [128, 7, 8], F32, o); o += 224
            m8 = A("nm8", [128, 8], F32, o); o += 32
            itmp = A("nitmp", [128, 4, 32], F32, o); o += 512
            n0 = NSA0
            load_w_cols(Wz, l, [(0, n0, 320), (320, n0 + 384, 64), (384, n0 + 512, 64),
                                (448, n0 + 320, 64), (512, n0 + 448, 64), (576, n0 + 576, 64), (640, n0 + 640, 12)], "Wz")
            S.dma(Wk[:], dr["nsa_cmp_w"][l, 0].rearrange("(l d) o -> d l o", d=64), writes=["nWk"], q="pool")
            S.dma(Wv[:], dr["nsa_cmp_w"][l, 1].rearrange("(l d) o -> d l o", d=64), writes=["nWv"], q="pool")
            S.dma(Wkf[:], dr["nsa_cmp_w"][l, 0].rearrange("(j p) o -> p j o", p=128), writes=["nWkf"], q="pool")
            S.dma(Wvf[:], dr["nsa_cmp_w"][l, 1].rearrange("(j p) o -> p j o", p=128), writes=["nWvf"], q="pool")
            S.dma(pek[:], dr["nsa_cmp_pe"][l, 0].rearrange("(j p) -> p j", p=128), writes=["npek"], q="pool", allow_slow_non_contiguous=True)
            S.dma(pev[:], dr["nsa_cmp_pe"][l, 1].rearrange("(j p) -> p j", p=128), writes=["npev"], q="pool", allow_slow_non_contiguous=True)
            S.dma(ksT[64:96, :], dr["c_E"], writes=["nksT"], q="pool")
            S.op("pool", lambda e: e.memset(blkb[:], 0.0), writes=["nblkb"])
            S.op("pool", lambda e: e.memset(Vs[:, :, 64:65], 1.0), writes=["nVs"])
            S.op("pool", lambda e: e.memset(Vw[:, :, 64:65], 1.0), writes=["nVw"])
            q7_ = [q7, A("nq7b", [128, 7, 64], BF16, o)]; o += 896
            vc_tm_ = [vc_tm, A("nvc_tmb", [128, 64], BF16, o)]; o += 128
            rA_ = [rA, A("nrAb", [128, 7, 8], F32, o)]; o += 224
            rB_ = [rB, A("nrBb", [128, 7, 8], F32, o)]; o += 224
            zsA_ = [A("nzsA%d" % i, [128, 448], F32, o + i * 1792) for i in range(2)]; o += 3584
            zsB_ = [A("nzsB%d" % i, [128, 204], F32, o + i * 832) for i in range(2)]; o += 1664

            def nsa_A(t):
                p = t % 2
                sp = str(p)
                q7, vc_tm, rA, rB = q7_[p], vc_tm_[p], rA_[p], rB_[p]
                (pa, pak0), (pb, pbk0) = project_tile(t, Wz, [(0, 448), (448, 204)], "Wz")
                warm(NWARM)
                za, zb = zsA_[p], zsB_[p]
                pak, pbk = "nzsA" + sp, "nzsB" + sp
                S.op("act", lambda e: e.copy(out=za[:], in_=pa[:, 0:448]), reads=[pak0], writes=[pak])
                S.op("act", lambda e: e.copy(out=zb[:], in_=pb[:, 0:204]), reads=[pbk0], writes=[pbk])
                pa3 = za[:].rearrange("p (h d) -> p h d", h=7)
                rope(q7[:, :, 0:8], q7[:, :, 8:16], pa3[:, :, 0:8], pa3[:, :, 8:16], COS(t, 16, 24), SIN(t, 16, 24), [128, 7, 8],
                     rA[:], rB[:], [pak], ["nq7" + sp], "nr" + sp)
                S.op("act", lambda e: e.copy(out=q7[:, :, 16:64], in_=pa3[:, :, 16:64]), reads=[pak], writes=["nq7N" + sp])
                S.op("dve", lambda e: e.tensor_copy(out=vc_tm[:], in_=zb[:, 0:64]), reads=[pbk], writes=["nvc_tm" + sp])
                S.op("dve", lambda e: e.tensor_copy(out=Vs[:, t, 0:64], in_=zb[:, 64:128]), reads=[pbk], writes=["nVs"])
                S.op("dve", lambda e: e.tensor_copy(out=Vw[:, t, 0:64], in_=zb[:, 128:192]), reads=[pbk], writes=["nVw"])
                S.op("act", lambda e: e.activation(out=gt[:, t, :], in_=zb[:, 192:204], func=AF.Sigmoid), reads=[pbk], writes=["ngt"])

            def nsa_B(t):
                p = t % 2
                sp = str(p)
                q7, vc_tm, rA, rB = q7_[p], vc_tm_[p], rA_[p], rB_[p]
                transposes(QT[0:64, :, t * 128:(t + 1) * 128], [q7[:, h, :] for h in range(4)], 64, ["nq7" + sp, "nq7N" + sp], ["nQT"], eng="act")
                ts_ = slice(t * 128, (t + 1) * 128)
                transposes(k4T[0:64, :, ts_], [q7[:, 4, :], q7[:, 5, :], q7[:, 6, :], vc_tm[:]], 64, ["nq7" + sp, "nq7N" + sp, "nvc_tm" + sp],
                           ["nkcT", "nksT", "nkwT", "nvcT"], eng="act")
                warm(NWARM)
            for step in range(NT + 1):
                if step < NT:
                    nsa_A(step)
                if step >= 1:
                    nsa_B(step - 1)
            pk = ps_mm[0]
            for li in range(32):
                S.op("pe", lambda e, li=li: e.matmul(pk[0:64, 0:127], lhsT=Wk[:, li, :], rhs=kcT[0:64, li:li + 16 * 126 + 1:16], start=(li == 0), stop=False),
                     reads=["nWk", "nkcT"], writes=["ps_mm0"], sig=False)
            for j in range(16):
                S.op("pe", lambda e, j=j: e.matmul(pk[0:64, 0:127], lhsT=Wkf[:, j, :], rhs=pek[:, j:j + 1].to_broadcast([128, 127]), start=False, stop=(j == 15)),
                     reads=["nWkf", "npek"], writes=["ps_mm0"], sig=(j == 15))
            S.op("dve", lambda e: e.tensor_copy(out=kcmpT[:, 0:127], in_=pk[0:64, 0:127]), reads=["ps_mm0"], writes=["nkcmpT"])
            pv = ps_mm[1]
            for li in range(32):
                S.op("pe", lambda e, li=li: e.matmul(pv[0:127, 0:64], lhsT=vcT[0:64, li:li + 16 * 126 + 1:16], rhs=Wv[:, li, :], start=(li == 0), stop=False),
                     reads=["nWv", "nvcT"], writes=["ps_mm1"], sig=False)
            for j in range(16):
                S.op("pe", lambda e, j=j: e.matmul(pv[0:127, 0:64], lhsT=pev[:, j:j + 1].to_broadcast([128, 127]), rhs=Wvf[:, j, :], start=False, stop=(j == 15)),
                     reads=["nWvf", "npev"], writes=["ps_mm1"], sig=(j == 15))
            S.op("pool", lambda e: e.memset(vcx[:, 64:65], 1.0), writes=["nvcx"])
            S.op("dve", lambda e: e.tensor_copy(out=vcx[0:127, 0:64], in_=pv[0:127, 0:64]), reads=["ps_mm1"], writes=["nvcx"])
            S.op("pool", lambda e: e.tensor_copy(out=vcx[:, 65:97], in_=ovl[:]), reads=["const"], writes=["nvcx"])
            if l == 0:
                dump("nsa_kcmpT", kcmpT[:], [64, 128], ["nkcmpT"])
                dump("nsa_vcx", vcx[:], [128, 97], ["nvcx"])

            def gate_fn(path, h):
                return lambda c: (gt[:, 4 * c:4 * c + 4, path * 4 + h], "ngt")
            for h in range(4):
                def cmp_fin(c, acc, akey, h=h):
                    rec = misc[:, 64:68]
                    S.op("dve", lambda e: e.tensor_scalar(out=rec, in0=acc[:, :, 64], scalar1=1e-30, scalar2=None, op0=ALU.max), reads=[akey], writes=["rec"])
                    S.op("dve", lambda e: e.reciprocal(out=misc[:, 68:72], in_=rec), reads=["rec"], writes=["rec2"])
                    S.op("dve", lambda e: e.tensor_tensor(out=misc[:, 72:76], in0=misc[:, 68:72], in1=gt[:, 4 * c:4 * c + 4, h], op=ALU.mult), reads=["rec2", "ngt"], writes=["rec3"])
                    dst = oacc[:, 4 * c:4 * c + 4, h * 64:(h + 1) * 64]
                    S.op("dve", lambda e: e.tensor_tensor(out=dst, in0=acc[:, :, 0:64], in1=misc[:, 72:76].unsqueeze(2).to_broadcast([128, 4, 64]), op=ALU.mult),
                         reads=[akey, "rec3"], writes=["noacc"])
                    idst = imp[:, 4 * c:4 * c + 4, :]
                    if h == 0:
                        S.op("dve", lambda e: e.tensor_tensor(out=idst, in0=acc[:, :, 65:97], in1=misc[:, 68:72].unsqueeze(2).to_broadcast([128, 4, 32]), op=ALU.mult),
                             reads=[akey, "rec2"], writes=["nimp"])
                    else:
                        S.op("dve", lambda e: e.tensor_tensor(out=itmp[:], in0=acc[:, :, 65:97], in1=misc[:, 68:72].unsqueeze(2).to_broadcast([128, 4, 32]), op=ALU.mult),
                             reads=[akey, "rec2"], writes=["nitmp"])
                        S.op("pool", lambda e: e.tensor_tensor(out=idst, in0=idst, in1=itmp[:], op=ALU.add), reads=["nitmp", "nimp"], writes=["nimp"])
                attn_core("ncmp", lambda q0, q1, h=h: QT[0:64, h, q0:q1], lambda kt: kcmpT[:, 0:127], lambda kt: vcx[0:127, 0:97], 0.125,
                          lambda c: [(0, 4 * c, 4 * c + 4)],
                          lambda kt, qt: [],
                          cmp_fin, ["nQT", "nkcmpT", "nvcx"], nk=127, vw=97,
                          range_bias_fn=lambda kt, qlo, qhi: [(identb[0:127, 0:127], cmpbias[0:127, qlo * 128:qhi * 128], ["const"])])
            S.op("dve", lambda e: e.tensor_tensor(out=imp[:], in0=imp[:], in1=fkeep[:], op=ALU.mult), reads=["nimp", "const"], writes=["nimp"])
            S.op("dve", lambda e: e.tensor_tensor(out=imp[:], in0=imp[:], in1=fbase[:], op=ALU.add), reads=["nimp", "const"], writes=["nimp"])
            for t in range(NT):
                S.op("dve", lambda e, t=t: e.max(out=m8[:], in_=imp[:, t, :]), reads=["nimp"], writes=["nm8"])
                S.op("dve", lambda e, t=t: e.tensor_scalar(out=blkb[:, 64:96], in0=imp[:, t, :], scalar1=m8[:, 7:8], scalar2=NEGB, op0=ALU.is_lt, op1=ALU.mult), reads=["nimp", "nm8"], writes=["nblkb"])
                bank = t % 2
                ptr = ps_trs[bank]
                tk = "ps_tr%d" % bank
                S.op("pe", lambda e, ptr=ptr: e.transpose(out=ptr[0:96, 0, :], in_=blkb[:], identity=identb[:]), reads=["nblkb", "const"], writes=[tk])
                S.op("act", lambda e, ptr=ptr, t=t: e.copy(out=QT[64:96, :, t * 128:(t + 1) * 128], in_=ptr[64:96, 0:1, :].to_broadcast([32, 4, 128])), reads=[tk], writes=["nQT"])
            if l == 0:
                dump("nsa_imp", imp[:], [128, NT, 32], ["nimp"])

            def sel_bias(kt, qt):
                if kt == qt:
                    return [(identb[:], caust[:], ["const"])]
                return []

            def win_steps(c):
                out = []
                for kt in range(max(0, 4 * c - 4), 4 * c + 4):
                    qlo = max(4 * c, kt)
                    qhi = min(4 * c + 4, kt + 5)
                    if qhi > qlo:
                        out.append((kt, qlo, qhi))
                return out

            def win_bias(kt, qt):
                if kt == qt:
                    return [(identb[:], caust[:], ["const"])]
                if kt == qt - 4:
                    return [(identb[:], wint[:], ["const"])]
                return []
            for h in range(4):
                attn_core("nsel", lambda q0, q1, h=h: QT[0:96, h, q0:q1], lambda kt: ksT[0:96, kt * 128:(kt + 1) * 128], lambda kt: Vs[:, kt, 0:65], 0.125,
                          causal_steps, sel_bias, std_fin(oacc, h, "noacc", gate=gate_fn(1, h), accumulate=True), ["nQT", "nksT", "nVs"], ndummy=NDUMMY)
                attn_core("nwin", lambda q0, q1, h=h: QT[0:64, h, q0:q1], lambda kt: kwT[0:64, kt * 128:(kt + 1) * 128], lambda kt: Vw[:, kt, 0:65], 0.125,
                          win_steps, win_bias, std_fin(oacc, h, "noacc", gate=gate_fn(2, h), accumulate=True), ["nQT", "nkwT", "nVw"], ndummy=NDUMMY)
            for t in range(NT):
                S.op("pool", lambda e, t=t: e.tensor_copy(out=o_tm[:, t, :], in_=oacc[:, t, :]), reads=["noacc"], writes=["no_tm"])
            if l == 0:
                dump("o_nsa", o_tm[:], [128, NT, 256], ["no_tm"])
            finish_mixer(1, o_tm, "no_tm")
            S.barrier()

        mixed = A("mixed", [128, NT, D], BF16, OFF0 + 32 * KB)
        x_sb = A("x_sb", [128, NT, D], F32, OFF0 + 64 * KB)
        wo = A("wo", [128, KD, D], BF16, OFF0 + 128 * KB)
        for kh in range(4):
            S.dma(wo[:, kh * 2:(kh + 1) * 2, :], dr["w_out"][l].rearrange("(k p) c -> p k c", p=128)[:, kh * 2:(kh + 1) * 2, :], writes=["wo"], q="pool")
        for t in range(7, NT):
            S.dma(x_sb[:, t, :], x_src[t * 128:(t + 1) * 128, :], writes=["x_sb%d" % t])
        S.dma(g2[:], dr["norm2_g"][l:l + 1, :].to_broadcast([128, D]), writes=["g2"])
        o = OFF0 + 64 * KB
        Wg = [A("Wg%d" % i, [128, KD, 512], BF16, o + i * 8 * KB) for i in range(2)]; o += 16 * KB
        Wb = [A("Wb%d" % i, [128, 2, 512], BF16, o + i * 2 * KB) for i in range(2)]; o += 4 * KB
        sg = [A("sg%d" % i, [128, 512], F32, o + i * 2 * KB) for i in range(2)]; o += 4 * KB
        pr = [A("pr%d" % i, [128, 512], BF16, o + i * KB) for i in range(2)]; o += 2 * KB
        it = 0

        def load_gate_w(i):
            n_, cc_ = divmod(i, 2)
            b_ = i % 2
            gsrc = dr["w_in"][l, :, GATE0 + n_ * D + cc_ * 512:GATE0 + n_ * D + (cc_ + 1) * 512].rearrange("(k p) c -> p k c", p=128)
            for kh in range(2):
                S.dma(Wg[b_][:, kh * 4:(kh + 1) * 4, :], gsrc[:, kh * 4:(kh + 1) * 4, :], writes=["Wg%d" % b_], q="pool")
            S.dma(Wb[b_][:], dr["w_branch"][l, n_, :, cc_ * 512:(cc_ + 1) * 512].rearrange("(k p) c -> p k c", p=128), writes=["Wb%d" % b_], q="pool")
        load_gate_w(0)
        for n in range(4):
            for cc in range(2):
                b = it % 2
                it += 1
                if it < 8:
                    load_gate_w(it)
                for t in range(NT):
                    pg = ps_mm[t % 2]
                    pgk = "ps_mm%d" % (t % 2)
                    for k in range(KD):
                        S.op("pe", lambda e, k=k, pg=pg: e.matmul(pg[:, 0:512], lhsT=actT[:, k, t * 128:(t + 1) * 128], rhs=Wg[b][:, k, :], start=(k == 0), stop=(k == KD - 1)),
                             reads=["actT", "Wg%d" % b], writes=[pgk], sig=(k == KD - 1))
                    plf = ps_st[t % 2]
                    plk = "ps_st%d" % (t % 2)
                    for k in range(2):
                        S.op("pe", lambda e, k=k, plf=plf: e.matmul(plf[:, 0:512], lhsT=oT[:, n, k, t * 128:(t + 1) * 128], rhs=Wb[b][:, k, :], start=(k == 0), stop=(k == 1)),
                             reads=["oT", "Wb%d" % b], writes=[plk], sig=(k == 1))
                    s_ = sg[t % 2]
                    sk = "sg%d" % (t % 2)
                    S.op("act", lambda e, s_=s_, pg=pg: e.activation(out=s_[:], in_=pg[:, 0:512], func=AF.Sigmoid), reads=[pgk], writes=[sk])
                    dst = mixed[:, t, cc * 512:(cc + 1) * 512]
                    if n == 0:
                        S.op("dve", lambda e, s_=s_, plf=plf, dst=dst: e.tensor_tensor(out=dst, in0=s_[:], in1=plf[:, 0:512], op=ALU.mult), reads=[sk, plk], writes=["mixed"])
                    else:
                        p_ = pr[t % 2]
                        pk_ = "pr%d" % (t % 2)
                        S.op("dve", lambda e, s_=s_, plf=plf, p_=p_: e.tensor_tensor(out=p_[:], in0=s_[:], in1=plf[:, 0:512], op=ALU.mult), reads=[sk, plk], writes=[pk_])
                        S.op("pool", lambda e, p_=p_, dst=dst: e.tensor_tensor(out=dst, in0=dst, in1=p_[:], op=ALU.add), reads=[pk_, "mixed"], writes=["mixed"])
        if l == 0:
            dump("mixed", mixed[:], [128, NT, D], ["mixed"])
        S.barrier()
        for t in range(7):
            S.dma(x_sb[:, t, :], x_src[t * 128:(t + 1) * 128, :], writes=["x_sb%d" % t])
        for t in range(NT):
            transposes(actT[:, :, t * 128:(t + 1) * 128], [mixed[:, t, k * 128:(k + 1) * 128] for k in range(KD)], 128, ["mixed"], ["actT"], eng=("act" if t % 2 else "dve"))
        h2T = A("h2T", [128, KD, S_LEN], BF16, OFF0)
        aT = A("aT", [128, 8, S_LEN], BF16, ACT0)
        wup = A("wup", [128, KD, 1024], BF16, OFF0 + 32 * KB)
        wdn = A("wdn", [128, 8, D], BF16, OFF0 + 48 * KB)
        o = OFF0 + 144 * KB
        hb2 = [A("hb2_%d" % i, [128, D], BF16, o + i * 2 * KB) for i in range(2)]; o += 4 * KB
        sq4 = A("sq4", [128, D], BF16, o); o += 2 * KB
        nst4 = A("nst4", [128, 8], F32, o); o += 32
        usq = [A("usq%d" % i, [128, 512], F32, OFF0 + 128 * KB + i * 2 * KB) for i in range(2)]
        xkeys = ["x_sb%d" % t for t in range(NT)]
        def p3_mm(t):
            xk = "x_sb%d" % t
            for cc in range(2):
                pg = ps_mm[cc]
                for k in range(KD):
                    S.op("pe", lambda e, k=k, pg=pg, cc=cc: e.matmul(pg[:, 0:512], lhsT=actT[:, k, t * 128:(t + 1) * 128], rhs=wo[:, k, cc * 512:(cc + 1) * 512], start=(k == 0), stop=(k == KD - 1)),
                         reads=["actT", "wo"], writes=["ps_mm%d" % cc], sig=(k == KD - 1))
                dst = x_sb[:, t, cc * 512:(cc + 1) * 512]
                S.op("dve", lambda e, pg=pg, dst=dst: e.tensor_tensor(out=dst, in0=dst, in1=pg[:, 0:512], op=ALU.add), reads=["ps_mm%d" % cc, xk], writes=[xk])

        def p3_norm(t):
            xk = "x_sb%d" % t
            b = t % 2
            rmsnorm_tile(x_sb[:, t, :], D, g2[:], hb2[b][:], (nst4, sq4[:]), "n2", ["g2"], ["hb2_%d" % b], [xk])
            transposes(h2T[:, :, t * 128:(t + 1) * 128], [hb2[b][:, k * 128:(k + 1) * 128] for k in range(KD)], 128, ["hb2_%d" % b], ["h2T"], eng=("act" if t % 2 else "dve"))
        for step in range(NT + 2):
            if step < NT:
                p3_mm(step)
            if step >= 2:
                p3_norm(step - 2)
        if l == 0:
            dump("x_attn", x_sb[:], [128, NT, D], xkeys)
        ui = 0
        for g in range(4):
            usrc = dr["w_up"][l, :, g * 1024:(g + 1) * 1024].rearrange("(k p) c -> p k c", p=128)
            for kh in range(4):
                S.dma(wup[:, kh * 2:(kh + 1) * 2, :], usrc[:, kh * 2:(kh + 1) * 2, :], writes=["wup"] + (["mixed"] if g == 0 else []), q="pool")
            dsrc = dr["w_down"][l, g * 1024:(g + 1) * 1024, :].rearrange("(f p) c -> p f c", p=128)
            for kh in range(4):
                S.dma(wdn[:, kh * 2:(kh + 1) * 2, :], dsrc[:, kh * 2:(kh + 1) * 2, :], writes=["wdn"] + (["mixed"] if g == 0 else []), q="pool")
            for fc in range(8):
                for tc4 in range(4):
                    pu = ps_mm[ui % 2]
                    puk = "ps_mm%d" % (ui % 2)
                    for k in range(KD):
                        S.op("pe", lambda e, k=k, pu=pu, fc=fc, tc4=tc4: e.matmul(pu[:, 0:512], lhsT=wup[:, k, fc * 128:(fc + 1) * 128], rhs=h2T[:, k, tc4 * 512:(tc4 + 1) * 512],
                                                                              start=(k == 0), stop=(k == KD - 1)),
                             reads=["wup", "h2T"], writes=[puk], sig=(k == KD - 1))
                    u2 = usq[ui % 2]
                    uk = "usq%d" % (ui % 2)
                    S.op("act", lambda e, pu=pu, u2=u2: e.activation(out=u2[:], in_=pu[:, 0:512], func=AF.Square), reads=[puk], writes=[uk])
                    S.op("dve", lambda e, pu=pu, u2=u2, fc=fc, tc4=tc4: e.scalar_tensor_tensor(out=aT[:, fc, tc4 * 512:(tc4 + 1) * 512], in0=pu[:, 0:512], scalar=0.0, in1=u2[:],
                                                                                         op0=ALU.is_gt, op1=ALU.mult), reads=[puk, uk], writes=["aT", "actT"])
                    ui += 1
            for t in range(NT):
                for cc in range(2):
                    pd = ps_st[cc]
                    pdk = "ps_st%d" % cc
                    for fc in range(8):
                        S.op("pe", lambda e, fc=fc, pd=pd, cc=cc: e.matmul(pd[:, 0:512], lhsT=aT[:, fc, t * 128:(t + 1) * 128], rhs=wdn[:, fc, cc * 512:(cc + 1) * 512], start=(fc == 0), stop=(fc == 7)),
                             reads=["aT", "wdn"], writes=[pdk], sig=(fc == 7))
                    dst = x_sb[:, t, cc * 512:(cc + 1) * 512]
                    S.op("dve", lambda e, pd=pd, dst=dst: e.tensor_tensor(out=dst, in0=dst, in1=pd[:, 0:512], op=ALU.add), reads=[pdk, "x_sb"], writes=["x_sb"])
        if l == 0:
            dump("x_l0", x_sb[:], [128, NT, D], ["x_sb"])
        if l < depth - 1:
            for t in range(NT):
                S.dma(xs[t * 128:(t + 1) * 128, :], x_sb[:, t, :], reads=["x_sb"], writes=["xs"])
        else:
            S.dma(g2[:], dr["final_g"][0:1, :].to_broadcast([128, D]), reads=["g2"], writes=["g2"])
            fo = [A("fo%d" % i, [128, D], F32, OFF0 + 32 * KB + i * 4 * KB) for i in range(2)]
            for t in range(NT):
                b = t % 2
                rmsnorm_tile(x_sb[:, t, :], D, g2[:], fo[b][:], (nst4, sq4[:]), "n3", ["g2"], ["fo%d" % b], ["x_sb"])
                S.dma(y[t * 128:(t + 1) * 128, :], fo[b][:], reads=["fo%d" % b], writes=["y"])
        S.barrier()


_CACHE = {}


def prepare_inputs(inputs, b):
    m = {}
    m["x"] = np.ascontiguousarray(inputs["x"][b]).astype(np.float32, copy=False)
    m["pos"] = np.ascontiguousarray(np.asarray(inputs["positions"][b]).reshape(NT, 128).T).astype(np.int32)
    for k in ["norm1_g", "w_in", "mla_q_norm_g", "mla_w_uq", "mla_kv_norm_g", "mla_w_ukv", "nsa_cmp_w", "fox_f_bias",
              "w_branch", "w_out", "norm2_g", "w_up", "w_down"]:
        m[k] = np.ascontiguousarray(inputs[k], dtype=np.float32)
    m["nsa_cmp_pe"] = np.ascontiguousarray(np.asarray(inputs["nsa_cmp_pe"], dtype=np.float32).reshape(DEPTH, 2, 2048))
    m["final_g"] = np.ascontiguousarray(np.asarray(inputs["final_g"], dtype=np.float32).reshape(1, D))
    m.update(make_consts())
    return m


def kernel(**inputs):
    inputs = {k: np.asarray(v) for k, v in inputs.items()}
    if "nc" not in _CACHE:
        _CACHE["nc"] = build_program()[0]
    nc = _CACHE["nc"]
    B = inputs["x"].shape[0]
    in_maps = [prepare_inputs(inputs, b) for b in range(B)]
    res = run_bass_kernel_spmd(nc, in_maps, core_ids=list(range(B)))
    out = np.stack([np.asarray(r["y"]) for r in res.results], axis=0).astype(np.float32)
    return out
```

```python
import numpy as np
import concourse.bass as bass
import concourse.mybir as mybir
from concourse.bass_utils import run_bass_kernel_spmd

F32 = mybir.dt.float32
BF16 = mybir.dt.bfloat16
I32 = mybir.dt.int32
ALU = mybir.AluOpType
AF = mybir.ActivationFunctionType
AX = mybir.AxisListType

S_LEN = 2048
NT = 16
D = 1024
KD = 8
DEPTH = 2
NEGB = -30000.0
NDUMMY = 1
NWARM = 0
PI = float(np.pi)


class StopBuild(Exception):
    pass


class Sched:
    limit = None

    def __init__(self, nc, n_dma_sems=24):
        self.nc = nc
        self.eng = {"pe": nc.tensor, "dve": nc.vector, "act": nc.scalar, "pool": nc.gpsimd, "sp": nc.sync}
        self.sem = {k: nc.alloc_semaphore("s_" + k) for k in self.eng}
        self.cnt = {k: 0 for k in self.eng}
        self.dsem = [nc.alloc_semaphore("d%d" % i) for i in range(n_dma_sems)]
        self.dcnt = [0] * n_dma_sems
        half = n_dma_sems // 2
        self.dpool = {"sp": list(range(0, half)), "pool": list(range(half, n_dma_sems))}
        self.dnext = {"sp": 0, "pool": 0}
        self.seen = {k: {} for k in self.eng}
        self.lastw = {}
        self.readers = {}
        self.pending = {k: ([], []) for k in self.eng}
        self.semobj = {}
        self.all_tokens = {}
        self.nwaits = 0
        self.nops = 0

    def _tok_wait(self, e, tok):
        sid, val = tok
        if self.seen[e].get(sid, 0) >= val:
            return
        self.seen[e][sid] = val
        self.eng[e].wait_ge(self.semobj[sid], val)
        self.nwaits += 1

    def _deps(self, e, reads, writes):
        writes = list(writes) + [k for k in reads if k.startswith("ps_")]
        toks = []
        for k in reads:
            t = self.lastw.get(k)
            if t is not None:
                toks.append(t)
        for k in writes:
            t = self.lastw.get(k)
            if t is not None:
                toks.append(t)
            toks.extend(self.readers.get(k, ()))
        own = id(self.sem[e])
        for t in toks:
            if e == "pe" and t[0] == own:
                continue
            self._tok_wait(e, t)

    def _commit(self, tok, reads, writes):
        writes = list(writes) + [k for k in reads if k.startswith("ps_")]
        for k in writes:
            self.lastw[k] = tok
            self.readers[k] = []
        for k in reads:
            lst = self.readers.setdefault(k, [])
            lst.append(tok)
            if len(lst) > 16:
                best = {}
                for s, v in lst:
                    best[s] = max(best.get(s, 0), v)
                self.readers[k] = list(best.items())
        self.all_tokens[tok[0]] = max(self.all_tokens.get(tok[0], 0), tok[1])

    def op(self, e, fn, reads=(), writes=(), sig=True):
        self.nops += 1
        if self.limit is not None and self.nops > self.limit:
            raise StopBuild()
        self._deps(e, reads, writes)
        ins = fn(self.eng[e])
        if not sig:
            pr, pw = self.pending[e]
            pr.extend(reads)
            pw.extend(writes)
            return
        self.cnt[e] += 1
        s = self.sem[e]
        self.semobj[id(s)] = s
        ins.then_inc(s, 1)
        tok = (id(s), self.cnt[e])
        pr, pw = self.pending[e]
        self._commit(tok, list(reads) + pr, list(writes) + pw)
        self.pending[e] = ([], [])

    def dma(self, out, in_, reads=(), writes=(), q="sp", **kw):
        self.nops += 1
        if self.limit is not None and self.nops > self.limit:
            raise StopBuild()
        self._deps(q, reads, writes)
        lst = self.dpool[q]
        i = lst[self.dnext[q] % len(lst)]
        self.dnext[q] += 1
        s = self.dsem[i]
        self.semobj[id(s)] = s
        self.dcnt[i] += 16
        self.eng[q].dma_start(out=out, in_=in_, **kw).then_inc(s, 16)
        tok = (id(s), self.dcnt[i])
        self._commit(tok, reads, writes)
        return tok

    def barrier(self):
        for e in self.eng:
            for sid, val in list(self.all_tokens.items()):
                self._tok_wait(e, (sid, val))
        self.lastw = {}
        self.readers = {}

    def wait_all(self, e="sp"):
        for sid, val in list(self.all_tokens.items()):
            self._tok_wait(e, (sid, val))


def make_consts():
    c = {}
    k = np.arange(128)[:, None]
    q = np.arange(128)[None, :]
    c["c_ident"] = np.eye(128, dtype=np.float32)
    c["c_caust"] = np.where(q >= k, 0.0, NEGB).astype(np.float32)
    c["c_wint"] = np.where(k > q, 0.0, NEGB).astype(np.float32)
    c["c_causqk"] = np.where(q <= k, 0.0, -1e30).astype(np.float32)
    cc = np.arange(128)[:, None]
    t = np.arange(S_LEN)[None, :]
    cmpb = np.where((16 * cc + 31 <= t) & (cc < 127), 0.0, NEGB).astype(np.float32)
    c["c_cmpbias"] = cmpb
    j = np.arange(32)[:, None]
    kk = np.arange(S_LEN)[None, :]
    c["c_E"] = (kk // 64 == j).astype(np.float32)
    cs = np.arange(128) * 16
    sb = np.arange(32) * 64
    ov = np.maximum(np.minimum(cs[:, None] + 32, sb[None, :] + 64) - np.maximum(cs[:, None], sb[None, :]), 0) / 32.0
    ov[127, :] = 0.0
    c["c_ovl"] = ov.astype(np.float32)
    tt = np.arange(S_LEN)
    tb = tt[:, None] // 64
    sbi = np.arange(32)[None, :]
    forced = (sbi == 0) | (sbi == tb) | (sbi == tb - 1)
    causal = (sbi * 64) <= tt[:, None]
    base = np.where(causal, np.where(forced, 1e4, 0.0), -1e30).astype(np.float32)
    keep = np.where(causal & ~forced, 1.0, 0.0).astype(np.float32)
    c["c_fbase"] = base.reshape(NT, 128, 32).transpose(1, 0, 2).copy()
    c["c_fkeep"] = keep.reshape(NT, 128, 32).transpose(1, 0, 2).copy()
    theta = np.float32(500000.0)
    def inv(rot):
        return (theta ** (-np.arange(0, rot, 2, dtype=np.float32) / np.float32(rot))).astype(np.float32)
    invs = np.concatenate([inv(32), inv(16), inv(8)]).astype(np.float32)
    c["c_inv"] = np.tile(invs[None, :], (128, 1)).astype(np.float32)
    c["c_pow2"] = np.tile((2.0 ** -np.arange(32, dtype=np.float32))[None, :], (128, 1)).astype(np.float32)
    c["c_U"] = (k <= q).astype(np.float32)
    c["c_ones"] = np.ones((128, 128), np.float32)
    return c


CONST_SHAPES = {k: v.shape for k, v in make_consts().items()}

IN_SHAPES = {
    "x": ([S_LEN, D], F32), "pos": ([128, NT], I32),
    "norm1_g": ([DEPTH, D], F32), "w_in": ([DEPTH, D, 6616], F32),
    "mla_q_norm_g": ([DEPTH, 256], F32), "mla_w_uq": ([DEPTH, 256, 384], F32),
    "mla_kv_norm_g": ([DEPTH, 128], F32), "mla_w_ukv": ([DEPTH, 128, 512], F32),
    "nsa_cmp_pe": ([DEPTH, 2, 2048], F32), "nsa_cmp_w": ([DEPTH, 2, 2048, 64], F32),
    "fox_f_bias": ([DEPTH, 4], F32), "w_branch": ([DEPTH, 4, 256, D], F32),
    "w_out": ([DEPTH, D, D], F32), "norm2_g": ([DEPTH, D], F32),
    "w_up": ([DEPTH, D, 4096], F32), "w_down": ([DEPTH, 4096, D], F32), "final_g": ([1, D], F32),
}

MLA0 = 0
NSA0 = 416
FOX0 = NSA0 + 652
DSA0 = FOX0 + 772
GATE0 = DSA0 + 680


def build_program(depth=DEPTH, debug=None, mixers=("mla", "nsa", "fox", "dsa"), stop=None, limit=None):
    nc = bass.Bass("TRN2", target_bir_lowering=False)
    S = Sched(nc)
    S.limit = limit
    dbg = {}
    try:
        _build_body(nc, S, dbg, depth, debug, mixers, stop)
    except StopBuild:
        S.limit = None
        print("stopped at limit", limit, flush=True)
    S.wait_all("sp")
    print("program built: ops", S.nops, "waits", S.nwaits, flush=True)
    return nc, dbg


def _build_body(nc, S, dbg, depth, debug, mixers, stop):
    dr = {}
    for name, (shape, dt) in IN_SHAPES.items():
        dr[name] = nc.dram_tensor(name, shape, dt, kind="ExternalInput").ap()
    for name, shape in CONST_SHAPES.items():
        dr[name] = nc.dram_tensor(name, list(shape), F32, kind="ExternalInput").ap()
    y = nc.dram_tensor("y", [S_LEN, D], F32, kind="ExternalOutput").ap()
    xs = nc.dram_tensor("xs", [S_LEN, D], F32, kind="Internal").ap()

    BASE = 16512
    KB = 1024

    def A(name, shape, dt, off):
        assert off % 32 == 0, (name, off)
        nbytes = int(np.prod(shape[1:])) * (2 if dt == BF16 else 4)
        assert off + nbytes <= 207 * KB + 512, (name, off, nbytes)
        return nc.alloc_sbuf_tensor_at(name, list(shape), dt, offset=BASE + off)

    def dump(name, ap, shape, reads):
        if debug is None or name not in debug:
            return
        t = nc.dram_tensor("dbg_" + name, list(shape), ap.dtype if hasattr(ap, "dtype") else F32, kind="ExternalOutput").ap()
        S.dma(t, ap, reads=reads, writes=["dbg_" + name])
        dbg[name] = t

    o = 0
    def CA(name, shape, dt):
        nonlocal o
        t = A(name, shape, dt, o)
        o += ((int(np.prod(shape[1:])) * (2 if dt == BF16 else 4) + 31) // 32) * 32
        return t
    identb = CA("identb", [128, 128], BF16)
    identf = CA("identf", [128, 128], F32)
    caust = CA("caust", [128, 128], BF16)
    wint = CA("wint", [128, 128], BF16)
    causqk = CA("causqk", [128, 128], F32)
    cmpbias = CA("cmpbias", [128, S_LEN], BF16)
    Emat = CA("Emat", [32, S_LEN], BF16)
    ovl = CA("ovl", [128, 32], BF16)
    fbase = CA("fbase", [128, NT, 32], F32)
    fkeep = CA("fkeep", [128, NT, 32], F32)
    invt = CA("invt", [128, 28], F32)
    pow2 = CA("pow2", [128, 32], F32)
    Umat = CA("Umat", [128, 128], F32)
    onesf = CA("onesf", [128, 128], F32)
    trig = CA("trig", [128, NT, 56], F32)
    posi = CA("posi", [128, NT], I32)
    posf = CA("posf", [128, NT], F32)
    cst = CA("cst", [128, 8], F32)
    g2 = CA("g2", [128, D], F32)
    assert o <= 24 * KB, o
    for dst, src, q in [(identb, "c_ident", "pool"), (identf, "c_ident", "sp"), (caust, "c_caust", "pool"), (wint, "c_wint", "pool"),
                        (causqk, "c_causqk", "sp"), (cmpbias, "c_cmpbias", "pool"), (Emat, "c_E", "pool"), (ovl, "c_ovl", "pool"),
                        (fbase, "c_fbase", "sp"), (fkeep, "c_fkeep", "sp"), (invt, "c_inv", "sp"), (pow2, "c_pow2", "sp"),
                        (Umat, "c_U", "sp"), (onesf, "c_ones", "sp")]:
        S.dma(dst[:], dr[src], writes=["const"], q=q)
    S.dma(posi[:], dr["pos"], writes=["const"])
    S.op("dve", lambda e: e.memset(cst[:, 0:1], 1e-6), writes=["const"])
    S.op("dve", lambda e: e.memset(cst[:, 1:2], 1.0), writes=["const"])
    S.op("dve", lambda e: e.memset(cst[:, 2:3], 0.0), writes=["const"])

    ACT0 = 24 * KB
    actT = A("actT", [128, KD, S_LEN], BF16, ACT0)
    OFF0 = 56 * KB

    ps_st = [nc.alloc_psum_tensor("ps_st%d" % i, [128, 512], F32) for i in range(2)]
    ps_acc = [nc.alloc_psum_tensor("ps_acc%d" % i, [128, 4, 128], F32) for i in range(2)]
    ps_mm = [nc.alloc_psum_tensor("ps_mm%d" % i, [128, 512], F32) for i in range(2)]
    ps_trs = [nc.alloc_psum_tensor("ps_tr%d" % i, [128, 8, 128], BF16) for i in range(2)]
    ps_x = ps_trs[1][:].rearrange("p a b -> p (a b)").bitcast(F32)
    PX = "ps_tr1"

    with_scr = A("ropescr", [128, NT, 56], F32, OFF0)
    kfi = A("ropekfi", [128, NT, 56], I32, OFF0 + 4 * KB)
    kff = A("ropekff", [128, NT, 56], F32, OFF0 + 8 * KB)
    ang = A("ropeang", [128, NT, 56], F32, OFF0 + 12 * KB)
    S.op("dve", lambda e: e.tensor_copy(out=posf[:], in_=posi[:]), reads=["const"], writes=["posf"])
    S.op("dve", lambda e: e.tensor_tensor(out=ang[:, :, 0:28], in0=posf[:].unsqueeze(2).to_broadcast([128, NT, 28]),
                                          in1=invt[:].unsqueeze(1).to_broadcast([128, NT, 28]), op=ALU.mult), reads=["posf", "const"], writes=["ang"])
    S.op("dve", lambda e: e.tensor_scalar(out=ang[:, :, 28:56], in0=ang[:, :, 0:28], scalar1=PI / 2, scalar2=None, op0=ALU.add), reads=["ang"], writes=["ang"])
    S.op("dve", lambda e: e.tensor_scalar(out=with_scr[:], in0=ang[:], scalar1=float(1.0 / (2 * np.pi)), scalar2=None, op0=ALU.mult), reads=["ang"], writes=["rscr"])
    S.op("dve", lambda e: e.tensor_copy(out=kfi[:], in_=with_scr[:]), reads=["rscr"], writes=["kfi"])
    S.op("dve", lambda e: e.tensor_copy(out=kff[:], in_=kfi[:]), reads=["kfi"], writes=["kff"])
    C1 = 6.28125
    C2 = float(2 * np.pi - 6.28125)
    S.op("dve", lambda e: e.scalar_tensor_tensor(out=with_scr[:], in0=kff[:], scalar=-C1, in1=ang[:], op0=ALU.mult, op1=ALU.add), reads=["kff", "ang"], writes=["rscr"])
    S.op("dve", lambda e: e.scalar_tensor_tensor(out=ang[:], in0=kff[:], scalar=-C2, in1=with_scr[:], op0=ALU.mult, op1=ALU.add), reads=["kff", "rscr"], writes=["ang"])
    S.op("dve", lambda e: e.tensor_scalar(out=ang[:], in0=ang[:], scalar1=-3.1415925, scalar2=3.1415925, op0=ALU.max, op1=ALU.min), reads=["ang"], writes=["ang"])
    S.op("act", lambda e: e.activation(out=trig[:], in_=ang[:], func=AF.Sin, bias=cst[:, 2:3], scale=1.0), reads=["ang", "const"], writes=["trig"])
    dump("trig", trig[:], [128, NT, 56], ["trig"])
    SIN = lambda t, a, b: trig[:, t, a:b]
    COS = lambda t, a, b: trig[:, t, 28 + a:28 + b]
    S.barrier()
    if stop == "p0":
        print("stop", stop, S.nops)
        return

    def rmsnorm_tile(src_ap, n, g_ap, dst_ap, scr, tag, rkeys, wkeys, src_keys):
        ss, sq = scr
        S.op("act", lambda e: e.activation(out=sq, in_=src_ap, func=AF.Square, accum_out=ss[:, 0:1]), reads=src_keys, writes=[tag + "sq", tag + "ss"])
        S.op("act", lambda e: e.activation(out=ss[:, 1:2], in_=ss[:, 0:1], func=AF.Sqrt, bias=cst[:, 0:1], scale=1.0 / n), reads=[tag + "ss"], writes=[tag + "ss1"])
        S.op("dve", lambda e: e.reciprocal(out=ss[:, 2:3], in_=ss[:, 1:2]), reads=[tag + "ss1"], writes=[tag + "ss2"])
        S.op("dve", lambda e: e.scalar_tensor_tensor(out=dst_ap, in0=src_ap, scalar=ss[:, 2:3], in1=g_ap, op0=ALU.mult, op1=ALU.mult),
             reads=src_keys + [tag + "ss2"] + rkeys, writes=wkeys)

    tr_i = [0]

    def transposes(dst_ap, srcs, np_out, rkeys, wkeys, eng="act", bank=None):
        n = len(srcs)
        if bank is None:
            bank = tr_i[0] % 2
            tr_i[0] += 1
        ps_tr = ps_trs[bank]
        tk = "ps_tr%d" % bank
        for j, s_ap in enumerate(srcs):
            w = s_ap.shape[-1]
            S.op("pe", lambda e, j=j, s_ap=s_ap, w=w: e.transpose(out=ps_tr[0:w, j, :], in_=s_ap, identity=identb[:]),
                 reads=rkeys + ["const"], writes=[tk], sig=(j == n - 1))
        if eng == "act":
            S.op("act", lambda e: e.copy(out=dst_ap, in_=ps_tr[0:np_out, 0:n, :]), reads=[tk], writes=wkeys)
        else:
            S.op("dve", lambda e: e.tensor_copy(out=dst_ap, in_=ps_tr[0:np_out, 0:n, :]), reads=[tk], writes=wkeys)

    def rope(dst1, dst2, x1, x2, cos, sin, shape, tA, tB, rkeys, wkeys, tag):
        cb = cos.unsqueeze(1).to_broadcast(shape) if len(shape) == 3 else cos
        sb = sin.unsqueeze(1).to_broadcast(shape) if len(shape) == 3 else sin
        S.op("dve", lambda e: e.tensor_tensor(out=tA, in0=x1, in1=cb, op=ALU.mult), reads=rkeys + ["trig"], writes=[tag + "A"])
        S.op("dve", lambda e: e.tensor_tensor(out=tB, in0=x2, in1=sb, op=ALU.mult), reads=rkeys + ["trig"], writes=[tag + "B"])
        S.op("dve", lambda e: e.tensor_tensor(out=dst1, in0=tA, in1=tB, op=ALU.subtract), reads=[tag + "A", tag + "B"], writes=wkeys)
        S.op("dve", lambda e: e.tensor_tensor(out=tA, in0=x2, in1=cb, op=ALU.mult), reads=rkeys + ["trig"], writes=[tag + "A"])
        S.op("dve", lambda e: e.tensor_tensor(out=tB, in0=x1, in1=sb, op=ALU.mult), reads=rkeys + ["trig"], writes=[tag + "B"])
        S.op("dve", lambda e: e.tensor_tensor(out=dst2, in0=tA, in1=tB, op=ALU.add), reads=[tag + "A", tag + "B"], writes=wkeys)

    def load_w_cols(dst, l, segs, wkey):
        for (dc, sc, n) in segs:
            src = dr["w_in"][l, :, sc:sc + n].rearrange("(k p) c -> p k c", p=128)
            for kh in range(2):
                S.dma(dst[:, kh * 4:(kh + 1) * 4, dc:dc + n], src[:, kh * 4:(kh + 1) * 4, :], writes=[wkey], q="pool")

    def project_tile(t, Wz, col_groups, wkey):
        outs = []
        for gi, (c0, n) in enumerate(col_groups):
            if t % 2 == 0:
                pt, pk = ps_mm[gi], "ps_mm%d" % gi
            else:
                pt, pk = ps_st[gi], "ps_st%d" % gi
            for k in range(KD):
                S.op("pe", lambda e, k=k, pt=pt, c0=c0, n=n: e.matmul(pt[:, 0:n], lhsT=actT[:, k, t * 128:(t + 1) * 128], rhs=Wz[:, k, c0:c0 + n],
                                                                     start=(k == 0), stop=(k == KD - 1)),
                     reads=["actT", wkey], writes=[pk], sig=(k == KD - 1))
            outs.append((pt, pk))
        return outs

    def warm(n, bank=0):
        if not NDUMMY:
            return
        dout = ps_acc[bank][:].rearrange("p a b -> p (a b)")
        for _ in range(n):
            S.op("pe", lambda e: e.matmul(dout, lhsT=identb[:], rhs=cmpbias[:, 0:512], start=True, stop=True, skip_group_check=True), sig=False)

    PT = [A("PT%d" % i, [128, 512], BF16, OFF0 + 32 * KB + 13 * KB + i * KB) for i in range(4)]
    misc = A("misc", [128, 512], F32, OFF0 + 32 * KB + 17 * KB)
    WS0 = OFF0 + 32 * KB + 19 * KB
    st_ctr = [0]

    def attn_core(tag, qT, kT, vaug, scale, steps_fn, bias_fn, fin_fn, rkeys, nk=128, vw=65, range_bias_fn=None, ndummy=0, dummy_out=None):
        steps = []
        for c in range(4):
            ss = steps_fn(c)
            contrib = {}
            for (kt, qlo, qhi) in ss:
                for qt in range(qlo, qhi):
                    contrib.setdefault(qt, []).append(kt)
            for si, (kt, qlo, qhi) in enumerate(ss):
                steps.append((c, kt, qlo, qhi, contrib, si == len(ss) - 1, si == 0))

        def issue_st(step, slot):
            c, kt, qlo, qhi, contrib, last, _f = step
            st = ps_st[slot % 2]
            skey = "ps_st%d" % (slot % 2)
            n = (qhi - qlo) * 128
            bl = []
            for qt in range(qlo, qhi):
                for (bl_l, bl_r, bkeys) in bias_fn(kt, qt):
                    bl.append(((qt - qlo) * 128, (qt - qlo + 1) * 128, bl_l, bl_r, bkeys))
            if range_bias_fn is not None:
                for (bl_l, bl_r, bkeys) in range_bias_fn(kt, qlo, qhi):
                    bl.append((0, n, bl_l, bl_r, bkeys))
            S.op("pe", lambda e: e.matmul(st[0:nk, 0:n], lhsT=kT(kt), rhs=qT(qlo * 128, qhi * 128), start=True, stop=(len(bl) == 0), skip_group_check=True),
                 reads=rkeys, writes=[skey], sig=(not bl))
            for bi, (c0, c1, bl_l, bl_r, bkeys) in enumerate(bl):
                S.op("pe", lambda e, c0=c0, c1=c1, bl_l=bl_l, bl_r=bl_r, bi=bi: e.matmul(st[0:nk, c0:c1], lhsT=bl_l, rhs=bl_r, start=False, stop=(bi == len(bl) - 1), skip_group_check=True),
                     reads=bkeys, writes=[skey], sig=(bi == len(bl) - 1))
            for _ in range(ndummy):
                dout = ps_mm[0][:, 0:512] if dummy_out is None else dummy_out
                S.op("pe", lambda e: e.matmul(dout, lhsT=identb[:], rhs=cmpbias[:, 0:512], start=True, stop=True, skip_group_check=True), sig=False)

        def issue_rest(step, slot):
            c, kt, qlo, qhi, contrib, last, _f = step
            st = ps_st[slot % 2]
            skey = "ps_st%d" % (slot % 2)
            n = (qhi - qlo) * 128
            pi = slot % 4
            pt = PT[pi]
            S.op("act", lambda e: e.activation(out=pt[0:nk, 0:n], in_=st[0:nk, 0:n], func=AF.Exp, scale=scale), reads=[skey], writes=["PT%d" % pi])
            acc = ps_acc[c % 2]
            for qt in range(qlo, qhi):
                first = (step[6] and qt == qlo)
                S.op("pe", lambda e, qt=qt, first=first: e.matmul(acc[:, qt - 4 * c, 0:vw], lhsT=pt[0:nk, (qt - qlo) * 128:(qt - qlo + 1) * 128], rhs=vaug(kt),
                                                       start=first, stop=(kt == contrib[qt][-1]), skip_group_check=True),
                     reads=["PT%d" % pi] + rkeys, writes=["ps_acc%d" % (c % 2)], sig=(qt == qhi - 1))
            if last:
                fin_fn(c, acc, "ps_acc%d" % (c % 2))

        base = st_ctr[0]
        for i, step in enumerate(steps):
            if i == 0:
                issue_st(step, base)
            if i + 1 < len(steps):
                issue_st(steps[i + 1], base + i + 1)
            issue_rest(step, base + i)
        st_ctr[0] = base + len(steps)

    def causal_steps(c):
        return [(kt, max(4 * c, kt), 4 * c + 4) for kt in range(4 * c + 4)]

    def causal_bias(kt, qt):
        if kt == qt:
            return [(identb[:], caust[:], ["const"])]
        return []

    def std_fin(o_tm, h, okey, gate=None, accumulate=False, vdim=64):
        def fin(c, acc, akey):
            rec = misc[:, 64:68]
            S.op("dve", lambda e: e.tensor_scalar(out=rec, in0=acc[:, :, vdim], scalar1=1e-30, scalar2=None, op0=ALU.max), reads=[akey], writes=["rec"])
            S.op("dve", lambda e: e.reciprocal(out=misc[:, 68:72], in_=rec), reads=["rec"], writes=["rec2"])
            r2 = misc[:, 68:72]
            rk = ["rec2"]
            if gate is not None:
                gap, gkey = gate(c)
                S.op("dve", lambda e: e.tensor_tensor(out=misc[:, 72:76], in0=r2, in1=gap, op=ALU.mult), reads=["rec2", gkey], writes=["rec3"])
                r2 = misc[:, 72:76]
                rk = ["rec3"]
            dst = o_tm[:, 4 * c:4 * c + 4, h * 64:(h + 1) * 64]
            if not accumulate:
                S.op("dve", lambda e: e.tensor_tensor(out=dst, in0=acc[:, :, 0:64], in1=r2.unsqueeze(2).to_broadcast([128, 4, 64]), op=ALU.mult),
                     reads=[akey] + rk, writes=[okey])
            else:
                tmp = misc[:, 128:384].rearrange("p (a b) -> p a b", a=4)
                S.op("dve", lambda e: e.tensor_tensor(out=tmp, in0=acc[:, :, 0:64], in1=r2.unsqueeze(2).to_broadcast([128, 4, 64]), op=ALU.mult),
                     reads=[akey] + rk, writes=["fintmp"])
                S.op("pool", lambda e: e.tensor_tensor(out=dst, in0=dst, in1=tmp, op=ALU.add), reads=["fintmp", okey], writes=[okey])
        return fin

    oT = A("oT", [128, 4, 2, S_LEN], BF16, OFF0)

    def finish_mixer(mi, o_tm, okey):
        for t in range(NT):
            transposes(oT[:, mi, :, t * 128:(t + 1) * 128], [o_tm[:, t, 0:128], o_tm[:, t, 128:256]], 128, [okey], ["oT"], eng=("act" if t % 2 else "dve"))

    for l in range(depth):
        x_src = dr["x"] if l == 0 else xs
        gbc = A("gbc", [128, D], F32, OFF0)
        xt = [A("xt%d" % i, [128, D], F32, OFF0 + 4 * KB + i * 4 * KB) for i in range(4)]
        hb = [A("hb%d" % i, [128, D], BF16, OFF0 + 20 * KB + i * 2 * KB) for i in range(3)]
        sqs_ = [A("sqs%d" % i, [128, D], BF16, OFF0 + 26 * KB + i * 2 * KB) for i in range(2)]
        nst_ = [A("nst%d" % i, [128, 8], F32, OFF0 + 30 * KB + i * 32) for i in range(2)]
        S.dma(gbc[:], dr["norm1_g"][l:l + 1, :].to_broadcast([128, D]), writes=["gbc"])

        def p1_load(t):
            S.dma(xt[t % 4][:], x_src[t * 128:(t + 1) * 128, :], writes=["xt%d" % (t % 4)])

        def p1_norm(t):
            rmsnorm_tile(xt[t % 4][:], D, gbc[:], hb[t % 3][:], (nst_[t % 2], sqs_[t % 2][:]), "n1_%d" % (t % 2), ["gbc"], ["hb%d" % (t % 3)], ["xt%d" % (t % 4)])

        def p1_tr(t):
            transposes(actT[:, :, t * 128:(t + 1) * 128], [hb[t % 3][:, k * 128:(k + 1) * 128] for k in range(KD)], 128, ["hb%d" % (t % 3)], ["actT"], eng=("act" if t % 2 else "dve"))
        p1_load(0)
        p1_load(1)
        for step in range(NT + 1):
            if step + 2 < NT:
                p1_load(step + 2)
            if step < NT:
                p1_norm(step)
            if step >= 1:
                p1_tr(step - 1)
        if l == 0:
            dump("hT", actT[:], [128, KD, S_LEN], ["actT"])

        if stop == "p1":
            print("stop", stop, S.nops)
            return
        Wz = A("Wz", [128, KD, 776], BF16, OFF0 + 32 * KB)

        if "mla" in mixers:
            o = WS0
            cqnT = A("cqnT", [128, 2, S_LEN], BF16, o); o += 8 * KB
            ckvnT = A("ckvnT", [128, S_LEN], BF16, o); o += 4 * KB
            wuq = A("wuq", [128, 2, 384], BF16, o); o += 1536
            wukv = A("wukv", [128, 512], BF16, o); o += 1024
            QT = A("mQT", [96, 4, S_LEN], BF16, o); o += 16 * KB
            KTt = A("mKT", [96, 4, S_LEN], BF16, o); o += 16 * KB
            Vg = A("mV", [128, NT, 4, 66], BF16, o); o += NT * 4 * 66 * 2
            o = (o + 31) // 32 * 32
            qtm_ = [A("mqtm%d" % i, [128, 4, 96], BF16, o + i * 768) for i in range(2)]; o += 1536
            ktm_ = [A("mktm%d" % i, [128, 4, 96], BF16, o + i * 768) for i in range(3)]; o += 2304
            cqn_ = [A("mcqn%d" % i, [128, 256], BF16, o + i * 512) for i in range(2)]; o += 1024
            ckvn_ = [A("mckvn%d" % i, [128, 128], BF16, o + i * 256) for i in range(2)]; o += 512
            gq = A("mgq", [128, 256], F32, o); o += 1024
            gkv = A("mgkv", [128, 128], F32, o); o += 512
            rA_ = [A("mrA%d" % i, [128, 64], F32, o + i * 256) for i in range(2)]; o += 512
            rB_ = [A("mrB%d" % i, [128, 64], F32, o + i * 256) for i in range(2)]; o += 512
            sq2_ = [A("msq%d" % i, [128, 256], F32, o + i * 1024) for i in range(2)]; o += 2048
            nst2_ = [A("mnst%d" % i, [128, 8], F32, o + i * 32) for i in range(2)]; o += 64
            o_tm = A("mo_tm", [128, NT, 256], BF16, o); o += 8 * KB
            load_w_cols(Wz, l, [(0, MLA0, 416)], "Wz")
            S.dma(wuq[:], dr["mla_w_uq"][l].rearrange("(k p) c -> p k c", p=128), writes=["wuq"], q="pool")
            S.dma(wukv[:], dr["mla_w_ukv"][l], writes=["wukv"], q="pool")
            S.dma(gq[:], dr["mla_q_norm_g"][l:l + 1, :].to_broadcast([128, 256]), writes=["gq"])
            S.dma(gkv[:], dr["mla_kv_norm_g"][l:l + 1, :].to_broadcast([128, 128]), writes=["gkv"])
            S.op("pool", lambda e: e.memset(Vg[:, :, :, 64:65], 1.0), writes=["mV"])
            zs_ = [A("mzs%d" % i, [128, 416], F32, o + i * 1664) for i in range(2)]; o += 3328
            qs_ = [A("mqs%d" % i, [128, 384], F32, o + i * 1536) for i in range(2)]; o += 3072
            kvs_ = [A("mkvs%d" % i, [128, 512], F32, o + i * 2048) for i in range(2)]; o += 4096

            def mla_vars(t):
                p = t % 2
                return p, str(p), qtm_[p], ktm_[t % 3], cqn_[p], ckvn_[p], rA_[p], rB_[p], sq2_[p], nst2_[p]

            def mla_A1(t):
                p, sp, qtm, ktm, cqn, ckvn, rA, rB, sq2, nst2 = mla_vars(t)
                zs = zs_[p]
                zk = "mzs" + sp
                ((pz, pzk),) = project_tile(t, Wz, [(0, 416)], "Wz")
                S.op("act", lambda e: e.copy(out=zs[:], in_=pz[:, 0:416]), reads=[pzk], writes=[zk])

            def mla_A1b(t):
                p, sp, qtm, ktm, cqn, ckvn, rA, rB, sq2, nst2 = mla_vars(t)
                zs = zs_[p]
                zk = "mzs" + sp
                rmsnorm_tile(zs[:, 0:256], 256, gq[:], cqn[:], (nst2, sq2[:]), "mq" + sp, ["gq"], ["cqn" + sp], [zk])
                rmsnorm_tile(zs[:, 256:384], 128, gkv[:], ckvn[:], (nst2, sq2[:, 0:128]), "mq" + sp, ["gkv"], ["ckvn" + sp], [zk])
                rope(ktm[:, 0, 64:80], ktm[:, 0, 80:96], zs[:, 384:400], zs[:, 400:416], COS(t, 0, 16), SIN(t, 0, 16), [128, 16],
                     rA[:, 0:16], rB[:, 0:16], [zk], ["ktm%d" % (t % 3)], "mr" + sp)
                S.op("pool", lambda e: e.tensor_copy(out=ktm[:, 1:4, 64:96], in_=ktm[:, 0:1, 64:96].to_broadcast([128, 3, 32])), reads=["ktm%d" % (t % 3)], writes=["ktm%d" % (t % 3)])

            def mla_A2(t):
                p, sp, qtm, ktm, cqn, ckvn, rA, rB, sq2, nst2 = mla_vars(t)
                transposes(cqnT[:, :, t * 128:(t + 1) * 128], [cqn[:, 0:128], cqn[:, 128:256]], 128, ["cqn" + sp], ["cqnT%d" % p])
                transposes(ckvnT[:, t * 128:(t + 1) * 128].unsqueeze(1), [ckvn[:]], 128, ["ckvn" + sp], ["ckvnT%d" % p])
                pq, pqk = (ps_mm[1], "ps_mm1") if p == 0 else (ps_st[1], "ps_st1")
                for kk in range(2):
                    S.op("pe", lambda e, kk=kk: e.matmul(pq[:, 0:384], lhsT=cqnT[:, kk, t * 128:(t + 1) * 128], rhs=wuq[:, kk, :], start=(kk == 0), stop=(kk == 1)),
                         reads=["cqnT%d" % p, "wuq"], writes=[pqk], sig=(kk == 1))
                S.op("act", lambda e: e.copy(out=qs_[p][:], in_=pq[:, 0:384]), reads=[pqk], writes=["mqs" + sp])
                pkv = ps_acc[p][:].rearrange("p a b -> p (a b)")
                pkk = "ps_acc%d" % p
                S.op("pe", lambda e: e.matmul(pkv[:, 0:512], lhsT=ckvnT[:, t * 128:(t + 1) * 128], rhs=wukv[:], start=True, stop=True),
                     reads=["ckvnT%d" % p, "wukv"], writes=[pkk])
                S.op("act", lambda e: e.copy(out=kvs_[p][:], in_=pkv[:, 0:512]), reads=[pkk], writes=["mkvs" + sp])

            def mla_B(t):
                p, sp, qtm, ktm, cqn, ckvn, rA, rB, sq2, nst2 = mla_vars(t)
                pq3 = qs_[p][:].rearrange("p (h d) -> p h d", h=4)
                pqk = "mqs" + sp
                S.op("act", lambda e: e.copy(out=qtm[:, :, 0:64], in_=pq3[:, :, 0:64]), reads=[pqk], writes=["qtmN" + sp])
                rope(qtm[:, :, 64:80], qtm[:, :, 80:96], pq3[:, :, 64:80], pq3[:, :, 80:96], COS(t, 0, 16), SIN(t, 0, 16), [128, 4, 16],
                     rA[:].rearrange("p (h d) -> p h d", h=4), rB[:].rearrange("p (h d) -> p h d", h=4), [pqk], ["qtm" + sp], "mr" + sp)
                transposes(QT[:, :, t * 128:(t + 1) * 128], [qtm[:, h, :] for h in range(4)], 96, ["qtm" + sp, "qtmN" + sp], ["mQT"], eng="act")
                pkv3 = kvs_[p][:].rearrange("p (h d) -> p h d", h=4)
                pkk = "mkvs" + sp
                S.op("dve", lambda e: e.tensor_copy(out=ktm[:, :, 0:64], in_=pkv3[:, :, 0:64]), reads=[pkk], writes=["ktmN%d" % (t % 3)])
                S.op("dve", lambda e: e.tensor_copy(out=Vg[:, t, :, 0:64], in_=pkv3[:, :, 64:128]), reads=[pkk], writes=["mV"])
                transposes(KTt[:, :, t * 128:(t + 1) * 128], [ktm[:, h, :] for h in range(4)], 96, ["ktm%d" % (t % 3), "ktmN%d" % (t % 3)], ["mKT"])
            for step in range(NT + 2):
                if step < NT:
                    mla_A1(step)
                if 1 <= step <= NT:
                    mla_A2(step - 1)
                if step >= 2:
                    mla_B(step - 2)
                if step < NT:
                    mla_A1b(step)
            if "fox" in mixers:
                load_w_cols(Wz, l, [(0, FOX0, 772)], "Wz")
                wz_prefetched = "fox"
            if stop == "mla_prep":
                dump("trig", QT[:], [96, 4, S_LEN], ["mQT"]) if False else None
                print("stop", stop, S.nops)
                return
            for h in range(4):
                attn_core("mla", lambda q0, q1, h=h: QT[:, h, q0:q1], lambda kt, h=h: KTt[:, h, kt * 128:(kt + 1) * 128],
                          lambda kt, h=h: Vg[:, kt, h, 0:65], float(96 ** -0.5), causal_steps, causal_bias,
                          std_fin(o_tm, h, "mo_tm"), ["mQT", "mKT", "mV"], ndummy=NDUMMY)
            if l == 0:
                dump("o_mla", o_tm[:], [128, NT, 256], ["mo_tm"])
            finish_mixer(0, o_tm, "mo_tm")
            S.barrier()

        if "fox" in mixers:
            o = WS0
            QT = A("fQT", [70, 4, S_LEN], BF16, o); o += 16 * KB
            KTt = A("fKT", [70, 4, S_LEN], BF16, o); o += 16 * KB
            Vg = A("fV", [128, NT, 4, 66], BF16, o); o += NT * 4 * 66 * 2
            o = (o + 31) // 32 * 32
            qk_tm = A("fqk_tm", [128, NT, 2, 4, 70], BF16, o); o += NT * 2 * 4 * 70 * 2
            o = (o + 31) // 32 * 32
            logf = A("flogf", [128, NT, 4], F32, o); o += 256
            cum = A("fcum", [128, NT, 4], F32, o); o += 256
            tot = A("ftot", [128, NT, 4], F32, o); o += 256
            car = A("fcar", [128, NT, 4], F32, o); o += 256
            fb = A("ffb", [128, 4], F32, o); o += 32
            ftmp = A("fftmp", [128, NT, 4], F32, o); o += 256
            chi = A("fchi", [128, NT, 4], BF16, o); o += 128
            cmid = A("fcmid", [128, NT, 4], BF16, o); o += 128
            clo = A("fclo", [128, NT, 4], BF16, o); o += 128
            r1 = A("fr1", [128, NT, 4], F32, o); o += 256
            r2_ = A("fr2", [128, NT, 4], F32, o); o += 256
            o_tm = A("fo_tm", [128, NT, 256], BF16, o); o += 8 * KB
            if not ("mla" in mixers):
                load_w_cols(Wz, l, [(0, FOX0, 772)], "Wz")
            S.dma(fb[:], dr["fox_f_bias"][l:l + 1, :].to_broadcast([128, 4]), writes=["ffb"])
            S.op("pool", lambda e: e.memset(Vg[:, :, :, 64:65], 1.0), writes=["fV"])
            S.op("pool", lambda e: e.memset(qk_tm[:, :, 0, :, 67:70], 1.0), writes=["fqk_tm"])
            S.op("pool", lambda e: e.memset(qk_tm[:, :, 1, :, 64:67], 1.0), writes=["fqk_tm"])
            for t in range(NT):
                (pa, pak), (pb, pbk) = project_tile(t, Wz, [(0, 512), (512, 260)], "Wz")
                warm(NWARM)
                pa4 = pa[:, 0:512].rearrange("p (a h d) -> p a h d", a=2, h=4)
                S.op("act", lambda e: e.copy(out=qk_tm[:, t, :, :, 0:64], in_=pa4), reads=[pak], writes=["fqk_tm"])
                S.op("dve", lambda e: e.tensor_copy(out=Vg[:, t, :, 0:64], in_=pb[:, 0:256].rearrange("p (h d) -> p h d", h=4)), reads=[pbk], writes=["fV"])
                S.op("dve", lambda e: e.tensor_tensor(out=ftmp[:, t, :], in0=pb[:, 256:260], in1=fb[:], op=ALU.add), reads=[pbk, "ffb"], writes=["fftmp"])
            if "dsa" in mixers:
                load_w_cols(Wz, l, [(0, DSA0, 680)], "Wz")
            S.op("act", lambda e: e.activation(out=logf[:], in_=ftmp[:], func=AF.Exp, scale=-1.0), reads=["fftmp"], writes=["flogf"])
            S.op("act", lambda e: e.activation(out=logf[:], in_=logf[:], func=AF.Ln, bias=cst[:, 1:2], scale=1.0), reads=["flogf", "const"], writes=["flogf"])
            S.op("dve", lambda e: e.tensor_scalar(out=logf[:], in0=logf[:], scalar1=-1.0, scalar2=None, op0=ALU.mult), reads=["flogf"], writes=["flogf"])
            pc = ps_x
            S.op("pe", lambda e: e.matmul(pc[:, 0:64], lhsT=Umat[:], rhs=logf[:].rearrange("p a b -> p (a b)"), start=True, stop=True), reads=["flogf", "const"], writes=[PX])
            S.op("dve", lambda e: e.tensor_copy(out=cum[:].rearrange("p a b -> p (a b)"), in_=pc[:, 0:64]), reads=[PX], writes=["fcum"])
            S.op("pe", lambda e: e.matmul(pc[:, 0:64], lhsT=onesf[:], rhs=logf[:].rearrange("p a b -> p (a b)"), start=True, stop=True), reads=["flogf", "const", "fcum"], writes=[PX])
            S.op("dve", lambda e: e.tensor_copy(out=tot[:].rearrange("p a b -> p (a b)"), in_=pc[:, 0:64]), reads=[PX], writes=["ftot"])
            S.op("dve", lambda e: e.memset(car[:, 0, :], 0.0), writes=["fcar"])
            for t in range(1, NT):
                S.op("dve", lambda e, t=t: e.tensor_tensor(out=car[:, t, :], in0=car[:, t - 1, :], in1=tot[:, t - 1, :], op=ALU.add), reads=["fcar", "ftot"], writes=["fcar"])
            S.op("dve", lambda e: e.tensor_tensor(out=cum[:], in0=cum[:], in1=car[:], op=ALU.add), reads=["fcum", "fcar"], writes=["fcum"])
            if l == 0:
                dump("fox_c", cum[:], [128, NT, 4], ["fcum"])
            S.op("dve", lambda e: e.tensor_scalar(out=r1[:], in0=cum[:], scalar1=8.0, scalar2=None, op0=ALU.mult), reads=["fcum"], writes=["fr1"])
            S.op("dve", lambda e: e.tensor_copy(out=chi[:], in_=r1[:]), reads=["fr1"], writes=["fchi"])
            S.op("dve", lambda e: e.tensor_tensor(out=r2_[:], in0=r1[:], in1=chi[:], op=ALU.subtract), reads=["fr1", "fchi"], writes=["fr2"])
            S.op("dve", lambda e: e.tensor_copy(out=cmid[:], in_=r2_[:]), reads=["fr2"], writes=["fcmid"])
            S.op("dve", lambda e: e.tensor_tensor(out=r1[:], in0=r2_[:], in1=cmid[:], op=ALU.subtract), reads=["fr2", "fcmid"], writes=["fr1"])
            S.op("dve", lambda e: e.tensor_copy(out=clo[:], in_=r1[:]), reads=["fr1"], writes=["fclo"])
            for j, part in enumerate([chi, cmid, clo]):
                S.op("dve", lambda e, j=j, part=part: e.tensor_copy(out=qk_tm[:, :, 0, :, 64 + j], in_=part[:]), reads=["fchi", "fcmid", "fclo"], writes=["fqk_tm"])
                S.op("dve", lambda e, j=j, part=part: e.tensor_scalar(out=qk_tm[:, :, 1, :, 67 + j], in0=part[:], scalar1=-1.0, scalar2=None, op0=ALU.mult),
                     reads=["fchi", "fcmid", "fclo"], writes=["fqk_tm"])
            for t in range(NT):
                transposes(QT[:, :, t * 128:(t + 1) * 128], [qk_tm[:, t, 0, h, :] for h in range(4)], 70, ["fqk_tm"], ["fQT"], eng="act")
                transposes(KTt[:, :, t * 128:(t + 1) * 128], [qk_tm[:, t, 1, h, :] for h in range(4)], 70, ["fqk_tm"], ["fKT"])
            for h in range(4):
                attn_core("fox", lambda q0, q1, h=h: QT[:, h, q0:q1], lambda kt, h=h: KTt[:, h, kt * 128:(kt + 1) * 128],
                          lambda kt, h=h: Vg[:, kt, h, 0:65], 0.125, causal_steps, causal_bias,
                          std_fin(o_tm, h, "fo_tm"), ["fQT", "fKT", "fV"], ndummy=NDUMMY)
            if l == 0:
                dump("o_fox", o_tm[:], [128, NT, 256], ["fo_tm"])
            finish_mixer(2, o_tm, "fo_tm")
            S.barrier()

        if "dsa" in mixers:
            o = WS0
            QK3 = A("dQK3", [128, 3, S_LEN], BF16, o); o += 12 * KB
            QT2 = QK3[:, 0:2, :]
            KT2 = QK3[:, 2, :]
            Vg = A("dV", [128, NT, 66], BF16, o); o += NT * 66 * 2
            o = (o + 31) // 32 * 32
            qki = A("dqki", [96, 4, S_LEN], BF16, o); o += 16 * KB
            qiT = qki[:, 0:3, :]
            kiT = qki[:, 3, :]
            score = [A("dscore%d" % i, [128, S_LEN], F32, o + i * 8 * KB) for i in range(4)]; o += 32 * KB
            Mb = [A("dMb0", [128, 4, 1536], BF16, o), A("dMb1", [128, 4, S_LEN], BF16, o + 12 * KB)]; o += 28 * KB
            o_c = [A("do_c%d" % i, [128, 4, 256], BF16, o + i * 2 * KB) for i in range(2)]; o += 4 * KB
            qk6 = A("dqk6", [128, 6, 64], BF16, o); o += 768
            qi9 = A("dqi9", [128, 12, 32], BF16, o); o += 768
            wq = A("dwq", [128, NT, 8], F32, o); o += 512
            rA = A("drA", [128, 9, 8], F32, o); o += 288
            rB = A("drB", [128, 9, 8], F32, o); o += 288
            bs = A("dbs", [128, 2, 64], F32, o); o += 512
            ki3 = A("dki3", [128, 96], BF16, o); o += 192
            junk1 = A("djunk1", [128, 16], BF16, o); o += 32
            wzo = OFF0 + 32 * KB
            Rb = [A("dR%d" % i, [128, 512], BF16, wzo + i * KB) for i in range(4)]
            diag = [A("ddiag%d" % i, [128, 8, 128], BF16, wzo + 4 * KB + i * 2 * KB) for i in range(2)]
            Rall = A("dRall", [128, S_LEN], BF16, wzo)
            if not ("fox" in mixers):
                load_w_cols(Wz, l, [(0, DSA0, 680)], "Wz")
            S.op("pool", lambda e: e.memset(Vg[:, :, 64:65], 1.0), writes=["dV"])
            S.op("pool", lambda e: e.memset(qi9[:], 0.0), writes=["dqi90", "dqi9N0"])
            qk6_ = [qk6, A("dqk6b", [128, 6, 64], BF16, o)]; o += 768
            qi9_ = [qi9, A("dqi9b", [128, 12, 32], BF16, o)]; o += 768
            ki3_ = [ki3, A("dki3b", [128, 96], BF16, o)]; o += 192
            rA_ = [rA, A("drAb", [128, 9, 8], F32, o)]; o += 288
            rB_ = [rB, A("drBb", [128, 9, 8], F32, o)]; o += 288
            S.op("pool", lambda e: e.memset(qi9_[1][:], 0.0), writes=["dqi91", "dqi9N1"])
            ptr0 = OFF0 + 32 * KB + 13 * KB
            zsA_ = [A("dzsA%d" % i, [128, 384], F32, ptr0 + i * 1536) for i in range(2)]
            zsB_ = [A("dzsB%d" % i, [128, 296], F32, ptr0 + 3072 + i * 1184) for i in range(2)]

            def dsa_A(t):
                p = t % 2
                sp = str(p)
                qk6, qi9, ki3, rA, rB = qk6_[p], qi9_[p], ki3_[p], rA_[p], rB_[p]
                (pa, pak0), (pb, pbk0) = project_tile(t, Wz, [(0, 384), (384, 296)], "Wz")
                warm(NWARM)
                za, zb = zsA_[p], zsB_[p]
                pak, pbk = "dzsA" + sp, "dzsB" + sp
                S.op("act", lambda e: e.copy(out=za[:], in_=pa[:, 0:384]), reads=[pak0], writes=[pak])
                S.op("act", lambda e: e.copy(out=zb[:], in_=pb[:, 0:296]), reads=[pbk0], writes=[pbk])
                pa3 = za[:, 0:320].rearrange("p (h d) -> p h d", h=5)
                rope(qk6[:, 0:5, 0:8], qk6[:, 0:5, 8:16], pa3[:, :, 0:8], pa3[:, :, 8:16], COS(t, 16, 24), SIN(t, 16, 24), [128, 5, 8],
                     rA[:, 0:5, :], rB[:, 0:5, :], [pak], ["dqk6" + sp], "dr" + sp)
                S.op("act", lambda e: e.copy(out=qk6[:, 0:5, 16:64], in_=pa3[:, :, 16:64]), reads=[pak], writes=["dqk6N" + sp])
                S.op("dve", lambda e: e.tensor_copy(out=Vg[:, t, 0:64], in_=za[:, 320:384]), reads=[pak], writes=["dV"])
                S.op("pool", lambda e: e.tensor_copy(out=qk6[:, 5, :], in_=qk6[:, 4, :]), reads=["dqk6" + sp, "dqk6N" + sp], writes=["dqk6D" + sp])
                pb3 = zb[:, 0:288].rearrange("p (h d) -> p h d", h=9)
                rope(qi9[:, 0:9, 0:4], qi9[:, 0:9, 4:8], pb3[:, :, 0:4], pb3[:, :, 4:8], COS(t, 24, 28), SIN(t, 24, 28), [128, 9, 4],
                     rA[:, :, 0:4], rB[:, :, 0:4], [pbk], ["dqi9" + sp], "dr" + sp)
                S.op("act", lambda e: e.copy(out=qi9[:, 0:9, 8:32], in_=pb3[:, :, 8:32]), reads=[pbk], writes=["dqi9N" + sp])
                S.op("dve", lambda e: e.tensor_copy(out=wq[:, t, :], in_=zb[:, 288:296]), reads=[pbk], writes=["dwq"])
                S.op("pool", lambda e: e.tensor_copy(out=ki3[:].rearrange("p (a b) -> p a b", a=3), in_=qi9[:, 8:9, :].to_broadcast([128, 3, 32])), reads=["dqi9" + sp, "dqi9N" + sp], writes=["dki3" + sp])

            def dsa_B(t):
                p = t % 2
                sp = str(p)
                qk6, qi9, ki3, rA, rB = qk6_[p], qi9_[p], ki3_[p], rA_[p], rB_[p]
                qf = qk6[:].rearrange("p a b -> p (a b)")
                transposes(QK3[:, :, t * 128:(t + 1) * 128], [qf[:, 0:128], qf[:, 128:256], qf[:, 256:384]], 128, ["dqk6" + sp, "dqk6N" + sp, "dqk6D" + sp], ["dQT", "dKT"], eng="act")
                qflat = qi9[:].rearrange("p a b -> p (a b)")
                transposes(qki[:, :, t * 128:(t + 1) * 128], [qflat[:, 0:96], qflat[:, 96:192], qflat[:, 192:288], ki3[:]], 96,
                           ["dqi9" + sp, "dqi9N" + sp, "dki3" + sp], ["dqiT", "dkiT"], eng="act")
                warm(NWARM)
            for step in range(NT + 1):
                if step < NT:
                    dsa_A(step)
                if step >= 1:
                    dsa_B(step - 1)
            S.barrier()
            NIT = 21
            ddum = ps_trs[0][:].rearrange("p a b -> p (a b)").bitcast(F32)
            pairs = [(2 * i, 2 * i + 1) for i in range(1, 8)]

            def sbuf(qt):
                i = qt % 4
                return score[i], "dscore%d" % i

            def dsa_scores(pair):
                for qt in pair:
                    L = (qt + 1) * 128
                    sc, sk = sbuf(qt)
                    dg = diag[qt % 2]
                    dk = "ddiag%d" % (qt % 2)
                    S.op("dve", lambda e: e.tensor_tensor(out=dg[:], in0=identb[:].unsqueeze(1).to_broadcast([128, 8, 128]),
                                                          in1=wq[:, qt, :].unsqueeze(2).to_broadcast([128, 8, 128]), op=ALU.mult), reads=["const", "dwq"], writes=[dk])
                    nkc = (L + 511) // 512
                    for kc in range(nkc):
                        k0 = kc * 512
                        n = min(512, L - k0)

                        def logit(h):
                            g, jj = divmod(h, 3)
                            pl = ps_mm[h % 2]
                            S.op("pe", lambda e: e.matmul(pl[:, 0:n], lhsT=qiT[32 * jj:32 * jj + 32, g, qt * 128:(qt + 1) * 128],
                                                          rhs=kiT[32 * jj:32 * jj + 32, k0:k0 + n], start=True, stop=True),
                                 reads=["dqiT", "dkiT"], writes=["ps_mm%d" % (h % 2)])
                            r = Rb[h % 4]
                            S.op("act", lambda e: e.activation(out=r[:, 0:n], in_=pl[:, 0:n], func=AF.Relu), reads=["ps_mm%d" % (h % 2)], writes=["dR%d" % (h % 4)])

                        def hsum(h):
                            r = Rb[h % 4]
                            S.op("pe", lambda e: e.matmul(ps_x[:, 0:n], lhsT=dg[:, h, :], rhs=r[:, 0:n], start=(h == 0), stop=(h == 7)),
                                 reads=["dR%d" % (h % 4), dk], writes=[PX])
                        logit(0)
                        for h in range(8):
                            if h + 1 < 8:
                                logit(h + 1)
                            hsum(h)
                            if h % 2 == 1 and NDUMMY:
                                S.op("pe", lambda e: e.matmul(ddum, lhsT=identb[:], rhs=cmpbias[:, 0:512], start=True, stop=True, skip_group_check=True), sig=False)
                        S.op("act", lambda e: e.copy(out=sc[:, k0:k0 + n], in_=ps_x[:, 0:n]), reads=[PX], writes=[sk])

            def dsa_bisect(pair, pi):
                st = {}
                for j, qt in enumerate(pair):
                    L = (qt + 1) * 128
                    sc, sk = sbuf(qt)
                    b = bs[:, j, :]
                    kx = "b%d_" % j
                    S.op("dve", lambda e, b=b, sc=sc, L=L: e.tensor_reduce(out=b[:, 0:1], in_=sc[:, 0:L], axis=AX.X, op=ALU.max, apply_absolute_value=True), reads=[sk], writes=[kx + "M"])
                    S.op("pool", lambda e, sc=sc, L=L: e.tensor_tensor(out=sc[:, L - 128:L], in0=sc[:, L - 128:L], in1=causqk[:], op=ALU.add), reads=[sk, "const", kx + "M"], writes=[sk])
                    S.op("pool", lambda e, b=b: e.tensor_scalar(out=b[:, 8:8 + NIT + 1], in0=pow2[:, 0:NIT + 1], scalar1=b[:, 0:1], scalar2=None, op0=ALU.mult), reads=[kx + "M", "const"], writes=[kx + "d"])
                    S.op("pool", lambda e, b=b: e.memset(b[:, 4:5], 0.0), writes=[kx + "mid0"])
                    st[qt] = (b, kx, sc, sk, L)
                for it in range(NIT):
                    for j, qt in enumerate(pair):
                        b, kx, sc, sk, L = st[qt]
                        m = b[:, 4 + (it % 2):5 + (it % 2)]
                        nm = b[:, 4 + ((it + 1) % 2):5 + ((it + 1) % 2)]
                        mk, nmk = kx + "mid%d" % (it % 2), kx + "mid%d" % ((it + 1) % 2)
                        if pi == len(pairs) - 1 and j == 1:
                            S.op("act", lambda e, m=m, sc=sc, L=L, b=b: e.activation(out=Rall[:, 0:L], in_=sc[:, 0:L], func=AF.Sign, bias=m, scale=-1.0, accum_out=b[:, 6:7]),
                                 reads=[sk, mk], writes=[kx + "cnt", "dR0", "dR1", "dR2", "dR3"])
                            S.op("pool", lambda e, b=b, it=it, L=L: e.tensor_scalar(out=b[:, 7:8], in0=b[:, 6:7], scalar1=float(L) - 510.5, scalar2=b[:, 8 + it:9 + it], op0=ALU.is_lt, op1=ALU.mult),
                                 reads=[kx + "cnt", kx + "d"], writes=[kx + "sel"])
                        else:
                            S.op("dve", lambda e, m=m, sc=sc, L=L, b=b, j=j: e.tensor_scalar(out=junk1[:, j:j + 1].to_broadcast([128, L]), in0=sc[:, 0:L], scalar1=m, scalar2=0.0, op0=ALU.is_ge, op1=ALU.add,
                                                                                    accum_out=b[:, 6:7]), reads=[sk, mk], writes=[kx + "cnt", kx + "junk"])
                            S.op("pool", lambda e, b=b, it=it: e.tensor_scalar(out=b[:, 7:8], in0=b[:, 6:7], scalar1=255.5, scalar2=b[:, 8 + it:9 + it], op0=ALU.is_ge, op1=ALU.mult),
                                 reads=[kx + "cnt", kx + "d"], writes=[kx + "sel"])
                        S.op("pool", lambda e, b=b, it=it, m=m, nm=nm: e.tensor_scalar(out=nm, in0=b[:, 7:8], scalar1=m, scalar2=b[:, 9 + it:10 + it], op0=ALU.add, op1=ALU.subtract),
                             reads=[kx + "sel", mk, kx + "d"], writes=[nmk])
                for j, qt in enumerate(pair):
                    b, kx, sc, sk, L = st[qt]
                    c = qt // 4
                    mb = Mb[c % 2]
                    fm = b[:, 4 + (NIT % 2):5 + (NIT % 2)]
                    S.op("pool", lambda e, b=b, fm=fm: e.tensor_tensor(out=b[:, 3:4], in0=fm, in1=b[:, 8 + NIT:9 + NIT], op=ALU.subtract), reads=[kx + "mid%d" % (NIT % 2), kx + "d"], writes=[kx + "thr"])
                    S.op("dve", lambda e, b=b, sc=sc, L=L, mb=mb, qt=qt, c=c: e.tensor_scalar(out=mb[:, qt - 4 * c, 0:L], in0=sc[:, 0:L], scalar1=b[:, 3:4], scalar2=NEGB, op0=ALU.is_lt, op1=ALU.mult),
                         reads=[sk, kx + "thr"], writes=["dMb%d" % (c % 2)])

            def dsa_attn(c):
                mb = Mb[c % 2]
                mbk = "dMb%d" % (c % 2)
                oc = o_c[c % 2]
                ock = "do_c%d" % (c % 2)

                def dsa_bias(kt, qt):
                    if qt < 2:
                        return causal_bias(kt, qt)
                    return [(mb[:, qt - 4 * c, kt * 128:(kt + 1) * 128], identb[:], [mbk, "const"])]

                def fin_for(h):
                    inner = std_fin(oc, h, ock)
                    return lambda cc, acc, akey: inner(0, acc, akey)
                for h in range(4):
                    p0 = (h % 2) * 64
                    attn_core("dsa", lambda q0, q1, h=h, p0=p0: QT2[p0:p0 + 64, h // 2, q0:q1], lambda kt, p0=p0: KT2[p0:p0 + 64, kt * 128:(kt + 1) * 128],
                              lambda kt: Vg[:, kt, 0:65], 0.125, lambda cc: causal_steps(c) if cc == c else [], dsa_bias,
                              fin_for(h), ["dQT", "dKT", "dV"], ndummy=NDUMMY, dummy_out=ddum)
                for tt in range(4):
                    t = 4 * c + tt
                    transposes(oT[:, 3, :, t * 128:(t + 1) * 128], [oc[:, tt, 0:128], oc[:, tt, 128:256]], 128, [ock], ["oT"], eng=("act" if tt % 2 else "dve"), bank=1)
                if l == 0:
                    dump("o_dsa%d" % c, oc[:], [128, 4, 256], [ock])

            dsa_scores(pairs[0])
            for i, pr_ in enumerate(pairs):
                if i + 1 < len(pairs):
                    dsa_scores(pairs[i + 1])
                dsa_bisect(pr_, i)
                if pr_[1] % 4 == 3:
                    dsa_attn(pr_[1] // 4)
            S.barrier()

        if "nsa" in mixers:
            o = WS0
            QT = A("nQT", [96, 4, S_LEN], BF16, o); o += 16 * KB
            k4T = A("nk4T", [96, 4, S_LEN], BF16, o); o += 16 * KB
            kcT = k4T[:, 0, :]
            ksT = k4T[:, 1, :]
            kwT = k4T[:, 2, :]
            vcT = k4T[:, 3, :]
            Vs = A("nVs", [128, NT, 66], BF16, o); o += NT * 66 * 2
            Vw = A("nVw", [128, NT, 66], BF16, o); o += NT * 66 * 2
            o = (o + 31) // 32 * 32
            Wk = A("nWk", [64, 32, 64], BF16, o); o += 4 * KB
            Wv = A("nWv", [64, 32, 64], BF16, o); o += 4 * KB
            Wkf = A("nWkf", [128, 16, 64], BF16, o); o += 2 * KB
            Wvf = A("nWvf", [128, 16, 64], BF16, o); o += 2 * KB
            pek = A("npek", [128, 16], BF16, o); o += 32
            pev = A("npev", [128, 16], BF16, o); o += 32
            kcmpT = A("nkcmpT", [64, 128], BF16, o); o += 256
            vcx = A("nvcx", [128, 97], BF16, o); o += 224
            imp = A("nimp", [128, NT, 32], F32, o); o += 2 * KB
            blkb = A("nblkb", [128, 96], BF16, o); o += 192
            gt = A("ngt", [128, NT, 12], F32, o); o += 768
            oacc = A("noacc", [128, NT, 256], F32, o); o += 16 * KB
            o_tm = A("no_tm", [128, NT, 256], BF16, o); o += 8 * KB
            q7 = A("nq7", [128, 7, 64], BF16, o); o += 896
            vc_tm = A("nvc_tm", [128, 64], BF16, o); o += 128
            rA = A("nrA", [128, 7, 8], F32, o); o += 224
            rB = A("nrB", [128, 7, 8], F32, o); o += 224
            m8 = A("nm8", [128, 8], F32, o); o += 32
            itmp = A("nitmp", [128, 4, 32], F32, o); o += 512
            n0 = NSA0
            load_w_cols(Wz, l, [(0, n0, 320), (320, n0 + 384, 64), (384, n0 + 512, 64),
                                (448, n0 + 320, 64), (512, n0 + 448, 64), (576, n0 + 576, 64), (640, n0 + 640, 12)], "Wz")
            S.dma(Wk[:], dr["nsa_cmp_w"][l, 0].rearrange("(l d) o -> d l o", d=64), writes=["nWk"], q="pool")
            S.dma(Wv[:], dr["nsa_cmp_w"][l, 1].rearrange("(l d) o -> d l o", d=64), writes=["nWv"], q="pool")
            S.dma(Wkf[:], dr["nsa_cmp_w"][l, 0].rearrange("(j p) o -> p j o", p=128), writes=["nWkf"], q="pool")
            S.dma(Wvf[:], dr["nsa_cmp_w"][l, 1].rearrange("(j p) o -> p j o", p=128), writes=["nWvf"], q="pool")
            S.dma(pek[:], dr["nsa_cmp_pe"][l, 0].rearrange("(j p) -> p j", p=128), writes=["npek"], q="pool", allow_slow_non_contiguous=True)
            S.dma(pev[:], dr["nsa_cmp_pe"][l, 1].rearrange("(j p) -> p j", p=128), writes=["npev"], q="pool", allow_slow_non_contiguous=True)
            S.dma(ksT[64:96, :], dr["c_E"], writes=["nksT"], q="pool")
            S.op("pool", lambda e: e.memset(blkb[:], 0.0), writes=["nblkb"])
            S.op("pool", lambda e: e.memset(Vs[:, :, 64:65], 1.0), writes=["nVs"])
            S.op("pool", lambda e: e.memset(Vw[:, :, 64:65], 1.0), writes=["nVw"])
            q7_ = [q7, A("nq7b", [128, 7, 64], BF16, o)]; o += 896
            vc_tm_ = [vc_tm, A("nvc_tmb", [128, 64], BF16, o)]; o += 128
            rA_ = [rA, A("nrAb", [128, 7, 8], F32, o)]; o += 224
            rB_ = [rB, A("nrBb", [128, 7, 8], F32, o)]; o += 224
            zsA_ = [A("nzsA%d" % i, [128, 448], F32, o + i * 1792) for i in range(2)]; o += 3584
            zsB_ = [A("nzsB%d" % i, [128, 204], F32, o + i * 832) for i in range(2)]; o += 1664

            def nsa_A(t):
                p = t % 2
                sp = str(p)
                q7, vc_tm, rA, rB = q7_[p], vc_tm_[p], rA_[p], rB_[p]
                (pa, pak0), (pb, pbk0) = project_tile(t, Wz, [(0, 448), (448, 204)], "Wz")
                warm(NWARM)
                za, zb = zsA_[p], zsB_[p]
                pak, pbk = "nzsA" + sp, "nzsB" + sp
                S.op("act", lambda e: e.copy(out=za[:], in_=pa[:, 0:448]), reads=[pak0], writes=[pak])
                S.op("act", lambda e: e.copy(out=zb[:], in_=pb[:, 0:204]), reads=[pbk0], writes=[pbk])
                pa3 = za[:].rearrange("p (h d) -> p h d", h=7)
                rope(q7[:, :, 0:8], q7[:, :, 8:16], pa3[:, :, 0:8], pa3[:, :, 8:16], COS(t, 16, 24), SIN(t, 16, 24), [128, 7, 8],
                     rA[:], rB[:], [pak], ["nq7" + sp], "nr" + sp)
                S.op("act", lambda e: e.copy(out=q7[:, :, 16:64], in_=pa3[:, :, 16:64]), reads=[pak], writes=["nq7N" + sp])
                S.op("dve", lambda e: e.tensor_copy(out=vc_tm[:], in_=zb[:, 0:64]), reads=[pbk], writes=["nvc_tm" + sp])
                S.op("dve", lambda e: e.tensor_copy(out=Vs[:, t, 0:64], in_=zb[:, 64:128]), reads=[pbk], writes=["nVs"])
                S.op("dve", lambda e: e.tensor_copy(out=Vw[:, t, 0:64], in_=zb[:, 128:192]), reads=[pbk], writes=["nVw"])
                S.op("act", lambda e: e.activation(out=gt[:, t, :], in_=zb[:, 192:204], func=AF.Sigmoid), reads=[pbk], writes=["ngt"])

            def nsa_B(t):
                p = t % 2
                sp = str(p)
                q7, vc_tm, rA, rB = q7_[p], vc_tm_[p], rA_[p], rB_[p]
                transposes(QT[0:64, :, t * 128:(t + 1) * 128], [q7[:, h, :] for h in range(4)], 64, ["nq7" + sp, "nq7N" + sp], ["nQT"], eng="act")
                ts_ = slice(t * 128, (t + 1) * 128)
                transposes(k4T[0:64, :, ts_], [q7[:, 4, :], q7[:, 5, :], q7[:, 6, :], vc_tm[:]], 64, ["nq7" + sp, "nq7N" + sp, "nvc_tm" + sp],
                           ["nkcT", "nksT", "nkwT", "nvcT"], eng="act")
                warm(NWARM)
            for step in range(NT + 1):
                if step < NT:
                    nsa_A(step)
                if step >= 1:
                    nsa_B(step - 1)
            pk = ps_mm[0]
            for li in range(32):
                S.op("pe", lambda e, li=li: e.matmul(pk[0:64, 0:127], lhsT=Wk[:, li, :], rhs=kcT[0:64, li:li + 16 * 126 + 1:16], start=(li == 0), stop=False),
                     reads=["nWk", "nkcT"], writes=["ps_mm0"], sig=False)
            for j in range(16):
                S.op("pe", lambda e, j=j: e.matmul(pk[0:64, 0:127], lhsT=Wkf[:, j, :], rhs=pek[:, j:j + 1].to_broadcast([128, 127]), start=False, stop=(j == 15)),
                     reads=["nWkf", "npek"], writes=["ps_mm0"], sig=(j == 15))
            S.op("dve", lambda e: e.tensor_copy(out=kcmpT[:, 0:127], in_=pk[0:64, 0:127]), reads=["ps_mm0"], writes=["nkcmpT"])
            pv = ps_mm[1]
            for li in range(32):
                S.op("pe", lambda e, li=li: e.matmul(pv[0:127, 0:64], lhsT=vcT[0:64, li:li + 16 * 126 + 1:16], rhs=Wv[:, li, :], start=(li == 0), stop=False),
                     reads=["nWv", "nvcT"], writes=["ps_mm1"], sig=False)
            for j in range(16):
                S.op("pe", lambda e, j=j: e.matmul(pv[0:127, 0:64], lhsT=pev[:, j:j + 1].to_broadcast([128, 127]), rhs=Wvf[:, j, :], start=False, stop=(j == 15)),
                     reads=["nWvf", "npev"], writes=["ps_mm1"], sig=(j == 15))
            S.op("pool", lambda e: e.memset(vcx[:, 64:65], 1.0), writes=["nvcx"])
            S.op("dve", lambda e: e.tensor_copy(out=vcx[0:127, 0:64], in_=pv[0:127, 0:64]), reads=["ps_mm1"], writes=["nvcx"])
            S.op("pool", lambda e: e.tensor_copy(out=vcx[:, 65:97], in_=ovl[:]), reads=["const"], writes=["nvcx"])
            if l == 0:
                dump("nsa_kcmpT", kcmpT[:], [64, 128], ["nkcmpT"])
                dump("nsa_vcx", vcx[:], [128, 97], ["nvcx"])

            def gate_fn(path, h):
                return lambda c: (gt[:, 4 * c:4 * c + 4, path * 4 + h], "ngt")
            for h in range(4):
                def cmp_fin(c, acc, akey, h=h):
                    rec = misc[:, 64:68]
                    S.op("dve", lambda e: e.tensor_scalar(out=rec, in0=acc[:, :, 64], scalar1=1e-30, scalar2=None, op0=ALU.max), reads=[akey], writes=["rec"])
                    S.op("dve", lambda e: e.reciprocal(out=misc[:, 68:72], in_=rec), reads=["rec"], writes=["rec2"])
                    S.op("dve", lambda e: e.tensor_tensor(out=misc[:, 72:76], in0=misc[:, 68:72], in1=gt[:, 4 * c:4 * c + 4, h], op=ALU.mult), reads=["rec2", "ngt"], writes=["rec3"])
                    dst = oacc[:, 4 * c:4 * c + 4, h * 64:(h + 1) * 64]
                    S.op("dve", lambda e: e.tensor_tensor(out=dst, in0=acc[:, :, 0:64], in1=misc[:, 72:76].unsqueeze(2).to_broadcast([128, 4, 64]), op=ALU.mult),
                         reads=[akey, "rec3"], writes=["noacc"])
                    idst = imp[:, 4 * c:4 * c + 4, :]
                    if h == 0:
                        S.op("dve", lambda e: e.tensor_tensor(out=idst, in0=acc[:, :, 65:97], in1=misc[:, 68:72].unsqueeze(2).to_broadcast([128, 4, 32]), op=ALU.mult),
                             reads=[akey, "rec2"], writes=["nimp"])
                    else:
                        S.op("dve", lambda e: e.tensor_tensor(out=itmp[:], in0=acc[:, :, 65:97], in1=misc[:, 68:72].unsqueeze(2).to_broadcast([128, 4, 32]), op=ALU.mult),
                             reads=[akey, "rec2"], writes=["nitmp"])
                        S.op("pool", lambda e: e.tensor_tensor(out=idst, in0=idst, in1=itmp[:], op=ALU.add), reads=["nitmp", "nimp"], writes=["nimp"])
                attn_core("ncmp", lambda q0, q1, h=h: QT[0:64, h, q0:q1], lambda kt: kcmpT[:, 0:127], lambda kt: vcx[0:127, 0:97], 0.125,
                          lambda c: [(0, 4 * c, 4 * c + 4)],
                          lambda kt, qt: [],
                          cmp_fin, ["nQT", "nkcmpT", "nvcx"], nk=127, vw=97,
                          range_bias_fn=lambda kt, qlo, qhi: [(identb[0:127, 0:127], cmpbias[0:127, qlo * 128:qhi * 128], ["const"])])
            S.op("dve", lambda e: e.tensor_tensor(out=imp[:], in0=imp[:], in1=fkeep[:], op=ALU.mult), reads=["nimp", "const"], writes=["nimp"])
            S.op("dve", lambda e: e.tensor_tensor(out=imp[:], in0=imp[:], in1=fbase[:], op=ALU.add), reads=["nimp", "const"], writes=["nimp"])
            for t in range(NT):
                S.op("dve", lambda e, t=t: e.max(out=m8[:], in_=imp[:, t, :]), reads=["nimp"], writes=["nm8"])
                S.op("dve", lambda e, t=t: e.tensor_scalar(out=blkb[:, 64:96], in0=imp[:, t, :], scalar1=m8[:, 7:8], scalar2=NEGB, op0=ALU.is_lt, op1=ALU.mult), reads=["nimp", "nm8"], writes=["nblkb"])
                bank = t % 2
                ptr = ps_trs[bank]
                tk = "ps_tr%d" % bank
                S.op("pe", lambda e, ptr=ptr: e.transpose(out=ptr[0:96, 0, :], in_=blkb[:], identity=identb[:]), reads=["nblkb", "const"], writes=[tk])
                S.op("act", lambda e, ptr=ptr, t=t: e.copy(out=QT[64:96, :, t * 128:(t + 1) * 128], in_=ptr[64:96, 0:1, :].to_broadcast([32, 4, 128])), reads=[tk], writes=["nQT"])
            if l == 0:
                dump("nsa_imp", imp[:], [128, NT, 32], ["nimp"])

            def sel_bias(kt, qt):
                if kt == qt:
                    return [(identb[:], caust[:], ["const"])]
                return []

            def win_steps(c):
                out = []
                for kt in range(max(0, 4 * c - 4), 4 * c + 4):
                    qlo = max(4 * c, kt)
                    qhi = min(4 * c + 4, kt + 5)
                    if qhi > qlo:
                        out.append((kt, qlo, qhi))
                return out

            def win_bias(kt, qt):
                if kt == qt:
                    return [(identb[:], caust[:], ["const"])]
                if kt == qt - 4:
                    return [(identb[:], wint[:], ["const"])]
                return []
            for h in range(4):
                attn_core("nsel", lambda q0, q1, h=h: QT[0:96, h, q0:q1], lambda kt: ksT[0:96, kt * 128:(kt + 1) * 128], lambda kt: Vs[:, kt, 0:65], 0.125,
                          causal_steps, sel_bias, std_fin(oacc, h, "noacc", gate=gate_fn(1, h), accumulate=True), ["nQT", "nksT", "nVs"], ndummy=NDUMMY)
                attn_core("nwin", lambda q0, q1, h=h: QT[0:64, h, q0:q1], lambda kt: kwT[0:64, kt * 128:(kt + 1) * 128], lambda kt: Vw[:, kt, 0:65], 0.125,
                          win_steps, win_bias, std_fin(oacc, h, "noacc", gate=gate_fn(2, h), accumulate=True), ["nQT", "nkwT", "nVw"], ndummy=NDUMMY)
            for t in range(NT):
                S.op("pool", lambda e, t=t: e.tensor_copy(out=o_tm[:, t, :], in_=oacc[:, t, :]), reads=["noacc"], writes=["no_tm"])
            if l == 0:
                dump("o_nsa", o_tm[:], [128, NT, 256], ["no_tm"])
            finish_mixer(1, o_tm, "no_tm")
            S.barrier()

        mixed = A("mixed", [128, NT, D], BF16, OFF0 + 32 * KB)
        x_sb = A("x_sb", [128, NT, D], F32, OFF0 + 64 * KB)
        wo = A("wo", [128, KD, D], BF16, OFF0 + 128 * KB)
        for kh in range(4):
            S.dma(wo[:, kh * 2:(kh + 1) * 2, :], dr["w_out"][l].rearrange("(k p) c -> p k c", p=128)[:, kh * 2:(kh + 1) * 2, :], writes=["wo"], q="pool")
        for t in range(7, NT):
            S.dma(x_sb[:, t, :], x_src[t * 128:(t + 1) * 128, :], writes=["x_sb%d" % t])
        S.dma(g2[:], dr["norm2_g"][l:l + 1, :].to_broadcast([128, D]), writes=["g2"])
        o = OFF0 + 64 * KB
        Wg = [A("Wg%d" % i, [128, KD, 512], BF16, o + i * 8 * KB) for i in range(2)]; o += 16 * KB
        Wb = [A("Wb%d" % i, [128, 2, 512], BF16, o + i * 2 * KB) for i in range(2)]; o += 4 * KB
        sg = [A("sg%d" % i, [128, 512], F32, o + i * 2 * KB) for i in range(2)]; o += 4 * KB
        pr = [A("pr%d" % i, [128, 512], BF16, o + i * KB) for i in range(2)]; o += 2 * KB
        it = 0

        def load_gate_w(i):
            n_, cc_ = divmod(i, 2)
            b_ = i % 2
            gsrc = dr["w_in"][l, :, GATE0 + n_ * D + cc_ * 512:GATE0 + n_ * D + (cc_ + 1) * 512].rearrange("(k p) c -> p k c", p=128)
            for kh in range(2):
                S.dma(Wg[b_][:, kh * 4:(kh + 1) * 4, :], gsrc[:, kh * 4:(kh + 1) * 4, :], writes=["Wg%d" % b_], q="pool")
            S.dma(Wb[b_][:], dr["w_branch"][l, n_, :, cc_ * 512:(cc_ + 1) * 512].rearrange("(k p) c -> p k c", p=128), writes=["Wb%d" % b_], q="pool")
        load_gate_w(0)
        for n in range(4):
            for cc in range(2):
                b = it % 2
                it += 1
                if it < 8:
                    load_gate_w(it)
                for t in range(NT):
                    pg = ps_mm[t % 2]
                    pgk = "ps_mm%d" % (t % 2)
                    for k in range(KD):
                        S.op("pe", lambda e, k=k, pg=pg: e.matmul(pg[:, 0:512], lhsT=actT[:, k, t * 128:(t + 1) * 128], rhs=Wg[b][:, k, :], start=(k == 0), stop=(k == KD - 1)),
                             reads=["actT", "Wg%d" % b], writes=[pgk], sig=(k == KD - 1))
                    plf = ps_st[t % 2]
                    plk = "ps_st%d" % (t % 2)
                    for k in range(2):
                        S.op("pe", lambda e, k=k, plf=plf: e.matmul(plf[:, 0:512], lhsT=oT[:, n, k, t * 128:(t + 1) * 128], rhs=Wb[b][:, k, :], start=(k == 0), stop=(k == 1)),
                             reads=["oT", "Wb%d" % b], writes=[plk], sig=(k == 1))
                    s_ = sg[t % 2]
                    sk = "sg%d" % (t % 2)
                    S.op("act", lambda e, s_=s_, pg=pg: e.activation(out=s_[:], in_=pg[:, 0:512], func=AF.Sigmoid), reads=[pgk], writes=[sk])
                    dst = mixed[:, t, cc * 512:(cc + 1) * 512]
                    if n == 0:
                        S.op("dve", lambda e, s_=s_, plf=plf, dst=dst: e.tensor_tensor(out=dst, in0=s_[:], in1=plf[:, 0:512], op=ALU.mult), reads=[sk, plk], writes=["mixed"])
                    else:
                        p_ = pr[t % 2]
                        pk_ = "pr%d" % (t % 2)
                        S.op("dve", lambda e, s_=s_, plf=plf, p_=p_: e.tensor_tensor(out=p_[:], in0=s_[:], in1=plf[:, 0:512], op=ALU.mult), reads=[sk, plk], writes=[pk_])
                        S.op("pool", lambda e, p_=p_, dst=dst: e.tensor_tensor(out=dst, in0=dst, in1=p_[:], op=ALU.add), reads=[pk_, "mixed"], writes=["mixed"])
        if l == 0:
            dump("mixed", mixed[:], [128, NT, D], ["mixed"])
        S.barrier()
        for t in range(7):
            S.dma(x_sb[:, t, :], x_src[t * 128:(t + 1) * 128, :], writes=["x_sb%d" % t])
        for t in range(NT):
            transposes(actT[:, :, t * 128:(t + 1) * 128], [mixed[:, t, k * 128:(k + 1) * 128] for k in range(KD)], 128, ["mixed"], ["actT"], eng=("act" if t % 2 else "dve"))
        h2T = A("h2T", [128, KD, S_LEN], BF16, OFF0)
        aT = A("aT", [128, 8, S_LEN], BF16, ACT0)
        wup = A("wup", [128, KD, 1024], BF16, OFF0 + 32 * KB)
        wdn = A("wdn", [128, 8, D], BF16, OFF0 + 48 * KB)
        o = OFF0 + 144 * KB
        hb2 = [A("hb2_%d" % i, [128, D], BF16, o + i * 2 * KB) for i in range(2)]; o += 4 * KB
        sq4 = A("sq4", [128, D], BF16, o); o += 2 * KB
        nst4 = A("nst4", [128, 8], F32, o); o += 32
        usq = [A("usq%d" % i, [128, 512], F32, OFF0 + 128 * KB + i * 2 * KB) for i in range(2)]
        xkeys = ["x_sb%d" % t for t in range(NT)]
        def p3_mm(t):
            xk = "x_sb%d" % t
            for cc in range(2):
                pg = ps_mm[cc]
                for k in range(KD):
                    S.op("pe", lambda e, k=k, pg=pg, cc=cc: e.matmul(pg[:, 0:512], lhsT=actT[:, k, t * 128:(t + 1) * 128], rhs=wo[:, k, cc * 512:(cc + 1) * 512], start=(k == 0), stop=(k == KD - 1)),
                         reads=["actT", "wo"], writes=["ps_mm%d" % cc], sig=(k == KD - 1))
                dst = x_sb[:, t, cc * 512:(cc + 1) * 512]
                S.op("dve", lambda e, pg=pg, dst=dst: e.tensor_tensor(out=dst, in0=dst, in1=pg[:, 0:512], op=ALU.add), reads=["ps_mm%d" % cc, xk], writes=[xk])

        def p3_norm(t):
            xk = "x_sb%d" % t
            b = t % 2
            rmsnorm_tile(x_sb[:, t, :], D, g2[:], hb2[b][:], (nst4, sq4[:]), "n2", ["g2"], ["hb2_%d" % b], [xk])
            transposes(h2T[:, :, t * 128:(t + 1) * 128], [hb2[b][:, k * 128:(k + 1) * 128] for k in range(KD)], 128, ["hb2_%d" % b], ["h2T"], eng=("act" if t % 2 else "dve"))
        for step in range(NT + 2):
            if step < NT:
                p3_mm(step)
            if step >= 2:
                p3_norm(step - 2)
        if l == 0:
            dump("x_attn", x_sb[:], [128, NT, D], xkeys)
        ui = 0
        for g in range(4):
            usrc = dr["w_up"][l, :, g * 1024:(g + 1) * 1024].rearrange("(k p) c -> p k c", p=128)
            for kh in range(4):
                S.dma(wup[:, kh * 2:(kh + 1) * 2, :], usrc[:, kh * 2:(kh + 1) * 2, :], writes=["wup"] + (["mixed"] if g == 0 else []), q="pool")
            dsrc = dr["w_down"][l, g * 1024:(g + 1) * 1024, :].rearrange("(f p) c -> p f c", p=128)
            for kh in range(4):
                S.dma(wdn[:, kh * 2:(kh + 1) * 2, :], dsrc[:, kh * 2:(kh + 1) * 2, :], writes=["wdn"] + (["mixed"] if g == 0 else []), q="pool")
            for fc in range(8):
                for tc4 in range(4):
                    pu = ps_mm[ui % 2]
                    puk = "ps_mm%d" % (ui % 2)
                    for k in range(KD):
                        S.op("pe", lambda e, k=k, pu=pu, fc=fc, tc4=tc4: e.matmul(pu[:, 0:512], lhsT=wup[:, k, fc * 128:(fc + 1) * 128], rhs=h2T[:, k, tc4 * 512:(tc4 + 1) * 512],
                                                                              start=(k == 0), stop=(k == KD - 1)),
                             reads=["wup", "h2T"], writes=[puk], sig=(k == KD - 1))
                    u2 = usq[ui % 2]
                    uk = "usq%d" % (ui % 2)
                    S.op("act", lambda e, pu=pu, u2=u2: e.activation(out=u2[:], in_=pu[:, 0:512], func=AF.Square), reads=[puk], writes=[uk])
                    S.op("dve", lambda e, pu=pu, u2=u2, fc=fc, tc4=tc4: e.scalar_tensor_tensor(out=aT[:, fc, tc4 * 512:(tc4 + 1) * 512], in0=pu[:, 0:512], scalar=0.0, in1=u2[:],
                                                                                         op0=ALU.is_gt, op1=ALU.mult), reads=[puk, uk], writes=["aT", "actT"])
                    ui += 1
            for t in range(NT):
                for cc in range(2):
                    pd = ps_st[cc]
                    pdk = "ps_st%d" % cc
                    for fc in range(8):
                        S.op("pe", lambda e, fc=fc, pd=pd, cc=cc: e.matmul(pd[:, 0:512], lhsT=aT[:, fc, t * 128:(t + 1) * 128], rhs=wdn[:, fc, cc * 512:(cc + 1) * 512], start=(fc == 0), stop=(fc == 7)),
                             reads=["aT", "wdn"], writes=[pdk], sig=(fc == 7))
                    dst = x_sb[:, t, cc * 512:(cc + 1) * 512]
                    S.op("dve", lambda e, pd=pd, dst=dst: e.tensor_tensor(out=dst, in0=dst, in1=pd[:, 0:512], op=ALU.add), reads=[pdk, "x_sb"], writes=["x_sb"])
        if l == 0:
            dump("x_l0", x_sb[:], [128, NT, D], ["x_sb"])
        if l < depth - 1:
            for t in range(NT):
                S.dma(xs[t * 128:(t + 1) * 128, :], x_sb[:, t, :], reads=["x_sb"], writes=["xs"])
        else:
            S.dma(g2[:], dr["final_g"][0:1, :].to_broadcast([128, D]), reads=["g2"], writes=["g2"])
            fo = [A("fo%d" % i, [128, D], F32, OFF0 + 32 * KB + i * 4 * KB) for i in range(2)]
            for t in range(NT):
                b = t % 2
                rmsnorm_tile(x_sb[:, t, :], D, g2[:], fo[b][:], (nst4, sq4[:]), "n3", ["g2"], ["fo%d" % b], ["x_sb"])
                S.dma(y[t * 128:(t + 1) * 128, :], fo[b][:], reads=["fo%d" % b], writes=["y"])
        S.barrier()


_CACHE = {}


def prepare_inputs(inputs, b):
    m = {}
    m["x"] = np.ascontiguousarray(inputs["x"][b]).astype(np.float32, copy=False)
    m["pos"] = np.ascontiguousarray(np.asarray(inputs["positions"][b]).reshape(NT, 128).T).astype(np.int32)
    for k in ["norm1_g", "w_in", "mla_q_norm_g", "mla_w_uq", "mla_kv_norm_g", "mla_w_ukv", "nsa_cmp_w", "fox_f_bias",
              "w_branch", "w_out", "norm2_g", "w_up", "w_down"]:
        m[k] = np.ascontiguousarray(inputs[k], dtype=np.float32)
    m["nsa_cmp_pe"] = np.ascontiguousarray(np.asarray(inputs["nsa_cmp_pe"], dtype=np.float32).reshape(DEPTH, 2, 2048))
    m["final_g"] = np.ascontiguousarray(np.asarray(inputs["final_g"], dtype=np.float32).reshape(1, D))
    m.update(make_consts())
    return m


def kernel(**inputs):
    inputs = {k: np.asarray(v) for k, v in inputs.items()}
    if "nc" not in _CACHE:
        _CACHE["nc"] = build_program()[0]
    nc = _CACHE["nc"]
    B = inputs["x"].shape[0]
    in_maps = [prepare_inputs(inputs, b) for b in range(B)]
    res = run_bass_kernel_spmd(nc, in_maps, core_ids=list(range(B)))
    out = np.stack([np.asarray(r["y"]) for r in res.results], axis=0).astype(np.float32)
    return out
```

```python
import numpy as np
import concourse.bass as bass
import concourse.mybir as mybir
from concourse.bass_utils import run_bass_kernel_spmd

F32 = mybir.dt.float32
BF16 = mybir.dt.bfloat16
I32 = mybir.dt.int32
ALU = mybir.AluOpType
AF = mybir.ActivationFunctionType
AX = mybir.AxisListType

S_LEN = 2048
NT = 16
D = 1024
KD = 8
DEPTH = 2
NEGB = -30000.0
NDUMMY = 1
NWARM = 0
PI = float(np.pi)


class StopBuild(Exception):
    pass


class Sched:
    limit = None

    def __init__(self, nc, n_dma_sems=24):
        self.nc = nc
        self.eng = {"pe": nc.tensor, "dve": nc.vector, "act": nc.scalar, "pool": nc.gpsimd, "sp": nc.sync}
        self.sem = {k: nc.alloc_semaphore("s_" + k) for k in self.eng}
        self.cnt = {k: 0 for k in self.eng}
        self.dsem = [nc.alloc_semaphore("d%d" % i) for i in range(n_dma_sems)]
        self.dcnt = [0] * n_dma_sems
        half = n_dma_sems // 2
        self.dpool = {"sp": list(range(0, half)), "pool": list(range(half, n_dma_sems))}
        self.dnext = {"sp": 0, "pool": 0}
        self.seen = {k: {} for k in self.eng}
        self.lastw = {}
        self.readers = {}
        self.pending = {k: ([], []) for k in self.eng}
        self.semobj = {}
        self.all_tokens = {}
        self.nwaits = 0
        self.nops = 0

    def _tok_wait(self, e, tok):
        sid, val = tok
        if self.seen[e].get(sid, 0) >= val:
            return
        self.seen[e][sid] = val
        self.eng[e].wait_ge(self.semobj[sid], val)
        self.nwaits += 1

    def _deps(self, e, reads, writes):
        writes = list(writes) + [k for k in reads if k.startswith("ps_")]
        toks = []
        for k in reads:
            t = self.lastw.get(k)
            if t is not None:
                toks.append(t)
        for k in writes:
            t = self.lastw.get(k)
            if t is not None:
                toks.append(t)
            toks.extend(self.readers.get(k, ()))
        own = id(self.sem[e])
        for t in toks:
            if e == "pe" and t[0] == own:
                continue
            self._tok_wait(e, t)

    def _commit(self, tok, reads, writes):
        writes = list(writes) + [k for k in reads if k.startswith("ps_")]
        for k in writes:
            self.lastw[k] = tok
            self.readers[k] = []
        for k in reads:
            lst = self.readers.setdefault(k, [])
            lst.append(tok)
            if len(lst) > 16:
                best = {}
                for s, v in lst:
                    best[s] = max(best.get(s, 0), v)
                self.readers[k] = list(best.items())
        self.all_tokens[tok[0]] = max(self.all_tokens.get(tok[0], 0), tok[1])

    def op(self, e, fn, reads=(), writes=(), sig=True):
        self.nops += 1
        if self.limit is not None and self.nops > self.limit:
            raise StopBuild()
        self._deps(e, reads, writes)
        ins = fn(self.eng[e])
        if not sig:
            pr, pw = self.pending[e]
            pr.extend(reads)
            pw.extend(writes)
            return
        self.cnt[e] += 1
        s = self.sem[e]
        self.semobj[id(s)] = s
        ins.then_inc(s, 1)
        tok = (id(s), self.cnt[e])
        pr, pw = self.pending[e]
        self._commit(tok, list(reads) + pr, list(writes) + pw)
        self.pending[e] = ([], [])

    def dma(self, out, in_, reads=(), writes=(), q="sp", **kw):
        self.nops += 1
        if self.limit is not None and self.nops > self.limit:
            raise StopBuild()
        self._deps(q, reads, writes)
        lst = self.dpool[q]
        i = lst[self.dnext[q] % len(lst)]
        self.dnext[q] += 1
        s = self.dsem[i]
        self.semobj[id(s)] = s
        self.dcnt[i] += 16
        self.eng[q].dma_start(out=out, in_=in_, **kw).then_inc(s, 16)
        tok = (id(s), self.dcnt[i])
        self._commit(tok, reads, writes)
        return tok

    def barrier(self):
        for e in self.eng:
            for sid, val in list(self.all_tokens.items()):
                self._tok_wait(e, (sid, val))
        self.lastw = {}
        self.readers = {}

    def wait_all(self, e="sp"):
        for sid, val in list(self.all_tokens.items()):
            self._tok_wait(e, (sid, val))


def make_consts():
    c = {}
    k = np.arange(128)[:, None]
    q = np.arange(128)[None, :]
    c["c_ident"] = np.eye(128, dtype=np.float32)
    c["c_caust"] = np.where(q >= k, 0.0, NEGB).astype(np.float32)
    c["c_wint"] = np.where(k > q, 0.0, NEGB).astype(np.float32)
    c["c_causqk"] = np.where(q <= k, 0.0, -1e30).astype(np.float32)
    cc = np.arange(128)[:, None]
    t = np.arange(S_LEN)[None, :]
    cmpb = np.where((16 * cc + 31 <= t) & (cc < 127), 0.0, NEGB).astype(np.float32)
    c["c_cmpbias"] = cmpb
    j = np.arange(32)[:, None]
    kk = np.arange(S_LEN)[None, :]
    c["c_E"] = (kk // 64 == j).astype(np.float32)
    cs = np.arange(128) * 16
    sb = np.arange(32) * 64
    ov = np.maximum(np.minimum(cs[:, None] + 32, sb[None, :] + 64) - np.maximum(cs[:, None], sb[None, :]), 0) / 32.0
    ov[127, :] = 0.0
    c["c_ovl"] = ov.astype(np.float32)
    tt = np.arange(S_LEN)
    tb = tt[:, None] // 64
    sbi = np.arange(32)[None, :]
    forced = (sbi == 0) | (sbi == tb) | (sbi == tb - 1)
    causal = (sbi * 64) <= tt[:, None]
    base = np.where(causal, np.where(forced, 1e4, 0.0), -1e30).astype(np.float32)
    keep = np.where(causal & ~forced, 1.0, 0.0).astype(np.float32)
    c["c_fbase"] = base.reshape(NT, 128, 32).transpose(1, 0, 2).copy()
    c["c_fkeep"] = keep.reshape(NT, 128, 32).transpose(1, 0, 2).copy()
    theta = np.float32(500000.0)
    def inv(rot):
        return (theta ** (-np.arange(0, rot, 2, dtype=np.float32) / np.float32(rot))).astype(np.float32)
    invs = np.concatenate([inv(32), inv(16), inv(8)]).astype(np.float32)
    c["c_inv"] = np.tile(invs[None, :], (128, 1)).astype(np.float32)
    c["c_pow2"] = np.tile((2.0 ** -np.arange(32, dtype=np.float32))[None, :], (128, 1)).astype(np.float32)
    c["c_U"] = (k <= q).astype(np.float32)
    c["c_ones"] = np.ones((128, 128), np.float32)
    return c


CONST_SHAPES = {k: v.shape for k, v in make_consts().items()}

IN_SHAPES = {
    "x": ([S_LEN, D], F32), "pos": ([128, NT], I32),
    "norm1_g": ([DEPTH, D], F32), "w_in": ([DEPTH, D, 6616], F32),
    "mla_q_norm_g": ([DEPTH, 256], F32), "mla_w_uq": ([DEPTH, 256, 384], F32),
    "mla_kv_norm_g": ([DEPTH, 128], F32), "mla_w_ukv": ([DEPTH, 128, 512], F32),
    "nsa_cmp_pe": ([DEPTH, 2, 2048], F32), "nsa_cmp_w": ([DEPTH, 2, 2048, 64], F32),
    "fox_f_bias": ([DEPTH, 4], F32), "w_branch": ([DEPTH, 4, 256, D], F32),
    "w_out": ([DEPTH, D, D], F32), "norm2_g": ([DEPTH, D], F32),
    "w_up": ([DEPTH, D, 4096], F32), "w_down": ([DEPTH, 4096, D], F32), "final_g": ([1, D], F32),
}

MLA0 = 0
NSA0 = 416
FOX0 = NSA0 + 652
DSA0 = FOX0 + 772
GATE0 = DSA0 + 680


def build_program(depth=DEPTH, debug=None, mixers=("mla", "nsa", "fox", "dsa"), stop=None, limit=None):
    nc = bass.Bass("TRN2", target_bir_lowering=False)
    S = Sched(nc)
    S.limit = limit
    dbg = {}
    try:
        _build_body(nc, S, dbg, depth, debug, mixers, stop)
    except StopBuild:
        S.limit = None
        print("stopped at limit", limit, flush=True)
    S.wait_all("sp")
    print("program built: ops", S.nops, "waits", S.nwaits, flush=True)
    return nc, dbg


def _build_body(nc, S, dbg, depth, debug, mixers, stop):
    dr = {}
    for name, (shape, dt) in IN_SHAPES.items():
        dr[name] = nc.dram_tensor(name, shape, dt, kind="ExternalInput").ap()
    for name, shape in CONST_SHAPES.items():
        dr[name] = nc.dram_tensor(name, list(shape), F32, kind="ExternalInput").ap()
    y = nc.dram_tensor("y", [S_LEN, D], F32, kind="ExternalOutput").ap()
    xs = nc.dram_tensor("xs", [S_LEN, D], F32, kind="Internal").ap()

    BASE = 16512
    KB = 1024

    def A(name, shape, dt, off):
        assert off % 32 == 0, (name, off)
        nbytes = int(np.prod(shape[1:])) * (2 if dt == BF16 else 4)
        assert off + nbytes <= 207 * KB + 512, (name, off, nbytes)
        return nc.alloc_sbuf_tensor_at(name, list(shape), dt, offset=BASE + off)

    def dump(name, ap, shape, reads):
        if debug is None or name not in debug:
            return
        t = nc.dram_tensor("dbg_" + name, list(shape), ap.dtype if hasattr(ap, "dtype") else F32, kind="ExternalOutput").ap()
        S.dma(t, ap, reads=reads, writes=["dbg_" + name])
        dbg[name] = t

    o = 0
    def CA(name, shape, dt):
        nonlocal o
        t = A(name, shape, dt, o)
        o += ((int(np.prod(shape[1:])) * (2 if dt == BF16 else 4) + 31) // 32) * 32
        return t
    identb = CA("identb", [128, 128], BF16)
    identf = CA("identf", [128, 128], F32)
    caust = CA("caust", [128, 128], BF16)
    wint = CA("wint", [128, 128], BF16)
    causqk = CA("causqk", [128, 128], F32)
    cmpbias = CA("cmpbias", [128, S_LEN], BF16)
    Emat = CA("Emat", [32, S_LEN], BF16)
    ovl = CA("ovl", [128, 32], BF16)
    fbase = CA("fbase", [128, NT, 32], F32)
    fkeep = CA("fkeep", [128, NT, 32], F32)
    invt = CA("invt", [128, 28], F32)
    pow2 = CA("pow2", [128, 32], F32)
    Umat = CA("Umat", [128, 128], F32)
    onesf = CA("onesf", [128, 128], F32)
    trig = CA("trig", [128, NT, 56], F32)
    posi = CA("posi", [128, NT], I32)
    posf = CA("posf", [128, NT], F32)
    cst = CA("cst", [128, 8], F32)
    g2 = CA("g2", [128, D], F32)
    assert o <= 24 * KB, o
    for dst, src, q in [(identb, "c_ident", "pool"), (identf, "c_ident", "sp"), (caust, "c_caust", "pool"), (wint, "c_wint", "pool"),
                        (causqk, "c_causqk", "sp"), (cmpbias, "c_cmpbias", "pool"), (Emat, "c_E", "pool"), (ovl, "c_ovl", "pool"),
                        (fbase, "c_fbase", "sp"), (fkeep, "c_fkeep", "sp"), (invt, "c_inv", "sp"), (pow2, "c_pow2", "sp"),
                        (Umat, "c_U", "sp"), (onesf, "c_ones", "sp")]:
        S.dma(dst[:], dr[src], writes=["const"], q=q)
    S.dma(posi[:], dr["pos"], writes=["const"])
    S.op("dve", lambda e: e.memset(cst[:, 0:1], 1e-6), writes=["const"])
    S.op("dve", lambda e: e.memset(cst[:, 1:2], 1.0), writes=["const"])
    S.op("dve", lambda e: e.memset(cst[:, 2:3], 0.0), writes=["const"])

    ACT0 = 24 * KB
    actT = A("actT", [128, KD, S_LEN], BF16, ACT0)
    OFF0 = 56 * KB

    ps_st = [nc.alloc_psum_tensor("ps_st%d" % i, [128, 512], F32) for i in range(2)]
    ps_acc = [nc.alloc_psum_tensor("ps_acc%d" % i, [128, 4, 128], F32) for i in range(2)]
    ps_mm = [nc.alloc_psum_tensor("ps_mm%d" % i, [128, 512], F32) for i in range(2)]
    ps_trs = [nc.alloc_psum_tensor("ps_tr%d" % i, [128, 8, 128], BF16) for i in range(2)]
    ps_x = ps_trs[1][:].rearrange("p a b -> p (a b)").bitcast(F32)
    PX = "ps_tr1"

    with_scr = A("ropescr", [128, NT, 56], F32, OFF0)
    kfi = A("ropekfi", [128, NT, 56], I32, OFF0 + 4 * KB)
    kff = A("ropekff", [128, NT, 56], F32, OFF0 + 8 * KB)
    ang = A("ropeang", [128, NT, 56], F32, OFF0 + 12 * KB)
    S.op("dve", lambda e: e.tensor_copy(out=posf[:], in_=posi[:]), reads=["const"], writes=["posf"])
    S.op("dve", lambda e: e.tensor_tensor(out=ang[:, :, 0:28], in0=posf[:].unsqueeze(2).to_broadcast([128, NT, 28]),
                                          in1=invt[:].unsqueeze(1).to_broadcast([128, NT, 28]), op=ALU.mult), reads=["posf", "const"], writes=["ang"])
    S.op("dve", lambda e: e.tensor_scalar(out=ang[:, :, 28:56], in0=ang[:, :, 0:28], scalar1=PI / 2, scalar2=None, op0=ALU.add), reads=["ang"], writes=["ang"])
    S.op("dve", lambda e: e.tensor_scalar(out=with_scr[:], in0=ang[:], scalar1=float(1.0 / (2 * np.pi)), scalar2=None, op0=ALU.mult), reads=["ang"], writes=["rscr"])
    S.op("dve", lambda e: e.tensor_copy(out=kfi[:], in_=with_scr[:]), reads=["rscr"], writes=["kfi"])
    S.op("dve", lambda e: e.tensor_copy(out=kff[:], in_=kfi[:]), reads=["kfi"], writes=["kff"])
    C1 = 6.28125
    C2 = float(2 * np.pi - 6.28125)
    S.op("dve", lambda e: e.scalar_tensor_tensor(out=with_scr[:], in0=kff[:], scalar=-C1, in1=ang[:], op0=ALU.mult, op1=ALU.add), reads=["kff", "ang"], writes=["rscr"])
    S.op("dve", lambda e: e.scalar_tensor_tensor(out=ang[:], in0=kff[:], scalar=-C2, in1=with_scr[:], op0=ALU.mult, op1=ALU.add), reads=["kff", "rscr"], writes=["ang"])
    S.op("dve", lambda e: e.tensor_scalar(out=ang[:], in0=ang[:], scalar1=-3.1415925, scalar2=3.1415925, op0=ALU.max, op1=ALU.min), reads=["ang"], writes=["ang"])
    S.op("act", lambda e: e.activation(out=trig[:], in_=ang[:], func=AF.Sin, bias=cst[:, 2:3], scale=1.0), reads=["ang", "const"], writes=["trig"])
    dump("trig", trig[:], [128, NT, 56], ["trig"])
    SIN = lambda t, a, b: trig[:, t, a:b]
    COS = lambda t, a, b: trig[:, t, 28 + a:28 + b]
    S.barrier()
    if stop == "p0":
        print("stop", stop, S.nops)
        return

    def rmsnorm_tile(src_ap, n, g_ap, dst_ap, scr, tag, rkeys, wkeys, src_keys):
        ss, sq = scr
        S.op("act", lambda e: e.activation(out=sq, in_=src_ap, func=AF.Square, accum_out=ss[:, 0:1]), reads=src_keys, writes=[tag + "sq", tag + "ss"])
        S.op("act", lambda e: e.activation(out=ss[:, 1:2], in_=ss[:, 0:1], func=AF.Sqrt, bias=cst[:, 0:1], scale=1.0 / n), reads=[tag + "ss"], writes=[tag + "ss1"])
        S.op("dve", lambda e: e.reciprocal(out=ss[:, 2:3], in_=ss[:, 1:2]), reads=[tag + "ss1"], writes=[tag + "ss2"])
        S.op("dve", lambda e: e.scalar_tensor_tensor(out=dst_ap, in0=src_ap, scalar=ss[:, 2:3], in1=g_ap, op0=ALU.mult, op1=ALU.mult),
             reads=src_keys + [tag + "ss2"] + rkeys, writes=wkeys)

    tr_i = [0]

    def transposes(dst_ap, srcs, np_out, rkeys, wkeys, eng="act", bank=None):
        n = len(srcs)
        if bank is None:
            bank = tr_i[0] % 2
            tr_i[0] += 1
        ps_tr = ps_trs[bank]
        tk = "ps_tr%d" % bank
        for j, s_ap in enumerate(srcs):
            w = s_ap.shape[-1]
            S.op("pe", lambda e, j=j, s_ap=s_ap, w=w: e.transpose(out=ps_tr[0:w, j, :], in_=s_ap, identity=identb[:]),
                 reads=rkeys + ["const"], writes=[tk], sig=(j == n - 1))
        if eng == "act":
            S.op("act", lambda e: e.copy(out=dst_ap, in_=ps_tr[0:np_out, 0:n, :]), reads=[tk], writes=wkeys)
        else:
            S.op("dve", lambda e: e.tensor_copy(out=dst_ap, in_=ps_tr[0:np_out, 0:n, :]), reads=[tk], writes=wkeys)

    def rope(dst1, dst2, x1, x2, cos, sin, shape, tA, tB, rkeys, wkeys, tag):
        cb = cos.unsqueeze(1).to_broadcast(shape) if len(shape) == 3 else cos
        sb = sin.unsqueeze(1).to_broadcast(shape) if len(shape) == 3 else sin
        S.op("dve", lambda e: e.tensor_tensor(out=tA, in0=x1, in1=cb, op=ALU.mult), reads=rkeys + ["trig"], writes=[tag + "A"])
        S.op("dve", lambda e: e.tensor_tensor(out=tB, in0=x2, in1=sb, op=ALU.mult), reads=rkeys + ["trig"], writes=[tag + "B"])
        S.op("dve", lambda e: e.tensor_tensor(out=dst1, in0=tA, in1=tB, op=ALU.subtract), reads=[tag + "A", tag + "B"], writes=wkeys)
        S.op("dve", lambda e: e.tensor_tensor(out=tA, in0=x2, in1=cb, op=ALU.mult), reads=rkeys + ["trig"], writes=[tag + "A"])
        S.op("dve", lambda e: e.tensor_tensor(out=tB, in0=x1, in1=sb, op=ALU.mult), reads=rkeys + ["trig"], writes=[tag + "B"])
        S.op("dve", lambda e: e.tensor_tensor(out=dst2, in0=tA, in1=tB, op=ALU.add), reads=[tag + "A", tag + "B"], writes=wkeys)

    def load_w_cols(dst, l, segs, wkey):
        for (dc, sc, n) in segs:
            src = dr["w_in"][l, :, sc:sc + n].rearrange("(k p) c -> p k c", p=128)
            for kh in range(2):
                S.dma(dst[:, kh * 4:(kh + 1) * 4, dc:dc + n], src[:, kh * 4:(kh + 1) * 4, :], writes=[wkey], q="pool")

    def project_tile(t, Wz, col_groups, wkey):
        outs = []
        for gi, (c0, n) in enumerate(col_groups):
            if t % 2 == 0:
                pt, pk = ps_mm[gi], "ps_mm%d" % gi
            else:
                pt, pk = ps_st[gi], "ps_st%d" % gi
            for k in range(KD):
                S.op("pe", lambda e, k=k, pt=pt, c0=c0, n=n: e.matmul(pt[:, 0:n], lhsT=actT[:, k, t * 128:(t + 1) * 128], rhs=Wz[:, k, c0:c0 + n],
                                                                     start=(k == 0), stop=(k == KD - 1)),
                     reads=["actT", wkey], writes=[pk], sig=(k == KD - 1))
            outs.append((pt, pk))
        return outs

    def warm(n, bank=0):
        if not NDUMMY:
            return
        dout = ps_acc[bank][:].rearrange("p a b -> p (a b)")
        for _ in range(n):
            S.op("pe", lambda e: e.matmul(dout, lhsT=identb[:], rhs=cmpbias[:, 0:512], start=True, stop=True, skip_group_check=True), sig=False)

    PT = [A("PT%d" % i, [128, 512], BF16, OFF0 + 32 * KB + 13 * KB + i * KB) for i in range(4)]
    misc = A("misc", [128, 512], F32, OFF0 + 32 * KB + 17 * KB)
    WS0 = OFF0 + 32 * KB + 19 * KB
    st_ctr = [0]

    def attn_core(tag, qT, kT, vaug, scale, steps_fn, bias_fn, fin_fn, rkeys, nk=128, vw=65, range_bias_fn=None, ndummy=0, dummy_out=None):
        steps = []
        for c in range(4):
            ss = steps_fn(c)
            contrib = {}
            for (kt, qlo, qhi) in ss:
                for qt in range(qlo, qhi):
                    contrib.setdefault(qt, []).append(kt)
            for si, (kt, qlo, qhi) in enumerate(ss):
                steps.append((c, kt, qlo, qhi, contrib, si == len(ss) - 1, si == 0))

        def issue_st(step, slot):
            c, kt, qlo, qhi, contrib, last, _f = step
            st = ps_st[slot % 2]
            skey = "ps_st%d" % (slot % 2)
            n = (qhi - qlo) * 128
            bl = []
            for qt in range(qlo, qhi):
                for (bl_l, bl_r, bkeys) in bias_fn(kt, qt):
                    bl.append(((qt - qlo) * 128, (qt - qlo + 1) * 128, bl_l, bl_r, bkeys))
            if range_bias_fn is not None:
                for (bl_l, bl_r, bkeys) in range_bias_fn(kt, qlo, qhi):
                    bl.append((0, n, bl_l, bl_r, bkeys))
            S.op("pe", lambda e: e.matmul(st[0:nk, 0:n], lhsT=kT(kt), rhs=qT(qlo * 128, qhi * 128), start=True, stop=(len(bl) == 0), skip_group_check=True),
                 reads=rkeys, writes=[skey], sig=(not bl))
            for bi, (c0, c1, bl_l, bl_r, bkeys) in enumerate(bl):
                S.op("pe", lambda e, c0=c0, c1=c1, bl_l=bl_l, bl_r=bl_r, bi=bi: e.matmul(st[0:nk, c0:c1], lhsT=bl_l, rhs=bl_r, start=False, stop=(bi == len(bl) - 1), skip_group_check=True),
                     reads=bkeys, writes=[skey], sig=(bi == len(bl) - 1))
            for _ in range(ndummy):
                dout = ps_mm[0][:, 0:512] if dummy_out is None else dummy_out
                S.op("pe", lambda e: e.matmul(dout, lhsT=identb[:], rhs=cmpbias[:, 0:512], start=True, stop=True, skip_group_check=True), sig=False)

        def issue_rest(step, slot):
            c, kt, qlo, qhi, contrib, last, _f = step
            st = ps_st[slot % 2]
            skey = "ps_st%d" % (slot % 2)
            n = (qhi - qlo) * 128
            pi = slot % 4
            pt = PT[pi]
            S.op("act", lambda e: e.activation(out=pt[0:nk, 0:n], in_=st[0:nk, 0:n], func=AF.Exp, scale=scale), reads=[skey], writes=["PT%d" % pi])
            acc = ps_acc[c % 2]
            for qt in range(qlo, qhi):
                first = (step[6] and qt == qlo)
                S.op("pe", lambda e, qt=qt, first=first: e.matmul(acc[:, qt - 4 * c, 0:vw], lhsT=pt[0:nk, (qt - qlo) * 128:(qt - qlo + 1) * 128], rhs=vaug(kt),
                                                       start=first, stop=(kt == contrib[qt][-1]), skip_group_check=True),
                     reads=["PT%d" % pi] + rkeys, writes=["ps_acc%d" % (c % 2)], sig=(qt == qhi - 1))
            if last:
                fin_fn(c, acc, "ps_acc%d" % (c % 2))

        base = st_ctr[0]
        for i, step in enumerate(steps):
            if i == 0:
                issue_st(step, base)
            if i + 1 < len(steps):
                issue_st(steps[i + 1], base + i + 1)
            issue_rest(step, base + i)
        st_ctr[0] = base + len(steps)

    def causal_steps(c):
        return [(kt, max(4 * c, kt), 4 * c + 4) for kt in range(4 * c + 4)]

    def causal_bias(kt, qt):
        if kt == qt:
            return [(identb[:], caust[:], ["const"])]
        return []

    def std_fin(o_tm, h, okey, gate=None, accumulate=False, vdim=64):
        def fin(c, acc, akey):
            rec = misc[:, 64:68]
            S.op("dve", lambda e: e.tensor_scalar(out=rec, in0=acc[:, :, vdim], scalar1=1e-30, scalar2=None, op0=ALU.max), reads=[akey], writes=["rec"])
            S.op("dve", lambda e: e.reciprocal(out=misc[:, 68:72], in_=rec), reads=["rec"], writes=["rec2"])
            r2 = misc[:, 68:72]
            rk = ["rec2"]
            if gate is not None:
                gap, gkey = gate(c)
                S.op("dve", lambda e: e.tensor_tensor(out=misc[:, 72:76], in0=r2, in1=gap, op=ALU.mult), reads=["rec2", gkey], writes=["rec3"])
                r2 = misc[:, 72:76]
                rk = ["rec3"]
            dst = o_tm[:, 4 * c:4 * c + 4, h * 64:(h + 1) * 64]
            if not accumulate:
                S.op("dve", lambda e: e.tensor_tensor(out=dst, in0=acc[:, :, 0:64], in1=r2.unsqueeze(2).to_broadcast([128, 4, 64]), op=ALU.mult),
                     reads=[akey] + rk, writes=[okey])
            else:
                tmp = misc[:, 128:384].rearrange("p (a b) -> p a b", a=4)
                S.op("dve", lambda e: e.tensor_tensor(out=tmp, in0=acc[:, :, 0:64], in1=r2.unsqueeze(2).to_broadcast([128, 4, 64]), op=ALU.mult),
                     reads=[akey] + rk, writes=["fintmp"])
                S.op("pool", lambda e: e.tensor_tensor(out=dst, in0=dst, in1=tmp, op=ALU.add), reads=["fintmp", okey], writes=[okey])
        return fin

    oT = A("oT", [128, 4, 2, S_LEN], BF16, OFF0)

    def finish_mixer(mi, o_tm, okey):
        for t in range(NT):
            transposes(oT[:, mi, :, t * 128:(t + 1) * 128], [o_tm[:, t, 0:128], o_tm[:, t, 128:256]], 128, [okey], ["oT"], eng=("act" if t % 2 else "dve"))

    for l in range(depth):
        x_src = dr["x"] if l == 0 else xs
        gbc = A("gbc", [128, D], F32, OFF0)
        xt = [A("xt%d" % i, [128, D], F32, OFF0 + 4 * KB + i * 4 * KB) for i in range(4)]
        hb = [A("hb%d" % i, [128, D], BF16, OFF0 + 20 * KB + i * 2 * KB) for i in range(3)]
        sqs_ = [A("sqs%d" % i, [128, D], BF16, OFF0 + 26 * KB + i * 2 * KB) for i in range(2)]
        nst_ = [A("nst%d" % i, [128, 8], F32, OFF0 + 30 * KB + i * 32) for i in range(2)]
        S.dma(gbc[:], dr["norm1_g"][l:l + 1, :].to_broadcast([128, D]), writes=["gbc"])

        resident = l > 0
        if resident:
            for t in range(NT):
                S.dma(xs[t * 128:(t + 1) * 128, :], x_prev[:, t, :], reads=["x_sb"], writes=["xs"])

        def p1_load(t):
            if not resident:
                S.dma(xt[t % 4][:], x_src[t * 128:(t + 1) * 128, :], writes=["xt%d" % (t % 4)])

        def p1_norm(t):
            if resident:
                rmsnorm_tile(x_prev[:, t, :], D, gbc[:], hb[t % 3][:], (nst_[t % 2], sqs_[t % 2][:]), "n1_%d" % (t % 2), ["gbc"], ["hb%d" % (t % 3)], ["x_sb"])
            else:
                rmsnorm_tile(xt[t % 4][:], D, gbc[:], hb[t % 3][:], (nst_[t % 2], sqs_[t % 2][:]), "n1_%d" % (t % 2), ["gbc"], ["hb%d" % (t % 3)], ["xt%d" % (t % 4)])

        def p1_tr(t):
            transposes(actT[:, :, t * 128:(t + 1) * 128], [hb[t % 3][:, k * 128:(k + 1) * 128] for k in range(KD)], 128, ["hb%d" % (t % 3)], ["actT"], eng=("act" if t % 2 else "dve"))
        p1_load(0)
        p1_load(1)
        for step in range(NT + 1):
            if step + 2 < NT:
                p1_load(step + 2)
            if step < NT:
                p1_norm(step)
            if step >= 1:
                p1_tr(step - 1)
        if l == 0:
            dump("hT", actT[:], [128, KD, S_LEN], ["actT"])
        if resident:
            S.barrier()

        if stop == "p1":
            print("stop", stop, S.nops)
            return
        Wz = A("Wz", [128, KD, 776], BF16, OFF0 + 32 * KB)

        if "mla" in mixers:
            o = WS0
            cqnT = A("cqnT", [128, 2, S_LEN], BF16, o); o += 8 * KB
            ckvnT = A("ckvnT", [128, S_LEN], BF16, o); o += 4 * KB
            wuq = A("wuq", [128, 2, 384], BF16, o); o += 1536
            wukv = A("wukv", [128, 512], BF16, o); o += 1024
            QT = A("mQT", [96, 4, S_LEN], BF16, o); o += 16 * KB
            KTt = A("mKT", [96, 4, S_LEN], BF16, o); o += 16 * KB
            Vg = A("mV", [128, NT, 4, 66], BF16, o); o += NT * 4 * 66 * 2
            o = (o + 31) // 32 * 32
            qtm_ = [A("mqtm%d" % i, [128, 4, 96], BF16, o + i * 768) for i in range(2)]; o += 1536
            ktm_ = [A("mktm%d" % i, [128, 4, 96], BF16, o + i * 768) for i in range(3)]; o += 2304
            cqn_ = [A("mcqn%d" % i, [128, 256], BF16, o + i * 512) for i in range(2)]; o += 1024
            ckvn_ = [A("mckvn%d" % i, [128, 128], BF16, o + i * 256) for i in range(2)]; o += 512
            gq = A("mgq", [128, 256], F32, o); o += 1024
            gkv = A("mgkv", [128, 128], F32, o); o += 512
            rA_ = [A("mrA%d" % i, [128, 64], F32, o + i * 256) for i in range(2)]; o += 512
            rB_ = [A("mrB%d" % i, [128, 64], F32, o + i * 256) for i in range(2)]; o += 512
            sq2_ = [A("msq%d" % i, [128, 256], F32, o + i * 1024) for i in range(2)]; o += 2048
            nst2_ = [A("mnst%d" % i, [128, 8], F32, o + i * 32) for i in range(2)]; o += 64
            o_tm = A("mo_tm", [128, NT, 256], BF16, o); o += 8 * KB
            load_w_cols(Wz, l, [(0, MLA0, 416)], "Wz")
            S.dma(wuq[:], dr["mla_w_uq"][l].rearrange("(k p) c -> p k c", p=128), writes=["wuq"], q="pool")
            S.dma(wukv[:], dr["mla_w_ukv"][l], writes=["wukv"], q="pool")
            S.dma(gq[:], dr["mla_q_norm_g"][l:l + 1, :].to_broadcast([128, 256]), writes=["gq"])
            S.dma(gkv[:], dr["mla_kv_norm_g"][l:l + 1, :].to_broadcast([128, 128]), writes=["gkv"])
            S.op("pool", lambda e: e.memset(Vg[:, :, :, 64:65], 1.0), writes=["mV"])
            zs_ = [A("mzs%d" % i, [128, 416], F32, o + i * 1664) for i in range(2)]; o += 3328
            qs_ = [A("mqs%d" % i, [128, 384], F32, o + i * 1536) for i in range(2)]; o += 3072
            kvs_ = [A("mkvs%d" % i, [128, 512], F32, o + i * 2048) for i in range(2)]; o += 4096

            def mla_vars(t):
                p = t % 2
                return p, str(p), qtm_[p], ktm_[t % 3], cqn_[p], ckvn_[p], rA_[p], rB_[p], sq2_[p], nst2_[p]

            def mla_A1(t):
                p, sp, qtm, ktm, cqn, ckvn, rA, rB, sq2, nst2 = mla_vars(t)
                zs = zs_[p]
                zk = "mzs" + sp
                ((pz, pzk),) = project_tile(t, Wz, [(0, 416)], "Wz")
                S.op("act", lambda e: e.copy(out=zs[:], in_=pz[:, 0:416]), reads=[pzk], writes=[zk])

            def mla_A1b(t):
                p, sp, qtm, ktm, cqn, ckvn, rA, rB, sq2, nst2 = mla_vars(t)
                zs = zs_[p]
                zk = "mzs" + sp
                rmsnorm_tile(zs[:, 0:256], 256, gq[:], cqn[:], (nst2, sq2[:]), "mq" + sp, ["gq"], ["cqn" + sp], [zk])
                rmsnorm_tile(zs[:, 256:384], 128, gkv[:], ckvn[:], (nst2, sq2[:, 0:128]), "mq" + sp, ["gkv"], ["ckvn" + sp], [zk])
                rope(ktm[:, 0, 64:80], ktm[:, 0, 80:96], zs[:, 384:400], zs[:, 400:416], COS(t, 0, 16), SIN(t, 0, 16), [128, 16],
                     rA[:, 0:16], rB[:, 0:16], [zk], ["ktm%d" % (t % 3)], "mr" + sp)
                S.op("pool", lambda e: e.tensor_copy(out=ktm[:, 1:4, 64:96], in_=ktm[:, 0:1, 64:96].to_broadcast([128, 3, 32])), reads=["ktm%d" % (t % 3)], writes=["ktm%d" % (t % 3)])

            def mla_A2(t):
                p, sp, qtm, ktm, cqn, ckvn, rA, rB, sq2, nst2 = mla_vars(t)
                transposes(cqnT[:, :, t * 128:(t + 1) * 128], [cqn[:, 0:128], cqn[:, 128:256]], 128, ["cqn" + sp], ["cqnT%d" % p])
                transposes(ckvnT[:, t * 128:(t + 1) * 128].unsqueeze(1), [ckvn[:]], 128, ["ckvn" + sp], ["ckvnT%d" % p])
                pq, pqk = (ps_mm[1], "ps_mm1") if p == 0 else (ps_st[1], "ps_st1")
                for kk in range(2):
                    S.op("pe", lambda e, kk=kk: e.matmul(pq[:, 0:384], lhsT=cqnT[:, kk, t * 128:(t + 1) * 128], rhs=wuq[:, kk, :], start=(kk == 0), stop=(kk == 1)),
                         reads=["cqnT%d" % p, "wuq"], writes=[pqk], sig=(kk == 1))
                S.op("act", lambda e: e.copy(out=qs_[p][:], in_=pq[:, 0:384]), reads=[pqk], writes=["mqs" + sp])
                pkv = ps_acc[p][:].rearrange("p a b -> p (a b)")
                pkk = "ps_acc%d" % p
                S.op("pe", lambda e: e.matmul(pkv[:, 0:512], lhsT=ckvnT[:, t * 128:(t + 1) * 128], rhs=wukv[:], start=True, stop=True),
                     reads=["ckvnT%d" % p, "wukv"], writes=[pkk])
                S.op("act", lambda e: e.copy(out=kvs_[p][:], in_=pkv[:, 0:512]), reads=[pkk], writes=["mkvs" + sp])

            def mla_B(t):
                p, sp, qtm, ktm, cqn, ckvn, rA, rB, sq2, nst2 = mla_vars(t)
                pq3 = qs_[p][:].rearrange("p (h d) -> p h d", h=4)
                pqk = "mqs" + sp
                S.op("act", lambda e: e.copy(out=qtm[:, :, 0:64], in_=pq3[:, :, 0:64]), reads=[pqk], writes=["qtmN" + sp])
                rope(qtm[:, :, 64:80], qtm[:, :, 80:96], pq3[:, :, 64:80], pq3[:, :, 80:96], COS(t, 0, 16), SIN(t, 0, 16), [128, 4, 16],
                     rA[:].rearrange("p (h d) -> p h d", h=4), rB[:].rearrange("p (h d) -> p h d", h=4), [pqk], ["qtm" + sp], "mr" + sp)
                transposes(QT[:, :, t * 128:(t + 1) * 128], [qtm[:, h, :] for h in range(4)], 96, ["qtm" + sp, "qtmN" + sp], ["mQT"], eng="act")
                pkv3 = kvs_[p][:].rearrange("p (h d) -> p h d", h=4)
                pkk = "mkvs" + sp
                S.op("dve", lambda e: e.tensor_copy(out=ktm[:, :, 0:64], in_=pkv3[:, :, 0:64]), reads=[pkk], writes=["ktmN%d" % (t % 3)])
                S.op("dve", lambda e: e.tensor_copy(out=Vg[:, t, :, 0:64], in_=pkv3[:, :, 64:128]), reads=[pkk], writes=["mV"])
                transposes(KTt[:, :, t * 128:(t + 1) * 128], [ktm[:, h, :] for h in range(4)], 96, ["ktm%d" % (t % 3), "ktmN%d" % (t % 3)], ["mKT"])
            for step in range(NT + 2):
                if step < NT:
                    mla_A1(step)
                if 1 <= step <= NT:
                    mla_A2(step - 1)
                if step >= 2:
                    mla_B(step - 2)
                if step < NT:
                    mla_A1b(step)
            if stop == "mla_prep":
                dump("trig", QT[:], [96, 4, S_LEN], ["mQT"]) if False else None
                print("stop", stop, S.nops)
                return
            for h in range(4):
                attn_core("mla", lambda q0, q1, h=h: QT[:, h, q0:q1], lambda kt, h=h: KTt[:, h, kt * 128:(kt + 1) * 128],
                          lambda kt, h=h: Vg[:, kt, h, 0:65], float(96 ** -0.5), causal_steps, causal_bias,
                          std_fin(o_tm, h, "mo_tm"), ["mQT", "mKT", "mV"], ndummy=NDUMMY)
            if l == 0:
                dump("o_mla", o_tm[:], [128, NT, 256], ["mo_tm"])
            finish_mixer(0, o_tm, "mo_tm")
            S.barrier()

        if "fox" in mixers:
            o = WS0
            QT = A("fQT", [70, 4, S_LEN], BF16, o); o += 16 * KB
            KTt = A("fKT", [70, 4, S_LEN], BF16, o); o += 16 * KB
            Vg = A("fV", [128, NT, 4, 66], BF16, o); o += NT * 4 * 66 * 2
            o = (o + 31) // 32 * 32
            qk_tm = A("fqk_tm", [128, NT, 2, 4, 70], BF16, o); o += NT * 2 * 4 * 70 * 2
            o = (o + 31) // 32 * 32
            logf = A("flogf", [128, NT, 4], F32, o); o += 256
            cum = A("fcum", [128, NT, 4], F32, o); o += 256
            tot = A("ftot", [128, NT, 4], F32, o); o += 256
            car = A("fcar", [128, NT, 4], F32, o); o += 256
            fb = A("ffb", [128, 4], F32, o); o += 32
            ftmp = A("fftmp", [128, NT, 4], F32, o); o += 256
            chi = A("fchi", [128, NT, 4], BF16, o); o += 128
            cmid = A("fcmid", [128, NT, 4], BF16, o); o += 128
            clo = A("fclo", [128, NT, 4], BF16, o); o += 128
            r1 = A("fr1", [128, NT, 4], F32, o); o += 256
            r2_ = A("fr2", [128, NT, 4], F32, o); o += 256
            o_tm = A("fo_tm", [128, NT, 256], BF16, o); o += 8 * KB
            load_w_cols(Wz, l, [(0, FOX0, 772)], "Wz")
            S.dma(fb[:], dr["fox_f_bias"][l:l + 1, :].to_broadcast([128, 4]), writes=["ffb"])
            S.op("pool", lambda e: e.memset(Vg[:, :, :, 64:65], 1.0), writes=["fV"])
            S.op("pool", lambda e: e.memset(qk_tm[:, :, 0, :, 67:70], 1.0), writes=["fqk_tm"])
            S.op("pool", lambda e: e.memset(qk_tm[:, :, 1, :, 64:67], 1.0), writes=["fqk_tm"])
            for t in range(NT):
                (pa, pak), (pb, pbk) = project_tile(t, Wz, [(0, 512), (512, 260)], "Wz")
                warm(NWARM)
                pa4 = pa[:, 0:512].rearrange("p (a h d) -> p a h d", a=2, h=4)
                S.op("act", lambda e: e.copy(out=qk_tm[:, t, :, :, 0:64], in_=pa4), reads=[pak], writes=["fqk_tm"])
                S.op("dve", lambda e: e.tensor_copy(out=Vg[:, t, :, 0:64], in_=pb[:, 0:256].rearrange("p (h d) -> p h d", h=4)), reads=[pbk], writes=["fV"])
                S.op("dve", lambda e: e.tensor_tensor(out=ftmp[:, t, :], in0=pb[:, 256:260], in1=fb[:], op=ALU.add), reads=[pbk, "ffb"], writes=["fftmp"])
            S.op("act", lambda e: e.activation(out=logf[:], in_=ftmp[:], func=AF.Exp, scale=-1.0), reads=["fftmp"], writes=["flogf"])
            S.op("act", lambda e: e.activation(out=logf[:], in_=logf[:], func=AF.Ln, bias=cst[:, 1:2], scale=1.0), reads=["flogf", "const"], writes=["flogf"])
            S.op("dve", lambda e: e.tensor_scalar(out=logf[:], in0=logf[:], scalar1=-1.0, scalar2=None, op0=ALU.mult), reads=["flogf"], writes=["flogf"])
            pc = ps_x
            S.op("pe", lambda e: e.matmul(pc[:, 0:64], lhsT=Umat[:], rhs=logf[:].rearrange("p a b -> p (a b)"), start=True, stop=True), reads=["flogf", "const"], writes=[PX])
            S.op("dve", lambda e: e.tensor_copy(out=cum[:].rearrange("p a b -> p (a b)"), in_=pc[:, 0:64]), reads=[PX], writes=["fcum"])
            S.op("pe", lambda e: e.matmul(pc[:, 0:64], lhsT=onesf[:], rhs=logf[:].rearrange("p a b -> p (a b)"), start=True, stop=True), reads=["flogf", "const", "fcum"], writes=[PX])
            S.op("dve", lambda e: e.tensor_copy(out=tot[:].rearrange("p a b -> p (a b)"), in_=pc[:, 0:64]), reads=[PX], writes=["ftot"])
            S.op("dve", lambda e: e.memset(car[:, 0, :], 0.0), writes=["fcar"])
            for t in range(1, NT):
                S.op("dve", lambda e, t=t: e.tensor_tensor(out=car[:, t, :], in0=car[:, t - 1, :], in1=tot[:, t - 1, :], op=ALU.add), reads=["fcar", "ftot"], writes=["fcar"])
            S.op("dve", lambda e: e.tensor_tensor(out=cum[:], in0=cum[:], in1=car[:], op=ALU.add), reads=["fcum", "fcar"], writes=["fcum"])
            if l == 0:
                dump("fox_c", cum[:], [128, NT, 4], ["fcum"])
            S.op("dve", lambda e: e.tensor_scalar(out=r1[:], in0=cum[:], scalar1=8.0, scalar2=None, op0=ALU.mult), reads=["fcum"], writes=["fr1"])
            S.op("dve", lambda e: e.tensor_copy(out=chi[:], in_=r1[:]), reads=["fr1"], writes=["fchi"])
            S.op("dve", lambda e: e.tensor_tensor(out=r2_[:], in0=r1[:], in1=chi[:], op=ALU.subtract), reads=["fr1", "fchi"], writes=["fr2"])
            S.op("dve", lambda e: e.tensor_copy(out=cmid[:], in_=r2_[:]), reads=["fr2"], writes=["fcmid"])
            S.op("dve", lambda e: e.tensor_tensor(out=r1[:], in0=r2_[:], in1=cmid[:], op=ALU.subtract), reads=["fr2", "fcmid"], writes=["fr1"])
            S.op("dve", lambda e: e.tensor_copy(out=clo[:], in_=r1[:]), reads=["fr1"], writes=["fclo"])
            for j, part in enumerate([chi, cmid, clo]):
                S.op("dve", lambda e, j=j, part=part: e.tensor_copy(out=qk_tm[:, :, 0, :, 64 + j], in_=part[:]), reads=["fchi", "fcmid", "fclo"], writes=["fqk_tm"])
                S.op("dve", lambda e, j=j, part=part: e.tensor_scalar(out=qk_tm[:, :, 1, :, 67 + j], in0=part[:], scalar1=-1.0, scalar2=None, op0=ALU.mult),
                     reads=["fchi", "fcmid", "fclo"], writes=["fqk_tm"])
            for t in range(NT):
                transposes(QT[:, :, t * 128:(t + 1) * 128], [qk_tm[:, t, 0, h, :] for h in range(4)], 70, ["fqk_tm"], ["fQT"], eng="act")
                transposes(KTt[:, :, t * 128:(t + 1) * 128], [qk_tm[:, t, 1, h, :] for h in range(4)], 70, ["fqk_tm"], ["fKT"])
            for h in range(4):
                attn_core("fox", lambda q0, q1, h=h: QT[:, h, q0:q1], lambda kt, h=h: KTt[:, h, kt * 128:(kt + 1) * 128],
                          lambda kt, h=h: Vg[:, kt, h, 0:65], 0.125, causal_steps, causal_bias,
                          std_fin(o_tm, h, "fo_tm"), ["fQT", "fKT", "fV"], ndummy=NDUMMY)
            if l == 0:
                dump("o_fox", o_tm[:], [128, NT, 256], ["fo_tm"])
            finish_mixer(2, o_tm, "fo_tm")
            S.barrier()

        if "dsa" in mixers:
            o = WS0
            QK3 = A("dQK3", [128, 3, S_LEN], BF16, o); o += 12 * KB
            QT2 = QK3[:, 0:2, :]
            KT2 = QK3[:, 2, :]
            Vg = A("dV", [128, NT, 66], BF16, o); o += NT * 66 * 2
            o = (o + 31) // 32 * 32
            qki = A("dqki", [96, 4, S_LEN], BF16, o); o += 16 * KB
            qiT = qki[:, 0:3, :]
            kiT = qki[:, 3, :]
            score = [A("dscore%d" % i, [128, S_LEN], F32, o + i * 8 * KB) for i in range(4)]; o += 32 * KB
            Mb = [A("dMb0", [128, 4, 1536], BF16, o), A("dMb1", [128, 4, S_LEN], BF16, o + 12 * KB)]; o += 28 * KB
            o_c = [A("do_c%d" % i, [128, 4, 256], BF16, o + i * 2 * KB) for i in range(2)]; o += 4 * KB
            qk6 = A("dqk6", [128, 6, 64], BF16, o); o += 768
            qi9 = A("dqi9", [128, 12, 32], BF16, o); o += 768
            wq = A("dwq", [128, NT, 8], F32, o); o += 512
            rA = A("drA", [128, 9, 8], F32, o); o += 288
            rB = A("drB", [128, 9, 8], F32, o); o += 288
            bs = A("dbs", [128, 2, 64], F32, o); o += 512
            ki3 = A("dki3", [128, 96], BF16, o); o += 192
            junk1 = A("djunk1", [128, 16], BF16, o); o += 32
            wzo = OFF0 + 32 * KB
            Rb = [A("dR%d" % i, [128, 512], BF16, wzo + i * KB) for i in range(4)]
            diag = [A("ddiag%d" % i, [128, 8, 128], BF16, wzo + 4 * KB + i * 2 * KB) for i in range(2)]
            Rall = A("dRall", [128, S_LEN], BF16, wzo)
            load_w_cols(Wz, l, [(0, DSA0, 680)], "Wz")
            S.op("pool", lambda e: e.memset(Vg[:, :, 64:65], 1.0), writes=["dV"])
            S.op("pool", lambda e: e.memset(qi9[:], 0.0), writes=["dqi90", "dqi9N0"])
            qk6_ = [qk6, A("dqk6b", [128, 6, 64], BF16, o)]; o += 768
            qi9_ = [qi9, A("dqi9b", [128, 12, 32], BF16, o)]; o += 768
            ki3_ = [ki3, A("dki3b", [128, 96], BF16, o)]; o += 192
            rA_ = [rA, A("drAb", [128, 9, 8], F32, o)]; o += 288
            rB_ = [rB, A("drBb", [128, 9, 8], F32, o)]; o += 288
            S.op("pool", lambda e: e.memset(qi9_[1][:], 0.0), writes=["dqi91", "dqi9N1"])
            ptr0 = OFF0 + 32 * KB + 13 * KB
            zsA_ = [A("dzsA%d" % i, [128, 384], F32, ptr0 + i * 1536) for i in range(2)]
            zsB_ = [A("dzsB%d" % i, [128, 296], F32, ptr0 + 3072 + i * 1184) for i in range(2)]

            def dsa_A(t):
                p = t % 2
                sp = str(p)
                qk6, qi9, ki3, rA, rB = qk6_[p], qi9_[p], ki3_[p], rA_[p], rB_[p]
                (pa, pak0), (pb, pbk0) = project_tile(t, Wz, [(0, 384), (384, 296)], "Wz")
                warm(NWARM)
                za, zb = zsA_[p], zsB_[p]
                pak, pbk = "dzsA" + sp, "dzsB" + sp
                S.op("act", lambda e: e.copy(out=za[:], in_=pa[:, 0:384]), reads=[pak0], writes=[pak])
                S.op("act", lambda e: e.copy(out=zb[:], in_=pb[:, 0:296]), reads=[pbk0], writes=[pbk])
                pa3 = za[:, 0:320].rearrange("p (h d) -> p h d", h=5)
                rope(qk6[:, 0:5, 0:8], qk6[:, 0:5, 8:16], pa3[:, :, 0:8], pa3[:, :, 8:16], COS(t, 16, 24), SIN(t, 16, 24), [128, 5, 8],
                     rA[:, 0:5, :], rB[:, 0:5, :], [pak], ["dqk6" + sp], "dr" + sp)
                S.op("act", lambda e: e.copy(out=qk6[:, 0:5, 16:64], in_=pa3[:, :, 16:64]), reads=[pak], writes=["dqk6N" + sp])
                S.op("dve", lambda e: e.tensor_copy(out=Vg[:, t, 0:64], in_=za[:, 320:384]), reads=[pak], writes=["dV"])
                S.op("pool", lambda e: e.tensor_copy(out=qk6[:, 5, :], in_=qk6[:, 4, :]), reads=["dqk6" + sp, "dqk6N" + sp], writes=["dqk6D" + sp])
                pb3 = zb[:, 0:288].rearrange("p (h d) -> p h d", h=9)
                rope(qi9[:, 0:9, 0:4], qi9[:, 0:9, 4:8], pb3[:, :, 0:4], pb3[:, :, 4:8], COS(t, 24, 28), SIN(t, 24, 28), [128, 9, 4],
                     rA[:, :, 0:4], rB[:, :, 0:4], [pbk], ["dqi9" + sp], "dr" + sp)
                S.op("act", lambda e: e.copy(out=qi9[:, 0:9, 8:32], in_=pb3[:, :, 8:32]), reads=[pbk], writes=["dqi9N" + sp])
                S.op("dve", lambda e: e.tensor_copy(out=wq[:, t, :], in_=zb[:, 288:296]), reads=[pbk], writes=["dwq"])
                S.op("pool", lambda e: e.tensor_copy(out=ki3[:].rearrange("p (a b) -> p a b", a=3), in_=qi9[:, 8:9, :].to_broadcast([128, 3, 32])), reads=["dqi9" + sp, "dqi9N" + sp], writes=["dki3" + sp])

            def dsa_B(t):
                p = t % 2
                sp = str(p)
                qk6, qi9, ki3, rA, rB = qk6_[p], qi9_[p], ki3_[p], rA_[p], rB_[p]
                qf = qk6[:].rearrange("p a b -> p (a b)")
                transposes(QK3[:, :, t * 128:(t + 1) * 128], [qf[:, 0:128], qf[:, 128:256], qf[:, 256:384]], 128, ["dqk6" + sp, "dqk6N" + sp, "dqk6D" + sp], ["dQT", "dKT"], eng="act")
                qflat = qi9[:].rearrange("p a b -> p (a b)")
                transposes(qki[:, :, t * 128:(t + 1) * 128], [qflat[:, 0:96], qflat[:, 96:192], qflat[:, 192:288], ki3[:]], 96,
                           ["dqi9" + sp, "dqi9N" + sp, "dki3" + sp], ["dqiT", "dkiT"], eng="act")
                warm(NWARM)
            for step in range(NT + 1):
                if step < NT:
                    dsa_A(step)
                if step >= 1:
                    dsa_B(step - 1)
            S.barrier()
            NIT = 21
            ddum = ps_trs[0][:].rearrange("p a b -> p (a b)").bitcast(F32)
            pairs = [(2 * i, 2 * i + 1) for i in range(1, 8)]

            def sbuf(qt):
                i = qt % 4
                return score[i], "dscore%d" % i

            def dsa_scores(pair):
                for qt in pair:
                    L = (qt + 1) * 128
                    sc, sk = sbuf(qt)
                    dg = diag[qt % 2]
                    dk = "ddiag%d" % (qt % 2)
                    S.op("dve", lambda e: e.tensor_tensor(out=dg[:], in0=identb[:].unsqueeze(1).to_broadcast([128, 8, 128]),
                                                          in1=wq[:, qt, :].unsqueeze(2).to_broadcast([128, 8, 128]), op=ALU.mult), reads=["const", "dwq"], writes=[dk])
                    nkc = (L + 511) // 512
                    for kc in range(nkc):
                        k0 = kc * 512
                        n = min(512, L - k0)

                        def logit(h):
                            g, jj = divmod(h, 3)
                            pl = ps_mm[h % 2]
                            S.op("pe", lambda e: e.matmul(pl[:, 0:n], lhsT=qiT[32 * jj:32 * jj + 32, g, qt * 128:(qt + 1) * 128],
                                                          rhs=kiT[32 * jj:32 * jj + 32, k0:k0 + n], start=True, stop=True),
                                 reads=["dqiT", "dkiT"], writes=["ps_mm%d" % (h % 2)])
                            r = Rb[h % 4]
                            S.op("act", lambda e: e.activation(out=r[:, 0:n], in_=pl[:, 0:n], func=AF.Relu), reads=["ps_mm%d" % (h % 2)], writes=["dR%d" % (h % 4)])

                        def hsum(h):
                            r = Rb[h % 4]
                            S.op("pe", lambda e: e.matmul(ps_x[:, 0:n], lhsT=dg[:, h, :], rhs=r[:, 0:n], start=(h == 0), stop=(h == 7)),
                                 reads=["dR%d" % (h % 4), dk], writes=[PX])
                        logit(0)
                        for h in range(8):
                            if h + 1 < 8:
                                logit(h + 1)
                            hsum(h)
                            if h % 2 == 1 and NDUMMY:
                                S.op("pe", lambda e: e.matmul(ddum, lhsT=identb[:], rhs=cmpbias[:, 0:512], start=True, stop=True, skip_group_check=True), sig=False)
                        S.op("act", lambda e: e.copy(out=sc[:, k0:k0 + n], in_=ps_x[:, 0:n]), reads=[PX], writes=[sk])

            def dsa_bisect(pair, pi):
                st = {}
                for j, qt in enumerate(pair):
                    L = (qt + 1) * 128
                    sc, sk = sbuf(qt)
                    b = bs[:, j, :]
                    kx = "b%d_" % j
                    S.op("dve", lambda e, b=b, sc=sc, L=L: e.tensor_reduce(out=b[:, 0:1], in_=sc[:, 0:L], axis=AX.X, op=ALU.max, apply_absolute_value=True), reads=[sk], writes=[kx + "M"])
                    S.op("pool", lambda e, sc=sc, L=L: e.tensor_tensor(out=sc[:, L - 128:L], in0=sc[:, L - 128:L], in1=causqk[:], op=ALU.add), reads=[sk, "const", kx + "M"], writes=[sk])
                    S.op("pool", lambda e, b=b: e.tensor_scalar(out=b[:, 8:8 + NIT + 1], in0=pow2[:, 0:NIT + 1], scalar1=b[:, 0:1], scalar2=None, op0=ALU.mult), reads=[kx + "M", "const"], writes=[kx + "d"])
                    S.op("pool", lambda e, b=b: e.memset(b[:, 4:5], 0.0), writes=[kx + "mid0"])
                    st[qt] = (b, kx, sc, sk, L)
                for it in range(NIT):
                    for j, qt in enumerate(pair):
                        b, kx, sc, sk, L = st[qt]
                        m = b[:, 4 + (it % 2):5 + (it % 2)]
                        nm = b[:, 4 + ((it + 1) % 2):5 + ((it + 1) % 2)]
                        mk, nmk = kx + "mid%d" % (it % 2), kx + "mid%d" % ((it + 1) % 2)
                        if pi == len(pairs) - 1 and j == 1:
                            S.op("act", lambda e, m=m, sc=sc, L=L, b=b: e.activation(out=Rall[:, 0:L], in_=sc[:, 0:L], func=AF.Sign, bias=m, scale=-1.0, accum_out=b[:, 6:7]),
                                 reads=[sk, mk], writes=[kx + "cnt", "dR0", "dR1", "dR2", "dR3"])
                            S.op("pool", lambda e, b=b, it=it, L=L: e.tensor_scalar(out=b[:, 7:8], in0=b[:, 6:7], scalar1=float(L) - 510.5, scalar2=b[:, 8 + it:9 + it], op0=ALU.is_lt, op1=ALU.mult),
                                 reads=[kx + "cnt", kx + "d"], writes=[kx + "sel"])
                        else:
                            S.op("dve", lambda e, m=m, sc=sc, L=L, b=b, j=j: e.tensor_scalar(out=junk1[:, j:j + 1].to_broadcast([128, L]), in0=sc[:, 0:L], scalar1=m, scalar2=0.0, op0=ALU.is_ge, op1=ALU.add,
                                                                                    accum_out=b[:, 6:7]), reads=[sk, mk], writes=[kx + "cnt", kx + "junk"])
                            S.op("pool", lambda e, b=b, it=it: e.tensor_scalar(out=b[:, 7:8], in0=b[:, 6:7], scalar1=255.5, scalar2=b[:, 8 + it:9 + it], op0=ALU.is_ge, op1=ALU.mult),
                                 reads=[kx + "cnt", kx + "d"], writes=[kx + "sel"])
                        S.op("pool", lambda e, b=b, it=it, m=m, nm=nm: e.tensor_scalar(out=nm, in0=b[:, 7:8], scalar1=m, scalar2=b[:, 9 + it:10 + it], op0=ALU.add, op1=ALU.subtract),
                             reads=[kx + "sel", mk, kx + "d"], writes=[nmk])
                for j, qt in enumerate(pair):
                    b, kx, sc, sk, L = st[qt]
                    c = qt // 4
                    mb = Mb[c % 2]
                    fm = b[:, 4 + (NIT % 2):5 + (NIT % 2)]
                    S.op("pool", lambda e, b=b, fm=fm: e.tensor_tensor(out=b[:, 3:4], in0=fm, in1=b[:, 8 + NIT:9 + NIT], op=ALU.subtract), reads=[kx + "mid%d" % (NIT % 2), kx + "d"], writes=[kx + "thr"])
                    S.op("dve", lambda e, b=b, sc=sc, L=L, mb=mb, qt=qt, c=c: e.tensor_scalar(out=mb[:, qt - 4 * c, 0:L], in0=sc[:, 0:L], scalar1=b[:, 3:4], scalar2=NEGB, op0=ALU.is_lt, op1=ALU.mult),
                         reads=[sk, kx + "thr"], writes=["dMb%d" % (c % 2)])

            def dsa_attn(c):
                mb = Mb[c % 2]
                mbk = "dMb%d" % (c % 2)
                oc = o_c[c % 2]
                ock = "do_c%d" % (c % 2)

                def dsa_bias(kt, qt):
                    if qt < 2:
                        return causal_bias(kt, qt)
                    return [(mb[:, qt - 4 * c, kt * 128:(kt + 1) * 128], identb[:], [mbk, "const"])]

                def fin_for(h):
                    inner = std_fin(oc, h, ock)
                    return lambda cc, acc, akey: inner(0, acc, akey)
                for h in range(4):
                    p0 = (h % 2) * 64
                    attn_core("dsa", lambda q0, q1, h=h, p0=p0: QT2[p0:p0 + 64, h // 2, q0:q1], lambda kt, p0=p0: KT2[p0:p0 + 64, kt * 128:(kt + 1) * 128],
                              lambda kt: Vg[:, kt, 0:65], 0.125, lambda cc: causal_steps(c) if cc == c else [], dsa_bias,
                              fin_for(h), ["dQT", "dKT", "dV"], ndummy=NDUMMY, dummy_out=ddum)
                for tt in range(4):
                    t = 4 * c + tt
                    transposes(oT[:, 3, :, t * 128:(t + 1) * 128], [oc[:, tt, 0:128], oc[:, tt, 128:256]], 128, [ock], ["oT"], eng=("act" if tt % 2 else "dve"), bank=1)
                if l == 0:
                    dump("o_dsa%d" % c, oc[:], [128, 4, 256], [ock])

            dsa_scores(pairs[0])
            for i, pr_ in enumerate(pairs):
                if i + 1 < len(pairs):
                    dsa_scores(pairs[i + 1])
                dsa_bisect(pr_, i)
                if pr_[1] % 4 == 3:
                    dsa_attn(pr_[1] // 4)
            S.barrier()

        if "nsa" in mixers:
            o = WS0
            QT = A("nQT", [96, 4, S_LEN], BF16, o); o += 16 * KB
            k4T = A("nk4T", [96, 4, S_LEN], BF16, o); o += 16 * KB
            kcT = k4T[:, 0, :]
            ksT = k4T[:, 1, :]
            kwT = k4T[:, 2, :]
            vcT = k4T[:, 3, :]
            Vs = A("nVs", [128, NT, 66], BF16, o); o += NT * 66 * 2
            Vw = A("nVw", [128, NT, 66], BF16, o); o += NT * 66 * 2
            o = (o + 31) // 32 * 32
            Wk = A("nWk", [64, 32, 64], BF16, o); o += 4 * KB
            Wv = A("nWv", [64, 32, 64], BF16, o); o += 4 * KB
            Wkf = A("nWkf", [128, 16, 64], BF16, o); o += 2 * KB
            Wvf = A("nWvf", [128, 16, 64], BF16, o); o += 2 * KB
            pek = A("npek", [128, 16], BF16, o); o += 32
            pev = A("npev", [128, 16], BF16, o); o += 32
            kcmpT = A("nkcmpT", [64, 128], BF16, o); o += 256
            vcx = A("nvcx", [128, 97], BF16, o); o += 224
            imp = A("nimp", [128, NT, 32], F32, o); o += 2 * KB
            blkb = A("nblkb", [128, 96], BF16, o); o += 192
            gt = A("ngt", [128, NT, 12], F32, o); o += 768
            oacc = A("noacc", [128, NT, 256], F32, o); o += 16 * KB
            o_tm = A("no_tm", [128, NT, 256], BF16, o); o += 8 * KB
            q7 = A("nq7", [128, 7, 64], BF16, o); o += 896
            vc_tm = A("nvc_tm", [128, 64], BF16, o); o += 128
            rA = A("nrA", [128, 7, 8], F32, o); o += 224
            rB = A("nrB", [128, 7, 8], F32, o); o += 224
            m8 = A("nm8", [128, 8], F32, o); o += 32
            itmp = A("nitmp", [128, 4, 32], F32, o); o += 512
            n0 = NSA0
            load_w_cols(Wz, l, [(0, n0, 320), (320, n0 + 384, 64), (384, n0 + 512, 64),
                                (448, n0 + 320, 64), (512, n0 + 448, 64), (576, n0 + 576, 64), (640, n0 + 640, 12)], "Wz")
            S.dma(Wk[:], dr["nsa_cmp_w"][l, 0].rearrange("(l d) o -> d l o", d=64), writes=["nWk"], q="pool")
            S.dma(Wv[:], dr["nsa_cmp_w"][l, 1].rearrange("(l d) o -> d l o", d=64), writes=["nWv"], q="pool")
            S.dma(Wkf[:], dr["nsa_cmp_w"][l, 0].rearrange("(j p) o -> p j o", p=128), writes=["nWkf"], q="pool")
            S.dma(Wvf[:], dr["nsa_cmp_w"][l, 1].rearrange("(j p) o -> p j o", p=128), writes=["nWvf"], q="pool")
            S.dma(pek[:], dr["nsa_cmp_pe"][l, 0].rearrange("(j p) -> p j", p=128), writes=["npek"], q="pool", allow_slow_non_contiguous=True)
            S.dma(pev[:], dr["nsa_cmp_pe"][l, 1].rearrange("(j p) -> p j", p=128), writes=["npev"], q="pool", allow_slow_non_contiguous=True)
            S.dma(ksT[64:96, :], dr["c_E"], writes=["nksT"], q="pool")
            S.op("pool", lambda e: e.memset(blkb[:], 0.0), writes=["nblkb"])
            S.op("pool", lambda e: e.memset(Vs[:, :, 64:65], 1.0), writes=["nVs"])
            S.op("pool", lambda e: e.memset(Vw[:, :, 64:65], 1.0), writes=["nVw"])
            q7_ = [q7, A("nq7b", [128, 7, 64], BF16, o)]; o += 896
            vc_tm_ = [vc_tm, A("nvc_tmb", [128, 64], BF16, o)]; o += 128
            rA_ = [rA, A("nrAb", [128, 7, 8], F32, o)]; o += 224
            rB_ = [rB, A("nrBb", [128, 7, 8], F32, o)]; o += 224
            zsA_ = [A("nzsA%d" % i, [128, 448], F32, o + i * 1792) for i in range(2)]; o += 3584
            zsB_ = [A("nzsB%d" % i, [128, 204], F32, o + i * 832) for i in range(2)]; o += 1664

            def nsa_A(t):
                p = t % 2
                sp = str(p)
                q7, vc_tm, rA, rB = q7_[p], vc_tm_[p], rA_[p], rB_[p]
                (pa, pak0), (pb, pbk0) = project_tile(t, Wz, [(0, 448), (448, 204)], "Wz")
                warm(NWARM)
                za, zb = zsA_[p], zsB_[p]
                pak, pbk = "nzsA" + sp, "nzsB" + sp
                S.op("act", lambda e: e.copy(out=za[:], in_=pa[:, 0:448]), reads=[pak0], writes=[pak])
                S.op("act", lambda e: e.copy(out=zb[:], in_=pb[:, 0:204]), reads=[pbk0], writes=[pbk])
                pa3 = za[:].rearrange("p (h d) -> p h d", h=7)
                rope(q7[:, :, 0:8], q7[:, :, 8:16], pa3[:, :, 0:8], pa3[:, :, 8:16], COS(t, 16, 24), SIN(t, 16, 24), [128, 7, 8],
                     rA[:], rB[:], [pak], ["nq7" + sp], "nr" + sp)
                S.op("act", lambda e: e.copy(out=q7[:, :, 16:64], in_=pa3[:, :, 16:64]), reads=[pak], writes=["nq7N" + sp])
                S.op("dve", lambda e: e.tensor_copy(out=vc_tm[:], in_=zb[:, 0:64]), reads=[pbk], writes=["nvc_tm" + sp])
                S.op("dve", lambda e: e.tensor_copy(out=Vs[:, t, 0:64], in_=zb[:, 64:128]), reads=[pbk], writes=["nVs"])
                S.op("dve", lambda e: e.tensor_copy(out=Vw[:, t, 0:64], in_=zb[:, 128:192]), reads=[pbk], writes=["nVw"])
                S.op("act", lambda e: e.activation(out=gt[:, t, :], in_=zb[:, 192:204], func=AF.Sigmoid), reads=[pbk], writes=["ngt"])

            def nsa_B(t):
                p = t % 2
                sp = str(p)
                q7, vc_tm, rA, rB = q7_[p], vc_tm_[p], rA_[p], rB_[p]
                transposes(QT[0:64, :, t * 128:(t + 1) * 128], [q7[:, h, :] for h in range(4)], 64, ["nq7" + sp, "nq7N" + sp], ["nQT"], eng="act")
                ts_ = slice(t * 128, (t + 1) * 128)
                transposes(k4T[0:64, :, ts_], [q7[:, 4, :], q7[:, 5, :], q7[:, 6, :], vc_tm[:]], 64, ["nq7" + sp, "nq7N" + sp, "nvc_tm" + sp],
                           ["nkcT", "nksT", "nkwT", "nvcT"], eng="act")
                warm(NWARM)
            for step in range(NT + 1):
                if step < NT:
                    nsa_A(step)
                if step >= 1:
                    nsa_B(step - 1)
            pk = ps_mm[0]
            for li in range(32):
                S.op("pe", lambda e, li=li: e.matmul(pk[0:64, 0:127], lhsT=Wk[:, li, :], rhs=kcT[0:64, li:li + 16 * 126 + 1:16], start=(li == 0), stop=False),
                     reads=["nWk", "nkcT"], writes=["ps_mm0"], sig=False)
            for j in range(16):
                S.op("pe", lambda e, j=j: e.matmul(pk[0:64, 0:127], lhsT=Wkf[:, j, :], rhs=pek[:, j:j + 1].to_broadcast([128, 127]), start=False, stop=(j == 15)),
                     reads=["nWkf", "npek"], writes=["ps_mm0"], sig=(j == 15))
            S.op("dve", lambda e: e.tensor_copy(out=kcmpT[:, 0:127], in_=pk[0:64, 0:127]), reads=["ps_mm0"], writes=["nkcmpT"])
            pv = ps_mm[1]
            for li in range(32):
                S.op("pe", lambda e, li=li: e.matmul(pv[0:127, 0:64], lhsT=vcT[0:64, li:li + 16 * 126 + 1:16], rhs=Wv[:, li, :], start=(li == 0), stop=False),
                     reads=["nWv", "nvcT"], writes=["ps_mm1"], sig=False)
            for j in range(16):
                S.op("pe", lambda e, j=j: e.matmul(pv[0:127, 0:64], lhsT=pev[:, j:j + 1].to_broadcast([128, 127]), rhs=Wvf[:, j, :], start=False, stop=(j == 15)),
                     reads=["nWvf", "npev"], writes=["ps_mm1"], sig=(j == 15))
            S.op("pool", lambda e: e.memset(vcx[:, 64:65], 1.0), writes=["nvcx"])
            S.op("dve", lambda e: e.tensor_copy(out=vcx[0:127, 0:64], in_=pv[0:127, 0:64]), reads=["ps_mm1"], writes=["nvcx"])
            S.op("pool", lambda e: e.tensor_copy(out=vcx[:, 65:97], in_=ovl[:]), reads=["const"], writes=["nvcx"])
            if l == 0:
                dump("nsa_kcmpT", kcmpT[:], [64, 128], ["nkcmpT"])
                dump("nsa_vcx", vcx[:], [128, 97], ["nvcx"])

            def gate_fn(path, h):
                return lambda c: (gt[:, 4 * c:4 * c + 4, path * 4 + h], "ngt")
            for h in range(4):
                def cmp_fin(c, acc, akey, h=h):
                    rec = misc[:, 64:68]
                    S.op("dve", lambda e: e.tensor_scalar(out=rec, in0=acc[:, :, 64], scalar1=1e-30, scalar2=None, op0=ALU.max), reads=[akey], writes=["rec"])
                    S.op("dve", lambda e: e.reciprocal(out=misc[:, 68:72], in_=rec), reads=["rec"], writes=["rec2"])
                    S.op("dve", lambda e: e.tensor_tensor(out=misc[:, 72:76], in0=misc[:, 68:72], in1=gt[:, 4 * c:4 * c + 4, h], op=ALU.mult), reads=["rec2", "ngt"], writes=["rec3"])
                    dst = oacc[:, 4 * c:4 * c + 4, h * 64:(h + 1) * 64]
                    S.op("dve", lambda e: e.tensor_tensor(out=dst, in0=acc[:, :, 0:64], in1=misc[:, 72:76].unsqueeze(2).to_broadcast([128, 4, 64]), op=ALU.mult),
                         reads=[akey, "rec3"], writes=["noacc"])
                    idst = imp[:, 4 * c:4 * c + 4, :]
                    if h == 0:
                        S.op("dve", lambda e: e.tensor_tensor(out=idst, in0=acc[:, :, 65:97], in1=misc[:, 68:72].unsqueeze(2).to_broadcast([128, 4, 32]), op=ALU.mult),
                             reads=[akey, "rec2"], writes=["nimp"])
                    else:
                        S.op("dve", lambda e: e.tensor_tensor(out=itmp[:], in0=acc[:, :, 65:97], in1=misc[:, 68:72].unsqueeze(2).to_broadcast([128, 4, 32]), op=ALU.mult),
                             reads=[akey, "rec2"], writes=["nitmp"])
                        S.op("pool", lambda e: e.tensor_tensor(out=idst, in0=idst, in1=itmp[:], op=ALU.add), reads=["nitmp", "nimp"], writes=["nimp"])
                attn_core("ncmp", lambda q0, q1, h=h: QT[0:64, h, q0:q1], lambda kt: kcmpT[:, 0:127], lambda kt: vcx[0:127, 0:97], 0.125,
                          lambda c: [(0, 4 * c, 4 * c + 4)],
                          lambda kt, qt: [],
                          cmp_fin, ["nQT", "nkcmpT", "nvcx"], nk=127, vw=97,
                          range_bias_fn=lambda kt, qlo, qhi: [(identb[0:127, 0:127], cmpbias[0:127, qlo * 128:qhi * 128], ["const"])])
            S.op("dve", lambda e: e.tensor_tensor(out=imp[:], in0=imp[:], in1=fkeep[:], op=ALU.mult), reads=["nimp", "const"], writes=["nimp"])
            S.op("dve", lambda e: e.tensor_tensor(out=imp[:], in0=imp[:], in1=fbase[:], op=ALU.add), reads=["nimp", "const"], writes=["nimp"])
            for t in range(NT):
                S.op("dve", lambda e, t=t: e.max(out=m8[:], in_=imp[:, t, :]), reads=["nimp"], writes=["nm8"])
                S.op("dve", lambda e, t=t: e.tensor_scalar(out=blkb[:, 64:96], in0=imp[:, t, :], scalar1=m8[:, 7:8], scalar2=NEGB, op0=ALU.is_lt, op1=ALU.mult), reads=["nimp", "nm8"], writes=["nblkb"])
                bank = t % 2
                ptr = ps_trs[bank]
                tk = "ps_tr%d" % bank
                S.op("pe", lambda e, ptr=ptr: e.transpose(out=ptr[0:96, 0, :], in_=blkb[:], identity=identb[:]), reads=["nblkb", "const"], writes=[tk])
                S.op("act", lambda e, ptr=ptr, t=t: e.copy(out=QT[64:96, :, t * 128:(t + 1) * 128], in_=ptr[64:96, 0:1, :].to_broadcast([32, 4, 128])), reads=[tk], writes=["nQT"])
            if l == 0:
                dump("nsa_imp", imp[:], [128, NT, 32], ["nimp"])

            def sel_bias(kt, qt):
                if kt == qt:
                    return [(identb[:], caust[:], ["const"])]
                return []

            def win_steps(c):
                out = []
                for kt in range(max(0, 4 * c - 4), 4 * c + 4):
                    qlo = max(4 * c, kt)
                    qhi = min(4 * c + 4, kt + 5)
                    if qhi > qlo:
                        out.append((kt, qlo, qhi))
                return out

            def win_bias(kt, qt):
                if kt == qt:
                    return [(identb[:], caust[:], ["const"])]
                if kt == qt - 4:
                    return [(identb[:], wint[:], ["const"])]
                return []
            for h in range(4):
                attn_core("nsel", lambda q0, q1, h=h: QT[0:96, h, q0:q1], lambda kt: ksT[0:96, kt * 128:(kt + 1) * 128], lambda kt: Vs[:, kt, 0:65], 0.125,
                          causal_steps, sel_bias, std_fin(oacc, h, "noacc", gate=gate_fn(1, h), accumulate=True), ["nQT", "nksT", "nVs"], ndummy=NDUMMY)
                attn_core("nwin", lambda q0, q1, h=h: QT[0:64, h, q0:q1], lambda kt: kwT[0:64, kt * 128:(kt + 1) * 128], lambda kt: Vw[:, kt, 0:65], 0.125,
                          win_steps, win_bias, std_fin(oacc, h, "noacc", gate=gate_fn(2, h), accumulate=True), ["nQT", "nkwT", "nVw"], ndummy=NDUMMY)
            for t in range(NT):
                S.op("pool", lambda e, t=t: e.tensor_copy(out=o_tm[:, t, :], in_=oacc[:, t, :]), reads=["noacc"], writes=["no_tm"])
            if l == 0:
                dump("o_nsa", o_tm[:], [128, NT, 256], ["no_tm"])
            finish_mixer(1, o_tm, "no_tm")
            S.barrier()

        mixed = A("mixed", [128, NT, D], BF16, OFF0 + 32 * KB)
        x_sb = A("x_sb", [128, NT, D], F32, OFF0 + 64 * KB)
        wo = A("wo", [128, KD, D], BF16, OFF0 + 128 * KB)
        for kh in range(4):
            S.dma(wo[:, kh * 2:(kh + 1) * 2, :], dr["w_out"][l].rearrange("(k p) c -> p k c", p=128)[:, kh * 2:(kh + 1) * 2, :], writes=["wo"], q="pool")
        for t in range(7, NT):
            S.dma(x_sb[:, t, :], x_src[t * 128:(t + 1) * 128, :], writes=["x_sb%d" % t])
        S.dma(g2[:], dr["norm2_g"][l:l + 1, :].to_broadcast([128, D]), writes=["g2"])
        o = OFF0 + 64 * KB
        Wg = [A("Wg%d" % i, [128, KD, 512], BF16, o + i * 8 * KB) for i in range(2)]; o += 16 * KB
        Wb = [A("Wb%d" % i, [128, 2, 512], BF16, o + i * 2 * KB) for i in range(2)]; o += 4 * KB
        sg = [A("sg%d" % i, [128, 512], F32, o + i * 2 * KB) for i in range(2)]; o += 4 * KB
        pr = [A("pr%d" % i, [128, 512], BF16, o + i * KB) for i in range(2)]; o += 2 * KB
        it = 0

        def load_gate_w(i):
            n_, cc_ = divmod(i, 2)
            b_ = i % 2
            gsrc = dr["w_in"][l, :, GATE0 + n_ * D + cc_ * 512:GATE0 + n_ * D + (cc_ + 1) * 512].rearrange("(k p) c -> p k c", p=128)
            for kh in range(2):
                S.dma(Wg[b_][:, kh * 4:(kh + 1) * 4, :], gsrc[:, kh * 4:(kh + 1) * 4, :], writes=["Wg%d" % b_], q="pool")
            S.dma(Wb[b_][:], dr["w_branch"][l, n_, :, cc_ * 512:(cc_ + 1) * 512].rearrange("(k p) c -> p k c", p=128), writes=["Wb%d" % b_], q="pool")
        load_gate_w(0)
        for n in range(4):
            for cc in range(2):
                b = it % 2
                it += 1
                if it < 8:
                    load_gate_w(it)
                for t in range(NT):
                    pg = ps_mm[t % 2]
                    pgk = "ps_mm%d" % (t % 2)
                    for k in range(KD):
                        S.op("pe", lambda e, k=k, pg=pg: e.matmul(pg[:, 0:512], lhsT=actT[:, k, t * 128:(t + 1) * 128], rhs=Wg[b][:, k, :], start=(k == 0), stop=(k == KD - 1)),
                             reads=["actT", "Wg%d" % b], writes=[pgk], sig=(k == KD - 1))
                    plf = ps_st[t % 2]
                    plk = "ps_st%d" % (t % 2)
                    for k in range(2):
                        S.op("pe", lambda e, k=k, plf=plf: e.matmul(plf[:, 0:512], lhsT=oT[:, n, k, t * 128:(t + 1) * 128], rhs=Wb[b][:, k, :], start=(k == 0), stop=(k == 1)),
                             reads=["oT", "Wb%d" % b], writes=[plk], sig=(k == 1))
                    s_ = sg[t % 2]
                    sk = "sg%d" % (t % 2)
                    S.op("act", lambda e, s_=s_, pg=pg: e.activation(out=s_[:], in_=pg[:, 0:512], func=AF.Sigmoid), reads=[pgk], writes=[sk])
                    dst = mixed[:, t, cc * 512:(cc + 1) * 512]
                    if n == 0:
                        S.op("dve", lambda e, s_=s_, plf=plf, dst=dst: e.tensor_tensor(out=dst, in0=s_[:], in1=plf[:, 0:512], op=ALU.mult), reads=[sk, plk], writes=["mixed"])
                    else:
                        p_ = pr[t % 2]
                        pk_ = "pr%d" % (t % 2)
                        S.op("dve", lambda e, s_=s_, plf=plf, p_=p_: e.tensor_tensor(out=p_[:], in0=s_[:], in1=plf[:, 0:512], op=ALU.mult), reads=[sk, plk], writes=[pk_])
                        S.op("pool", lambda e, p_=p_, dst=dst: e.tensor_tensor(out=dst, in0=dst, in1=p_[:], op=ALU.add), reads=[pk_, "mixed"], writes=["mixed"])
        if l == 0:
            dump("mixed", mixed[:], [128, NT, D], ["mixed"])
        S.barrier()
        for t in range(7):
            S.dma(x_sb[:, t, :], x_src[t * 128:(t + 1) * 128, :], writes=["x_sb%d" % t])
        for t in range(NT):
            transposes(actT[:, :, t * 128:(t + 1) * 128], [mixed[:, t, k * 128:(k + 1) * 128] for k in range(KD)], 128, ["mixed"], ["actT"], eng=("act" if t % 2 else "dve"))
        h2T = A("h2T", [128, KD, S_LEN], BF16, OFF0)
        aT = A("aT", [128, 8, S_LEN], BF16, ACT0)
        wup = A("wup", [128, KD, 1024], BF16, OFF0 + 32 * KB)
        wdn = A("wdn", [128, 8, D], BF16, OFF0 + 48 * KB)
        o = OFF0 + 144 * KB
        hb2 = [A("hb2_%d" % i, [128, D], BF16, o + i * 2 * KB) for i in range(2)]; o += 4 * KB
        sq4 = A("sq4", [128, D], BF16, o); o += 2 * KB
        nst4 = A("nst4", [128, 8], F32, o); o += 32
        usq = [A("usq%d" % i, [128, 512], F32, OFF0 + 128 * KB + i * 2 * KB) for i in range(2)]
        xkeys = ["x_sb%d" % t for t in range(NT)]
        def p3_mm(t):
            xk = "x_sb%d" % t
            for cc in range(2):
                pg = ps_mm[cc]
                for k in range(KD):
                    S.op("pe", lambda e, k=k, pg=pg, cc=cc: e.matmul(pg[:, 0:512], lhsT=actT[:, k, t * 128:(t + 1) * 128], rhs=wo[:, k, cc * 512:(cc + 1) * 512], start=(k == 0), stop=(k == KD - 1)),
                         reads=["actT", "wo"], writes=["ps_mm%d" % cc], sig=(k == KD - 1))
                dst = x_sb[:, t, cc * 512:(cc + 1) * 512]
                S.op("dve", lambda e, pg=pg, dst=dst: e.tensor_tensor(out=dst, in0=dst, in1=pg[:, 0:512], op=ALU.add), reads=["ps_mm%d" % cc, xk], writes=[xk])

        def p3_norm(t):
            xk = "x_sb%d" % t
            b = t % 2
            rmsnorm_tile(x_sb[:, t, :], D, g2[:], hb2[b][:], (nst4, sq4[:]), "n2", ["g2"], ["hb2_%d" % b], [xk])
            transposes(h2T[:, :, t * 128:(t + 1) * 128], [hb2[b][:, k * 128:(k + 1) * 128] for k in range(KD)], 128, ["hb2_%d" % b], ["h2T"], eng=("act" if t % 2 else "dve"))
        for step in range(NT + 2):
            if step < NT:
                p3_mm(step)
            if step >= 2:
                p3_norm(step - 2)
        if l == 0:
            dump("x_attn", x_sb[:], [128, NT, D], xkeys)
        ui = 0
        for g in range(4):
            usrc = dr["w_up"][l, :, g * 1024:(g + 1) * 1024].rearrange("(k p) c -> p k c", p=128)
            for kh in range(4):
                S.dma(wup[:, kh * 2:(kh + 1) * 2, :], usrc[:, kh * 2:(kh + 1) * 2, :], writes=["wup"] + (["mixed"] if g == 0 else []), q="pool")
            dsrc = dr["w_down"][l, g * 1024:(g + 1) * 1024, :].rearrange("(f p) c -> p f c", p=128)
            for kh in range(4):
                S.dma(wdn[:, kh * 2:(kh + 1) * 2, :], dsrc[:, kh * 2:(kh + 1) * 2, :], writes=["wdn"] + (["mixed"] if g == 0 else []), q="pool")
            for fc in range(8):
                for tc4 in range(4):
                    pu = ps_mm[ui % 2]
                    puk = "ps_mm%d" % (ui % 2)
                    for k in range(KD):
                        S.op("pe", lambda e, k=k, pu=pu, fc=fc, tc4=tc4: e.matmul(pu[:, 0:512], lhsT=wup[:, k, fc * 128:(fc + 1) * 128], rhs=h2T[:, k, tc4 * 512:(tc4 + 1) * 512],
                                                                              start=(k == 0), stop=(k == KD - 1)),
                             reads=["wup", "h2T"], writes=[puk], sig=(k == KD - 1))
                    u2 = usq[ui % 2]
                    uk = "usq%d" % (ui % 2)
                    S.op("act", lambda e, pu=pu, u2=u2: e.activation(out=u2[:], in_=pu[:, 0:512], func=AF.Square), reads=[puk], writes=[uk])
                    S.op("dve", lambda e, pu=pu, u2=u2, fc=fc, tc4=tc4: e.scalar_tensor_tensor(out=aT[:, fc, tc4 * 512:(tc4 + 1) * 512], in0=pu[:, 0:512], scalar=0.0, in1=u2[:],
                                                                                         op0=ALU.is_gt, op1=ALU.mult), reads=[puk, uk], writes=["aT", "actT"])
                    ui += 1
            for t in range(NT):
                for cc in range(2):
                    pd = ps_st[cc]
                    pdk = "ps_st%d" % cc
                    for fc in range(8):
                        S.op("pe", lambda e, fc=fc, pd=pd, cc=cc: e.matmul(pd[:, 0:512], lhsT=aT[:, fc, t * 128:(t + 1) * 128], rhs=wdn[:, fc, cc * 512:(cc + 1) * 512], start=(fc == 0), stop=(fc == 7)),
                             reads=["aT", "wdn"], writes=[pdk], sig=(fc == 7))
                    dst = x_sb[:, t, cc * 512:(cc + 1) * 512]
                    S.op("dve", lambda e, pd=pd, dst=dst: e.tensor_tensor(out=dst, in0=dst, in1=pd[:, 0:512], op=ALU.add), reads=[pdk, "x_sb"], writes=["x_sb"])
        if l == 0:
            dump("x_l0", x_sb[:], [128, NT, D], ["x_sb"])
        if l < depth - 1:
            x_prev = x_sb
        else:
            S.dma(g2[:], dr["final_g"][0:1, :].to_broadcast([128, D]), reads=["g2"], writes=["g2"])
            fo = [A("fo%d" % i, [128, D], F32, OFF0 + 32 * KB + i * 4 * KB) for i in range(2)]
            for t in range(NT):
                b = t % 2
                rmsnorm_tile(x_sb[:, t, :], D, g2[:], fo[b][:], (nst4, sq4[:]), "n3", ["g2"], ["fo%d" % b], ["x_sb"])
                S.dma(y[t * 128:(t + 1) * 128, :], fo[b][:], reads=["fo%d" % b], writes=["y"])
        S.barrier()


_CACHE = {}


def prepare_inputs(inputs, b):
    m = {}
    m["x"] = np.ascontiguousarray(inputs["x"][b]).astype(np.float32, copy=False)
    m["pos"] = np.ascontiguousarray(np.asarray(inputs["positions"][b]).reshape(NT, 128).T).astype(np.int32)
    for k in ["norm1_g", "w_in", "mla_q_norm_g", "mla_w_uq", "mla_kv_norm_g", "mla_w_ukv", "nsa_cmp_w", "fox_f_bias",
              "w_branch", "w_out", "norm2_g", "w_up", "w_down"]:
        m[k] = np.ascontiguousarray(inputs[k], dtype=np.float32)
    m["nsa_cmp_pe"] = np.ascontiguousarray(np.asarray(inputs["nsa_cmp_pe"], dtype=np.float32).reshape(DEPTH, 2, 2048))
    m["final_g"] = np.ascontiguousarray(np.asarray(inputs["final_g"], dtype=np.float32).reshape(1, D))
    m.update(make_consts())
    return m


def kernel(**inputs):
    inputs = {k: np.asarray(v) for k, v in inputs.items()}
    if "nc" not in _CACHE:
        _CACHE["nc"] = build_program()[0]
    nc = _CACHE["nc"]
    B = inputs["x"].shape[0]
    in_maps = [prepare_inputs(inputs, b) for b in range(B)]
    res = run_bass_kernel_spmd(nc, in_maps, core_ids=list(range(B)))
    out = np.stack([np.asarray(r["y"]) for r in res.results], axis=0).astype(np.float32)
    return out
```
